# Optimizing a Trainium2 kernel written in Bass

```python
import jax, jax.numpy as jnp
from jax import lax
import numpy as np

D_MODEL = 1024
BATCH = 8
SEQ = 4096
DEPTH = 2

CHUNK = 64
PLE_DIM = 256
N_EVEN = (DEPTH + 1) // 2
N_ODD = DEPTH // 2
CONV_WIDTH = 4
NORM_EPS = 1e-6

RWKV_WIDTH = D_MODEL
RWKV_HEAD_DIM = 64
RWKV_HEADS = RWKV_WIDTH // RWKV_HEAD_DIM
DECAY_RANK = 64
ICL_RANK = 64
DECAY_SCALE = 0.6065306597126334
RWKV_GN_EPS = 64e-5

MLSTM_WIDTH = D_MODEL
MLSTM_HEADS = 4
MLSTM_HEAD_DIM = MLSTM_WIDTH // MLSTM_HEADS
QKV_BLOCK = 4
MLSTM_EPS = 1e-6
MLSTM_LN_EPS = 1e-5

AB_SPLIT_SIZES = (RWKV_WIDTH, RWKV_WIDTH, RWKV_WIDTH, DECAY_RANK, ICL_RANK, MLSTM_WIDTH, RWKV_WIDTH, MLSTM_WIDTH)
AB_IN_WIDTH = 4 * RWKV_WIDTH + DECAY_RANK + ICL_RANK + 2 * MLSTM_WIDTH
AB_OUT_WIDTH = RWKV_WIDTH + MLSTM_WIDTH

LRU_WIDTH = 2 * D_MODEL
LRU_BLOCKS = 16
LRU_C = 8.0

kernel_name = 'rwkv7_mlstm_rglru_hybrid_trunk'


def rmsnorm(x, g):
    xf = x.astype(jnp.float32)
    y = xf * lax.rsqrt(jnp.mean(xf * xf, axis=-1, keepdims=True) + NORM_EPS)
    return (y * g.astype(jnp.float32)).astype(x.dtype)


def head_layernorm(x, eps):
    mu = jnp.mean(x, axis=-1, keepdims=True)
    var = jnp.mean(jnp.square(x - mu), axis=-1, keepdims=True)
    return (x - mu) * lax.rsqrt(var + eps)


def causal_conv(x, w, b):
    c = x.shape[-1]
    y = lax.conv_general_dilated(x, w[:, None, :].astype(x.dtype), window_strides=(1,),
                                 padding=[(CONV_WIDTH - 1, 0)],
                                 dimension_numbers=('NWC', 'WIO', 'NWC'),
                                 feature_group_count=c)
    return y + b


def token_shift_mix(u, mu):
    prev = jnp.pad(u, ((0, 0), (1, 0), (0, 0)))[:, :-1]
    return u + (prev - u) * mu


def block_diag_linear(x, w):
    nb, bi, bo = w.shape
    xs = x.reshape(x.shape[:-1] + (nb, bi))
    return jnp.einsum('...gi,gio->...go', xs, w).reshape(x.shape[:-1] + (nb * bo,))


def rwkv7_recurrence(r, w, k, v, kk, a):
    b_, _, h_, n_ = r.shape
    xs = tuple(jnp.moveaxis(t, 1, 0) for t in (r, w, k, v, kk, a))

    def step(state, inp):
        r_t, w_t, k_t, v_t, kk_t, a_t = inp
        sa = jnp.einsum('bhvk,bhk->bhv', state, kk_t)
        state = (state * w_t[:, :, None, :]
                 - sa[..., None] * (kk_t * a_t)[:, :, None, :]
                 + v_t[..., None] * k_t[:, :, None, :])
        y_t = jnp.einsum('bhvk,bhk->bhv', state, r_t)
        return state, y_t

    s0 = jnp.zeros((b_, h_, n_, n_), jnp.float32)
    _, y = lax.scan(step, s0, xs)
    return jnp.moveaxis(y, 0, 1)


def mlstm_chunkwise(q, k, v, li, lf):
    b_, s_, h_, d_ = q.shape
    nc = s_ // CHUNK

    def to_chunks(t):
        return t.reshape(b_, nc, CHUNK, h_, d_).transpose(1, 0, 3, 2, 4)

    def gate_chunks(t):
        return t.reshape(b_, nc, CHUNK, h_).transpose(1, 0, 3, 2)

    causal = jnp.tril(jnp.ones((CHUNK, CHUNK), dtype=bool))

    def step(carry, inp):
        c_mat, n_vec, m = carry
        q_c, k_c, v_c, li_c, lf_c = inp
        bcum = jnp.cumsum(lf_c, axis=-1)
        log_d = jnp.where(causal, bcum[..., :, None] - bcum[..., None, :] + li_c[..., None, :], -jnp.inf)
        log_inter = bcum + m[..., None]
        m_t = jnp.maximum(jnp.max(log_d, axis=-1), log_inter)
        s = jnp.einsum('bhtd,bhsd->bhts', q_c, k_c) * jnp.exp(log_d - m_t[..., None])
        w_inter = jnp.exp(log_inter - m_t)
        num = (jnp.einsum('bhts,bhsd->bhtd', s, v_c)
               + w_inter[..., None] * jnp.einsum('bhvk,bhtk->bhtv', c_mat, q_c))
        den = jnp.sum(s, axis=-1) + w_inter * jnp.einsum('bhk,bhtk->bht', n_vec, q_c)
        h = num / (jnp.maximum(jnp.abs(den), jnp.exp(-m_t)) + MLSTM_EPS)[..., None]
        b_last = bcum[..., -1]
        log_g = b_last[..., None] - bcum + li_c
        m_new = jnp.maximum(b_last + m, jnp.max(log_g, axis=-1))
        g = jnp.exp(log_g - m_new[..., None])
        decay = jnp.exp(b_last + m - m_new)
        c_mat = decay[..., None, None] * c_mat + jnp.einsum('bhsv,bhsk->bhvk', g[..., None] * v_c, k_c)
        n_vec = decay[..., None] * n_vec + jnp.einsum('bhs,bhsk->bhk', g, k_c)
        return (c_mat, n_vec, m_new), h

    init = (jnp.zeros((b_, h_, d_, d_), jnp.float32),
            jnp.zeros((b_, h_, d_), jnp.float32),
            jnp.zeros((b_, h_), jnp.float32))
    xs = (to_chunks(q), to_chunks(k), to_chunks(v), gate_chunks(li), gate_chunks(lf))
    _, h = lax.scan(step, init, xs)
    return h.transpose(1, 0, 3, 2, 4).reshape(b_, s_, h_, d_)


def rwkv_mlstm_layer(xn, w_in, mu, mu_lora, w0, w_up, a0, a_up, k_k, k_a, r_k, ln_w, ln_b,
                     conv_w, conv_b, wq, wk, wv, w_if, b_if, m_norm, m_skip, w_out):
    f32 = jnp.float32
    b_, s_, _ = xn.shape
    z = xn @ w_in
    split_at = np.cumsum(AB_SPLIT_SIZES)[:-1].tolist()
    zr, zk, zv, zwd, zad, xm, gr, gm = jnp.split(z, split_at, axis=-1)

    r = token_shift_mix(zr, mu[0])
    k = token_shift_mix(zk, mu[1])
    v = token_shift_mix(zv, mu[2])
    wd = token_shift_mix(zwd, mu_lora[0])
    ad = token_shift_mix(zad, mu_lora[1])
    log_w = -DECAY_SCALE * jax.nn.sigmoid((w0 + jnp.tanh(wd) @ w_up).astype(f32))
    a = jax.nn.sigmoid((a0 + ad @ a_up).astype(f32))

    def heads(t):
        return t.reshape(b_, s_, RWKV_HEADS, RWKV_HEAD_DIM)

    kk = heads((k * k_k).astype(f32))
    kk = kk / jnp.maximum(jnp.sqrt(jnp.sum(kk * kk, axis=-1, keepdims=True)), 1e-12)
    k_eff = heads(k.astype(f32) * (1.0 + (a - 1.0) * k_a.astype(f32)))
    rh = heads(r.astype(f32))
    vh = heads(v.astype(f32))
    y = rwkv7_recurrence(rh, heads(jnp.exp(log_w)), k_eff, vh, kk, heads(a))
    y = (head_layernorm(y, RWKV_GN_EPS) * ln_w.reshape(RWKV_HEADS, RWKV_HEAD_DIM)
         + ln_b.reshape(RWKV_HEADS, RWKV_HEAD_DIM))
    y = y + jnp.sum(rh * k_eff * r_k, axis=-1, keepdims=True) * vh
    y_rwkv = y.reshape(b_, s_, RWKV_WIDTH).astype(xn.dtype)

    xc = jax.nn.silu(causal_conv(xm, conv_w, conv_b))
    q = block_diag_linear(xc, wq)
    km = block_diag_linear(xc, wk)
    vm = block_diag_linear(xm, wv)
    gates = (jnp.concatenate([q, km, vm], axis=-1) @ w_if + b_if).astype(f32)
    li = gates[..., :MLSTM_HEADS]
    lf = jax.nn.log_sigmoid(gates[..., MLSTM_HEADS:])

    def mheads(t):
        return t.reshape(b_, s_, MLSTM_HEADS, MLSTM_HEAD_DIM).astype(f32)

    h = mlstm_chunkwise(mheads(q), mheads(km) * (MLSTM_HEAD_DIM ** -0.5), mheads(vm), li, lf)
    h = head_layernorm(h, MLSTM_LN_EPS).reshape(b_, s_, MLSTM_WIDTH) * m_norm
    y_m = (h + m_skip * xc.astype(f32)).astype(xn.dtype)

    merged = jnp.concatenate([y_rwkv * jax.nn.silu(gr), y_m * jax.nn.silu(gm)], axis=-1)
    return merged @ w_out


def rglru_layer(xn, w_in, conv_w, conv_b, wr, br, wi, bi, lam, w_out):
    f32 = jnp.float32
    z = xn @ w_in
    xl, g = jnp.split(z, [LRU_WIDTH], axis=-1)
    xc = causal_conv(xl, conv_w, conv_b)
    rg = jax.nn.sigmoid((block_diag_linear(xc, wr) + br).astype(f32))
    ig = jax.nn.sigmoid((block_diag_linear(xc, wi) + bi).astype(f32))
    log_a = -LRU_C * rg * jax.nn.softplus(-lam.astype(f32))
    a = jnp.exp(log_a)
    mult = jnp.sqrt(jnp.maximum(-jnp.expm1(2.0 * log_a), 0.0))
    u = xc.astype(f32) * ig * mult

    def combine(c1, c2):
        a1, b1 = c1
        a2, b2 = c2
        return a1 * a2, a2 * b1 + b2

    _, h = lax.associative_scan(combine, (a, u), axis=1)
    return (h.astype(xn.dtype) * jax.nn.silu(g)) @ w_out


def setup_inputs(seed: int = 0) -> dict:
    key = jax.random.key(seed)
    keys = iter(jax.random.split(key, 48))

    def nrm(shape, scale):
        return scale * jax.random.normal(next(keys), shape, jnp.float32)

    def unif(shape, lo, hi):
        return jax.random.uniform(next(keys), shape, jnp.float32, minval=lo, maxval=hi)

    x = nrm((BATCH, SEQ, D_MODEL), 1.0)
    p = nrm((DEPTH, BATCH, SEQ, PLE_DIM), 1.0)
    mix_norm = 1.0 + nrm((DEPTH, D_MODEL), 0.02)
    pe_norm = 1.0 + nrm((DEPTH, D_MODEL), 0.02)
    final_norm = 1.0 + nrm((D_MODEL,), 0.02)
    pe_up = nrm((DEPTH, PLE_DIM, D_MODEL), PLE_DIM ** -0.5)
    pe_gate = nrm((DEPTH, D_MODEL, D_MODEL), D_MODEL ** -0.5)

    ab_w_in = nrm((N_EVEN, D_MODEL, AB_IN_WIDTH), D_MODEL ** -0.5)
    rwkv_mu = unif((N_EVEN, 3, RWKV_WIDTH), 0.0, 1.0)
    rwkv_mu_lora = unif((N_EVEN, 2, DECAY_RANK), 0.0, 1.0)
    rwkv_w0 = unif((N_EVEN, RWKV_WIDTH), -6.0, 1.0)
    rwkv_w_up = nrm((N_EVEN, DECAY_RANK, RWKV_WIDTH), 0.5 * DECAY_RANK ** -0.5)
    rwkv_a0 = nrm((N_EVEN, RWKV_WIDTH), 0.1)
    rwkv_a_up = nrm((N_EVEN, ICL_RANK, RWKV_WIDTH), 0.5 * ICL_RANK ** -0.5)
    rwkv_k_k = 0.85 + nrm((N_EVEN, RWKV_WIDTH), 0.05)
    rwkv_k_a = 1.0 + nrm((N_EVEN, RWKV_WIDTH), 0.05)
    rwkv_r_k = nrm((N_EVEN, RWKV_HEADS, RWKV_HEAD_DIM), 0.1)
    rwkv_ln_w = 1.0 + nrm((N_EVEN, RWKV_WIDTH), 0.02)
    rwkv_ln_b = nrm((N_EVEN, RWKV_WIDTH), 0.02)

    mlstm_conv_w = nrm((N_EVEN, CONV_WIDTH, MLSTM_WIDTH), 0.5)
    mlstm_conv_b = nrm((N_EVEN, MLSTM_WIDTH), 0.02)
    nblk = MLSTM_WIDTH // QKV_BLOCK
    mlstm_wq = nrm((N_EVEN, nblk, QKV_BLOCK, QKV_BLOCK), QKV_BLOCK ** -0.5)
    mlstm_wk = nrm((N_EVEN, nblk, QKV_BLOCK, QKV_BLOCK), QKV_BLOCK ** -0.5)
    mlstm_wv = nrm((N_EVEN, nblk, QKV_BLOCK, QKV_BLOCK), QKV_BLOCK ** -0.5)
    mlstm_w_if = nrm((N_EVEN, 3 * MLSTM_WIDTH, 2 * MLSTM_HEADS), 0.1 * (3 * MLSTM_WIDTH) ** -0.5)
    i_bias = nrm((N_EVEN, MLSTM_HEADS), 0.1)
    f_bias = jnp.linspace(3.0, 6.0, MLSTM_HEADS, dtype=jnp.float32)[None, :] + nrm((N_EVEN, MLSTM_HEADS), 0.01)
    mlstm_b_if = jnp.concatenate([i_bias, f_bias], axis=-1)
    mlstm_norm = 1.0 + nrm((N_EVEN, MLSTM_WIDTH), 0.02)
    mlstm_skip = 1.0 + nrm((N_EVEN, MLSTM_WIDTH), 0.02)
    ab_w_out = nrm((N_EVEN, AB_OUT_WIDTH, D_MODEL), AB_OUT_WIDTH ** -0.5)

    c_w_in = nrm((N_ODD, D_MODEL, 2 * LRU_WIDTH), D_MODEL ** -0.5)
    c_conv_w = nrm((N_ODD, CONV_WIDTH, LRU_WIDTH), 0.5)
    c_conv_b = nrm((N_ODD, LRU_WIDTH), 0.02)
    bw = LRU_WIDTH // LRU_BLOCKS
    c_wr = nrm((N_ODD, LRU_BLOCKS, bw, bw), bw ** -0.5)
    c_br = nrm((N_ODD, LRU_WIDTH), 0.02)
    c_wi = nrm((N_ODD, LRU_BLOCKS, bw, bw), bw ** -0.5)
    c_bi = nrm((N_ODD, LRU_WIDTH), 0.02)
    a_c = unif((N_ODD, LRU_WIDTH), 0.81, 0.998)
    s_lam = a_c ** (1.0 / LRU_C)
    c_lambda = jnp.log(s_lam) - jnp.log1p(-s_lam)
    c_w_out = nrm((N_ODD, LRU_WIDTH, D_MODEL), LRU_WIDTH ** -0.5)

    return {'x': x, 'p': p, 'mix_norm': mix_norm, 'pe_norm': pe_norm, 'final_norm': final_norm,
            'pe_up': pe_up, 'pe_gate': pe_gate, 'ab_w_in': ab_w_in, 'rwkv_mu': rwkv_mu,
            'rwkv_mu_lora': rwkv_mu_lora, 'rwkv_w0': rwkv_w0, 'rwkv_w_up': rwkv_w_up,
            'rwkv_a0': rwkv_a0, 'rwkv_a_up': rwkv_a_up, 'rwkv_k_k': rwkv_k_k, 'rwkv_k_a': rwkv_k_a,
            'rwkv_r_k': rwkv_r_k, 'rwkv_ln_w': rwkv_ln_w, 'rwkv_ln_b': rwkv_ln_b,
            'mlstm_conv_w': mlstm_conv_w, 'mlstm_conv_b': mlstm_conv_b, 'mlstm_wq': mlstm_wq,
            'mlstm_wk': mlstm_wk, 'mlstm_wv': mlstm_wv, 'mlstm_w_if': mlstm_w_if,
            'mlstm_b_if': mlstm_b_if, 'mlstm_norm': mlstm_norm, 'mlstm_skip': mlstm_skip,
            'ab_w_out': ab_w_out, 'c_w_in': c_w_in, 'c_conv_w': c_conv_w, 'c_conv_b': c_conv_b,
            'c_wr': c_wr, 'c_br': c_br, 'c_wi': c_wi, 'c_bi': c_bi, 'c_lambda': c_lambda,
            'c_w_out': c_w_out}


def reference(x, p, mix_norm, pe_norm, final_norm, pe_up, pe_gate, ab_w_in, rwkv_mu,
              rwkv_mu_lora, rwkv_w0, rwkv_w_up, rwkv_a0, rwkv_a_up, rwkv_k_k, rwkv_k_a,
              rwkv_r_k, rwkv_ln_w, rwkv_ln_b, mlstm_conv_w, mlstm_conv_b, mlstm_wq,
              mlstm_wk, mlstm_wv, mlstm_w_if, mlstm_b_if, mlstm_norm, mlstm_skip,
              ab_w_out, c_w_in, c_conv_w, c_conv_b, c_wr, c_br, c_wi, c_bi, c_lambda,
              c_w_out):
    h = x
    for i in range(DEPTH):
        j = i // 2
        xn = rmsnorm(h, mix_norm[i])
        if i % 2 == 0:
            h = h + rwkv_mlstm_layer(xn, ab_w_in[j], rwkv_mu[j], rwkv_mu_lora[j], rwkv_w0[j],
                                     rwkv_w_up[j], rwkv_a0[j], rwkv_a_up[j], rwkv_k_k[j],
                                     rwkv_k_a[j], rwkv_r_k[j], rwkv_ln_w[j], rwkv_ln_b[j],
                                     mlstm_conv_w[j], mlstm_conv_b[j], mlstm_wq[j], mlstm_wk[j],
                                     mlstm_wv[j], mlstm_w_if[j], mlstm_b_if[j], mlstm_norm[j],
                                     mlstm_skip[j], ab_w_out[j])
        else:
            h = h + rglru_layer(xn, c_w_in[j], c_conv_w[j], c_conv_b[j], c_wr[j], c_br[j],
                                c_wi[j], c_bi[j], c_lambda[j], c_w_out[j])
        hn = rmsnorm(h, pe_norm[i])
        h = h + (p[i] @ pe_up[i]) * jax.nn.sigmoid(hn @ pe_gate[i])
    return rmsnorm(h, final_norm)
```

```python
import numpy as np
import concourse.bass as bass
import concourse.mybir as mybir
from concourse.bass_utils import run_bass_kernel_spmd
from contextlib import ExitStack

F32 = mybir.dt.float32
BF16 = mybir.dt.bfloat16
F32R = mybir.dt.float32r
SDT = F32
AF = mybir.ActivationFunctionType
ALU = mybir.AluOpType
AX = mybir.AxisListType

ENGS = ('pe', 'act', 'dve', 'pool', 'sp')
EIDX = {e: i for i, e in enumerate(ENGS)}
EPOCH = 30000

D = 1024
TT = 512
NT = TT // 128
PLE = 256
DECAY_SCALE = 0.6065306597126334


class TB:
    __slots__ = ('name', 'h', 'lw', 'rd', 'dkey', 'root', 'psum')

    def __init__(self, name, h, root=None, psum=False):
        self.name = name
        self.h = h
        self.lw = None
        self.rd = {}
        self.dkey = None
        self.root = root if root is not None else self
        self.psum = psum or (root is not None and root.psum)

    def __getitem__(self, k):
        return self.h[k]


class _Rec:
    def __init__(self):
        self.call = None

    def __getattr__(self, name):
        def f(*a, **k):
            self.call = (name, a, k)
        return f


class Prog:
    def __init__(self, nc):
        self.nc = nc
        self.es = ExitStack()
        self.ops = {e: [] for e in ENGS}
        self.cnt = {e: 0 for e in ENGS}
        self.seen = {e: {} for e in ENGS}
        self.clk = {e: [] for e in ENGS}
        self.dcnt = {}
        self.dclk = {}
        self.sems = {}
        self.nsem = 0
        self.ndma = 0
        self.sbytes = 0

    def sb(self, name, shape, dt=F32):
        n = 1
        for s in shape[1:]:
            n *= s
        self.sbytes += n * (2 if dt == BF16 else 4)
        return TB(name, self.es.enter_context(self.nc.sbuf_tensor(name, list(shape), dt)))

    def ps(self, name, shape, dt=F32):
        return TB(name, self.es.enter_context(self.nc.psum_tensor(name, list(shape), dt)), psum=True)

    def dram(self, name, shape, dt, kind):
        return TB(name, self.nc.dram_tensor(name, list(shape), dt, kind=kind))

    def _sem(self, key):
        s = self.sems.get(key)
        if s is None:
            s = self.es.enter_context(self.nc.semaphore("s%d" % self.nsem))
            self.nsem += 1
            self.sems[key] = s
        return s

    def _semval(self, key, count):
        if key in EIDX:
            ep = (count - 1) // EPOCH
            return self._sem((key, ep)), count - ep * EPOCH
        return self._sem(key), 16 * count

    def _deps(self, e, reads, writes):
        seen = self.seen[e]
        need = {}

        def req(ev):
            if ev is None:
                return
            k, c = ev
            if k == 'pe' and e == 'pe':
                return
            if seen.get(k, 0) >= c:
                return
            if need.get(k, 0) < c:
                need[k] = c
        for t in reads:
            req(t.lw)
        for t in writes:
            req(t.lw)
            for k, c in t.rd.items():
                req((k, c))
        keys = list(need.keys())
        for k in keys:
            if k not in need:
                continue
            c = need[k]
            ck = self.clk[k][c - 1] if k in EIDX else self.dclk[(k, c)]
            for k2 in keys:
                if k2 != k and k2 in EIDX and k2 in need and ck[EIDX[k2]] >= need[k2]:
                    del need[k2]
        waits = []
        for k, c in need.items():
            waits.append(self._semval(k, c))
            ck = self.clk[k][c - 1] if k in EIDX else self.dclk[(k, c)]
            for e2, v in zip(ENGS, ck):
                if seen.get(e2, 0) < v:
                    seen[e2] = v
            if seen.get(k, 0) < c:
                seen[k] = c
        return waits

    def _snapshot(self, e):
        s = self.seen[e]
        return tuple(s.get(x, 0) for x in ENGS)

    def op(self, e, fn, reads=(), writes=()):
        writes = [t.root for t in writes] + [t.root for t in reads if t.psum]
        reads = [t.root for t in reads if not t.psum]
        rec = _Rec()
        fn(rec)
        fn = rec.call
        waits = self._deps(e, reads, writes)
        self.cnt[e] += 1
        c = self.cnt[e]
        snap = list(self._snapshot(e))
        snap[EIDX[e]] = c
        self.clk[e].append(tuple(snap))
        inc = self._semval(e, c)[0]
        self.ops[e].append((waits, fn, inc, 1))
        ev = (e, c)
        for t in reads:
            if t.rd.get(e, 0) < c:
                t.rd[e] = c
        for t in writes:
            t.lw = ev
            t.rd = {}
        return ev

    def dma(self, q, out, in_, reads=(), writes=(), key=None, **kw):
        if key is None:
            t0 = (list(writes) + list(reads))[0]
            if t0.dkey is None:
                t0.dkey = ('dma', self.ndma)
                self.ndma += 1
            key = t0.dkey
        reads = [t.root for t in reads]
        writes = [t.root for t in writes]
        waits = self._deps(q, reads, writes)
        self.dcnt[key] = self.dcnt.get(key, 0) + 1
        c = self.dcnt[key]
        self.dclk[(key, c)] = self._snapshot(q)
        sem = self._sem(key)
        self.ops[q].append((waits, ('dma_start', (), dict(out=out, in_=in_, **kw)), sem, 16))
        ev = (key, c)
        for t in reads:
            if t.rd.get(key, 0) < c:
                t.rd[key] = c
        for t in writes:
            t.lw = ev
            t.rd = {}
        return ev

    def finish(self):
        e = 'sp'
        tail = []
        for key, c in self.dcnt.items():
            tail.append(self._semval(key, c))
        for x in ENGS:
            if x != e and self.cnt[x] > 0:
                tail.append(self._semval(x, self.cnt[x]))
        blk = self.es.enter_context(self.nc.Block())
        engobj = {'pe': blk.tensor, 'act': blk.scalar, 'dve': blk.vector, 'pool': blk.gpsimd, 'sp': blk.sync}

        def make(ename):
            ops = self.ops[ename]

            def body(eng):
                for waits, fn, inc, n in ops:
                    for (s, v) in waits[1:]:
                        eng.wait_ge(s, v)
                    ins = getattr(eng, fn[0])(*fn[1], **fn[2])
                    if waits:
                        ins._wait_ge(waits[0][0], waits[0][1])
                    ins.then_inc(inc, n)
                if ename == e:
                    for (s, v) in tail:
                        eng.wait_ge(s, v)
            return body
        for ename in ENGS:
            if self.ops[ename] or ename == e:
                engobj[ename](make(ename))
        self.es.close()


PC_SPEC = [
    ('mixn0', 8), ('mixn1', 8), ('pen0', 8), ('pen1', 8), ('finn', 8),
    ('mu_r', 8), ('mu_k', 8), ('mu_v', 8), ('mu_l', 1),
    ('w0', 8), ('a0', 8), ('k_k', 8), ('k_a', 8), ('r_k', 8), ('ln_w', 8), ('ln_b', 8),
    ('mcw0', 8), ('mcw1', 8), ('mcw2', 8), ('mcw3', 8), ('mcb', 8), ('mnorm', 8), ('mskip', 8),
    ('bif_i', 1), ('bif_f', 1),
    ('ccw0', 16), ('ccw1', 16), ('ccw2', 16), ('ccw3', 16), ('ccb', 16), ('cbr', 16), ('cbi', 16), ('clam', 16),
]
PC = {}
_o = 0
for _n, _w in PC_SPEC:
    PC[_n] = _o
    _o += _w
NPC = _o
PD_SPEC = [('omu_r', 8), ('omu_k', 8), ('omu_v', 8), ('omu_l', 1), ('negbf', 1), ('csph', 16), ('cneg', 16), ('t0', 16), ('t1', 16)]
PD = {}
_o = 0
for _n, _w in PD_SPEC:
    PD[_n] = _o
    _o += _w
NPD = _o


def _cols(v):
    v = np.asarray(v, np.float32).reshape(-1)
    return np.ascontiguousarray(v.reshape(-1, 128).T)


def host_consts():
    c = {}
    c['ident'] = np.eye(128, dtype=np.float32)
    bd = np.zeros((128, 128), np.float32)
    bd[:64, :64] = 1
    bd[64:, 64:] = 1
    c['bdones'] = bd
    j = np.arange(128)[:, None]
    i = np.arange(128)[None, :]
    su = ((j < i) & ((j // 64) == (i // 64))).astype(np.float32)
    ui = ((j <= i) & ((j // 64) == (i // 64))).astype(np.float32)
    sl = ((j > i) & ((j // 64) == (i // 64))).astype(np.float32)
    on = np.ones((128, 128), np.float32)
    c['maskA'] = np.stack([su, su, ui, ui], axis=1)
    c['maskB'] = sl
    rm = np.ones((128, TT), np.float32)
    rm[:, ::64] = 0
    c['resetm'] = rm
    c['causal'] = (j <= i).astype(np.float32)
    sel = np.zeros((4, 4, 128), np.float32)
    for h in range(4):
        sel[h, h, :] = 1
    c['sel'] = sel
    return c


def host_prep(inp):
    g = {}
    f = lambda a: np.ascontiguousarray(np.asarray(a, np.float32))

    def chunked(w):
        K, M = w.shape
        return f(w.reshape(K // 128, 128, M // 128, 128).transpose(2, 1, 0, 3))
    g['w0in'] = chunked(f(inp['ab_w_in'])[0])
    g['w0out'] = chunked(f(inp['ab_w_out'])[0])
    g['w1in'] = chunked(f(inp['c_w_in'])[0])
    g['w1out'] = chunked(f(inp['c_w_out'])[0])
    g['pegate'] = np.stack([chunked(f(inp['pe_gate'])[i]) for i in range(2)])
    g['peup'] = np.stack([chunked(f(inp['pe_up'])[i]) for i in range(2)])
    wr = f(inp['c_wr'])[0]
    wi = f(inp['c_wi'])[0]
    g['wri'] = f(np.stack([wr, wi], axis=2))
    lup = np.zeros((128, 2, 1024), np.float32)
    lup[:64, 0, :] = f(inp['rwkv_w_up'])[0]
    lup[64:, 1, :] = f(inp['rwkv_a_up'])[0]
    g['lup'] = lup
    wbd = np.zeros((128, 3, 8, 128), np.float32)
    for t, nm in enumerate(['mlstm_wq', 'mlstm_wk', 'mlstm_wv']):
        w = f(inp[nm])[0]
        for c in range(8):
            for gg in range(32):
                wbd[4 * gg:4 * gg + 4, t, c, 4 * gg:4 * gg + 4] = w[c * 32 + gg]
    g['wqkv'] = wbd
    wif = f(inp['mlstm_w_if'])[0]
    g['wif'] = f(wif.reshape(24, 128, 2, 4).transpose(1, 2, 0, 3))
    pc = np.zeros((128, NPC), np.float32)

    def put(name, v):
        cc = _cols(v)
        pc[:, PC[name]:PC[name] + cc.shape[1]] = cc
    put('mixn0', f(inp['mix_norm'])[0]); put('mixn1', f(inp['mix_norm'])[1])
    put('pen0', f(inp['pe_norm'])[0]); put('pen1', f(inp['pe_norm'])[1])
    put('finn', f(inp['final_norm']))
    mu = f(inp['rwkv_mu'])[0]
    put('mu_r', mu[0]); put('mu_k', mu[1]); put('mu_v', mu[2])
    ml = f(inp['rwkv_mu_lora'])[0]
    put('mu_l', np.concatenate([ml[0], ml[1]]))
    put('w0', f(inp['rwkv_w0'])[0]); put('a0', f(inp['rwkv_a0'])[0])
    put('k_k', f(inp['rwkv_k_k'])[0]); put('k_a', f(inp['rwkv_k_a'])[0])
    put('r_k', f(inp['rwkv_r_k'])[0].reshape(-1))
    put('ln_w', f(inp['rwkv_ln_w'])[0]); put('ln_b', f(inp['rwkv_ln_b'])[0])
    mcw = f(inp['mlstm_conv_w'])[0]
    for j in range(4):
        put('mcw%d' % j, mcw[j])
    put('mcb', f(inp['mlstm_conv_b'])[0])
    put('mnorm', f(inp['mlstm_norm'])[0]); put('mskip', f(inp['mlstm_skip'])[0])
    bif = f(inp['mlstm_b_if'])[0]
    pc[0:4, PC['bif_i']] = bif[0:4]
    pc[0:4, PC['bif_f']] = bif[4:8]
    ccw = f(inp['c_conv_w'])[0]
    for j in range(4):
        put('ccw%d' % j, ccw[j])
    put('ccb', f(inp['c_conv_b'])[0]); put('cbr', f(inp['c_br'])[0]); put('cbi', f(inp['c_bi'])[0])
    put('clam', f(inp['c_lambda'])[0])
    g['pc'] = pc
    g.update(host_consts())
    return g


SHARED_SHAPES = {
    'w0in': [49, 128, 8, 128], 'w0out': [8, 128, 16, 128], 'w1in': [32, 128, 8, 128], 'w1out': [8, 128, 16, 128],
    'pegate': [2, 8, 128, 8, 128], 'peup': [2, 8, 128, 2, 128], 'wri': [16, 128, 2, 128], 'lup': [128, 2, 1024],
    'wqkv': [128, 3, 8, 128], 'wif': [128, 2, 24, 4], 'pc': [128, NPC],
    'ident': [128, 128], 'bdones': [128, 128], 'maskA': [128, 4, 128], 'maskB': [128, 128],
    'resetm': [128, TT], 'causal': [128, 128], 'sel': [4, 4, 128],
}


def build_program(S, do_l0=True, do_rwkv=True, do_mlstm=True, do_l1=True, dbg=None):
    nc = bass.Bass("TRN2", target_bir_lowering=False)
    P = Prog(nc)
    ntiles = S // TT
    din = {}
    for k, shp in SHARED_SHAPES.items():
        din[k] = nc.dram_tensor(k, shp, F32, kind="ExternalInput")
    x_d = nc.dram_tensor("x", [S, D], F32, kind="ExternalInput")
    p_d = nc.dram_tensor("p", [2, S, PLE], F32, kind="ExternalInput")
    o_d = nc.dram_tensor("out", [S, D], F32, kind="ExternalOutput")
    dbg_d = None
    dbg = dbg or []
    if dbg:
        dbg_d = nc.dram_tensor("dbg", [len(dbg), 128, TT], F32, kind="ExternalOutput")

    def dump(name, tb, ap):
        if name in dbg:
            P.dma('sp', dbg_d.ap()[dbg.index(name)], ap, reads=[tb], key=('dma', 'dbg'))

    MUL, ADD, SUB, MAX = ALU.mult, ALU.add, ALU.subtract, ALU.max

    def cload(name, dt=F32, rows=128):
        shp = SHARED_SHAPES[name]
        t = P.sb("c_" + name, shp, dt)
        P.dma('pool' if dt != F32 else 'sp', t[:], din[name].ap(), writes=[t])
        return t
    ident = cload('ident')
    pc = cload('pc')
    pd = P.sb("pd", [128, NPD])
    ones_bf = P.sb("ones_bf", [128, 128], BF16)
    P.op('pool', lambda e: e.memset(ones_bf[:], 1.0), [], [ones_bf])

    def pcc(name, c=0, n=1):
        return pc[:, PC[name] + c:PC[name] + c + n]

    def pdc(name, c=0, n=1):
        return pd[:, PD[name] + c:PD[name] + c + n]

    if do_l1:
        P.op('act', lambda e: e.activation(out=pdc('t0', 0, 16), in_=pcc('clam', 0, 16), func=AF.Exp, scale=-1.0), [pc], [pd])
        P.op('act', lambda e: e.activation(out=pdc('t1', 0, 16), in_=pdc('t0', 0, 16), func=AF.Ln, bias=1.0), [pd], [pd])
        P.op('dve', lambda e: e.tensor_scalar(out=pdc('csph', 0, 16), in0=pdc('t1', 0, 16), scalar1=4.0, scalar2=None, op0=MUL), [pd], [pd])
        P.op('dve', lambda e: e.tensor_scalar(out=pdc('cneg', 0, 16), in0=pdc('t1', 0, 16), scalar1=-8.0, scalar2=None, op0=MUL), [pd], [pd])
    if do_l0:
        for nm, w in (('r', 8), ('k', 8), ('v', 8), ('l', 1)):
            P.op('dve', lambda e, nm=nm, w=w: e.tensor_scalar(out=pdc('omu_' + nm, 0, w), in0=pcc('mu_' + nm, 0, w), scalar1=-1.0, scalar2=1.0, op0=MUL, op1=ADD),
                 [pc], [pd])
        P.op('dve', lambda e: e.tensor_scalar(out=pdc('negbf'), in0=pcc('bif_f'), scalar1=-1.0, scalar2=None, op0=MUL), [pc], [pd])

    scr = {}

    def mkscr(name, src_ap, shape, group=8):
        t = P.dram("scr_" + name, shape, BF16, "Internal")
        nm = shape[0]
        for j0 in range(0, nm, group):
            j1 = min(nm, j0 + group)
            P.dma('pool', t[j0:j1].rearrange("j p k m -> p j k m"), src_ap[j0:j1].rearrange("j p k m -> p j k m"),
                  writes=[t], key=('dma', 'scr_' + name))
        scr[name] = t
        return t
    if do_l0:
        lup_bf = cload('lup', BF16)
        wqkv_bf = cload('wqkv', BF16)
        wif_bf = cload('wif', BF16)
        mkscr('w0in', din['w0in'].ap(), [49, 128, 8, 128])
        mkscr('w0out', din['w0out'].ap(), [8, 128, 16, 128], group=4)
    for i in range(2):
        if (i == 0 and do_l0) or (i == 1 and do_l1):
            mkscr('pegate%d' % i, din['pegate'].ap()[i], [8, 128, 8, 128])
            mkscr('peup%d' % i, din['peup'].ap()[i], [8, 128, 2, 128])
    if do_l1:
        mkscr('w1in', din['w1in'].ap(), [32, 128, 8, 128])
        mkscr('wri', din['wri'].ap(), [16, 128, 2, 128], group=16)
        mkscr('w1out', din['w1out'].ap(), [8, 128, 16, 128], group=4)

    NWB = 5
    wring = [P.sb("wb%d" % i, [128, 16, 128], BF16) for i in range(NWB)]
    wstate = {'i': 0}

    def wld(name, j, nk):
        wb = wring[wstate['i'] % NWB]
        wstate['i'] += 1
        P.dma('sp', wb[:, 0:nk, :], scr[name][j], reads=[scr[name]], writes=[wb])
        return wb

    PS = [P.ps("ps%d" % i, [128, TT]) for i in range(8)]
    pstate = {'i': 0}

    def nextps():
        b = PS[pstate['i'] % 2]
        pstate['i'] += 1
        return b

    hT = P.sb("hT", [128, 8, TT])
    xnT = P.sb("xnT", [128, 8, TT], BF16)
    merged = P.sb("merged", [128, 16, TT], BF16)
    ext = P.sb("ext", [128, TT + 3])
    NG, NH = 20, 12
    G = [P.sb("g%d" % i, [128, TT]) for i in range(NG)]
    H = [P.sb("h%d" % i, [128, 2 * TT]) for i in range(NH)]
    lnv, rstd = G[19], G[18]
    ntmp = [G[16], G[17]]

    def bfv(tb, n=None):
        v = tb[:].bitcast(BF16)
        return v

    if do_l1:
        l1_tail = P.sb("l1_tail", [128, 16, 3])
        l1_h = P.sb("l1_h", [128, 16])
        P.op('pool', lambda e: e.memset(l1_tail[:], 0.0), [], [l1_tail])
        P.op('pool', lambda e: e.memset(l1_h[:], 0.0), [], [l1_h])

    def norm_apply(k, gname, out_ap, out_tb):
        if k % 2 == 0:
            P.op('dve', lambda e: e.scalar_tensor_tensor(out=out_ap, in0=hT[:, k, :], scalar=pcc(gname, k), in1=rstd[:],
                                                         op0=MUL, op1=MUL), [hT, pc, rstd], [out_tb])
        else:
            nt = ntmp[(k // 2) % 2]
            P.op('act', lambda e: e.activation(out=nt[:], in_=hT[:, k, :], func=AF.Identity, scale=pcc(gname, k)), [hT, pc], [nt])
            P.op('pool', lambda e: e.tensor_tensor(out=out_ap, in0=nt[:], in1=rstd[:], op=MUL), [nt, rstd], [out_tb])

    def norm_stats():
        ps = nextps()
        for half in range(2):
            sq = H[4 + half]
            sqv = bfv(sq).rearrange("p (k t) -> p k t", k=4)
            P.op('act', lambda e, half=half, sqv=sqv: e.activation(out=sqv, in_=hT[:, 4 * half:4 * half + 4, :], func=AF.Square), [hT], [sq])
            for k in range(4):
                P.op('pe', lambda e, k=k, half=half, sqv=sqv: e.matmul(ps[:], lhsT=ones_bf[:], rhs=sqv[:, k, :], start=(half == 0 and k == 0), stop=(half == 1 and k == 3)),
                     [ones_bf, sq], [ps])
        P.op('act', lambda e: e.activation(out=lnv[:], in_=ps[:], func=AF.Ln, scale=1.0 / D, bias=1e-6), [ps], [lnv])
        P.op('act', lambda e: e.activation(out=rstd[:], in_=lnv[:], func=AF.Exp, scale=-0.5), [lnv], [rstd])

    def rmsnorm_x(gname):
        norm_stats()
        for k in range(8):
            norm_apply(k, gname, xnT[:, k, :], xnT)

    def proj(ps, wb, nk, rhs_tb, rhs_fn):
        for k in range(nk):
            P.op('pe', lambda e, k=k: e.matmul(ps[:], lhsT=wb[:, k, :], rhs=rhs_fn(k), start=(k == 0), stop=(k == nk - 1)),
                 [wb, rhs_tb], [ps])

    def proj_x(name, j):
        w = wld(name, j, 8)
        ps = nextps()
        proj(ps, w, 8, xnT, lambda k: xnT[:, k, :])
        return ps

    def ple(i, t0):
        ptok = H[0]
        ptv = ptok[:].rearrange("p (tb d) -> p tb d", tb=NT)
        pTt = G[12]
        pT = bfv(pTt).rearrange("p (k t) -> p k t", k=2)
        sg, tmpa = G[13], G[14]
        P.dma('sp', ptv, p_d.ap()[i, t0:t0 + TT, :].rearrange("(tb p) d -> p tb d", p=128), writes=[ptok])
        for kc in range(2):
            ps = nextps()
            for tb in range(NT):
                P.op('pe', lambda e, kc=kc, tb=tb, ps=ps: e.transpose(ps[:, tb * 128:(tb + 1) * 128], ptv[:, tb, kc * 128:(kc + 1) * 128], ident[:]),
                     [ptok, ident], [ps])
            P.op('act', lambda e, kc=kc, ps=ps: e.activation(out=pT[:, kc, :], in_=ps[:], func=AF.Copy), [ps], [pTt])
        rmsnorm_x('pen%d' % i)
        for m in range(8):
            wg = wld('pegate%d' % i, m, 8)
            wu = wld('peup%d' % i, m, 2)
            ps = nextps()
            proj(ps, wg, 8, xnT, lambda k: xnT[:, k, :])
            P.op('act', lambda e, ps=ps: e.activation(out=sg[:], in_=ps[:], func=AF.Sigmoid), [ps], [sg])
            ps2 = nextps()
            proj(ps2, wu, 2, pTt, lambda k: pT[:, k, :])
            P.op('dve', lambda e, ps2=ps2: e.tensor_tensor(out=tmpa[:], in0=ps2[:], in1=sg[:], op=MUL), [ps2, sg], [tmpa])
            P.op('pool', lambda e, m=m: e.tensor_tensor(out=hT[:, m, :], in0=hT[:, m, :], in1=tmpa[:], op=ADD), [hT, tmpa], [hT])

    def out_proj(name):
        for m in range(8):
            wo = wld(name, m, 16)
            ps = nextps()
            proj(ps, wo, 16, merged, lambda k: merged[:, k, :])
            P.op('dve', lambda e, m=m, ps=ps: e.tensor_tensor(out=hT[:, m, :], in0=hT[:, m, :], in1=ps[:], op=ADD), [hT, ps], [hT])

    def conv4(ps, c, tail_tb, wname, bname, acc):
        P.op('pool', lambda e: e.tensor_copy(out=ext[:, 0:3], in_=tail_tb[:, c, :]), [tail_tb], [ext])
        P.op('act', lambda e: e.activation(out=ext[:, 3:TT + 3], in_=ps[:], func=AF.Copy), [ps], [ext])
        P.op('act', lambda e: e.activation(out=acc[:], in_=ps[:], func=AF.Identity, scale=pcc(wname + '3', c), bias=pcc(bname, c)), [ps, pc], [acc])
        P.op('pool', lambda e: e.tensor_copy(out=tail_tb[:, c, :], in_=ext[:, TT:TT + 3]), [ext], [tail_tb])
        for j in range(3):
            P.op('dve', lambda e, j=j: e.scalar_tensor_tensor(out=acc[:], in0=ext[:, j:j + TT], scalar=pcc(wname + str(j), c), in1=acc[:],
                                                              op0=MUL, op1=ADD), [ext, pc, acc], [acc])

    def layer1(t0):
        rmsnorm_x('mixn1')
        xc32, xcb, rg, ig, aa, a2, th, m2, mult, u, hs, gs = G[0:12]
        xcb_ap = bfv(xcb)[:, 0:TT]
        for c in range(16):
            ps = proj_x('w1in', c)
            conv4(ps, c, l1_tail, 'ccw', 'ccb', xc32)
            P.op('act', lambda e: e.activation(out=xcb_ap, in_=xc32[:], func=AF.Copy), [xc32], [xcb])
            wb = wld('wri', c, 2)
            ps_r = nextps()
            P.op('pe', lambda e, ps_r=ps_r, wb=wb: e.matmul(ps_r[:], lhsT=wb[:, 0, :], rhs=xcb_ap, start=True, stop=True), [wb, xcb], [ps_r])
            P.op('act', lambda e, c=c, ps_r=ps_r: e.activation(out=rg[:], in_=ps_r[:], func=AF.Sigmoid, bias=pcc('cbr', c)), [ps_r, pc], [rg])
            ps_i = nextps()
            P.op('pe', lambda e, ps_i=ps_i, wb=wb: e.matmul(ps_i[:], lhsT=wb[:, 1, :], rhs=xcb_ap, start=True, stop=True), [wb, xcb], [ps_i])
            P.op('act', lambda e, c=c, ps_i=ps_i: e.activation(out=ig[:], in_=ps_i[:], func=AF.Sigmoid, bias=pcc('cbi', c)), [ps_i, pc], [ig])
            P.op('act', lambda e, c=c: e.activation(out=a2[:], in_=rg[:], func=AF.Exp, scale=pdc('cneg', c)), [rg, pd], [a2])
            P.op('act', lambda e, c=c: e.activation(out=th[:], in_=rg[:], func=AF.Tanh, scale=pdc('csph', c)), [rg, pd], [th])
            P.op('dve', lambda e: e.scalar_tensor_tensor(out=m2[:], in0=a2[:], scalar=1.0, in1=th[:], op0=ADD, op1=MUL), [a2, th], [m2])
            P.op('dve', lambda e: e.tensor_scalar(out=aa[:], in0=m2[:], scalar1=-1.0, scalar2=1.0, op0=MUL, op1=ADD), [m2], [aa])
            P.op('dve', lambda e: e.scalar_tensor_tensor(out=m2[:], in0=aa[:], scalar=1.0, in1=m2[:], op0=ADD, op1=MUL), [aa, m2], [m2])
            P.op('act', lambda e: e.activation(out=mult[:], in_=m2[:], func=AF.Sqrt), [m2], [mult])
            P.op('pool', lambda e: e.tensor_tensor(out=u[:], in0=xc32[:], in1=ig[:], op=MUL), [xc32, ig], [u])
            P.op('pool', lambda e: e.tensor_tensor(out=u[:], in0=u[:], in1=mult[:], op=MUL), [u, mult], [u])
            P.op('dve', lambda e, c=c: e.tensor_tensor_scan(out=hs[:], data0=aa[:], data1=u[:], initial=l1_h[:, c:c + 1], op0=MUL, op1=ADD),
                 [aa, u, l1_h], [hs])
            P.op('pool', lambda e, c=c: e.tensor_copy(out=l1_h[:, c:c + 1], in_=hs[:, TT - 1:TT]), [hs], [l1_h])
            ps_g = proj_x('w1in', 16 + c)
            P.op('act', lambda e, ps_g=ps_g: e.activation(out=gs[:], in_=ps_g[:], func=AF.Silu), [ps_g], [gs])
            P.op('dve', lambda e, c=c: e.tensor_tensor(out=merged[:, c, :], in0=hs[:], in1=gs[:], op=MUL), [hs, gs], [merged])
        out_proj('w1out')

    if do_l0 and do_rwkv:
        identr = P.sb("identr", [128, 128], SDT)
        bdones = cload('bdones')
        bdones_bf = P.sb("bdones_bf", [128, 128], BF16)
        bdones_r = P.sb("bdones_r", [128, 128], SDT)
        maskA = cload('maskA')
        maskB = cload('maskB')
        resetm = cload('resetm')
        P.op('dve', lambda e: e.tensor_copy(out=identr[:], in_=ident[:]), [ident], [identr])
        P.op('dve', lambda e: e.tensor_copy(out=bdones_bf[:], in_=bdones[:]), [bdones], [bdones_bf])
        P.op('dve', lambda e: e.tensor_copy(out=bdones_r[:], in_=bdones[:]), [bdones], [bdones_r])
        rwc = P.sb("rwc", [128, 25])
        P.op('pool', lambda e: e.memset(rwc[:], 0.0), [], [rwc])
        lora_bf = P.sb("lora_bf", [128, TT], BF16)
        Sst = [[P.sb("S%d_%d" % (c, q), [128, 128], SDT) for q in range(2)] for c in range(8)]
        for c in range(8):
            P.op('pool', lambda e, c=c: e.memset(Sst[c][0][:].bitcast(F32), 0.0), [], [Sst[c][0]])
        spar = [0] * 8
        rhs_sb = P.sb("rhs_sb", [128, 128], SDT)
        u_sb = P.sb("u_sb", [128, 128], SDT)
        psRHS = TB("psRHS", PS[7].h[:, 0:128], root=PS[7])
        psU = TB("psU", PS[7].h[:, 128:256], root=PS[7])
        psYT = TB("psYT", PS[7].h[:, 256:384], root=PS[7])
        psSN = TB("psSN", PS[7].h[:, 384:512], root=PS[7])

    def tshift(ps, dst, col, mu_ap, omu_ap):
        P.op('act', lambda e: e.activation(out=dst[:], in_=ps[:], func=AF.Identity, scale=omu_ap), [ps, pd], [dst])
        P.op('dve', lambda e: e.scalar_tensor_tensor(out=dst[:, 1:TT], in0=ps[:, 0:TT - 1], scalar=mu_ap, in1=dst[:, 1:TT], op0=MUL, op1=ADD),
             [ps, pc, dst], [dst])
        P.op('dve', lambda e: e.scalar_tensor_tensor(out=dst[:, 0:1], in0=rwc[:, col:col + 1], scalar=mu_ap, in1=dst[:, 0:1], op0=MUL, op1=ADD),
             [rwc, pc, dst], [dst])
        P.op('act', lambda e: e.activation(out=rwc[:, col:col + 1], in_=ps[:, TT - 1:TT], func=AF.Copy), [ps], [rwc])

    def r3(ap, n=8):
        return ap.rearrange("p (n t) -> p n t", n=n)

    def rwkv(t0):
        Rbd, Abd, Bbd, Kbd, Vbd = H[0:5]
        for i in range(5):
            P.op('pool', lambda e, i=i: e.memset(H[i][:], 0.0), [], [H[i]])
        bdv = [r3(H[i][:].bitcast(SDT)) for i in range(5)]
        ps = proj_x('w0in', 24)
        lora32 = G[16]
        tshift(ps, lora32, 24, pcc('mu_l'), pdc('omu_l'))
        P.op('act', lambda e: e.activation(out=lora_bf[0:64, :], in_=lora32[0:64, :], func=AF.Tanh), [lora32], [lora_bf])
        P.op('act', lambda e: e.activation(out=lora_bf[64:128, :], in_=lora32[64:128, :], func=AF.Copy), [lora32], [lora_bf])
        for c in range(8):
            rwkv_pair(c, bdv)

    def rwkv_pair(c, bdv):
        r32, k32, v32, gr32, lw, cum, cump, Gt, Ginv, Gp, a32, kk32, rn, nkk, kka, keff, t1, misc = G[0:18]
        Rv, Av, Bv, Kv, Vv = bdv
        Rbd, Abd, Bbd, Kbd, Vbd = H[0:5]
        for dst, mch, nm, col in ((r32, c, 'r', c), (k32, 8 + c, 'k', 8 + c), (v32, 16 + c, 'v', 16 + c)):
            ps = proj_x('w0in', mch)
            tshift(ps, dst, col, pcc('mu_' + nm, c), pdc('omu_' + nm, c))
        ps = proj_x('w0in', 33 + c)
        P.op('act', lambda e, ps=ps: e.activation(out=gr32[:], in_=ps[:], func=AF.Silu), [ps], [gr32])
        ps = nextps()
        P.op('pe', lambda e, ps=ps: e.matmul(ps[:], lhsT=lup_bf[:, 0, c * 128:(c + 1) * 128], rhs=lora_bf[:], start=True, stop=True), [lup_bf, lora_bf], [ps])
        P.op('act', lambda e, ps=ps: e.activation(out=lw[:], in_=ps[:], func=AF.Sigmoid, bias=pcc('w0', c)), [ps, pc], [lw])
        P.op('pool', lambda e: e.tensor_scalar(out=lw[:], in0=lw[:], scalar1=-DECAY_SCALE, scalar2=None, op0=MUL), [lw], [lw])
        ps = nextps()
        P.op('pe', lambda e, ps=ps: e.matmul(ps[:], lhsT=lup_bf[:, 1, c * 128:(c + 1) * 128], rhs=lora_bf[:], start=True, stop=True), [lup_bf, lora_bf], [ps])
        P.op('act', lambda e, ps=ps: e.activation(out=a32[:], in_=ps[:], func=AF.Sigmoid, bias=pcc('a0', c)), [ps, pc], [a32])
        P.op('dve', lambda e: e.tensor_tensor_scan(out=cum[:], data0=resetm[:], data1=lw[:], initial=0.0, op0=MUL, op1=ADD), [resetm, lw], [cum])
        P.op('pool', lambda e: e.tensor_tensor(out=cump[:], in0=cum[:], in1=lw[:], op=SUB), [cum, lw], [cump])
        P.op('act', lambda e: e.activation(out=Gt[:], in_=cum[:], func=AF.Exp), [cum], [Gt])
        P.op('act', lambda e: e.activation(out=Ginv[:], in_=cum[:], func=AF.Exp, scale=-1.0), [cum], [Ginv])
        P.op('act', lambda e: e.activation(out=Gp[:], in_=cump[:], func=AF.Exp), [cump], [Gp])
        P.op('pool', lambda e: e.tensor_scalar(out=kk32[:], in0=k32[:], scalar1=pcc('k_k', c), scalar2=None, op0=MUL), [k32, pc], [kk32])
        sqk = bfv(misc)[:, 0:TT]
        P.op('act', lambda e: e.activation(out=sqk, in_=kk32[:], func=AF.Square), [kk32], [misc])
        ps = nextps()
        P.op('pe', lambda e, ps=ps: e.matmul(ps[:], lhsT=bdones_bf[:], rhs=sqk, start=True, stop=True), [bdones_bf, misc], [ps])
        P.op('act', lambda e, ps=ps: e.activation(out=rn[:], in_=ps[:], func=AF.Ln, bias=1e-20), [ps], [rn])
        P.op('act', lambda e: e.activation(out=rn[:], in_=rn[:], func=AF.Exp, scale=-0.5), [rn], [rn])
        P.op('dve', lambda e: e.scalar_tensor_tensor(out=nkk[:], in0=kk32[:], scalar=-1.0, in1=rn[:], op0=MUL, op1=MUL), [kk32, rn], [nkk])
        P.op('dve', lambda e: e.scalar_tensor_tensor(out=kka[:], in0=nkk[:], scalar=-1.0, in1=a32[:], op0=MUL, op1=MUL), [nkk, a32], [kka])
        P.op('pool', lambda e: e.tensor_scalar(out=t1[:], in0=a32[:], scalar1=-1.0, scalar2=pcc('k_a', c), op0=ADD, op1=MUL), [a32, pc], [t1])
        P.op('dve', lambda e: e.scalar_tensor_tensor(out=keff[:], in0=t1[:], scalar=1.0, in1=k32[:], op0=ADD, op1=MUL), [t1, k32], [keff])
        rkr = bfv(t1)[:, 0:TT]
        P.op('dve', lambda e: e.scalar_tensor_tensor(out=rkr, in0=r32[:], scalar=pcc('r_k', c), in1=keff[:], op0=MUL, op1=MUL), [r32, pc, keff], [t1])
        ps = nextps()
        P.op('pe', lambda e, ps=ps: e.matmul(ps[:], lhsT=bdones_bf[:], rhs=rkr, start=True, stop=True), [bdones_bf, t1], [ps])
        bon = cum
        P.op('act', lambda e, ps=ps: e.activation(out=bon[:], in_=ps[:], func=AF.Copy), [ps], [bon])
        for hh in range(2):
            hs_ = slice(hh * 64, hh * 64 + 64)
            for dstv, dtb, a_, b_ in ((Rv, Rbd, r32, Gt), (Av, Abd, nkk, Gp), (Bv, Bbd, kka, Ginv), (Kv, Kbd, keff, Ginv)):
                P.op('dve', lambda e, dstv=dstv, a_=a_, b_=b_, hs_=hs_: e.tensor_tensor(out=dstv[hs_, :, hs_], in0=r3(a_[hs_, :]), in1=r3(b_[hs_, :]), op=MUL),
                     [a_, b_], [dtb])
            P.op('act', lambda e, hs_=hs_: e.activation(out=Vv[hs_, :, hs_], in_=r3(v32[hs_, :]), func=AF.Copy), [v32], [Vbd])
        y32 = lw
        for hf in range(2):
            SCt = [H[5], H[6]]
            QTt = [H[7], H[8]]
            SCv = [t[:].bitcast(SDT).rearrange("p (j m k) -> p j m k", j=2, m=4) for t in SCt]
            QTv = [t[:].bitcast(SDT).rearrange("p (j m k) -> p j m k", j=2, m=4) for t in QTt]
            SCf = [t[:].rearrange("p (j m k) -> p j m k", j=2, m=4) for t in SCt]
            PQt = [H[9], H[10]]
            PQv = [[PQt[par][:].bitcast(SDT).rearrange("p (b j m k) -> p b j m k", b=2, j=2, m=2)[:, b] for b in range(2)] for par in range(2)]
            Xt = H[11]
            Xv = [Xt[:].bitcast(SDT).rearrange("p (q j k) -> p q j k", q=2, j=4)[:, q] for q in range(2)]
            Xf = [Xt[:].rearrange("p (q j k) -> p q j k", q=2, j=4)[:, q] for q in range(2)]
            for j in range(4):
                n = hf * 4 + j
                sc = SCv[j // 2][:, j % 2]
                qt = QTv[j // 2][:, j % 2]
                A_, B_ = PS[2], PS[3]
                for m, (l_, r_) in enumerate(((Bv, Av), (Kv, Av), (Bv, Rv), (Kv, Rv))):
                    P.op('pe', lambda e, m=m, l_=l_, r_=r_, n=n: e.matmul(A_[:, m * 128:(m + 1) * 128], lhsT=l_[:, n, :], rhs=r_[:, n, :], start=True, stop=True),
                         [Rbd, Abd, Bbd, Kbd], [A_])
                P.op('dve', lambda e, sc=sc: e.tensor_tensor(out=sc, in0=A_[:].rearrange("p (m k) -> p m k", m=4), in1=maskA[:], op=MUL),
                     [A_, maskA], [SCt[j // 2]])
                P.op('pe', lambda e, n=n: e.matmul(B_[:, 0:128], lhsT=Av[:, n, :], rhs=Bv[:, n, :], start=True, stop=True), [Abd, Bbd], [B_])
                for m, l_ in enumerate((Bv, Kv, Vv)):
                    P.op('pe', lambda e, m=m, l_=l_, n=n: e.matmul(B_[:, (m + 1) * 128:(m + 2) * 128], lhsT=l_[:, n, :], rhs=identr[:], start=True, stop=True),
                         [Bbd, Kbd, Vbd, identr], [B_])
                P.op('dve', lambda e, qt=qt: e.tensor_tensor(out=qt[:, 0, :], in0=B_[:, 0:128], in1=maskB[:], op=MUL), [B_, maskB], [QTt[j // 2]])
                P.op('act', lambda e, qt=qt: e.activation(out=qt[:, 1:4, :], in_=B_[:, 128:512].rearrange("p (m k) -> p m k", m=3), func=AF.Copy),
                     [B_], [QTt[j // 2]])
            for j in range(4):
                P.op('dve', lambda e, j=j: e.tensor_tensor(out=Xv[1][:, j, :], in0=SCf[j // 2][:, j % 2, 0, :], in1=ident[:], op=ADD),
                     [SCt[j // 2], ident], [Xt])
            for lvl in range(1, 6):
                par = lvl % 2
                for b in range(2):
                    bank = PS[4 + b]
                    for jj in range(2):
                        j = 2 * b + jj
                        if lvl == 1:
                            Pp = SCv[j // 2][:, j % 2, 0, :]
                            Qp = QTv[j // 2][:, j % 2, 0, :]
                            rd = [SCt[j // 2], QTt[j // 2]]
                        else:
                            Pp = PQv[1 - par][b][:, jj, 0, :]
                            Qp = PQv[1 - par][b][:, jj, 1, :]
                            rd = [PQt[1 - par]]
                        P.op('pe', lambda e, jj=jj, Pp=Pp, Qp=Qp, bank=bank: e.matmul(bank[:, (2 * jj) * 128:(2 * jj + 1) * 128], lhsT=Qp, rhs=Pp, start=True, stop=True), rd, [bank])
                        P.op('pe', lambda e, jj=jj, Pp=Pp, Qp=Qp, bank=bank: e.matmul(bank[:, (2 * jj + 1) * 128:(2 * jj + 2) * 128], lhsT=Pp, rhs=Qp, start=True, stop=True), rd, [bank])
                    P.op('act', lambda e, b=b, bank=bank, par=par: e.activation(out=PQv[par][b], in_=bank[:].rearrange("p (j m k) -> p j m k", j=2, m=2), func=AF.Copy),
                         [bank], [PQt[par]])
                XB = PS[6]
                for j in range(4):
                    Ql = PQv[par][j // 2][:, j % 2, 1, :]
                    P.op('pe', lambda e, j=j, Ql=Ql, par=par: e.matmul(XB[:, j * 128:(j + 1) * 128], lhsT=Ql, rhs=Xv[par][:, j, :], start=True, stop=True), [PQt[par], Xt], [XB])
                P.op('dve', lambda e, par=par: e.tensor_tensor(out=Xv[1 - par], in0=XB[:].rearrange("p (j k) -> p j k", j=4), in1=Xf[par], op=ADD), [XB, Xt], [Xt])
            for j in range(4):
                n = hf * 4 + j
                sc = SCv[j // 2][:, j % 2]
                qt = QTv[j // 2][:, j % 2]
                sct, qtt = SCt[j // 2], QTt[j // 2]
                S0 = Sst[c][spar[c]]
                S1 = Sst[c][1 - spar[c]]
                spar[c] = 1 - spar[c]
                P.op('pe', lambda e, n=n, S0=S0: e.matmul(psRHS[:], lhsT=Av[:, n, :], rhs=S0[:], start=True, stop=False), [Abd, S0], [psRHS])
                P.op('pe', lambda e, sc=sc, qt=qt: e.matmul(psRHS[:], lhsT=sc[:, 1, :], rhs=qt[:, 3, :], start=False, stop=True), [sct, qtt], [psRHS])
                P.op('act', lambda e: e.activation(out=rhs_sb[:], in_=psRHS[:], func=AF.Copy), [psRHS], [rhs_sb])
                P.op('pe', lambda e, j=j: e.matmul(psU[:], lhsT=Xv[0][:, j, :], rhs=rhs_sb[:], start=True, stop=True), [Xt, rhs_sb], [psU])
                P.op('act', lambda e: e.activation(out=u_sb[:], in_=psU[:], func=AF.Copy), [psU], [u_sb])
                P.op('pe', lambda e, qt=qt: e.matmul(psSN[:], lhsT=qt[:, 1, :], rhs=u_sb[:], start=True, stop=False), [qtt, u_sb], [psSN])
                P.op('pe', lambda e, qt=qt: e.matmul(psSN[:], lhsT=qt[:, 2, :], rhs=qt[:, 3, :], start=False, stop=False), [qtt], [psSN])
                P.op('pe', lambda e, S0=S0: e.matmul(psSN[:], lhsT=identr[:], rhs=S0[:], start=False, stop=True), [identr, S0], [psSN])
                P.op('act', lambda e, n=n, S1=S1: e.activation(out=S1[:], in_=psSN[:], func=AF.Identity, scale=Gt[:, n * 64 + 63:n * 64 + 64]), [psSN, Gt], [S1])
                P.op('pe', lambda e, n=n, S0=S0: e.matmul(psYT[:], lhsT=S0[:], rhs=Rv[:, n, :], start=True, stop=False), [S0, Rbd], [psYT])
                P.op('pe', lambda e, sc=sc: e.matmul(psYT[:], lhsT=u_sb[:], rhs=sc[:, 2, :], start=False, stop=False), [u_sb, sct], [psYT])
                P.op('pe', lambda e, sc=sc, qt=qt: e.matmul(psYT[:], lhsT=qt[:, 3, :], rhs=sc[:, 3, :], start=False, stop=True), [qtt, sct], [psYT])
                for hh in range(2):
                    hs_ = slice(hh * 64, hh * 64 + 64)
                    P.op('act', lambda e, hs_=hs_, n=n: e.activation(out=y32[hs_, n * 64:(n + 1) * 64], in_=psYT[hs_, hs_], func=AF.Copy), [psYT], [y32])
        dump('y_rwkv%d' % c, y32, y32[:])
        yr = cump
        yrv = yr[:].bitcast(SDT)
        P.op('act', lambda e: e.activation(out=yrv, in_=y32[:], func=AF.Copy), [y32], [yr])
        ps = nextps()
        P.op('pe', lambda e, ps=ps: e.matmul(ps[:], lhsT=bdones_r[:], rhs=yrv, start=True, stop=True), [bdones_r, yr], [ps])
        dd = Gt
        ddv = dd[:].bitcast(SDT)
        P.op('dve', lambda e, ps=ps: e.scalar_tensor_tensor(out=dd[:], in0=ps[:], scalar=-1.0 / 64, in1=y32[:], op0=MUL, op1=ADD), [ps, y32], [dd])
        sq = Ginv
        sqv = sq[:].bitcast(SDT)
        P.op('act', lambda e: e.activation(out=sqv, in_=dd[:], func=AF.Square), [dd], [sq])
        ps = nextps()
        P.op('pe', lambda e, ps=ps: e.matmul(ps[:], lhsT=bdones_r[:], rhs=sqv, start=True, stop=True), [bdones_r, sq], [ps])
        rs = Gp
        P.op('act', lambda e, ps=ps: e.activation(out=rs[:], in_=ps[:], func=AF.Ln, scale=1.0 / 64, bias=64e-5), [ps], [rs])
        P.op('act', lambda e: e.activation(out=rs[:], in_=rs[:], func=AF.Exp, scale=-0.5), [rs], [rs])
        P.op('dve', lambda e: e.tensor_tensor(out=dd[:], in0=dd[:], in1=rs[:], op=MUL), [dd, rs], [dd])
        P.op('act', lambda e: e.activation(out=dd[:], in_=dd[:], func=AF.Identity, scale=pcc('ln_w', c), bias=pcc('ln_b', c)), [dd, pc], [dd])
        P.op('pool', lambda e: e.tensor_tensor(out=bon[:], in0=bon[:], in1=v32[:], op=MUL), [bon, v32], [bon])
        P.op('pool', lambda e: e.tensor_tensor(out=dd[:], in0=dd[:], in1=bon[:], op=ADD), [dd, bon], [dd])
        P.op('dve', lambda e: e.tensor_tensor(out=merged[:, c, :], in0=dd[:], in1=gr32[:], op=MUL), [dd, gr32], [merged])

    if do_l0 and do_mlstm:
        causal = cload('causal')
        sel = P.sb("c_sel", [4, 4, 128])
        P.dma('sp', sel[:], din['sel'].ap(), writes=[sel])
        onesf = P.sb("onesf", [128, TT])
        P.op('pool', lambda e: e.memset(onesf[:], 1.0), [], [onesf])
        ones_r = P.sb("ones_r", [128, 128], SDT)
        P.op('dve', lambda e: e.tensor_copy(out=ones_r[:], in_=onesf[:, 0:128]), [onesf], [ones_r])
        m_tail = P.sb("m_tail", [128, 8, 3])
        P.op('pool', lambda e: e.memset(m_tail[:], 0.0), [], [m_tail])
        CT32 = P.sb("CT32", [128, 4, 2, 256])
        CTbf = P.sb("CTbf", [128, 4, 2, 256], BF16)
        nr32 = P.sb("nr32", [128, 4, 2, 128])
        nrbf = P.sb("nrbf", [128, 4, 2, 128], BF16)
        for t_ in (CT32, CTbf, nr32, nrbf):
            P.op('pool', lambda e, t_=t_: e.memset(t_[:], 0.0), [], [t_])
        mcar = P.sb("mcar", [4, 2])
        P.op('pool', lambda e: e.memset(mcar[:], 0.0), [], [mcar])
        MxE = P.sb("MxE", [4, 5])
        dec = P.sb("dec", [4, 4])
        smallb = P.sb("smallb", [128, 32])
        qe = P.sb("qe", [128, 2, 128], BF16)
        qw = P.sb("qw", [128, 2, 128], BF16)
        kg = P.sb("kg", [128, 2, 128], BF16)
        kgt = P.sb("kgt", [128, 256], BF16)
        st_bf = P.sb("st_bf", [128, 128], BF16)
        ddm = P.sb("ddm", [128, 128])
        recm = P.sb("recm", [128, 128])

    def mlstm(t0):
        qTt, kTt, vTt = [H[0], H[1]], [H[2], H[3]], [H[4], H[5]]
        ktt, vtt = [H[6], H[7]], [H[8], H[9]]
        xct = [H[10], H[11]]

        def fm(tl, c):
            return bfv(tl[c // 4]).rearrange("p (k t) -> p k t", k=4)[:, c % 4, :]

        def tokv(tl, tb):
            return bfv(tl[tb // 2]).rearrange("p (b d) -> p b d", b=2)[:, tb % 2, :]
        acc, xmb = G[9], G[10]
        xmb_ap = bfv(xmb)[:, 0:TT]
        gi_ps, gf_ps = PS[2], PS[3]
        for c in range(8):
            ps = proj_x('w0in', 25 + c)
            conv4(ps, c, m_tail, 'mcw', 'mcb', acc)
            P.op('act', lambda e, c=c: e.activation(out=fm(xct, c), in_=acc[:], func=AF.Silu), [acc], [xct[c // 4]])
            P.op('pool', lambda e: e.tensor_copy(out=xmb_ap, in_=ext[:, 3:TT + 3]), [ext], [xmb])
            for t_, (dst, src_ap, src_tb) in enumerate(((qTt, fm(xct, c), xct[c // 4]), (kTt, fm(xct, c), xct[c // 4]), (vTt, xmb_ap, xmb))):
                ps = nextps()
                P.op('pe', lambda e, ps=ps, t_=t_, src_ap=src_ap: e.matmul(ps[:], lhsT=wqkv_bf[:, t_, c, :], rhs=src_ap, start=True, stop=True),
                     [wqkv_bf, src_tb], [ps])
                P.op('act' if t_ != 1 else 'dve',
                     (lambda e, ps=ps, dst=dst: e.activation(out=fm(dst, c), in_=ps[:], func=AF.Copy)) if t_ != 1 else
                     (lambda e, ps=ps, dst=dst: e.tensor_copy(out=fm(dst, c), in_=ps[:])), [ps], [dst[c // 4]])
                j = t_ * 8 + c
                first, last = (c == 0 and t_ == 0), (c == 7 and t_ == 2)
                P.op('pe', lambda e, j=j, dst=dst, first=first, last=last: e.matmul(gi_ps[0:4, :], lhsT=wif_bf[:, 0, j, :], rhs=fm(dst, c), start=first, stop=last),
                     [wif_bf, dst[c // 4]], [gi_ps])
                P.op('pe', lambda e, j=j, dst=dst, first=first, last=last: e.matmul(gf_ps[0:4, :], lhsT=wif_bf[:, 1, j, :], rhs=fm(dst, c), start=first, stop=last),
                     [wif_bf, dst[c // 4]], [gf_ps])
            for t_, (dstl, src_ap, src_tb) in ((1, (ktt, fm(xct, c), xct[c // 4])), (2, (vtt, xmb_ap, xmb))):
                ps = nextps()
                for tb in range(NT):
                    P.op('pe', lambda e, ps=ps, tb=tb, t_=t_, src_ap=src_ap: e.matmul(ps[:, tb * 128:(tb + 1) * 128], lhsT=src_ap[:, tb * 128:(tb + 1) * 128], rhs=wqkv_bf[:, t_, c, :],
                                                                                      start=True, stop=True), [wqkv_bf, src_tb], [ps])
                for half in range(2):
                    dv = bfv(dstl[half]).rearrange("p (b d) -> p b d", b=2)[:, :, c * 128:(c + 1) * 128]
                    P.op('act' if half == 0 else 'dve',
                         (lambda e, ps=ps, dv=dv, half=half: e.activation(out=dv, in_=ps[:, half * 256:(half + 1) * 256].rearrange("p (b d) -> p b d", b=2), func=AF.Copy)) if half == 0 else
                         (lambda e, ps=ps, dv=dv, half=half: e.tensor_copy(out=dv, in_=ps[:, half * 256:(half + 1) * 256].rearrange("p (b d) -> p b d", b=2))),
                         [ps], [dstl[half]])
        rg_, re2, rwi, rem, rli, rFp, rag, rMx, rtmp = G[11], G[12], G[13], G[14], G[15], G[16], G[17], G[18], G[19]
        R4 = slice(0, 4)
        P.op('act', lambda e: e.activation(out=rli[R4, :], in_=gi_ps[R4, :], func=AF.Identity, bias=pc[R4, PC['bif_i']:PC['bif_i'] + 1]), [gi_ps, pc], [rli])
        P.op('act', lambda e: e.activation(out=rtmp[R4, :], in_=gf_ps[R4, :], func=AF.Exp, scale=-1.0, bias=pd[R4, PD['negbf']:PD['negbf'] + 1]), [gf_ps, pd], [rtmp])
        P.op('act', lambda e: e.activation(out=rtmp[R4, :], in_=rtmp[R4, :], func=AF.Ln, bias=1.0), [rtmp], [rtmp])
        P.op('dve', lambda e: e.tensor_tensor_scan(out=rFp[R4, :], data0=onesf[R4, :], data1=rtmp[R4, :], initial=mcar[:, 0:1], op0=MUL, op1=ADD),
             [onesf, rtmp, mcar], [rFp])
        P.op('pool', lambda e: e.tensor_tensor(out=rag[R4, :], in0=rli[R4, :], in1=rFp[R4, :], op=ADD), [rli, rFp], [rag])
        P.op('dve', lambda e: e.tensor_tensor_scan(out=rMx[R4, :], data0=rag[R4, :], data1=rag[R4, :], initial=mcar[:, 1:2], op0=MAX, op1=MAX),
             [rag, mcar], [rMx])
        P.op('pool', lambda e: e.tensor_copy(out=MxE[:, 0:1], in_=mcar[:, 1:2]), [mcar], [MxE])
        P.op('pool', lambda e: e.tensor_copy(out=MxE[:, 1:5], in_=rMx[R4, 127:TT:128]), [rMx], [MxE])
        P.op('pool', lambda e: e.tensor_copy(out=mcar[:, 0:1], in_=rFp[R4, TT - 1:TT]), [rFp], [mcar])
        P.op('pool', lambda e: e.tensor_copy(out=mcar[:, 1:2], in_=rMx[R4, TT - 1:TT]), [rMx], [mcar])
        v3 = lambda t_: t_[R4, :].rearrange("p (q t) -> p q t", q=NT)
        mend = MxE[:, 1:5].unsqueeze(2).broadcast_to([4, NT, 128])
        mprev = MxE[:, 0:4].unsqueeze(2).broadcast_to([4, NT, 128])
        P.op('dve', lambda e: e.tensor_tensor(out=v3(rg_), in0=v3(rag), in1=mend, op=SUB), [rag, MxE], [rg_])
        P.op('act', lambda e: e.activation(out=rg_[R4, :], in_=rg_[R4, :], func=AF.Exp), [rg_], [rg_])
        P.op('dve', lambda e: e.tensor_tensor(out=v3(re2), in0=mend, in1=v3(rMx), op=SUB), [rMx, MxE], [re2])
        P.op('act', lambda e: e.activation(out=re2[R4, :], in_=re2[R4, :], func=AF.Exp), [re2], [re2])
        P.op('dve', lambda e: e.tensor_tensor(out=v3(rwi), in0=mprev, in1=v3(rMx), op=SUB), [rMx, MxE], [rwi])
        P.op('act', lambda e: e.activation(out=rwi[R4, :], in_=rwi[R4, :], func=AF.Exp), [rwi], [rwi])
        P.op('dve', lambda e: e.tensor_tensor(out=rem[R4, :], in0=rFp[R4, :], in1=rMx[R4, :], op=SUB), [rFp, rMx], [rem])
        P.op('act', lambda e: e.activation(out=rem[R4, :], in_=rem[R4, :], func=AF.Exp), [rem], [rem])
        P.op('dve', lambda e: e.tensor_tensor(out=dec[:], in0=MxE[:, 0:4], in1=MxE[:, 1:5], op=SUB), [MxE], [dec])
        P.op('act', lambda e: e.activation(out=dec[:], in_=dec[:], func=AF.Exp), [dec], [dec])
        sp_ = PS[5]
        for q in range(NT):
            P.op('pe', lambda e, q=q: e.matmul(sp_[:, q * 4:(q + 1) * 4], lhsT=rg_[R4, q * 128:(q + 1) * 128], rhs=ident[0:4, 0:4], start=True, stop=True),
                 [rg_, ident], [sp_])
        for h in range(4):
            P.op('pe', lambda e, h=h: e.matmul(sp_[:, 16 + h * 4:16 + (h + 1) * 4], lhsT=sel[:, h, :], rhs=dec[:], start=True, stop=True), [sel, dec], [sp_])
        P.op('act', lambda e: e.activation(out=smallb[:], in_=sp_[:, 0:32], func=AF.Copy), [sp_], [smallb])
        for h in range(4):
            mlstm_head(h, qTt, kTt, ktt, vtt, xct, fm, tokv, (rg_, re2, rwi, rem))

    def mlstm_head(h, qTt, kTt, ktt, vtt, xct, fm, tokv, rows):
        bc = G[0:4]
        for i in range(4):
            ps = nextps()
            P.op('pe', lambda e, ps=ps, i=i: e.matmul(ps[:], lhsT=sel[:, h, :], rhs=rows[i][0:4, :], start=True, stop=True), [sel, rows[i]], [ps])
            P.op('act' if i % 2 == 0 else 'dve',
                 (lambda e, ps=ps, i=i: e.activation(out=bc[i][:], in_=ps[:], func=AF.Copy)) if i % 2 == 0 else
                 (lambda e, ps=ps, i=i: e.tensor_copy(out=bc[i][:], in_=ps[:])), [ps], [bc[i]])
        g_bc, e2_bc, wi_bc, em_bc = bc
        h32 = [G[4], G[5]]
        gms = [G[6], G[7]]
        for vc in range(2):
            ps = proj_x('w0in', 41 + 2 * h + vc)
            P.op('act', lambda e, ps=ps, vc=vc: e.activation(out=gms[vc][:], in_=ps[:], func=AF.Silu), [ps], [gms[vc]])
        qt_ = qTt[h // 2]
        kt_ = kTt[h // 2]
        qv = bfv(qt_).rearrange("p (k t) -> p k t", k=4)[:, 2 * (h % 2):2 * (h % 2) + 2, :]
        kv = bfv(kt_).rearrange("p (k t) -> p k t", k=4)[:, 2 * (h % 2):2 * (h % 2) + 2, :]
        CB, SB_, NB = PS[6], PS[7], PS[5]
        for q in range(NT):
            ts_ = slice(q * 128, (q + 1) * 128)
            bcast = lambda t_: t_[:, ts_].unsqueeze(1).broadcast_to([128, 2, 128])
            P.op('dve', lambda e: e.tensor_tensor(out=qe[:], in0=qv[:, :, ts_], in1=bcast(e2_bc), op=MUL), [qt_, e2_bc], [qe])
            P.op('pool', lambda e: e.tensor_tensor(out=qw[:], in0=qv[:, :, ts_], in1=bcast(wi_bc), op=MUL), [qt_, wi_bc], [qw])
            P.op('dve', lambda e: e.scalar_tensor_tensor(out=kg[:], in0=kv[:, :, ts_], scalar=0.0625, in1=bcast(g_bc), op0=MUL, op1=MUL), [kt_, g_bc], [kg])
            ktok = tokv(ktt, q)[:, h * 256:(h + 1) * 256]
            vtok = tokv(vtt, q)[:, h * 256:(h + 1) * 256]
            P.op('pool', lambda e: e.tensor_scalar(out=kgt[:], in0=ktok, scalar1=smallb[:, q * 4 + h:q * 4 + h + 1], scalar2=0.0625, op0=MUL, op1=MUL),
                 [ktt[q // 2], smallb], [kgt])
            for kc in range(2):
                P.op('pe', lambda e, kc=kc: e.matmul(CB[:, 0:128], lhsT=kg[:, kc, :], rhs=qe[:, kc, :], start=(kc == 0), stop=(kc == 1)), [kg, qe], [CB])
            P.op('dve', lambda e: e.tensor_tensor(out=st_bf[:], in0=CB[:, 0:128], in1=causal[:], op=MUL), [CB, causal], [st_bf])
            for vc in range(2):
                o_ = CB[:, 128 + vc * 128:256 + vc * 128]
                P.op('pe', lambda e, o_=o_, vc=vc: e.matmul(o_, lhsT=vtok[:, vc * 128:(vc + 1) * 128], rhs=st_bf[:], start=True, stop=False), [vtt[q // 2], st_bf], [CB])
                for kc in range(2):
                    P.op('pe', lambda e, o_=o_, vc=vc, kc=kc: e.matmul(o_, lhsT=CTbf[:, h, kc, vc * 128:(vc + 1) * 128], rhs=qw[:, kc, :], start=False, stop=(kc == 1)),
                         [CTbf, qw], [CB])
            o_ = CB[:, 384:512]
            P.op('pe', lambda e, o_=o_: e.matmul(o_, lhsT=ones_bf[:], rhs=st_bf[:], start=True, stop=False), [ones_bf, st_bf], [CB])
            for kc in range(2):
                P.op('pe', lambda e, o_=o_, kc=kc: e.matmul(o_, lhsT=nrbf[:, h, kc, :], rhs=qw[:, kc, :], start=False, stop=(kc == 1)), [nrbf, qw], [CB])
            P.op('act', lambda e: e.activation(out=ddm[:], in_=CB[:, 384:512], func=AF.Abs), [CB], [ddm])
            P.op('dve', lambda e: e.tensor_tensor(out=ddm[:], in0=ddm[:], in1=em_bc[:, ts_], op=MAX), [ddm, em_bc], [ddm])
            P.op('dve', lambda e: e.tensor_scalar(out=ddm[:], in0=ddm[:], scalar1=1e-6, scalar2=None, op0=ADD), [ddm], [ddm])
            P.op('dve', lambda e: e.reciprocal(out=recm[:], in_=ddm[:]), [ddm], [recm])
            for vc in range(2):
                P.op('dve', lambda e, vc=vc: e.tensor_tensor(out=h32[vc][:, ts_], in0=CB[:, 128 + vc * 128:256 + vc * 128], in1=recm[:], op=MUL), [CB, recm], [h32[vc]])
            for kc in range(2):
                P.op('pe', lambda e, kc=kc: e.matmul(SB_[:, kc * 256:(kc + 1) * 256], lhsT=kgt[:, kc * 128:(kc + 1) * 128], rhs=vtok, start=True, stop=True),
                     [kgt, vtt[q // 2]], [SB_])
                P.op('pe', lambda e, kc=kc: e.matmul(NB[:, 64 + kc * 128:64 + (kc + 1) * 128], lhsT=kgt[:, kc * 128:(kc + 1) * 128], rhs=ones_bf[:], start=True, stop=True),
                     [kgt, ones_bf], [NB])
            dcol = smallb[:, 16 + h * 4 + q:16 + h * 4 + q + 1]
            P.op('dve', lambda e: e.scalar_tensor_tensor(out=CT32[:, h], in0=CT32[:, h], scalar=dcol, in1=SB_[:].rearrange("p (k v) -> p k v", k=2), op0=MUL, op1=ADD),
                 [CT32, smallb, SB_], [CT32])
            P.op('act', lambda e: e.activation(out=CTbf[:, h], in_=CT32[:, h], func=AF.Copy), [CT32], [CTbf])
            P.op('dve', lambda e: e.scalar_tensor_tensor(out=nr32[:, h], in0=nr32[:, h], scalar=dcol, in1=NB[:, 64:320].rearrange("p (k v) -> p k v", k=2), op0=MUL, op1=ADD),
                 [nr32, smallb, NB], [nr32])
            P.op('pool', lambda e: e.tensor_copy(out=nrbf[:, h], in_=nr32[:, h]), [nr32], [nrbf])
        dump('h_m%d' % h, h32[0], h32[0][:])
        hr = [G[8], G[9]]
        ps = nextps()
        for vc in range(2):
            P.op('act', lambda e, vc=vc: e.activation(out=hr[vc][:].bitcast(SDT), in_=h32[vc][:], func=AF.Copy), [h32[vc]], [hr[vc]])
            P.op('pe', lambda e, ps=ps, vc=vc: e.matmul(ps[:], lhsT=ones_r[:], rhs=hr[vc][:].bitcast(SDT), start=(vc == 0), stop=(vc == 1)), [ones_r, hr[vc]], [ps])
        for vc in range(2):
            P.op('dve', lambda e, ps=ps, vc=vc: e.scalar_tensor_tensor(out=h32[vc][:], in0=ps[:], scalar=-1.0 / 256, in1=h32[vc][:], op0=MUL, op1=ADD), [ps, h32[vc]], [h32[vc]])
        ps2 = nextps()
        for vc in range(2):
            P.op('act', lambda e, vc=vc: e.activation(out=hr[vc][:].bitcast(SDT), in_=h32[vc][:], func=AF.Square), [h32[vc]], [hr[vc]])
            P.op('pe', lambda e, ps2=ps2, vc=vc: e.matmul(ps2[:], lhsT=ones_r[:], rhs=hr[vc][:].bitcast(SDT), start=(vc == 0), stop=(vc == 1)), [ones_r, hr[vc]], [ps2])
        rs = G[10]
        P.op('act', lambda e, ps2=ps2: e.activation(out=rs[:], in_=ps2[:], func=AF.Ln, scale=1.0 / 256, bias=1e-5), [ps2], [rs])
        P.op('act', lambda e: e.activation(out=rs[:], in_=rs[:], func=AF.Exp, scale=-0.5), [rs], [rs])
        for vc in range(2):
            ch = 2 * h + vc
            P.op('dve', lambda e, vc=vc: e.tensor_tensor(out=h32[vc][:], in0=h32[vc][:], in1=rs[:], op=MUL), [h32[vc], rs], [h32[vc]])
            P.op('act', lambda e, vc=vc, ch=ch: e.activation(out=h32[vc][:], in_=h32[vc][:], func=AF.Identity, scale=pcc('mnorm', ch)), [h32[vc], pc], [h32[vc]])
            P.op('dve', lambda e, vc=vc, ch=ch: e.scalar_tensor_tensor(out=h32[vc][:], in0=fm(xct, ch), scalar=pcc('mskip', ch), in1=h32[vc][:], op0=MUL, op1=ADD),
                 [xct[ch // 4], pc, h32[vc]], [h32[vc]])
            P.op('pool', lambda e, vc=vc, ch=ch: e.tensor_tensor(out=merged[:, 8 + ch, :], in0=h32[vc][:], in1=gms[vc][:], op=MUL), [h32[vc], gms[vc]], [merged])

    def layer0(t0):
        rmsnorm_x('mixn0')
        if do_rwkv:
            rwkv(t0)
        if do_mlstm:
            mlstm(t0)
        out_proj('w0out')

    if do_l0 and not (do_rwkv and do_mlstm):
        P.op('pool', lambda e: e.memset(merged[:], 0.0), [], [merged])

    for ti in range(ntiles):
        t0 = ti * TT
        for tb in range(NT):
            P.dma('sp', H[tb][:], x_d.ap()[t0 + tb * 128:t0 + (tb + 1) * 128, :], writes=[H[tb]])
        for kc in range(8):
            ps = nextps()
            for tb in range(NT):
                P.op('pe', lambda e, kc=kc, tb=tb, ps=ps: e.transpose(ps[:, tb * 128:(tb + 1) * 128], H[tb][:, kc * 128:(kc + 1) * 128], ident[:]),
                     [H[tb], ident], [ps])
            P.op('act' if kc % 2 == 0 else 'dve',
                 (lambda e, kc=kc, ps=ps: e.activation(out=hT[:, kc, :], in_=ps[:], func=AF.Copy)) if kc % 2 == 0 else
                 (lambda e, kc=kc, ps=ps: e.tensor_copy(out=hT[:, kc, :], in_=ps[:])), [ps], [hT])
        if do_l0:
            layer0(t0)
            ple(0, t0)
        if do_l1:
            layer1(t0)
            ple(1, t0)
        norm_stats()
        for k in range(8):
            of = G[k % 4]
            norm_apply(k, 'finn', of[:], of)
            ps = nextps()
            for tb in range(NT):
                P.op('pe', lambda e, tb=tb, ps=ps, of=of: e.transpose(ps[:, tb * 128:(tb + 1) * 128], of[:, tb * 128:(tb + 1) * 128], ident[:]),
                     [of, ident], [ps])
            for tb in range(NT):
                P.op('act' if k % 2 == 0 else 'dve',
                     (lambda e, k=k, ps=ps, tb=tb: e.activation(out=H[tb][:, k * 128:(k + 1) * 128], in_=ps[:, tb * 128:(tb + 1) * 128], func=AF.Copy)) if k % 2 == 0 else
                     (lambda e, k=k, ps=ps, tb=tb: e.tensor_copy(out=H[tb][:, k * 128:(k + 1) * 128], in_=ps[:, tb * 128:(tb + 1) * 128])),
                     [ps], [H[tb]])
        for tb in range(NT):
            P.dma('sp', o_d.ap()[t0 + tb * 128:t0 + (tb + 1) * 128, :], H[tb][:], reads=[H[tb]])
    P.finish()
    print("sbuf bytes/partition:", P.sbytes, "sems:", P.nsem, "instr:", {e: P.cnt[e] for e in ENGS})
    return nc


def kernel(**inputs):
    x = np.asarray(inputs['x'], np.float32)
    p = np.asarray(inputs['p'], np.float32)
    B, S, _ = x.shape
    shared = host_prep(inputs)
    nc = build_program(S)
    in_maps = []
    for b in range(B):
        m = dict(shared)
        m['x'] = np.ascontiguousarray(x[b])
        m['p'] = np.ascontiguousarray(p[:, b])
        in_maps.append(m)
    res = run_bass_kernel_spmd(nc, in_maps, core_ids=list(range(B)))
    return np.stack([np.asarray(r['out'], np.float32) for r in res.results], axis=0)
```

```python
import numpy as np
import concourse.bass as bass
import concourse.mybir as mybir
from concourse.bass_utils import run_bass_kernel_spmd
from contextlib import ExitStack

F32 = mybir.dt.float32
BF16 = mybir.dt.bfloat16
F32R = mybir.dt.float32r
SDT = F32
AF = mybir.ActivationFunctionType
ALU = mybir.AluOpType
AX = mybir.AxisListType

ENGS = ('pe', 'act', 'dve', 'pool', 'sp')
EIDX = {e: i for i, e in enumerate(ENGS)}
EPOCH = 30000

D = 1024
TT = 512
NT = TT // 128
PLE = 256
DECAY_SCALE = 0.6065306597126334


class TB:
    __slots__ = ('name', 'h', 'lw', 'rd', 'dkey', 'root', 'psum')

    def __init__(self, name, h, root=None, psum=False):
        self.name = name
        self.h = h
        self.lw = None
        self.rd = {}
        self.dkey = None
        self.root = root if root is not None else self
        self.psum = psum or (root is not None and root.psum)

    def __getitem__(self, k):
        return self.h[k]


class _Rec:
    def __init__(self):
        self.call = None

    def __getattr__(self, name):
        def f(*a, **k):
            self.call = (name, a, k)
        return f


class Prog:
    def __init__(self, nc):
        self.nc = nc
        self.es = ExitStack()
        self.ops = {e: [] for e in ENGS}
        self.cnt = {e: 0 for e in ENGS}
        self.seen = {e: {} for e in ENGS}
        self.clk = {e: [] for e in ENGS}
        self.dcnt = {}
        self.dclk = {}
        self.sems = {}
        self.nsem = 0
        self.ndma = 0
        self.sbytes = 0

    def sb(self, name, shape, dt=F32):
        n = 1
        for s in shape[1:]:
            n *= s
        self.sbytes += n * (2 if dt == BF16 else 4)
        return TB(name, self.es.enter_context(self.nc.sbuf_tensor(name, list(shape), dt)))

    def ps(self, name, shape, dt=F32):
        return TB(name, self.es.enter_context(self.nc.psum_tensor(name, list(shape), dt)), psum=True)

    def dram(self, name, shape, dt, kind):
        return TB(name, self.nc.dram_tensor(name, list(shape), dt, kind=kind))

    def _sem(self, key):
        s = self.sems.get(key)
        if s is None:
            s = self.es.enter_context(self.nc.semaphore("s%d" % self.nsem))
            self.nsem += 1
            self.sems[key] = s
        return s

    def _semval(self, key, count):
        if key in EIDX:
            ep = (count - 1) // EPOCH
            return self._sem((key, ep)), count - ep * EPOCH
        return self._sem(key), 16 * count

    def _deps(self, e, reads, writes):
        seen = self.seen[e]
        need = {}

        def req(ev):
            if ev is None:
                return
            k, c = ev
            if k == 'pe' and e == 'pe':
                return
            if seen.get(k, 0) >= c:
                return
            if need.get(k, 0) < c:
                need[k] = c
        for t in reads:
            req(t.lw)
        for t in writes:
            req(t.lw)
            for k, c in t.rd.items():
                req((k, c))
        keys = list(need.keys())
        for k in keys:
            if k not in need:
                continue
            c = need[k]
            ck = self.clk[k][c - 1] if k in EIDX else self.dclk[(k, c)]
            for k2 in keys:
                if k2 != k and k2 in EIDX and k2 in need and ck[EIDX[k2]] >= need[k2]:
                    del need[k2]
        waits = []
        for k, c in need.items():
            waits.append(self._semval(k, c))
            ck = self.clk[k][c - 1] if k in EIDX else self.dclk[(k, c)]
            for e2, v in zip(ENGS, ck):
                if seen.get(e2, 0) < v:
                    seen[e2] = v
            if seen.get(k, 0) < c:
                seen[k] = c
        return waits

    def _snapshot(self, e):
        s = self.seen[e]
        return tuple(s.get(x, 0) for x in ENGS)

    def op(self, e, fn, reads=(), writes=()):
        writes = [t.root for t in writes] + [t.root for t in reads if t.psum]
        reads = [t.root for t in reads if not t.psum]
        rec = _Rec()
        fn(rec)
        fn = rec.call
        waits = self._deps(e, reads, writes)
        self.cnt[e] += 1
        c = self.cnt[e]
        snap = list(self._snapshot(e))
        snap[EIDX[e]] = c
        self.clk[e].append(tuple(snap))
        inc = self._semval(e, c)[0]
        self.ops[e].append((waits, fn, inc, 1))
        ev = (e, c)
        for t in reads:
            if t.rd.get(e, 0) < c:
                t.rd[e] = c
        for t in writes:
            t.lw = ev
            t.rd = {}
        return ev

    def dma(self, q, out, in_, reads=(), writes=(), key=None, **kw):
        if key is None:
            t0 = (list(writes) + list(reads))[0]
            if t0.dkey is None:
                t0.dkey = ('dma', self.ndma)
                self.ndma += 1
            key = t0.dkey
        reads = [t.root for t in reads]
        writes = [t.root for t in writes]
        waits = self._deps(q, reads, writes)
        self.dcnt[key] = self.dcnt.get(key, 0) + 1
        c = self.dcnt[key]
        self.dclk[(key, c)] = self._snapshot(q)
        sem = self._sem(key)
        self.ops[q].append((waits, ('dma_start', (), dict(out=out, in_=in_, **kw)), sem, 16))
        ev = (key, c)
        for t in reads:
            if t.rd.get(key, 0) < c:
                t.rd[key] = c
        for t in writes:
            t.lw = ev
            t.rd = {}
        return ev

    def finish(self):
        e = 'sp'
        tail = []
        for key, c in self.dcnt.items():
            tail.append(self._semval(key, c))
        for x in ENGS:
            if x != e and self.cnt[x] > 0:
                tail.append(self._semval(x, self.cnt[x]))
        blk = self.es.enter_context(self.nc.Block())
        engobj = {'pe': blk.tensor, 'act': blk.scalar, 'dve': blk.vector, 'pool': blk.gpsimd, 'sp': blk.sync}

        def make(ename):
            ops = self.ops[ename]

            def body(eng):
                for waits, fn, inc, n in ops:
                    for (s, v) in waits[1:]:
                        eng.wait_ge(s, v)
                    ins = getattr(eng, fn[0])(*fn[1], **fn[2])
                    if waits:
                        ins._wait_ge(waits[0][0], waits[0][1])
                    ins.then_inc(inc, n)
                if ename == e:
                    for (s, v) in tail:
                        eng.wait_ge(s, v)
            return body
        for ename in ENGS:
            if self.ops[ename] or ename == e:
                engobj[ename](make(ename))
        self.es.close()


PC_SPEC = [
    ('mixn0', 8), ('mixn1', 8), ('pen0', 8), ('pen1', 8), ('finn', 8),
    ('mu_r', 8), ('mu_k', 8), ('mu_v', 8), ('mu_l', 1),
    ('w0', 8), ('a0', 8), ('k_k', 8), ('k_a', 8), ('r_k', 8), ('ln_w', 8), ('ln_b', 8),
    ('mcw0', 8), ('mcw1', 8), ('mcw2', 8), ('mcw3', 8), ('mcb', 8), ('mnorm', 8), ('mskip', 8),
    ('bif_i', 1), ('bif_f', 1),
    ('ccw0', 16), ('ccw1', 16), ('ccw2', 16), ('ccw3', 16), ('ccb', 16), ('cbr', 16), ('cbi', 16), ('clam', 16),
]
PC = {}
_o = 0
for _n, _w in PC_SPEC:
    PC[_n] = _o
    _o += _w
NPC = _o
PD_SPEC = [('omu_r', 8), ('omu_k', 8), ('omu_v', 8), ('omu_l', 1), ('negbf', 1), ('csph', 16), ('cneg', 16), ('t0', 16), ('t1', 16)]
PD = {}
_o = 0
for _n, _w in PD_SPEC:
    PD[_n] = _o
    _o += _w
NPD = _o


def _cols(v):
    v = np.asarray(v, np.float32).reshape(-1)
    return np.ascontiguousarray(v.reshape(-1, 128).T)


def host_consts():
    c = {}
    c['ident'] = np.eye(128, dtype=np.float32)
    bd = np.zeros((128, 128), np.float32)
    bd[:64, :64] = 1
    bd[64:, 64:] = 1
    c['bdones'] = bd
    j = np.arange(128)[:, None]
    i = np.arange(128)[None, :]
    su = ((j < i) & ((j // 64) == (i // 64))).astype(np.float32)
    ui = ((j <= i) & ((j // 64) == (i // 64))).astype(np.float32)
    sl = ((j > i) & ((j // 64) == (i // 64))).astype(np.float32)
    on = np.ones((128, 128), np.float32)
    c['maskA'] = np.stack([su, su, ui, ui], axis=1)
    c['maskB'] = sl
    rm = np.ones((128, TT), np.float32)
    rm[:, ::64] = 0
    c['resetm'] = rm
    c['causal'] = (j <= i).astype(np.float32)
    sel = np.zeros((4, 4, 128), np.float32)
    for h in range(4):
        sel[h, h, :] = 1
    c['sel'] = sel
    return c


def host_prep(inp):
    g = {}
    f = lambda a: np.ascontiguousarray(np.asarray(a, np.float32))

    def chunked(w):
        K, M = w.shape
        return f(w.reshape(K // 128, 128, M // 128, 128).transpose(2, 1, 0, 3))
    g['w0in'] = chunked(f(inp['ab_w_in'])[0])
    g['w0out'] = chunked(f(inp['ab_w_out'])[0])
    g['w1in'] = chunked(f(inp['c_w_in'])[0])
    g['w1out'] = chunked(f(inp['c_w_out'])[0])
    g['pegate'] = np.stack([chunked(f(inp['pe_gate'])[i]) for i in range(2)])
    g['peup'] = np.stack([chunked(f(inp['pe_up'])[i]) for i in range(2)])
    wr = f(inp['c_wr'])[0]
    wi = f(inp['c_wi'])[0]
    g['wri'] = f(np.stack([wr, wi], axis=2))
    lup = np.zeros((128, 2, 1024), np.float32)
    lup[:64, 0, :] = f(inp['rwkv_w_up'])[0]
    lup[64:, 1, :] = f(inp['rwkv_a_up'])[0]
    g['lup'] = lup
    wbd = np.zeros((128, 3, 8, 128), np.float32)
    for t, nm in enumerate(['mlstm_wq', 'mlstm_wk', 'mlstm_wv']):
        w = f(inp[nm])[0]
        for c in range(8):
            for gg in range(32):
                wbd[4 * gg:4 * gg + 4, t, c, 4 * gg:4 * gg + 4] = w[c * 32 + gg]
    g['wqkv'] = wbd
    wif = f(inp['mlstm_w_if'])[0]
    g['wif'] = f(wif.reshape(24, 128, 2, 4).transpose(1, 2, 0, 3))
    pc = np.zeros((128, NPC), np.float32)

    def put(name, v):
        cc = _cols(v)
        pc[:, PC[name]:PC[name] + cc.shape[1]] = cc
    put('mixn0', f(inp['mix_norm'])[0]); put('mixn1', f(inp['mix_norm'])[1])
    put('pen0', f(inp['pe_norm'])[0]); put('pen1', f(inp['pe_norm'])[1])
    put('finn', f(inp['final_norm']))
    mu = f(inp['rwkv_mu'])[0]
    put('mu_r', mu[0]); put('mu_k', mu[1]); put('mu_v', mu[2])
    ml = f(inp['rwkv_mu_lora'])[0]
    put('mu_l', np.concatenate([ml[0], ml[1]]))
    put('w0', f(inp['rwkv_w0'])[0]); put('a0', f(inp['rwkv_a0'])[0])
    put('k_k', f(inp['rwkv_k_k'])[0]); put('k_a', f(inp['rwkv_k_a'])[0])
    put('r_k', f(inp['rwkv_r_k'])[0].reshape(-1))
    put('ln_w', f(inp['rwkv_ln_w'])[0]); put('ln_b', f(inp['rwkv_ln_b'])[0])
    mcw = f(inp['mlstm_conv_w'])[0]
    for j in range(4):
        put('mcw%d' % j, mcw[j])
    put('mcb', f(inp['mlstm_conv_b'])[0])
    put('mnorm', f(inp['mlstm_norm'])[0]); put('mskip', f(inp['mlstm_skip'])[0])
    bif = f(inp['mlstm_b_if'])[0]
    pc[0:4, PC['bif_i']] = bif[0:4]
    pc[0:4, PC['bif_f']] = bif[4:8]
    ccw = f(inp['c_conv_w'])[0]
    for j in range(4):
        put('ccw%d' % j, ccw[j])
    put('ccb', f(inp['c_conv_b'])[0]); put('cbr', f(inp['c_br'])[0]); put('cbi', f(inp['c_bi'])[0])
    put('clam', f(inp['c_lambda'])[0])
    g['pc'] = pc
    g.update(host_consts())
    return g


SHARED_SHAPES = {
    'w0in': [49, 128, 8, 128], 'w0out': [8, 128, 16, 128], 'w1in': [32, 128, 8, 128], 'w1out': [8, 128, 16, 128],
    'pegate': [2, 8, 128, 8, 128], 'peup': [2, 8, 128, 2, 128], 'wri': [16, 128, 2, 128], 'lup': [128, 2, 1024],
    'wqkv': [128, 3, 8, 128], 'wif': [128, 2, 24, 4], 'pc': [128, NPC],
    'ident': [128, 128], 'bdones': [128, 128], 'maskA': [128, 4, 128], 'maskB': [128, 128],
    'resetm': [128, TT], 'causal': [128, 128], 'sel': [4, 4, 128],
}


def build_program(S, do_l0=True, do_rwkv=True, do_mlstm=True, do_l1=True, dbg=None):
    nc = bass.Bass("TRN2", target_bir_lowering=False)
    P = Prog(nc)
    ntiles = S // TT
    din = {}
    for k, shp in SHARED_SHAPES.items():
        din[k] = nc.dram_tensor(k, shp, F32, kind="ExternalInput")
    x_d = nc.dram_tensor("x", [S, D], F32, kind="ExternalInput")
    p_d = nc.dram_tensor("p", [2, S, PLE], F32, kind="ExternalInput")
    o_d = nc.dram_tensor("out", [S, D], F32, kind="ExternalOutput")
    dbg_d = None
    dbg = dbg or []
    if dbg:
        dbg_d = nc.dram_tensor("dbg", [len(dbg), 128, TT], F32, kind="ExternalOutput")

    def dump(name, tb, ap):
        if name in dbg:
            P.dma('sp', dbg_d.ap()[dbg.index(name)], ap, reads=[tb], key=('dma', 'dbg'))

    MUL, ADD, SUB, MAX = ALU.mult, ALU.add, ALU.subtract, ALU.max

    def cload(name, dt=F32, rows=128):
        shp = SHARED_SHAPES[name]
        t = P.sb("c_" + name, shp, dt)
        P.dma('pool' if dt != F32 else 'sp', t[:], din[name].ap(), writes=[t])
        return t
    ident = cload('ident')
    pc = cload('pc')
    pd = P.sb("pd", [128, NPD])
    ones_bf = P.sb("ones_bf", [128, 128], BF16)
    P.op('pool', lambda e: e.memset(ones_bf[:], 1.0), [], [ones_bf])

    def pcc(name, c=0, n=1):
        return pc[:, PC[name] + c:PC[name] + c + n]

    def pdc(name, c=0, n=1):
        return pd[:, PD[name] + c:PD[name] + c + n]

    if do_l1:
        P.op('act', lambda e: e.activation(out=pdc('t0', 0, 16), in_=pcc('clam', 0, 16), func=AF.Exp, scale=-1.0), [pc], [pd])
        P.op('act', lambda e: e.activation(out=pdc('t1', 0, 16), in_=pdc('t0', 0, 16), func=AF.Ln, bias=1.0), [pd], [pd])
        P.op('dve', lambda e: e.tensor_scalar(out=pdc('csph', 0, 16), in0=pdc('t1', 0, 16), scalar1=4.0, scalar2=None, op0=MUL), [pd], [pd])
        P.op('dve', lambda e: e.tensor_scalar(out=pdc('cneg', 0, 16), in0=pdc('t1', 0, 16), scalar1=-8.0, scalar2=None, op0=MUL), [pd], [pd])
    if do_l0:
        for nm, w in (('r', 8), ('k', 8), ('v', 8), ('l', 1)):
            P.op('dve', lambda e, nm=nm, w=w: e.tensor_scalar(out=pdc('omu_' + nm, 0, w), in0=pcc('mu_' + nm, 0, w), scalar1=-1.0, scalar2=1.0, op0=MUL, op1=ADD),
                 [pc], [pd])
        P.op('dve', lambda e: e.tensor_scalar(out=pdc('negbf'), in0=pcc('bif_f'), scalar1=-1.0, scalar2=None, op0=MUL), [pc], [pd])

    scr = {}

    def mkscr(name, src_ap, shape, group=8):
        t = P.dram("scr_" + name, shape, BF16, "Internal")
        nm = shape[0]
        for j0 in range(0, nm, group):
            j1 = min(nm, j0 + group)
            P.dma('pool', t[j0:j1].rearrange("j p k m -> p j k m"), src_ap[j0:j1].rearrange("j p k m -> p j k m"),
                  writes=[t], key=('dma', 'scr_' + name))
        scr[name] = t
        return t
    if do_l0:
        lup_bf = cload('lup', BF16)
        wqkv_bf = cload('wqkv', BF16)
        wif_bf = cload('wif', BF16)
        mkscr('w0in', din['w0in'].ap(), [49, 128, 8, 128])
        mkscr('w0out', din['w0out'].ap(), [8, 128, 16, 128], group=4)
    for i in range(2):
        if (i == 0 and do_l0) or (i == 1 and do_l1):
            mkscr('pegate%d' % i, din['pegate'].ap()[i], [8, 128, 8, 128])
            mkscr('peup%d' % i, din['peup'].ap()[i], [8, 128, 2, 128])
    if do_l1:
        mkscr('w1in', din['w1in'].ap(), [32, 128, 8, 128])
        mkscr('wri', din['wri'].ap(), [16, 128, 2, 128], group=16)
        mkscr('w1out', din['w1out'].ap(), [8, 128, 16, 128], group=4)

    NWB = 5
    wring = [P.sb("wb%d" % i, [128, 16, 128], BF16) for i in range(NWB)]
    wstate = {'i': 0}

    def wld(name, j, nk):
        wb = wring[wstate['i'] % NWB]
        wstate['i'] += 1
        P.dma('sp', wb[:, 0:nk, :], scr[name][j], reads=[scr[name]], writes=[wb])
        return wb

    PS = [P.ps("ps%d" % i, [128, TT]) for i in range(8)]
    pstate = {'i': 0}

    pstate['set'] = list(range(8))

    def nextps():
        s_ = pstate['set']
        b = PS[s_[pstate['i'] % len(s_)]]
        pstate['i'] += 1
        return b

    hT = P.sb("hT", [128, 8, TT])
    xnT = P.sb("xnT", [128, 8, TT], BF16)
    merged = P.sb("merged", [128, 16, TT], BF16)
    ext = P.sb("ext", [128, TT + 3])
    NG, NH = 20, 12
    G = [P.sb("g%d" % i, [128, TT]) for i in range(NG)]
    H = [P.sb("h%d" % i, [128, 2 * TT]) for i in range(NH)]
    lnv, rstd = G[19], G[18]
    ntmp = [G[16], G[17]]

    def bfv(tb, n=None):
        v = tb[:].bitcast(BF16)
        return v

    if do_l1:
        l1_tail = P.sb("l1_tail", [128, 16, 3])
        l1_h = P.sb("l1_h", [128, 16])
        P.op('pool', lambda e: e.memset(l1_tail[:], 0.0), [], [l1_tail])
        P.op('pool', lambda e: e.memset(l1_h[:], 0.0), [], [l1_h])

    def norm_apply(k, gname, out_ap, out_tb):
        if k % 2 == 0:
            P.op('dve', lambda e: e.scalar_tensor_tensor(out=out_ap, in0=hT[:, k, :], scalar=pcc(gname, k), in1=rstd[:],
                                                         op0=MUL, op1=MUL), [hT, pc, rstd], [out_tb])
        else:
            nt = ntmp[(k // 2) % 2]
            P.op('act', lambda e: e.activation(out=nt[:], in_=hT[:, k, :], func=AF.Identity, scale=pcc(gname, k)), [hT, pc], [nt])
            P.op('pool', lambda e: e.tensor_tensor(out=out_ap, in0=nt[:], in1=rstd[:], op=MUL), [nt, rstd], [out_tb])

    def norm_stats():
        ps = nextps()
        for half in range(2):
            sq = H[4 + half]
            sqv = bfv(sq).rearrange("p (k t) -> p k t", k=4)
            P.op('act', lambda e, half=half, sqv=sqv: e.activation(out=sqv, in_=hT[:, 4 * half:4 * half + 4, :], func=AF.Square), [hT], [sq])
            for k in range(4):
                P.op('pe', lambda e, k=k, half=half, sqv=sqv: e.matmul(ps[:], lhsT=ones_bf[:], rhs=sqv[:, k, :], start=(half == 0 and k == 0), stop=(half == 1 and k == 3)),
                     [ones_bf, sq], [ps])
        P.op('act', lambda e: e.activation(out=lnv[:], in_=ps[:], func=AF.Ln, scale=1.0 / D, bias=1e-6), [ps], [lnv])
        P.op('act', lambda e: e.activation(out=rstd[:], in_=lnv[:], func=AF.Exp, scale=-0.5), [lnv], [rstd])

    def rmsnorm_x(gname):
        norm_stats()
        for k in range(8):
            norm_apply(k, gname, xnT[:, k, :], xnT)

    def proj(ps, wb, nk, rhs_tb, rhs_fn):
        for k in range(nk):
            P.op('pe', lambda e, k=k: e.matmul(ps[:], lhsT=wb[:, k, :], rhs=rhs_fn(k), start=(k == 0), stop=(k == nk - 1)),
                 [wb, rhs_tb], [ps])

    def proj_x(name, j):
        w = wld(name, j, 8)
        ps = nextps()
        proj(ps, w, 8, xnT, lambda k: xnT[:, k, :])
        return ps

    def ple(i, t0):
        ptok = H[0]
        ptv = ptok[:].rearrange("p (tb d) -> p tb d", tb=NT)
        pTt = G[12]
        pT = bfv(pTt).rearrange("p (k t) -> p k t", k=2)
        sgs, tmps = [G[0], G[1]], [G[2], G[3]]
        P.dma('sp', ptv, p_d.ap()[i, t0:t0 + TT, :].rearrange("(tb p) d -> p tb d", p=128), writes=[ptok])
        for kc in range(2):
            ps = nextps()
            for tb in range(NT):
                P.op('pe', lambda e, kc=kc, tb=tb, ps=ps: e.transpose(ps[:, tb * 128:(tb + 1) * 128], ptv[:, tb, kc * 128:(kc + 1) * 128], ident[:]),
                     [ptok, ident], [ps])
            P.op('act', lambda e, kc=kc, ps=ps: e.activation(out=pT[:, kc, :], in_=ps[:], func=AF.Copy), [ps], [pTt])
        rmsnorm_x('pen%d' % i)
        for m in range(8):
            sg, tmpa = sgs[m % 2], tmps[m % 2]
            wg = wld('pegate%d' % i, m, 8)
            wu = wld('peup%d' % i, m, 2)
            ps = nextps()
            proj(ps, wg, 8, xnT, lambda k: xnT[:, k, :])
            P.op('act', lambda e, ps=ps: e.activation(out=sg[:], in_=ps[:], func=AF.Sigmoid), [ps], [sg])
            ps2 = nextps()
            proj(ps2, wu, 2, pTt, lambda k: pT[:, k, :])
            P.op('dve', lambda e, ps2=ps2: e.tensor_tensor(out=tmpa[:], in0=ps2[:], in1=sg[:], op=MUL), [ps2, sg], [tmpa])
            P.op('pool', lambda e, m=m: e.tensor_tensor(out=hT[:, m, :], in0=hT[:, m, :], in1=tmpa[:], op=ADD), [hT, tmpa], [hT])

    def out_proj(name):
        for m in range(8):
            wo = wld(name, m, 16)
            ps = nextps()
            proj(ps, wo, 16, merged, lambda k: merged[:, k, :])
            P.op('dve', lambda e, m=m, ps=ps: e.tensor_tensor(out=hT[:, m, :], in0=hT[:, m, :], in1=ps[:], op=ADD), [hT, ps], [hT])

    def conv4(ps, c, tail_tb, wname, bname, acc):
        P.op('pool', lambda e: e.tensor_copy(out=ext[:, 0:3], in_=tail_tb[:, c, :]), [tail_tb], [ext])
        P.op('act', lambda e: e.activation(out=ext[:, 3:TT + 3], in_=ps[:], func=AF.Copy), [ps], [ext])
        P.op('act', lambda e: e.activation(out=acc[:], in_=ps[:], func=AF.Identity, scale=pcc(wname + '3', c), bias=pcc(bname, c)), [ps, pc], [acc])
        P.op('pool', lambda e: e.tensor_copy(out=tail_tb[:, c, :], in_=ext[:, TT:TT + 3]), [ext], [tail_tb])
        for j in range(3):
            P.op('dve', lambda e, j=j: e.scalar_tensor_tensor(out=acc[:], in0=ext[:, j:j + TT], scalar=pcc(wname + str(j), c), in1=acc[:],
                                                              op0=MUL, op1=ADD), [ext, pc, acc], [acc])

    def layer1(t0):
        rmsnorm_x('mixn1')
        for c in range(16):
            xc32, xcb, rg, ig, a2, th = G[6 * (c % 2):6 * (c % 2) + 6]
            xcb_ap = bfv(xcb)[:, 0:TT]
            ps = proj_x('w1in', c)
            conv4(ps, c, l1_tail, 'ccw', 'ccb', xc32)
            P.op('act', lambda e: e.activation(out=xcb_ap, in_=xc32[:], func=AF.Copy), [xc32], [xcb])
            wb = wld('wri', c, 2)
            ps_r = nextps()
            P.op('pe', lambda e: e.matmul(ps_r[:], lhsT=wb[:, 0, :], rhs=xcb_ap, start=True, stop=True), [wb, xcb], [ps_r])
            P.op('act', lambda e: e.activation(out=rg[:], in_=ps_r[:], func=AF.Sigmoid, bias=pcc('cbr', c)), [ps_r, pc], [rg])
            ps_i = nextps()
            P.op('pe', lambda e: e.matmul(ps_i[:], lhsT=wb[:, 1, :], rhs=xcb_ap, start=True, stop=True), [wb, xcb], [ps_i])
            P.op('act', lambda e: e.activation(out=ig[:], in_=ps_i[:], func=AF.Sigmoid, bias=pcc('cbi', c)), [ps_i, pc], [ig])
            P.op('act', lambda e: e.activation(out=a2[:], in_=rg[:], func=AF.Exp, scale=pdc('cneg', c)), [rg, pd], [a2])
            P.op('act', lambda e: e.activation(out=th[:], in_=rg[:], func=AF.Tanh, scale=pdc('csph', c)), [rg, pd], [th])
            P.op('dve', lambda e: e.scalar_tensor_tensor(out=th[:], in0=a2[:], scalar=1.0, in1=th[:], op0=ADD, op1=MUL), [a2, th], [th])
            P.op('dve', lambda e: e.tensor_scalar(out=a2[:], in0=th[:], scalar1=-1.0, scalar2=1.0, op0=MUL, op1=ADD), [th], [a2])
            P.op('dve', lambda e: e.scalar_tensor_tensor(out=th[:], in0=a2[:], scalar=1.0, in1=th[:], op0=ADD, op1=MUL), [a2, th], [th])
            P.op('act', lambda e: e.activation(out=th[:], in_=th[:], func=AF.Sqrt), [th], [th])
            P.op('pool', lambda e: e.tensor_tensor(out=ig[:], in0=xc32[:], in1=ig[:], op=MUL), [xc32, ig], [ig])
            P.op('pool', lambda e: e.tensor_tensor(out=ig[:], in0=ig[:], in1=th[:], op=MUL), [ig, th], [ig])
            P.op('dve', lambda e: e.tensor_tensor_scan(out=rg[:], data0=a2[:], data1=ig[:], initial=l1_h[:, c:c + 1], op0=MUL, op1=ADD),
                 [a2, ig, l1_h], [rg])
            P.op('pool', lambda e: e.tensor_copy(out=l1_h[:, c:c + 1], in_=rg[:, TT - 1:TT]), [rg], [l1_h])
            ps_g = proj_x('w1in', 16 + c)
            P.op('act', lambda e: e.activation(out=xc32[:], in_=ps_g[:], func=AF.Silu), [ps_g], [xc32])
            P.op('dve', lambda e: e.tensor_tensor(out=merged[:, c, :], in0=rg[:], in1=xc32[:], op=MUL), [rg, xc32], [merged])
        out_proj('w1out')

    if do_l0 and do_rwkv:
        identr = P.sb("identr", [128, 128], SDT)
        bdones = cload('bdones')
        bdones_bf = P.sb("bdones_bf", [128, 128], BF16)
        bdones_r = P.sb("bdones_r", [128, 128], SDT)
        maskA = cload('maskA')
        maskB = cload('maskB')
        resetm = cload('resetm')
        P.op('dve', lambda e: e.tensor_copy(out=identr[:], in_=ident[:]), [ident], [identr])
        P.op('dve', lambda e: e.tensor_copy(out=bdones_bf[:], in_=bdones[:]), [bdones], [bdones_bf])
        P.op('dve', lambda e: e.tensor_copy(out=bdones_r[:], in_=bdones[:]), [bdones], [bdones_r])
        rwc = P.sb("rwc", [128, 25])
        P.op('pool', lambda e: e.memset(rwc[:], 0.0), [], [rwc])
        lora_bf = P.sb("lora_bf", [128, TT], BF16)
        Sst = [[P.sb("S%d_%d" % (c, q), [128, 128], SDT) for q in range(2)] for c in range(8)]
        for c in range(8):
            P.op('pool', lambda e, c=c: e.memset(Sst[c][0][:].bitcast(F32), 0.0), [], [Sst[c][0]])
        spar = [0] * 8
        rhs_sb = P.sb("rhs_sb", [128, 128], SDT)
        u_sb = P.sb("u_sb", [128, 128], SDT)
        psRHS = TB("psRHS", PS[7].h[:, 0:128], root=PS[7])
        psU = TB("psU", PS[7].h[:, 128:256], root=PS[7])
        psYT = TB("psYT", PS[7].h[:, 256:384], root=PS[7])
        psSN = TB("psSN", PS[7].h[:, 384:512], root=PS[7])

    def tshift(ps, dst, col, mu_ap, omu_ap):
        P.op('act', lambda e: e.activation(out=dst[:], in_=ps[:], func=AF.Identity, scale=omu_ap), [ps, pd], [dst])
        P.op('dve', lambda e: e.scalar_tensor_tensor(out=dst[:, 1:TT], in0=ps[:, 0:TT - 1], scalar=mu_ap, in1=dst[:, 1:TT], op0=MUL, op1=ADD),
             [ps, pc, dst], [dst])
        P.op('dve', lambda e: e.scalar_tensor_tensor(out=dst[:, 0:1], in0=rwc[:, col:col + 1], scalar=mu_ap, in1=dst[:, 0:1], op0=MUL, op1=ADD),
             [rwc, pc, dst], [dst])
        P.op('act', lambda e: e.activation(out=rwc[:, col:col + 1], in_=ps[:, TT - 1:TT], func=AF.Copy), [ps], [rwc])

    def r3(ap, n=8):
        return ap.rearrange("p (n t) -> p n t", n=n)

    def rwkv(t0):
        Rbd, Abd, Bbd, Kbd, Vbd = H[0:5]
        for i in range(5):
            P.op('pool', lambda e, i=i: e.memset(H[i][:], 0.0), [], [H[i]])
        bdv = [r3(H[i][:].bitcast(SDT)) for i in range(5)]
        ps = proj_x('w0in', 24)
        lora32 = G[16]
        tshift(ps, lora32, 24, pcc('mu_l'), pdc('omu_l'))
        P.op('act', lambda e: e.activation(out=lora_bf[0:64, :], in_=lora32[0:64, :], func=AF.Tanh), [lora32], [lora_bf])
        P.op('act', lambda e: e.activation(out=lora_bf[64:128, :], in_=lora32[64:128, :], func=AF.Copy), [lora32], [lora_bf])
        for c in range(8):
            rwkv_pair(c, bdv)

    def rwkv_pair(c, bdv):
        r32, k32, v32, gr32, lw, cum, cump, Gt, Ginv, Gp, a32, kk32, rn, nkk, kka, keff, t1, misc = G[0:18]
        Rv, Av, Bv, Kv, Vv = bdv
        Rbd, Abd, Bbd, Kbd, Vbd = H[0:5]
        for dst, mch, nm, col in ((r32, c, 'r', c), (k32, 8 + c, 'k', 8 + c), (v32, 16 + c, 'v', 16 + c)):
            ps = proj_x('w0in', mch)
            tshift(ps, dst, col, pcc('mu_' + nm, c), pdc('omu_' + nm, c))
        ps = proj_x('w0in', 33 + c)
        P.op('act', lambda e, ps=ps: e.activation(out=gr32[:], in_=ps[:], func=AF.Silu), [ps], [gr32])
        ps = nextps()
        P.op('pe', lambda e, ps=ps: e.matmul(ps[:], lhsT=lup_bf[:, 0, c * 128:(c + 1) * 128], rhs=lora_bf[:], start=True, stop=True), [lup_bf, lora_bf], [ps])
        P.op('act', lambda e, ps=ps: e.activation(out=lw[:], in_=ps[:], func=AF.Sigmoid, bias=pcc('w0', c)), [ps, pc], [lw])
        P.op('pool', lambda e: e.tensor_scalar(out=lw[:], in0=lw[:], scalar1=-DECAY_SCALE, scalar2=None, op0=MUL), [lw], [lw])
        ps = nextps()
        P.op('pe', lambda e, ps=ps: e.matmul(ps[:], lhsT=lup_bf[:, 1, c * 128:(c + 1) * 128], rhs=lora_bf[:], start=True, stop=True), [lup_bf, lora_bf], [ps])
        P.op('act', lambda e, ps=ps: e.activation(out=a32[:], in_=ps[:], func=AF.Sigmoid, bias=pcc('a0', c)), [ps, pc], [a32])
        P.op('dve', lambda e: e.tensor_tensor_scan(out=cum[:], data0=resetm[:], data1=lw[:], initial=0.0, op0=MUL, op1=ADD), [resetm, lw], [cum])
        P.op('pool', lambda e: e.tensor_tensor(out=cump[:], in0=cum[:], in1=lw[:], op=SUB), [cum, lw], [cump])
        P.op('act', lambda e: e.activation(out=Gt[:], in_=cum[:], func=AF.Exp), [cum], [Gt])
        P.op('act', lambda e: e.activation(out=Ginv[:], in_=cum[:], func=AF.Exp, scale=-1.0), [cum], [Ginv])
        P.op('act', lambda e: e.activation(out=Gp[:], in_=cump[:], func=AF.Exp), [cump], [Gp])
        P.op('pool', lambda e: e.tensor_scalar(out=kk32[:], in0=k32[:], scalar1=pcc('k_k', c), scalar2=None, op0=MUL), [k32, pc], [kk32])
        sqk = bfv(misc)[:, 0:TT]
        P.op('act', lambda e: e.activation(out=sqk, in_=kk32[:], func=AF.Square), [kk32], [misc])
        ps = nextps()
        P.op('pe', lambda e, ps=ps: e.matmul(ps[:], lhsT=bdones_bf[:], rhs=sqk, start=True, stop=True), [bdones_bf, misc], [ps])
        P.op('act', lambda e, ps=ps: e.activation(out=rn[:], in_=ps[:], func=AF.Ln, bias=1e-20), [ps], [rn])
        P.op('act', lambda e: e.activation(out=rn[:], in_=rn[:], func=AF.Exp, scale=-0.5), [rn], [rn])
        P.op('dve', lambda e: e.scalar_tensor_tensor(out=nkk[:], in0=kk32[:], scalar=-1.0, in1=rn[:], op0=MUL, op1=MUL), [kk32, rn], [nkk])
        P.op('dve', lambda e: e.scalar_tensor_tensor(out=kka[:], in0=nkk[:], scalar=-1.0, in1=a32[:], op0=MUL, op1=MUL), [nkk, a32], [kka])
        P.op('pool', lambda e: e.tensor_scalar(out=t1[:], in0=a32[:], scalar1=-1.0, scalar2=pcc('k_a', c), op0=ADD, op1=MUL), [a32, pc], [t1])
        P.op('dve', lambda e: e.scalar_tensor_tensor(out=keff[:], in0=t1[:], scalar=1.0, in1=k32[:], op0=ADD, op1=MUL), [t1, k32], [keff])
        rkr = bfv(t1)[:, 0:TT]
        P.op('dve', lambda e: e.scalar_tensor_tensor(out=rkr, in0=r32[:], scalar=pcc('r_k', c), in1=keff[:], op0=MUL, op1=MUL), [r32, pc, keff], [t1])
        ps = nextps()
        P.op('pe', lambda e, ps=ps: e.matmul(ps[:], lhsT=bdones_bf[:], rhs=rkr, start=True, stop=True), [bdones_bf, t1], [ps])
        bon = cum
        P.op('act', lambda e, ps=ps: e.activation(out=bon[:], in_=ps[:], func=AF.Copy), [ps], [bon])
        for hh in range(2):
            hs_ = slice(hh * 64, hh * 64 + 64)
            for dstv, dtb, a_, b_ in ((Rv, Rbd, r32, Gt), (Av, Abd, nkk, Gp), (Bv, Bbd, kka, Ginv), (Kv, Kbd, keff, Ginv)):
                P.op('dve', lambda e, dstv=dstv, a_=a_, b_=b_, hs_=hs_: e.tensor_tensor(out=dstv[hs_, :, hs_], in0=r3(a_[hs_, :]), in1=r3(b_[hs_, :]), op=MUL),
                     [a_, b_], [dtb])
            P.op('act', lambda e, hs_=hs_: e.activation(out=Vv[hs_, :, hs_], in_=r3(v32[hs_, :]), func=AF.Copy), [v32], [Vbd])
        y32 = lw
        for hf in range(2):
            SCt = [H[5], H[6]]
            QTt = [H[7], H[8]]
            SCv = [t[:].bitcast(SDT).rearrange("p (j m k) -> p j m k", j=2, m=4) for t in SCt]
            QTv = [t[:].bitcast(SDT).rearrange("p (j m k) -> p j m k", j=2, m=4) for t in QTt]
            SCf = [t[:].rearrange("p (j m k) -> p j m k", j=2, m=4) for t in SCt]
            PQt = [H[9], H[10]]
            PQv = [[PQt[b][:].bitcast(SDT).rearrange("p (q j m k) -> p q j m k", q=2, j=2, m=2)[:, par] for b in range(2)] for par in range(2)]
            Xt = H[11]
            Xv = [Xt[:].bitcast(SDT).rearrange("p (q j k) -> p q j k", q=2, j=4)[:, q] for q in range(2)]
            Xf = [Xt[:].rearrange("p (q j k) -> p q j k", q=2, j=4)[:, q] for q in range(2)]
            for j in range(4):
                n = hf * 4 + j
                sc = SCv[j // 2][:, j % 2]
                qt = QTv[j // 2][:, j % 2]
                A_, B_ = PS[2], PS[3]
                for m, (l_, r_) in enumerate(((Bv, Av), (Kv, Av), (Bv, Rv), (Kv, Rv))):
                    P.op('pe', lambda e, m=m, l_=l_, r_=r_, n=n: e.matmul(A_[:, m * 128:(m + 1) * 128], lhsT=l_[:, n, :], rhs=r_[:, n, :], start=True, stop=True),
                         [Rbd, Abd, Bbd, Kbd], [A_])
                P.op('dve', lambda e, sc=sc: e.tensor_tensor(out=sc, in0=A_[:].rearrange("p (m k) -> p m k", m=4), in1=maskA[:], op=MUL),
                     [A_, maskA], [SCt[j // 2]])
                P.op('pe', lambda e, n=n: e.matmul(B_[:, 0:128], lhsT=Av[:, n, :], rhs=Bv[:, n, :], start=True, stop=True), [Abd, Bbd], [B_])
                for m, l_ in enumerate((Bv, Kv, Vv)):
                    P.op('pe', lambda e, m=m, l_=l_, n=n: e.matmul(B_[:, (m + 1) * 128:(m + 2) * 128], lhsT=l_[:, n, :], rhs=identr[:], start=True, stop=True),
                         [Bbd, Kbd, Vbd, identr], [B_])
                P.op('dve', lambda e, qt=qt: e.tensor_tensor(out=qt[:, 0, :], in0=B_[:, 0:128], in1=maskB[:], op=MUL), [B_, maskB], [QTt[j // 2]])
                P.op('act', lambda e, qt=qt: e.activation(out=qt[:, 1:4, :], in_=B_[:, 128:512].rearrange("p (m k) -> p m k", m=3), func=AF.Copy),
                     [B_], [QTt[j // 2]])
            for j in range(4):
                P.op('dve', lambda e, j=j: e.tensor_tensor(out=Xv[1][:, j, :], in0=SCf[j // 2][:, j % 2, 0, :], in1=ident[:], op=ADD),
                     [SCt[j // 2], ident], [Xt])
            for lvl in range(1, 6):
                par = lvl % 2
                for b in range(2):
                    bank = PS[4 + b]
                    for jj in range(2):
                        j = 2 * b + jj
                        if lvl == 1:
                            Pp = SCv[j // 2][:, j % 2, 0, :]
                            Qp = QTv[j // 2][:, j % 2, 0, :]
                            rd = [SCt[j // 2], QTt[j // 2]]
                        else:
                            Pp = PQv[1 - par][b][:, jj, 0, :]
                            Qp = PQv[1 - par][b][:, jj, 1, :]
                            rd = [PQt[b]]
                        P.op('pe', lambda e, jj=jj, Pp=Pp, Qp=Qp, bank=bank: e.matmul(bank[:, (2 * jj) * 128:(2 * jj + 1) * 128], lhsT=Qp, rhs=Pp, start=True, stop=True), rd, [bank])
                        P.op('pe', lambda e, jj=jj, Pp=Pp, Qp=Qp, bank=bank: e.matmul(bank[:, (2 * jj + 1) * 128:(2 * jj + 2) * 128], lhsT=Pp, rhs=Qp, start=True, stop=True), rd, [bank])
                    if b == 0:
                        P.op('act', lambda e, b=b, bank=bank, par=par: e.activation(out=PQv[par][b], in_=bank[:].rearrange("p (j m k) -> p j m k", j=2, m=2), func=AF.Copy),
                             [bank], [PQt[b]])
                    else:
                        P.op('dve', lambda e, b=b, bank=bank, par=par: e.tensor_copy(out=PQv[par][b], in_=bank[:].rearrange("p (j m k) -> p j m k", j=2, m=2)),
                             [bank], [PQt[b]])
                XB = PS[6]
                for j in range(4):
                    Ql = PQv[par][j // 2][:, j % 2, 1, :]
                    P.op('pe', lambda e, j=j, Ql=Ql, par=par: e.matmul(XB[:, j * 128:(j + 1) * 128], lhsT=Ql, rhs=Xv[par][:, j, :], start=True, stop=True), [PQt[j // 2], Xt], [XB])
                P.op('dve', lambda e, par=par: e.tensor_tensor(out=Xv[1 - par], in0=XB[:].rearrange("p (j k) -> p j k", j=4), in1=Xf[par], op=ADD), [XB, Xt], [Xt])
            for j in range(4):
                n = hf * 4 + j
                sc = SCv[j // 2][:, j % 2]
                qt = QTv[j // 2][:, j % 2]
                sct, qtt = SCt[j // 2], QTt[j // 2]
                S0 = Sst[c][spar[c]]
                S1 = Sst[c][1 - spar[c]]
                spar[c] = 1 - spar[c]
                P.op('pe', lambda e, n=n, S0=S0: e.matmul(psRHS[:], lhsT=Av[:, n, :], rhs=S0[:], start=True, stop=False), [Abd, S0], [psRHS])
                P.op('pe', lambda e, sc=sc, qt=qt: e.matmul(psRHS[:], lhsT=sc[:, 1, :], rhs=qt[:, 3, :], start=False, stop=True), [sct, qtt], [psRHS])
                P.op('act', lambda e: e.activation(out=rhs_sb[:], in_=psRHS[:], func=AF.Copy), [psRHS], [rhs_sb])
                P.op('pe', lambda e, j=j: e.matmul(psU[:], lhsT=Xv[0][:, j, :], rhs=rhs_sb[:], start=True, stop=True), [Xt, rhs_sb], [psU])
                P.op('act', lambda e: e.activation(out=u_sb[:], in_=psU[:], func=AF.Copy), [psU], [u_sb])
                P.op('pe', lambda e, qt=qt: e.matmul(psSN[:], lhsT=qt[:, 1, :], rhs=u_sb[:], start=True, stop=False), [qtt, u_sb], [psSN])
                P.op('pe', lambda e, qt=qt: e.matmul(psSN[:], lhsT=qt[:, 2, :], rhs=qt[:, 3, :], start=False, stop=False), [qtt], [psSN])
                P.op('pe', lambda e, S0=S0: e.matmul(psSN[:], lhsT=identr[:], rhs=S0[:], start=False, stop=True), [identr, S0], [psSN])
                P.op('act', lambda e, n=n, S1=S1: e.activation(out=S1[:], in_=psSN[:], func=AF.Identity, scale=Gt[:, n * 64 + 63:n * 64 + 64]), [psSN, Gt], [S1])
                P.op('pe', lambda e, n=n, S0=S0: e.matmul(psYT[:], lhsT=S0[:], rhs=Rv[:, n, :], start=True, stop=False), [S0, Rbd], [psYT])
                P.op('pe', lambda e, sc=sc: e.matmul(psYT[:], lhsT=u_sb[:], rhs=sc[:, 2, :], start=False, stop=False), [u_sb, sct], [psYT])
                P.op('pe', lambda e, sc=sc, qt=qt: e.matmul(psYT[:], lhsT=qt[:, 3, :], rhs=sc[:, 3, :], start=False, stop=True), [qtt, sct], [psYT])
                for hh in range(2):
                    hs_ = slice(hh * 64, hh * 64 + 64)
                    P.op('act', lambda e, hs_=hs_, n=n: e.activation(out=y32[hs_, n * 64:(n + 1) * 64], in_=psYT[hs_, hs_], func=AF.Copy), [psYT], [y32])
        dump('y_rwkv%d' % c, y32, y32[:])
        yr = cump
        yrv = yr[:].bitcast(SDT)
        P.op('act', lambda e: e.activation(out=yrv, in_=y32[:], func=AF.Copy), [y32], [yr])
        ps = nextps()
        P.op('pe', lambda e, ps=ps: e.matmul(ps[:], lhsT=bdones_r[:], rhs=yrv, start=True, stop=True), [bdones_r, yr], [ps])
        dd = Gt
        ddv = dd[:].bitcast(SDT)
        P.op('dve', lambda e, ps=ps: e.scalar_tensor_tensor(out=dd[:], in0=ps[:], scalar=-1.0 / 64, in1=y32[:], op0=MUL, op1=ADD), [ps, y32], [dd])
        sq = Ginv
        sqv = sq[:].bitcast(SDT)
        P.op('act', lambda e: e.activation(out=sqv, in_=dd[:], func=AF.Square), [dd], [sq])
        ps = nextps()
        P.op('pe', lambda e, ps=ps: e.matmul(ps[:], lhsT=bdones_r[:], rhs=sqv, start=True, stop=True), [bdones_r, sq], [ps])
        rs = Gp
        P.op('act', lambda e, ps=ps: e.activation(out=rs[:], in_=ps[:], func=AF.Ln, scale=1.0 / 64, bias=64e-5), [ps], [rs])
        P.op('act', lambda e: e.activation(out=rs[:], in_=rs[:], func=AF.Exp, scale=-0.5), [rs], [rs])
        P.op('dve', lambda e: e.tensor_tensor(out=dd[:], in0=dd[:], in1=rs[:], op=MUL), [dd, rs], [dd])
        P.op('act', lambda e: e.activation(out=dd[:], in_=dd[:], func=AF.Identity, scale=pcc('ln_w', c), bias=pcc('ln_b', c)), [dd, pc], [dd])
        P.op('pool', lambda e: e.tensor_tensor(out=bon[:], in0=bon[:], in1=v32[:], op=MUL), [bon, v32], [bon])
        P.op('pool', lambda e: e.tensor_tensor(out=dd[:], in0=dd[:], in1=bon[:], op=ADD), [dd, bon], [dd])
        P.op('dve', lambda e: e.tensor_tensor(out=merged[:, c, :], in0=dd[:], in1=gr32[:], op=MUL), [dd, gr32], [merged])

    if do_l0 and do_mlstm:
        causal = cload('causal')
        sel = P.sb("c_sel", [4, 4, 128])
        P.dma('sp', sel[:], din['sel'].ap(), writes=[sel])
        onesf = P.sb("onesf", [128, TT])
        P.op('pool', lambda e: e.memset(onesf[:], 1.0), [], [onesf])
        ones_r = P.sb("ones_r", [128, 128], SDT)
        P.op('dve', lambda e: e.tensor_copy(out=ones_r[:], in_=onesf[:, 0:128]), [onesf], [ones_r])
        m_tail = P.sb("m_tail", [128, 8, 3])
        P.op('pool', lambda e: e.memset(m_tail[:], 0.0), [], [m_tail])
        CT32 = P.sb("CT32", [128, 4, 2, 256])
        CTbf = P.sb("CTbf", [128, 4, 2, 256], BF16)
        nr32 = P.sb("nr32", [128, 4, 2, 128])
        nrbf = P.sb("nrbf", [128, 4, 2, 128], BF16)
        for t_ in (CT32, CTbf, nr32, nrbf):
            P.op('pool', lambda e, t_=t_: e.memset(t_[:], 0.0), [], [t_])
        mcar = P.sb("mcar", [4, 2])
        P.op('pool', lambda e: e.memset(mcar[:], 0.0), [], [mcar])
        MxE = P.sb("MxE", [4, 5])
        dec = P.sb("dec", [4, 4])
        smallb = P.sb("smallb", [128, 32])
        qe = P.sb("qe", [128, 2, 128], BF16)
        qw = P.sb("qw", [128, 2, 128], BF16)
        kg = P.sb("kg", [128, 2, 128], BF16)
        kgt = P.sb("kgt", [128, 256], BF16)
        st_bf = P.sb("st_bf", [128, 128], BF16)
        ddm = P.sb("ddm", [128, 128])
        recm = P.sb("recm", [128, 128])

    def mlstm(t0):
        qTt, kTt, vTt = [H[0], H[1]], [H[2], H[3]], [H[4], H[5]]
        ktt, vtt = [H[6], H[7]], [H[8], H[9]]
        xct = [H[10], H[11]]

        def fm(tl, c):
            return bfv(tl[c // 4]).rearrange("p (k t) -> p k t", k=4)[:, c % 4, :]

        def tokv(tl, tb):
            return bfv(tl[tb // 2]).rearrange("p (b d) -> p b d", b=2)[:, tb % 2, :]
        acc, xmb = G[9], G[10]
        xmb_ap = bfv(xmb)[:, 0:TT]
        gi_ps, gf_ps = PS[2], PS[3]
        for c in range(8):
            ps = proj_x('w0in', 25 + c)
            conv4(ps, c, m_tail, 'mcw', 'mcb', acc)
            P.op('act', lambda e, c=c: e.activation(out=fm(xct, c), in_=acc[:], func=AF.Silu), [acc], [xct[c // 4]])
            P.op('pool', lambda e: e.tensor_copy(out=xmb_ap, in_=ext[:, 3:TT + 3]), [ext], [xmb])
            for t_, (dst, src_ap, src_tb) in enumerate(((qTt, fm(xct, c), xct[c // 4]), (kTt, fm(xct, c), xct[c // 4]), (vTt, xmb_ap, xmb))):
                ps = nextps()
                P.op('pe', lambda e, ps=ps, t_=t_, src_ap=src_ap: e.matmul(ps[:], lhsT=wqkv_bf[:, t_, c, :], rhs=src_ap, start=True, stop=True),
                     [wqkv_bf, src_tb], [ps])
                P.op('act' if t_ != 1 else 'dve',
                     (lambda e, ps=ps, dst=dst: e.activation(out=fm(dst, c), in_=ps[:], func=AF.Copy)) if t_ != 1 else
                     (lambda e, ps=ps, dst=dst: e.tensor_copy(out=fm(dst, c), in_=ps[:])), [ps], [dst[c // 4]])
                j = t_ * 8 + c
                first, last = (c == 0 and t_ == 0), (c == 7 and t_ == 2)
                P.op('pe', lambda e, j=j, dst=dst, first=first, last=last: e.matmul(gi_ps[0:4, :], lhsT=wif_bf[:, 0, j, :], rhs=fm(dst, c), start=first, stop=last),
                     [wif_bf, dst[c // 4]], [gi_ps])
                P.op('pe', lambda e, j=j, dst=dst, first=first, last=last: e.matmul(gf_ps[0:4, :], lhsT=wif_bf[:, 1, j, :], rhs=fm(dst, c), start=first, stop=last),
                     [wif_bf, dst[c // 4]], [gf_ps])
            for t_, (dstl, src_ap, src_tb) in ((1, (ktt, fm(xct, c), xct[c // 4])), (2, (vtt, xmb_ap, xmb))):
                ps = nextps()
                for tb in range(NT):
                    P.op('pe', lambda e, ps=ps, tb=tb, t_=t_, src_ap=src_ap: e.matmul(ps[:, tb * 128:(tb + 1) * 128], lhsT=src_ap[:, tb * 128:(tb + 1) * 128], rhs=wqkv_bf[:, t_, c, :],
                                                                                      start=True, stop=True), [wqkv_bf, src_tb], [ps])
                for half in range(2):
                    dv = bfv(dstl[half]).rearrange("p (b d) -> p b d", b=2)[:, :, c * 128:(c + 1) * 128]
                    P.op('act' if half == 0 else 'dve',
                         (lambda e, ps=ps, dv=dv, half=half: e.activation(out=dv, in_=ps[:, half * 256:(half + 1) * 256].rearrange("p (b d) -> p b d", b=2), func=AF.Copy)) if half == 0 else
                         (lambda e, ps=ps, dv=dv, half=half: e.tensor_copy(out=dv, in_=ps[:, half * 256:(half + 1) * 256].rearrange("p (b d) -> p b d", b=2))),
                         [ps], [dstl[half]])
        rg_, re2, rwi, rem, rli, rFp, rag, rMx, rtmp = G[11], G[12], G[13], G[14], G[15], G[16], G[17], G[18], G[19]
        R4 = slice(0, 4)
        P.op('act', lambda e: e.activation(out=rli[R4, :], in_=gi_ps[R4, :], func=AF.Identity, bias=pc[R4, PC['bif_i']:PC['bif_i'] + 1]), [gi_ps, pc], [rli])
        P.op('act', lambda e: e.activation(out=rtmp[R4, :], in_=gf_ps[R4, :], func=AF.Exp, scale=-1.0, bias=pd[R4, PD['negbf']:PD['negbf'] + 1]), [gf_ps, pd], [rtmp])
        P.op('act', lambda e: e.activation(out=rtmp[R4, :], in_=rtmp[R4, :], func=AF.Ln, bias=1.0), [rtmp], [rtmp])
        P.op('dve', lambda e: e.tensor_tensor_scan(out=rFp[R4, :], data0=onesf[R4, :], data1=rtmp[R4, :], initial=mcar[:, 0:1], op0=MUL, op1=ADD),
             [onesf, rtmp, mcar], [rFp])
        P.op('pool', lambda e: e.tensor_tensor(out=rag[R4, :], in0=rli[R4, :], in1=rFp[R4, :], op=ADD), [rli, rFp], [rag])
        P.op('dve', lambda e: e.tensor_tensor_scan(out=rMx[R4, :], data0=rag[R4, :], data1=rag[R4, :], initial=mcar[:, 1:2], op0=MAX, op1=MAX),
             [rag, mcar], [rMx])
        P.op('pool', lambda e: e.tensor_copy(out=MxE[:, 0:1], in_=mcar[:, 1:2]), [mcar], [MxE])
        P.op('pool', lambda e: e.tensor_copy(out=MxE[:, 1:5], in_=rMx[R4, 127:TT:128]), [rMx], [MxE])
        P.op('pool', lambda e: e.tensor_copy(out=mcar[:, 0:1], in_=rFp[R4, TT - 1:TT]), [rFp], [mcar])
        P.op('pool', lambda e: e.tensor_copy(out=mcar[:, 1:2], in_=rMx[R4, TT - 1:TT]), [rMx], [mcar])
        v3 = lambda t_: t_[R4, :].rearrange("p (q t) -> p q t", q=NT)
        mend = MxE[:, 1:5].unsqueeze(2).broadcast_to([4, NT, 128])
        mprev = MxE[:, 0:4].unsqueeze(2).broadcast_to([4, NT, 128])
        P.op('dve', lambda e: e.tensor_tensor(out=v3(rg_), in0=v3(rag), in1=mend, op=SUB), [rag, MxE], [rg_])
        P.op('act', lambda e: e.activation(out=rg_[R4, :], in_=rg_[R4, :], func=AF.Exp), [rg_], [rg_])
        P.op('dve', lambda e: e.tensor_tensor(out=v3(re2), in0=mend, in1=v3(rMx), op=SUB), [rMx, MxE], [re2])
        P.op('act', lambda e: e.activation(out=re2[R4, :], in_=re2[R4, :], func=AF.Exp), [re2], [re2])
        P.op('dve', lambda e: e.tensor_tensor(out=v3(rwi), in0=mprev, in1=v3(rMx), op=SUB), [rMx, MxE], [rwi])
        P.op('act', lambda e: e.activation(out=rwi[R4, :], in_=rwi[R4, :], func=AF.Exp), [rwi], [rwi])
        P.op('dve', lambda e: e.tensor_tensor(out=rem[R4, :], in0=rFp[R4, :], in1=rMx[R4, :], op=SUB), [rFp, rMx], [rem])
        P.op('act', lambda e: e.activation(out=rem[R4, :], in_=rem[R4, :], func=AF.Exp), [rem], [rem])
        P.op('dve', lambda e: e.tensor_tensor(out=dec[:], in0=MxE[:, 0:4], in1=MxE[:, 1:5], op=SUB), [MxE], [dec])
        P.op('act', lambda e: e.activation(out=dec[:], in_=dec[:], func=AF.Exp), [dec], [dec])
        sp_ = PS[5]
        for q in range(NT):
            P.op('pe', lambda e, q=q: e.matmul(sp_[:, q * 4:(q + 1) * 4], lhsT=rg_[R4, q * 128:(q + 1) * 128], rhs=ident[0:4, 0:4], start=True, stop=True),
                 [rg_, ident], [sp_])
        for h in range(4):
            P.op('pe', lambda e, h=h: e.matmul(sp_[:, 16 + h * 4:16 + (h + 1) * 4], lhsT=sel[:, h, :], rhs=dec[:], start=True, stop=True), [sel, dec], [sp_])
        P.op('act', lambda e: e.activation(out=smallb[:], in_=sp_[:, 0:32], func=AF.Copy), [sp_], [smallb])
        for h in range(4):
            mlstm_head(h, qTt, kTt, ktt, vtt, xct, fm, tokv, (rg_, re2, rwi, rem))

    def mlstm_head(h, qTt, kTt, ktt, vtt, xct, fm, tokv, rows):
        bc = G[0:4]
        for i in range(4):
            ps = nextps()
            P.op('pe', lambda e, ps=ps, i=i: e.matmul(ps[:], lhsT=sel[:, h, :], rhs=rows[i][0:4, :], start=True, stop=True), [sel, rows[i]], [ps])
            P.op('act' if i % 2 == 0 else 'dve',
                 (lambda e, ps=ps, i=i: e.activation(out=bc[i][:], in_=ps[:], func=AF.Copy)) if i % 2 == 0 else
                 (lambda e, ps=ps, i=i: e.tensor_copy(out=bc[i][:], in_=ps[:])), [ps], [bc[i]])
        g_bc, e2_bc, wi_bc, em_bc = bc
        h32 = [G[4], G[5]]
        gms = [G[6], G[7]]
        for vc in range(2):
            ps = proj_x('w0in', 41 + 2 * h + vc)
            P.op('act', lambda e, ps=ps, vc=vc: e.activation(out=gms[vc][:], in_=ps[:], func=AF.Silu), [ps], [gms[vc]])
        qt_ = qTt[h // 2]
        kt_ = kTt[h // 2]
        qv = bfv(qt_).rearrange("p (k t) -> p k t", k=4)[:, 2 * (h % 2):2 * (h % 2) + 2, :]
        kv = bfv(kt_).rearrange("p (k t) -> p k t", k=4)[:, 2 * (h % 2):2 * (h % 2) + 2, :]
        CB, SB_, NB = PS[6], PS[7], PS[5]
        for q in range(NT):
            ts_ = slice(q * 128, (q + 1) * 128)
            bcast = lambda t_: t_[:, ts_].unsqueeze(1).broadcast_to([128, 2, 128])
            P.op('dve', lambda e: e.tensor_tensor(out=qe[:], in0=qv[:, :, ts_], in1=bcast(e2_bc), op=MUL), [qt_, e2_bc], [qe])
            P.op('pool', lambda e: e.tensor_tensor(out=qw[:], in0=qv[:, :, ts_], in1=bcast(wi_bc), op=MUL), [qt_, wi_bc], [qw])
            P.op('dve', lambda e: e.scalar_tensor_tensor(out=kg[:], in0=kv[:, :, ts_], scalar=0.0625, in1=bcast(g_bc), op0=MUL, op1=MUL), [kt_, g_bc], [kg])
            ktok = tokv(ktt, q)[:, h * 256:(h + 1) * 256]
            vtok = tokv(vtt, q)[:, h * 256:(h + 1) * 256]
            P.op('pool', lambda e: e.tensor_scalar(out=kgt[:], in0=ktok, scalar1=smallb[:, q * 4 + h:q * 4 + h + 1], scalar2=0.0625, op0=MUL, op1=MUL),
                 [ktt[q // 2], smallb], [kgt])
            for kc in range(2):
                P.op('pe', lambda e, kc=kc: e.matmul(CB[:, 0:128], lhsT=kg[:, kc, :], rhs=qe[:, kc, :], start=(kc == 0), stop=(kc == 1)), [kg, qe], [CB])
            P.op('dve', lambda e: e.tensor_tensor(out=st_bf[:], in0=CB[:, 0:128], in1=causal[:], op=MUL), [CB, causal], [st_bf])
            for vc in range(2):
                o_ = CB[:, 128 + vc * 128:256 + vc * 128]
                P.op('pe', lambda e, o_=o_, vc=vc: e.matmul(o_, lhsT=vtok[:, vc * 128:(vc + 1) * 128], rhs=st_bf[:], start=True, stop=False), [vtt[q // 2], st_bf], [CB])
                for kc in range(2):
                    P.op('pe', lambda e, o_=o_, vc=vc, kc=kc: e.matmul(o_, lhsT=CTbf[:, h, kc, vc * 128:(vc + 1) * 128], rhs=qw[:, kc, :], start=False, stop=(kc == 1)),
                         [CTbf, qw], [CB])
            o_ = CB[:, 384:512]
            P.op('pe', lambda e, o_=o_: e.matmul(o_, lhsT=ones_bf[:], rhs=st_bf[:], start=True, stop=False), [ones_bf, st_bf], [CB])
            for kc in range(2):
                P.op('pe', lambda e, o_=o_, kc=kc: e.matmul(o_, lhsT=nrbf[:, h, kc, :], rhs=qw[:, kc, :], start=False, stop=(kc == 1)), [nrbf, qw], [CB])
            P.op('act', lambda e: e.activation(out=ddm[:], in_=CB[:, 384:512], func=AF.Abs), [CB], [ddm])
            P.op('dve', lambda e: e.tensor_tensor(out=ddm[:], in0=ddm[:], in1=em_bc[:, ts_], op=MAX), [ddm, em_bc], [ddm])
            P.op('dve', lambda e: e.tensor_scalar(out=ddm[:], in0=ddm[:], scalar1=1e-6, scalar2=None, op0=ADD), [ddm], [ddm])
            P.op('dve', lambda e: e.reciprocal(out=recm[:], in_=ddm[:]), [ddm], [recm])
            for vc in range(2):
                P.op('dve', lambda e, vc=vc: e.tensor_tensor(out=h32[vc][:, ts_], in0=CB[:, 128 + vc * 128:256 + vc * 128], in1=recm[:], op=MUL), [CB, recm], [h32[vc]])
            for kc in range(2):
                P.op('pe', lambda e, kc=kc: e.matmul(SB_[:, kc * 256:(kc + 1) * 256], lhsT=kgt[:, kc * 128:(kc + 1) * 128], rhs=vtok, start=True, stop=True),
                     [kgt, vtt[q // 2]], [SB_])
                P.op('pe', lambda e, kc=kc: e.matmul(NB[:, 64 + kc * 128:64 + (kc + 1) * 128], lhsT=kgt[:, kc * 128:(kc + 1) * 128], rhs=ones_bf[:], start=True, stop=True),
                     [kgt, ones_bf], [NB])
            dcol = smallb[:, 16 + h * 4 + q:16 + h * 4 + q + 1]
            P.op('dve', lambda e: e.scalar_tensor_tensor(out=CT32[:, h], in0=CT32[:, h], scalar=dcol, in1=SB_[:].rearrange("p (k v) -> p k v", k=2), op0=MUL, op1=ADD),
                 [CT32, smallb, SB_], [CT32])
            P.op('act', lambda e: e.activation(out=CTbf[:, h], in_=CT32[:, h], func=AF.Copy), [CT32], [CTbf])
            P.op('dve', lambda e: e.scalar_tensor_tensor(out=nr32[:, h], in0=nr32[:, h], scalar=dcol, in1=NB[:, 64:320].rearrange("p (k v) -> p k v", k=2), op0=MUL, op1=ADD),
                 [nr32, smallb, NB], [nr32])
            P.op('pool', lambda e: e.tensor_copy(out=nrbf[:, h], in_=nr32[:, h]), [nr32], [nrbf])
        dump('h_m%d' % h, h32[0], h32[0][:])
        hr = [G[8], G[9]]
        ps = nextps()
        for vc in range(2):
            P.op('act', lambda e, vc=vc: e.activation(out=hr[vc][:].bitcast(SDT), in_=h32[vc][:], func=AF.Copy), [h32[vc]], [hr[vc]])
            P.op('pe', lambda e, ps=ps, vc=vc: e.matmul(ps[:], lhsT=ones_r[:], rhs=hr[vc][:].bitcast(SDT), start=(vc == 0), stop=(vc == 1)), [ones_r, hr[vc]], [ps])
        for vc in range(2):
            P.op('dve', lambda e, ps=ps, vc=vc: e.scalar_tensor_tensor(out=h32[vc][:], in0=ps[:], scalar=-1.0 / 256, in1=h32[vc][:], op0=MUL, op1=ADD), [ps, h32[vc]], [h32[vc]])
        ps2 = nextps()
        for vc in range(2):
            P.op('act', lambda e, vc=vc: e.activation(out=hr[vc][:].bitcast(SDT), in_=h32[vc][:], func=AF.Square), [h32[vc]], [hr[vc]])
            P.op('pe', lambda e, ps2=ps2, vc=vc: e.matmul(ps2[:], lhsT=ones_r[:], rhs=hr[vc][:].bitcast(SDT), start=(vc == 0), stop=(vc == 1)), [ones_r, hr[vc]], [ps2])
        rs = G[10]
        P.op('act', lambda e, ps2=ps2: e.activation(out=rs[:], in_=ps2[:], func=AF.Ln, scale=1.0 / 256, bias=1e-5), [ps2], [rs])
        P.op('act', lambda e: e.activation(out=rs[:], in_=rs[:], func=AF.Exp, scale=-0.5), [rs], [rs])
        for vc in range(2):
            ch = 2 * h + vc
            P.op('dve', lambda e, vc=vc: e.tensor_tensor(out=h32[vc][:], in0=h32[vc][:], in1=rs[:], op=MUL), [h32[vc], rs], [h32[vc]])
            P.op('act', lambda e, vc=vc, ch=ch: e.activation(out=h32[vc][:], in_=h32[vc][:], func=AF.Identity, scale=pcc('mnorm', ch)), [h32[vc], pc], [h32[vc]])
            P.op('dve', lambda e, vc=vc, ch=ch: e.scalar_tensor_tensor(out=h32[vc][:], in0=fm(xct, ch), scalar=pcc('mskip', ch), in1=h32[vc][:], op0=MUL, op1=ADD),
                 [xct[ch // 4], pc, h32[vc]], [h32[vc]])
            P.op('pool', lambda e, vc=vc, ch=ch: e.tensor_tensor(out=merged[:, 8 + ch, :], in0=h32[vc][:], in1=gms[vc][:], op=MUL), [h32[vc], gms[vc]], [merged])

    def layer0(t0):
        rmsnorm_x('mixn0')
        pstate['set'] = [0, 1]
        if do_rwkv:
            rwkv(t0)
        if do_mlstm:
            mlstm(t0)
        pstate['set'] = list(range(8))
        out_proj('w0out')

    if do_l0 and not (do_rwkv and do_mlstm):
        P.op('pool', lambda e: e.memset(merged[:], 0.0), [], [merged])

    for ti in range(ntiles):
        t0 = ti * TT
        for tb in range(NT):
            P.dma('sp', H[tb][:], x_d.ap()[t0 + tb * 128:t0 + (tb + 1) * 128, :], writes=[H[tb]])
        for kc in range(8):
            ps = nextps()
            for tb in range(NT):
                P.op('pe', lambda e, kc=kc, tb=tb, ps=ps: e.transpose(ps[:, tb * 128:(tb + 1) * 128], H[tb][:, kc * 128:(kc + 1) * 128], ident[:]),
                     [H[tb], ident], [ps])
            P.op('act' if kc % 2 == 0 else 'dve',
                 (lambda e, kc=kc, ps=ps: e.activation(out=hT[:, kc, :], in_=ps[:], func=AF.Copy)) if kc % 2 == 0 else
                 (lambda e, kc=kc, ps=ps: e.tensor_copy(out=hT[:, kc, :], in_=ps[:])), [ps], [hT])
        if do_l0:
            layer0(t0)
            ple(0, t0)
        if do_l1:
            layer1(t0)
            ple(1, t0)
        norm_stats()
        for k in range(8):
            of = G[k % 4]
            norm_apply(k, 'finn', of[:], of)
            ps = nextps()
            for tb in range(NT):
                P.op('pe', lambda e, tb=tb, ps=ps, of=of: e.transpose(ps[:, tb * 128:(tb + 1) * 128], of[:, tb * 128:(tb + 1) * 128], ident[:]),
                     [of, ident], [ps])
            for tb in range(NT):
                P.op('act' if k % 2 == 0 else 'dve',
                     (lambda e, k=k, ps=ps, tb=tb: e.activation(out=H[tb][:, k * 128:(k + 1) * 128], in_=ps[:, tb * 128:(tb + 1) * 128], func=AF.Copy)) if k % 2 == 0 else
                     (lambda e, k=k, ps=ps, tb=tb: e.tensor_copy(out=H[tb][:, k * 128:(k + 1) * 128], in_=ps[:, tb * 128:(tb + 1) * 128])),
                     [ps], [H[tb]])
        for tb in range(NT):
            P.dma('sp', o_d.ap()[t0 + tb * 128:t0 + (tb + 1) * 128, :], H[tb][:], reads=[H[tb]])
    P.finish()
    print("sbuf bytes/partition:", P.sbytes, "sems:", P.nsem, "instr:", {e: P.cnt[e] for e in ENGS})
    return nc


def kernel(**inputs):
    x = np.asarray(inputs['x'], np.float32)
    p = np.asarray(inputs['p'], np.float32)
    B, S, _ = x.shape
    shared = host_prep(inputs)
    nc = build_program(S)
    in_maps = []
    for b in range(B):
        m = dict(shared)
        m['x'] = np.ascontiguousarray(x[b])
        m['p'] = np.ascontiguousarray(p[:, b])
        in_maps.append(m)
    res = run_bass_kernel_spmd(nc, in_maps, core_ids=list(range(B)))
    return np.stack([np.asarray(r['out'], np.float32) for r in res.results], axis=0)
```

```python
import numpy as np
import concourse.bass as bass
import concourse.mybir as mybir
from concourse.bass_utils import run_bass_kernel_spmd
from contextlib import ExitStack

F32 = mybir.dt.float32
BF16 = mybir.dt.bfloat16
F32R = mybir.dt.float32r
SDT = F32R
AF = mybir.ActivationFunctionType
ALU = mybir.AluOpType
AX = mybir.AxisListType

ENGS = ('pe', 'act', 'dve', 'pool', 'sp')
EIDX = {e: i for i, e in enumerate(ENGS)}
EPOCH = 30000

D = 1024
TT = 512
NT = TT // 128
PLE = 256
DECAY_SCALE = 0.6065306597126334


class TB:
    __slots__ = ('name', 'h', 'lw', 'rd', 'dkey', 'root', 'psum')

    def __init__(self, name, h, root=None, psum=False):
        self.name = name
        self.h = h
        self.lw = None
        self.rd = {}
        self.dkey = None
        self.root = root if root is not None else self
        self.psum = psum or (root is not None and root.psum)

    def __getitem__(self, k):
        return self.h[k]


class _Rec:
    def __init__(self):
        self.call = None

    def __getattr__(self, name):
        def f(*a, **k):
            self.call = (name, a, k)
        return f


class Prog:
    def __init__(self, nc):
        self.nc = nc
        self.es = ExitStack()
        self.ops = {e: [] for e in ENGS}
        self.cnt = {e: 0 for e in ENGS}
        self.seen = {e: {} for e in ENGS}
        self.clk = {e: [] for e in ENGS}
        self.dcnt = {}
        self.dclk = {}
        self.sems = {}
        self.nsem = 0
        self.ndma = 0
        self.sbytes = 0

    def sb(self, name, shape, dt=F32):
        n = 1
        for s in shape[1:]:
            n *= s
        self.sbytes += n * (2 if dt == BF16 else 4)
        return TB(name, self.es.enter_context(self.nc.sbuf_tensor(name, list(shape), dt)))

    def ps(self, name, shape, dt=F32):
        return TB(name, self.es.enter_context(self.nc.psum_tensor(name, list(shape), dt)), psum=True)

    def dram(self, name, shape, dt, kind):
        return TB(name, self.nc.dram_tensor(name, list(shape), dt, kind=kind))

    def _sem(self, key):
        s = self.sems.get(key)
        if s is None:
            s = self.es.enter_context(self.nc.semaphore("s%d" % self.nsem))
            self.nsem += 1
            self.sems[key] = s
        return s

    def _semval(self, key, count):
        if key in EIDX:
            ep = (count - 1) // EPOCH
            return self._sem((key, ep)), count - ep * EPOCH
        return self._sem(key), 16 * count

    def _deps(self, e, reads, writes):
        seen = self.seen[e]
        need = {}

        def req(ev):
            if ev is None:
                return
            k, c = ev
            if k == 'pe' and e == 'pe':
                return
            if seen.get(k, 0) >= c:
                return
            if need.get(k, 0) < c:
                need[k] = c
        for t in reads:
            req(t.lw)
        for t in writes:
            req(t.lw)
            for k, c in t.rd.items():
                req((k, c))
        keys = list(need.keys())
        for k in keys:
            if k not in need:
                continue
            c = need[k]
            ck = self.clk[k][c - 1] if k in EIDX else self.dclk[(k, c)]
            for k2 in keys:
                if k2 != k and k2 in EIDX and k2 in need and ck[EIDX[k2]] >= need[k2]:
                    del need[k2]
        waits = []
        for k, c in need.items():
            waits.append(self._semval(k, c))
            ck = self.clk[k][c - 1] if k in EIDX else self.dclk[(k, c)]
            for e2, v in zip(ENGS, ck):
                if seen.get(e2, 0) < v:
                    seen[e2] = v
            if seen.get(k, 0) < c:
                seen[k] = c
        return waits

    def _snapshot(self, e):
        s = self.seen[e]
        return tuple(s.get(x, 0) for x in ENGS)

    def op(self, e, fn, reads=(), writes=()):
        writes = [t.root for t in writes] + [t.root for t in reads if t.psum]
        reads = [t.root for t in reads if not t.psum]
        rec = _Rec()
        fn(rec)
        fn = rec.call
        waits = self._deps(e, reads, writes)
        self.cnt[e] += 1
        c = self.cnt[e]
        snap = list(self._snapshot(e))
        snap[EIDX[e]] = c
        self.clk[e].append(tuple(snap))
        inc = self._semval(e, c)[0]
        self.ops[e].append((waits, fn, inc, 1))
        ev = (e, c)
        for t in reads:
            if t.rd.get(e, 0) < c:
                t.rd[e] = c
        for t in writes:
            t.lw = ev
            t.rd = {}
        return ev

    def dma(self, q, out, in_, reads=(), writes=(), key=None, **kw):
        if key is None:
            t0 = (list(writes) + list(reads))[0]
            if t0.dkey is None:
                t0.dkey = ('dma', self.ndma)
                self.ndma += 1
            key = t0.dkey
        reads = [t.root for t in reads]
        writes = [t.root for t in writes]
        waits = self._deps(q, reads, writes)
        self.dcnt[key] = self.dcnt.get(key, 0) + 1
        c = self.dcnt[key]
        self.dclk[(key, c)] = self._snapshot(q)
        sem = self._sem(key)
        self.ops[q].append((waits, ('dma_start', (), dict(out=out, in_=in_, **kw)), sem, 16))
        ev = (key, c)
        for t in reads:
            if t.rd.get(key, 0) < c:
                t.rd[key] = c
        for t in writes:
            t.lw = ev
            t.rd = {}
        return ev

    def finish(self):
        e = 'sp'
        tail = []
        for key, c in self.dcnt.items():
            tail.append(self._semval(key, c))
        for x in ENGS:
            if x != e and self.cnt[x] > 0:
                tail.append(self._semval(x, self.cnt[x]))
        blk = self.es.enter_context(self.nc.Block())
        engobj = {'pe': blk.tensor, 'act': blk.scalar, 'dve': blk.vector, 'pool': blk.gpsimd, 'sp': blk.sync}

        def make(ename):
            ops = self.ops[ename]

            def body(eng):
                for waits, fn, inc, n in ops:
                    for (s, v) in waits[1:]:
                        eng.wait_ge(s, v)
                    ins = getattr(eng, fn[0])(*fn[1], **fn[2])
                    if waits:
                        ins._wait_ge(waits[0][0], waits[0][1])
                    ins.then_inc(inc, n)
                if ename == e:
                    for (s, v) in tail:
                        eng.wait_ge(s, v)
            return body
        for ename in ENGS:
            if self.ops[ename] or ename == e:
                engobj[ename](make(ename))
        self.es.close()


PC_SPEC = [
    ('mixn0', 8), ('mixn1', 8), ('pen0', 8), ('pen1', 8), ('finn', 8),
    ('mu_r', 8), ('mu_k', 8), ('mu_v', 8), ('mu_l', 1),
    ('w0', 8), ('a0', 8), ('k_k', 8), ('k_a', 8), ('r_k', 8), ('ln_w', 8), ('ln_b', 8),
    ('mcw0', 8), ('mcw1', 8), ('mcw2', 8), ('mcw3', 8), ('mcb', 8), ('mnorm', 8), ('mskip', 8),
    ('bif_i', 1), ('bif_f', 1),
    ('ccw0', 16), ('ccw1', 16), ('ccw2', 16), ('ccw3', 16), ('ccb', 16), ('cbr', 16), ('cbi', 16), ('clam', 16),
]
PC = {}
_o = 0
for _n, _w in PC_SPEC:
    PC[_n] = _o
    _o += _w
NPC = _o
PD_SPEC = [('omu_r', 8), ('omu_k', 8), ('omu_v', 8), ('omu_l', 1), ('negbf', 1), ('csph', 16), ('cneg', 16), ('t0', 16), ('t1', 16)]
PD = {}
_o = 0
for _n, _w in PD_SPEC:
    PD[_n] = _o
    _o += _w
NPD = _o


def _cols(v):
    v = np.asarray(v, np.float32).reshape(-1)
    return np.ascontiguousarray(v.reshape(-1, 128).T)


def host_consts():
    c = {}
    c['ident'] = np.eye(128, dtype=np.float32)
    bd = np.zeros((128, 128), np.float32)
    bd[:64, :64] = 1
    bd[64:, 64:] = 1
    c['bdones'] = bd
    j = np.arange(128)[:, None]
    i = np.arange(128)[None, :]
    su = ((j < i) & ((j // 64) == (i // 64))).astype(np.float32)
    ui = ((j <= i) & ((j // 64) == (i // 64))).astype(np.float32)
    sl = ((j > i) & ((j // 64) == (i // 64))).astype(np.float32)
    on = np.ones((128, 128), np.float32)
    c['maskA'] = np.stack([su, su, ui, ui], axis=1)
    c['maskB'] = sl
    rm = np.ones((128, TT), np.float32)
    rm[:, ::64] = 0
    c['resetm'] = rm
    c['causal'] = (j <= i).astype(np.float32)
    sel = np.zeros((4, 4, 128), np.float32)
    for h in range(4):
        sel[h, h, :] = 1
    c['sel'] = sel
    return c


def host_prep(inp):
    g = {}
    f = lambda a: np.ascontiguousarray(np.asarray(a, np.float32))

    def chunked(w):
        K, M = w.shape
        return f(w.reshape(K // 128, 128, M // 128, 128).transpose(2, 1, 0, 3))
    g['w0in'] = chunked(f(inp['ab_w_in'])[0])
    g['w0out'] = chunked(f(inp['ab_w_out'])[0])
    g['w1in'] = chunked(f(inp['c_w_in'])[0])
    g['w1out'] = chunked(f(inp['c_w_out'])[0])
    g['pegate'] = np.stack([chunked(f(inp['pe_gate'])[i]) for i in range(2)])
    g['peup'] = np.stack([chunked(f(inp['pe_up'])[i]) for i in range(2)])
    wr = f(inp['c_wr'])[0]
    wi = f(inp['c_wi'])[0]
    g['wri'] = f(np.stack([wr, wi], axis=2))
    lup = np.zeros((128, 2, 1024), np.float32)
    lup[:64, 0, :] = f(inp['rwkv_w_up'])[0]
    lup[64:, 1, :] = f(inp['rwkv_a_up'])[0]
    g['lup'] = lup
    wbd = np.zeros((128, 3, 8, 128), np.float32)
    for t, nm in enumerate(['mlstm_wq', 'mlstm_wk', 'mlstm_wv']):
        w = f(inp[nm])[0]
        for c in range(8):
            for gg in range(32):
                wbd[4 * gg:4 * gg + 4, t, c, 4 * gg:4 * gg + 4] = w[c * 32 + gg]
    g['wqkv'] = wbd
    wif = f(inp['mlstm_w_if'])[0]
    g['wif'] = f(wif.reshape(24, 128, 2, 4).transpose(1, 2, 0, 3))
    pc = np.zeros((128, NPC), np.float32)

    def put(name, v):
        cc = _cols(v)
        pc[:, PC[name]:PC[name] + cc.shape[1]] = cc
    put('mixn0', f(inp['mix_norm'])[0]); put('mixn1', f(inp['mix_norm'])[1])
    put('pen0', f(inp['pe_norm'])[0]); put('pen1', f(inp['pe_norm'])[1])
    put('finn', f(inp['final_norm']))
    mu = f(inp['rwkv_mu'])[0]
    put('mu_r', mu[0]); put('mu_k', mu[1]); put('mu_v', mu[2])
    ml = f(inp['rwkv_mu_lora'])[0]
    put('mu_l', np.concatenate([ml[0], ml[1]]))
    put('w0', f(inp['rwkv_w0'])[0]); put('a0', f(inp['rwkv_a0'])[0])
    put('k_k', f(inp['rwkv_k_k'])[0]); put('k_a', f(inp['rwkv_k_a'])[0])
    put('r_k', f(inp['rwkv_r_k'])[0].reshape(-1))
    put('ln_w', f(inp['rwkv_ln_w'])[0]); put('ln_b', f(inp['rwkv_ln_b'])[0])
    mcw = f(inp['mlstm_conv_w'])[0]
    for j in range(4):
        put('mcw%d' % j, mcw[j])
    put('mcb', f(inp['mlstm_conv_b'])[0])
    put('mnorm', f(inp['mlstm_norm'])[0]); put('mskip', f(inp['mlstm_skip'])[0])
    bif = f(inp['mlstm_b_if'])[0]
    pc[0:4, PC['bif_i']] = bif[0:4]
    pc[0:4, PC['bif_f']] = bif[4:8]
    ccw = f(inp['c_conv_w'])[0]
    for j in range(4):
        put('ccw%d' % j, ccw[j])
    put('ccb', f(inp['c_conv_b'])[0]); put('cbr', f(inp['c_br'])[0]); put('cbi', f(inp['c_bi'])[0])
    put('clam', f(inp['c_lambda'])[0])
    g['pc'] = pc
    g.update(host_consts())
    return g


SHARED_SHAPES = {
    'w0in': [49, 128, 8, 128], 'w0out': [8, 128, 16, 128], 'w1in': [32, 128, 8, 128], 'w1out': [8, 128, 16, 128],
    'pegate': [2, 8, 128, 8, 128], 'peup': [2, 8, 128, 2, 128], 'wri': [16, 128, 2, 128], 'lup': [128, 2, 1024],
    'wqkv': [128, 3, 8, 128], 'wif': [128, 2, 24, 4], 'pc': [128, NPC],
    'ident': [128, 128], 'bdones': [128, 128], 'maskA': [128, 4, 128], 'maskB': [128, 128],
    'resetm': [128, TT], 'causal': [128, 128], 'sel': [4, 4, 128],
}


def build_program(S, do_l0=True, do_rwkv=True, do_mlstm=True, do_l1=True, dbg=None):
    nc = bass.Bass("TRN2", target_bir_lowering=False)
    P = Prog(nc)
    ntiles = S // TT
    din = {}
    for k, shp in SHARED_SHAPES.items():
        din[k] = nc.dram_tensor(k, shp, F32, kind="ExternalInput")
    x_d = nc.dram_tensor("x", [S, D], F32, kind="ExternalInput")
    p_d = nc.dram_tensor("p", [2, S, PLE], F32, kind="ExternalInput")
    o_d = nc.dram_tensor("out", [S, D], F32, kind="ExternalOutput")
    dbg_d = None
    dbg = dbg or []
    if dbg:
        dbg_d = nc.dram_tensor("dbg", [len(dbg), 128, TT], F32, kind="ExternalOutput")

    def dump(name, tb, ap):
        if name in dbg:
            P.dma('sp', dbg_d.ap()[dbg.index(name)], ap, reads=[tb], key=('dma', 'dbg'))

    MUL, ADD, SUB, MAX = ALU.mult, ALU.add, ALU.subtract, ALU.max

    def cload(name, dt=F32, rows=128):
        shp = SHARED_SHAPES[name]
        t = P.sb("c_" + name, shp, dt)
        P.dma('pool' if dt != F32 else 'sp', t[:], din[name].ap(), writes=[t])
        return t
    ident = cload('ident')
    pc = cload('pc')
    pd = P.sb("pd", [128, NPD])
    ones_bf = P.sb("ones_bf", [128, 128], BF16)
    P.op('pool', lambda e: e.memset(ones_bf[:], 1.0), [], [ones_bf])

    def pcc(name, c=0, n=1):
        return pc[:, PC[name] + c:PC[name] + c + n]

    def pdc(name, c=0, n=1):
        return pd[:, PD[name] + c:PD[name] + c + n]

    if do_l1:
        P.op('act', lambda e: e.activation(out=pdc('t0', 0, 16), in_=pcc('clam', 0, 16), func=AF.Exp, scale=-1.0), [pc], [pd])
        P.op('act', lambda e: e.activation(out=pdc('t1', 0, 16), in_=pdc('t0', 0, 16), func=AF.Ln, bias=1.0), [pd], [pd])
        P.op('dve', lambda e: e.tensor_scalar(out=pdc('csph', 0, 16), in0=pdc('t1', 0, 16), scalar1=4.0, scalar2=None, op0=MUL), [pd], [pd])
        P.op('dve', lambda e: e.tensor_scalar(out=pdc('cneg', 0, 16), in0=pdc('t1', 0, 16), scalar1=-8.0, scalar2=None, op0=MUL), [pd], [pd])
    if do_l0:
        for nm, w in (('r', 8), ('k', 8), ('v', 8), ('l', 1)):
            P.op('dve', lambda e, nm=nm, w=w: e.tensor_scalar(out=pdc('omu_' + nm, 0, w), in0=pcc('mu_' + nm, 0, w), scalar1=-1.0, scalar2=1.0, op0=MUL, op1=ADD),
                 [pc], [pd])
        P.op('dve', lambda e: e.tensor_scalar(out=pdc('negbf'), in0=pcc('bif_f'), scalar1=-1.0, scalar2=None, op0=MUL), [pc], [pd])

    scr = {}

    def mkscr(name, src_ap, shape, group=8):
        t = P.dram("scr_" + name, shape, BF16, "Internal")
        nm = shape[0]
        for j0 in range(0, nm, group):
            j1 = min(nm, j0 + group)
            P.dma('pool', t[j0:j1].rearrange("j p k m -> p j k m"), src_ap[j0:j1].rearrange("j p k m -> p j k m"),
                  writes=[t], key=('dma', 'scr_' + name))
        scr[name] = t
        return t
    if do_l0:
        lup_bf = cload('lup', BF16)
        wqkv_bf = cload('wqkv', BF16)
        wif_bf = cload('wif', BF16)
        mkscr('w0in', din['w0in'].ap(), [49, 128, 8, 128])
        mkscr('w0out', din['w0out'].ap(), [8, 128, 16, 128], group=4)
    for i in range(2):
        if (i == 0 and do_l0) or (i == 1 and do_l1):
            mkscr('pegate%d' % i, din['pegate'].ap()[i], [8, 128, 8, 128])
            mkscr('peup%d' % i, din['peup'].ap()[i], [8, 128, 2, 128])
    if do_l1:
        mkscr('w1in', din['w1in'].ap(), [32, 128, 8, 128])
        mkscr('wri', din['wri'].ap(), [16, 128, 2, 128], group=16)
        mkscr('w1out', din['w1out'].ap(), [8, 128, 16, 128], group=4)

    NWB = 5
    wring = [P.sb("wb%d" % i, [128, 16, 128], BF16) for i in range(NWB)]
    wstate = {'i': 0}

    def wld(name, j, nk):
        wb = wring[wstate['i'] % NWB]
        wstate['i'] += 1
        P.dma('sp', wb[:, 0:nk, :], scr[name][j], reads=[scr[name]], writes=[wb])
        return wb

    PS = [P.ps("ps%d" % i, [128, TT]) for i in range(8)]
    pstate = {'i': 0}

    pstate['set'] = list(range(8))

    def nextps():
        s_ = pstate['set']
        b = PS[s_[pstate['i'] % len(s_)]]
        pstate['i'] += 1
        return b

    hT = P.sb("hT", [128, 8, TT])
    xnT = P.sb("xnT", [128, 8, TT], BF16)
    merged = P.sb("merged", [128, 16, TT], BF16)
    ext = P.sb("ext", [128, TT + 3])
    NG, NH = 20, 12
    G = [P.sb("g%d" % i, [128, TT]) for i in range(NG)]
    H = [P.sb("h%d" % i, [128, 2 * TT]) for i in range(NH)]
    lnv, rstd = G[19], G[18]
    _alias = {}

    def RV(tb):
        if SDT == F32:
            return tb[:]
        a = _alias.get(tb.name)
        if a is None:
            ml = nc.lookup_mloc(tb.h)
            a = nc.alloc_sbuf_tensor_at(tb.name + "_r", [128, int(ml.dims[1]) // 4], F32R, offset=int(ml.addr))
            _alias[tb.name] = a
        return a[:]
    ntmp = [G[16], G[17]]

    def bfv(tb, n=None):
        v = tb[:].bitcast(BF16)
        return v

    if do_l1:
        l1_tail = P.sb("l1_tail", [128, 16, 3])
        l1_h = P.sb("l1_h", [128, 16])
        P.op('pool', lambda e: e.memset(l1_tail[:], 0.0), [], [l1_tail])
        P.op('pool', lambda e: e.memset(l1_h[:], 0.0), [], [l1_h])

    def norm_apply(k, gname, out_ap, out_tb):
        if k % 2 == 0:
            P.op('dve', lambda e: e.scalar_tensor_tensor(out=out_ap, in0=hT[:, k, :], scalar=pcc(gname, k), in1=rstd[:],
                                                         op0=MUL, op1=MUL), [hT, pc, rstd], [out_tb])
        else:
            nt = ntmp[(k // 2) % 2]
            P.op('act', lambda e: e.activation(out=nt[:], in_=hT[:, k, :], func=AF.Identity, scale=pcc(gname, k)), [hT, pc], [nt])
            P.op('pool', lambda e: e.tensor_tensor(out=out_ap, in0=nt[:], in1=rstd[:], op=MUL), [nt, rstd], [out_tb])

    def norm_stats():
        ps = nextps()
        for half in range(2):
            sq = H[4 + half]
            sqv = bfv(sq).rearrange("p (k t) -> p k t", k=4)
            P.op('act', lambda e, half=half, sqv=sqv: e.activation(out=sqv, in_=hT[:, 4 * half:4 * half + 4, :], func=AF.Square), [hT], [sq])
            for k in range(4):
                P.op('pe', lambda e, k=k, half=half, sqv=sqv: e.matmul(ps[:], lhsT=ones_bf[:], rhs=sqv[:, k, :], start=(half == 0 and k == 0), stop=(half == 1 and k == 3)),
                     [ones_bf, sq], [ps])
        P.op('act', lambda e: e.activation(out=lnv[:], in_=ps[:], func=AF.Ln, scale=1.0 / D, bias=1e-6), [ps], [lnv])
        P.op('act', lambda e: e.activation(out=rstd[:], in_=lnv[:], func=AF.Exp, scale=-0.5), [lnv], [rstd])

    def rmsnorm_x(gname):
        norm_stats()
        for k in range(8):
            norm_apply(k, gname, xnT[:, k, :], xnT)

    def proj(ps, wb, nk, rhs_tb, rhs_fn):
        for k in range(nk):
            P.op('pe', lambda e, k=k: e.matmul(ps[:], lhsT=wb[:, k, :], rhs=rhs_fn(k), start=(k == 0), stop=(k == nk - 1)),
                 [wb, rhs_tb], [ps])

    def proj_x(name, j):
        w = wld(name, j, 8)
        ps = nextps()
        proj(ps, w, 8, xnT, lambda k: xnT[:, k, :])
        return ps

    def ple(i, t0):
        ptok = H[0]
        ptv = ptok[:].rearrange("p (tb d) -> p tb d", tb=NT)
        pTt = G[12]
        pT = bfv(pTt).rearrange("p (k t) -> p k t", k=2)
        sgs, tmps = [G[0], G[1]], [G[2], G[3]]
        P.dma('sp', ptv, p_d.ap()[i, t0:t0 + TT, :].rearrange("(tb p) d -> p tb d", p=128), writes=[ptok])
        for kc in range(2):
            ps = nextps()
            for tb in range(NT):
                P.op('pe', lambda e, kc=kc, tb=tb, ps=ps: e.transpose(ps[:, tb * 128:(tb + 1) * 128], ptv[:, tb, kc * 128:(kc + 1) * 128], ident[:]),
                     [ptok, ident], [ps])
            P.op('act', lambda e, kc=kc, ps=ps: e.activation(out=pT[:, kc, :], in_=ps[:], func=AF.Copy), [ps], [pTt])
        rmsnorm_x('pen%d' % i)
        for m in range(8):
            sg, tmpa = sgs[m % 2], tmps[m % 2]
            wg = wld('pegate%d' % i, m, 8)
            wu = wld('peup%d' % i, m, 2)
            ps = nextps()
            proj(ps, wg, 8, xnT, lambda k: xnT[:, k, :])
            P.op('act', lambda e, ps=ps: e.activation(out=sg[:], in_=ps[:], func=AF.Sigmoid), [ps], [sg])
            ps2 = nextps()
            proj(ps2, wu, 2, pTt, lambda k: pT[:, k, :])
            P.op('dve', lambda e, ps2=ps2: e.tensor_tensor(out=tmpa[:], in0=ps2[:], in1=sg[:], op=MUL), [ps2, sg], [tmpa])
            P.op('pool', lambda e, m=m: e.tensor_tensor(out=hT[:, m, :], in0=hT[:, m, :], in1=tmpa[:], op=ADD), [hT, tmpa], [hT])

    def out_proj(name):
        for m in range(8):
            wo = wld(name, m, 16)
            ps = nextps()
            proj(ps, wo, 16, merged, lambda k: merged[:, k, :])
            P.op('dve', lambda e, m=m, ps=ps: e.tensor_tensor(out=hT[:, m, :], in0=hT[:, m, :], in1=ps[:], op=ADD), [hT, ps], [hT])

    def conv4(ps, c, tail_tb, wname, bname, acc):
        P.op('pool', lambda e: e.tensor_copy(out=ext[:, 0:3], in_=tail_tb[:, c, :]), [tail_tb], [ext])
        P.op('act', lambda e: e.activation(out=ext[:, 3:TT + 3], in_=ps[:], func=AF.Copy), [ps], [ext])
        P.op('act', lambda e: e.activation(out=acc[:], in_=ps[:], func=AF.Identity, scale=pcc(wname + '3', c), bias=pcc(bname, c)), [ps, pc], [acc])
        P.op('pool', lambda e: e.tensor_copy(out=tail_tb[:, c, :], in_=ext[:, TT:TT + 3]), [ext], [tail_tb])
        for j in range(3):
            P.op('dve', lambda e, j=j: e.scalar_tensor_tensor(out=acc[:], in0=ext[:, j:j + TT], scalar=pcc(wname + str(j), c), in1=acc[:],
                                                              op0=MUL, op1=ADD), [ext, pc, acc], [acc])

    def layer1(t0):
        rmsnorm_x('mixn1')
        for c in range(16):
            xc32, xcb, rg, ig, a2, th = G[6 * (c % 2):6 * (c % 2) + 6]
            xcb_ap = bfv(xcb)[:, 0:TT]
            ps = proj_x('w1in', c)
            conv4(ps, c, l1_tail, 'ccw', 'ccb', xc32)
            P.op('act', lambda e: e.activation(out=xcb_ap, in_=xc32[:], func=AF.Copy), [xc32], [xcb])
            wb = wld('wri', c, 2)
            ps_r = nextps()
            P.op('pe', lambda e: e.matmul(ps_r[:], lhsT=wb[:, 0, :], rhs=xcb_ap, start=True, stop=True), [wb, xcb], [ps_r])
            P.op('act', lambda e: e.activation(out=rg[:], in_=ps_r[:], func=AF.Sigmoid, bias=pcc('cbr', c)), [ps_r, pc], [rg])
            ps_i = nextps()
            P.op('pe', lambda e: e.matmul(ps_i[:], lhsT=wb[:, 1, :], rhs=xcb_ap, start=True, stop=True), [wb, xcb], [ps_i])
            P.op('act', lambda e: e.activation(out=ig[:], in_=ps_i[:], func=AF.Sigmoid, bias=pcc('cbi', c)), [ps_i, pc], [ig])
            P.op('act', lambda e: e.activation(out=a2[:], in_=rg[:], func=AF.Exp, scale=pdc('cneg', c)), [rg, pd], [a2])
            P.op('act', lambda e: e.activation(out=th[:], in_=rg[:], func=AF.Tanh, scale=pdc('csph', c)), [rg, pd], [th])
            P.op('dve', lambda e: e.scalar_tensor_tensor(out=th[:], in0=a2[:], scalar=1.0, in1=th[:], op0=ADD, op1=MUL), [a2, th], [th])
            P.op('dve', lambda e: e.tensor_scalar(out=a2[:], in0=th[:], scalar1=-1.0, scalar2=1.0, op0=MUL, op1=ADD), [th], [a2])
            P.op('dve', lambda e: e.scalar_tensor_tensor(out=th[:], in0=a2[:], scalar=1.0, in1=th[:], op0=ADD, op1=MUL), [a2, th], [th])
            P.op('act', lambda e: e.activation(out=th[:], in_=th[:], func=AF.Sqrt), [th], [th])
            P.op('pool', lambda e: e.tensor_tensor(out=ig[:], in0=xc32[:], in1=ig[:], op=MUL), [xc32, ig], [ig])
            P.op('pool', lambda e: e.tensor_tensor(out=ig[:], in0=ig[:], in1=th[:], op=MUL), [ig, th], [ig])
            P.op('dve', lambda e: e.tensor_tensor_scan(out=rg[:], data0=a2[:], data1=ig[:], initial=l1_h[:, c:c + 1], op0=MUL, op1=ADD),
                 [a2, ig, l1_h], [rg])
            P.op('pool', lambda e: e.tensor_copy(out=l1_h[:, c:c + 1], in_=rg[:, TT - 1:TT]), [rg], [l1_h])
            ps_g = proj_x('w1in', 16 + c)
            P.op('act', lambda e: e.activation(out=xc32[:], in_=ps_g[:], func=AF.Silu), [ps_g], [xc32])
            P.op('dve', lambda e: e.tensor_tensor(out=merged[:, c, :], in0=rg[:], in1=xc32[:], op=MUL), [rg, xc32], [merged])
        out_proj('w1out')

    if do_l0 and do_rwkv:
        identr = P.sb("identr", [128, 128], SDT)
        bdones = cload('bdones')
        bdones_bf = P.sb("bdones_bf", [128, 128], BF16)
        bdones_r = P.sb("bdones_r", [128, 128], SDT)
        maskA = cload('maskA')
        maskB = cload('maskB')
        resetm = cload('resetm')
        P.op('dve', lambda e: e.tensor_copy(out=identr[:], in_=ident[:]), [ident], [identr])
        P.op('dve', lambda e: e.tensor_copy(out=bdones_bf[:], in_=bdones[:]), [bdones], [bdones_bf])
        P.op('dve', lambda e: e.tensor_copy(out=bdones_r[:], in_=bdones[:]), [bdones], [bdones_r])
        rwc = P.sb("rwc", [128, 25])
        P.op('pool', lambda e: e.memset(rwc[:], 0.0), [], [rwc])
        lora_bf = P.sb("lora_bf", [128, TT], BF16)
        Sst = [[P.sb("S%d_%d" % (c, q), [128, 128], F32) for q in range(2)] for c in range(8)]
        for c in range(8):
            P.op('pool', lambda e, c=c: e.memset(Sst[c][0][:], 0.0), [], [Sst[c][0]])
        spar = [0] * 8
        rhs_sb = P.sb("rhs_sb", [128, 128], SDT)
        u_sb = P.sb("u_sb", [128, 128], SDT)
        psRHS = TB("psRHS", PS[7].h[:, 0:128], root=PS[7])
        psU = TB("psU", PS[7].h[:, 128:256], root=PS[7])
        psYT = TB("psYT", PS[7].h[:, 256:384], root=PS[7])
        psSN = TB("psSN", PS[7].h[:, 384:512], root=PS[7])

    def tshift(ps, dst, col, mu_ap, omu_ap):
        P.op('act', lambda e: e.activation(out=dst[:], in_=ps[:], func=AF.Identity, scale=omu_ap), [ps, pd], [dst])
        P.op('dve', lambda e: e.scalar_tensor_tensor(out=dst[:, 1:TT], in0=ps[:, 0:TT - 1], scalar=mu_ap, in1=dst[:, 1:TT], op0=MUL, op1=ADD),
             [ps, pc, dst], [dst])
        P.op('dve', lambda e: e.scalar_tensor_tensor(out=dst[:, 0:1], in0=rwc[:, col:col + 1], scalar=mu_ap, in1=dst[:, 0:1], op0=MUL, op1=ADD),
             [rwc, pc, dst], [dst])
        P.op('act', lambda e: e.activation(out=rwc[:, col:col + 1], in_=ps[:, TT - 1:TT], func=AF.Copy), [ps], [rwc])

    def r3(ap, n=8):
        return ap.rearrange("p (n t) -> p n t", n=n)

    def rwkv(t0):
        Rbd, Abd, Bbd, Kbd, Vbd = H[0:5]
        for i in range(5):
            P.op('pool', lambda e, i=i: e.memset(H[i][:], 0.0), [], [H[i]])
        bdv = [r3(RV(H[i])) for i in range(5)]
        ps = proj_x('w0in', 24)
        lora32 = G[16]
        tshift(ps, lora32, 24, pcc('mu_l'), pdc('omu_l'))
        P.op('act', lambda e: e.activation(out=lora_bf[0:64, :], in_=lora32[0:64, :], func=AF.Tanh), [lora32], [lora_bf])
        P.op('act', lambda e: e.activation(out=lora_bf[64:128, :], in_=lora32[64:128, :], func=AF.Copy), [lora32], [lora_bf])
        for c in range(8):
            rwkv_pair(c, bdv)

    def rwkv_pair(c, bdv):
        r32, k32, v32, gr32, lw, cum, cump, Gt, Ginv, Gp, a32, kk32, rn, nkk, kka, keff, t1, misc = G[0:18]
        Rv, Av, Bv, Kv, Vv = bdv
        Rbd, Abd, Bbd, Kbd, Vbd = H[0:5]
        for dst, mch, nm, col in ((r32, c, 'r', c), (k32, 8 + c, 'k', 8 + c), (v32, 16 + c, 'v', 16 + c)):
            ps = proj_x('w0in', mch)
            tshift(ps, dst, col, pcc('mu_' + nm, c), pdc('omu_' + nm, c))
        ps = proj_x('w0in', 33 + c)
        P.op('act', lambda e, ps=ps: e.activation(out=gr32[:], in_=ps[:], func=AF.Silu), [ps], [gr32])
        ps = nextps()
        P.op('pe', lambda e, ps=ps: e.matmul(ps[:], lhsT=lup_bf[:, 0, c * 128:(c + 1) * 128], rhs=lora_bf[:], start=True, stop=True), [lup_bf, lora_bf], [ps])
        P.op('act', lambda e, ps=ps: e.activation(out=lw[:], in_=ps[:], func=AF.Sigmoid, bias=pcc('w0', c)), [ps, pc], [lw])
        P.op('pool', lambda e: e.tensor_scalar(out=lw[:], in0=lw[:], scalar1=-DECAY_SCALE, scalar2=None, op0=MUL), [lw], [lw])
        ps = nextps()
        P.op('pe', lambda e, ps=ps: e.matmul(ps[:], lhsT=lup_bf[:, 1, c * 128:(c + 1) * 128], rhs=lora_bf[:], start=True, stop=True), [lup_bf, lora_bf], [ps])
        P.op('act', lambda e, ps=ps: e.activation(out=a32[:], in_=ps[:], func=AF.Sigmoid, bias=pcc('a0', c)), [ps, pc], [a32])
        P.op('dve', lambda e: e.tensor_tensor_scan(out=cum[:], data0=resetm[:], data1=lw[:], initial=0.0, op0=MUL, op1=ADD), [resetm, lw], [cum])
        P.op('pool', lambda e: e.tensor_tensor(out=cump[:], in0=cum[:], in1=lw[:], op=SUB), [cum, lw], [cump])
        P.op('act', lambda e: e.activation(out=Gt[:], in_=cum[:], func=AF.Exp), [cum], [Gt])
        P.op('act', lambda e: e.activation(out=Ginv[:], in_=cum[:], func=AF.Exp, scale=-1.0), [cum], [Ginv])
        P.op('act', lambda e: e.activation(out=Gp[:], in_=cump[:], func=AF.Exp), [cump], [Gp])
        P.op('pool', lambda e: e.tensor_scalar(out=kk32[:], in0=k32[:], scalar1=pcc('k_k', c), scalar2=None, op0=MUL), [k32, pc], [kk32])
        sqk = bfv(misc)[:, 0:TT]
        P.op('act', lambda e: e.activation(out=sqk, in_=kk32[:], func=AF.Square), [kk32], [misc])
        ps = nextps()
        P.op('pe', lambda e, ps=ps: e.matmul(ps[:], lhsT=bdones_bf[:], rhs=sqk, start=True, stop=True), [bdones_bf, misc], [ps])
        P.op('act', lambda e, ps=ps: e.activation(out=rn[:], in_=ps[:], func=AF.Ln, bias=1e-20), [ps], [rn])
        P.op('act', lambda e: e.activation(out=rn[:], in_=rn[:], func=AF.Exp, scale=-0.5), [rn], [rn])
        P.op('dve', lambda e: e.scalar_tensor_tensor(out=nkk[:], in0=kk32[:], scalar=-1.0, in1=rn[:], op0=MUL, op1=MUL), [kk32, rn], [nkk])
        P.op('dve', lambda e: e.scalar_tensor_tensor(out=kka[:], in0=nkk[:], scalar=-1.0, in1=a32[:], op0=MUL, op1=MUL), [nkk, a32], [kka])
        P.op('pool', lambda e: e.tensor_scalar(out=t1[:], in0=a32[:], scalar1=-1.0, scalar2=pcc('k_a', c), op0=ADD, op1=MUL), [a32, pc], [t1])
        P.op('dve', lambda e: e.scalar_tensor_tensor(out=keff[:], in0=t1[:], scalar=1.0, in1=k32[:], op0=ADD, op1=MUL), [t1, k32], [keff])
        rkr = bfv(t1)[:, 0:TT]
        P.op('dve', lambda e: e.scalar_tensor_tensor(out=rkr, in0=r32[:], scalar=pcc('r_k', c), in1=keff[:], op0=MUL, op1=MUL), [r32, pc, keff], [t1])
        ps = nextps()
        P.op('pe', lambda e, ps=ps: e.matmul(ps[:], lhsT=bdones_bf[:], rhs=rkr, start=True, stop=True), [bdones_bf, t1], [ps])
        bon = cum
        P.op('act', lambda e, ps=ps: e.activation(out=bon[:], in_=ps[:], func=AF.Copy), [ps], [bon])
        for hh in range(2):
            hs_ = slice(hh * 64, hh * 64 + 64)
            for dstv, dtb, a_, b_ in ((Rv, Rbd, r32, Gt), (Av, Abd, nkk, Gp), (Bv, Bbd, kka, Ginv), (Kv, Kbd, keff, Ginv)):
                P.op('dve', lambda e, dstv=dstv, a_=a_, b_=b_, hs_=hs_: e.tensor_tensor(out=dstv[hs_, :, hs_], in0=r3(a_[hs_, :]), in1=r3(b_[hs_, :]), op=MUL),
                     [a_, b_], [dtb])
            P.op('act', lambda e, hs_=hs_: e.activation(out=Vv[hs_, :, hs_], in_=r3(v32[hs_, :]), func=AF.Copy), [v32], [Vbd])
        y32 = lw
        for hf in range(2):
            SCt = [H[5], H[6]]
            QTt = [H[7], H[8]]
            SCv = [RV(t).rearrange("p (j m k) -> p j m k", j=2, m=4) for t in SCt]
            QTv = [RV(t).rearrange("p (j m k) -> p j m k", j=2, m=4) for t in QTt]
            SCf = [t[:].rearrange("p (j m k) -> p j m k", j=2, m=4) for t in SCt]
            PQt = [H[9], H[10]]
            PQv = [[RV(PQt[b]).rearrange("p (q j m k) -> p q j m k", q=2, j=2, m=2)[:, par] for b in range(2)] for par in range(2)]
            Xt = H[11]
            Xv = [RV(Xt).rearrange("p (q j k) -> p q j k", q=2, j=4)[:, q] for q in range(2)]
            Xf = [Xt[:].rearrange("p (q j k) -> p q j k", q=2, j=4)[:, q] for q in range(2)]
            for j in range(4):
                n = hf * 4 + j
                sc = SCv[j // 2][:, j % 2]
                qt = QTv[j // 2][:, j % 2]
                A_, B_ = PS[2], PS[3]
                for m, (l_, r_) in enumerate(((Bv, Av), (Kv, Av), (Bv, Rv), (Kv, Rv))):
                    P.op('pe', lambda e, m=m, l_=l_, r_=r_, n=n: e.matmul(A_[:, m * 128:(m + 1) * 128], lhsT=l_[:, n, :], rhs=r_[:, n, :], start=True, stop=True),
                         [Rbd, Abd, Bbd, Kbd], [A_])
                P.op('dve', lambda e, sc=sc: e.tensor_tensor(out=sc, in0=A_[:].rearrange("p (m k) -> p m k", m=4), in1=maskA[:], op=MUL),
                     [A_, maskA], [SCt[j // 2]])
                P.op('pe', lambda e, n=n: e.matmul(B_[:, 0:128], lhsT=Av[:, n, :], rhs=Bv[:, n, :], start=True, stop=True), [Abd, Bbd], [B_])
                for m, l_ in enumerate((Bv, Kv, Vv)):
                    P.op('pe', lambda e, m=m, l_=l_, n=n: e.matmul(B_[:, (m + 1) * 128:(m + 2) * 128], lhsT=l_[:, n, :], rhs=identr[:], start=True, stop=True),
                         [Bbd, Kbd, Vbd, identr], [B_])
                P.op('dve', lambda e, qt=qt: e.tensor_tensor(out=qt[:, 0, :], in0=B_[:, 0:128], in1=maskB[:], op=MUL), [B_, maskB], [QTt[j // 2]])
                P.op('act', lambda e, qt=qt: e.activation(out=qt[:, 1:4, :], in_=B_[:, 128:512].rearrange("p (m k) -> p m k", m=3), func=AF.Copy),
                     [B_], [QTt[j // 2]])
            for j in range(4):
                P.op('dve', lambda e, j=j: e.tensor_tensor(out=Xv[1][:, j, :], in0=SCf[j // 2][:, j % 2, 0, :], in1=ident[:], op=ADD),
                     [SCt[j // 2], ident], [Xt])
            for lvl in range(1, 6):
                par = lvl % 2
                for b in range(2):
                    bank = PS[4 + b]
                    for jj in range(2):
                        j = 2 * b + jj
                        if lvl == 1:
                            Pp = SCv[j // 2][:, j % 2, 0, :]
                            Qp = QTv[j // 2][:, j % 2, 0, :]
                            rd = [SCt[j // 2], QTt[j // 2]]
                        else:
                            Pp = PQv[1 - par][b][:, jj, 0, :]
                            Qp = PQv[1 - par][b][:, jj, 1, :]
                            rd = [PQt[b]]
                        P.op('pe', lambda e, jj=jj, Pp=Pp, Qp=Qp, bank=bank: e.matmul(bank[:, (2 * jj) * 128:(2 * jj + 1) * 128], lhsT=Qp, rhs=Pp, start=True, stop=True), rd, [bank])
                        P.op('pe', lambda e, jj=jj, Pp=Pp, Qp=Qp, bank=bank: e.matmul(bank[:, (2 * jj + 1) * 128:(2 * jj + 2) * 128], lhsT=Pp, rhs=Qp, start=True, stop=True), rd, [bank])
                    if b == 0:
                        P.op('act', lambda e, b=b, bank=bank, par=par: e.activation(out=PQv[par][b], in_=bank[:].rearrange("p (j m k) -> p j m k", j=2, m=2), func=AF.Copy),
                             [bank], [PQt[b]])
                    else:
                        P.op('dve', lambda e, b=b, bank=bank, par=par: e.tensor_copy(out=PQv[par][b], in_=bank[:].rearrange("p (j m k) -> p j m k", j=2, m=2)),
                             [bank], [PQt[b]])
                XB = PS[6]
                for j in range(4):
                    Ql = PQv[par][j // 2][:, j % 2, 1, :]
                    P.op('pe', lambda e, j=j, Ql=Ql, par=par: e.matmul(XB[:, j * 128:(j + 1) * 128], lhsT=Ql, rhs=Xv[par][:, j, :], start=True, stop=True), [PQt[j // 2], Xt], [XB])
                P.op('dve', lambda e, par=par: e.tensor_tensor(out=Xv[1 - par], in0=XB[:].rearrange("p (j k) -> p j k", j=4), in1=Xf[par], op=ADD), [XB, Xt], [Xt])
            for j in range(4):
                n = hf * 4 + j
                sc = SCv[j // 2][:, j % 2]
                qt = QTv[j // 2][:, j % 2]
                sct, qtt = SCt[j // 2], QTt[j // 2]
                S0 = Sst[c][spar[c]]
                S1 = Sst[c][1 - spar[c]]
                spar[c] = 1 - spar[c]
                P.op('pe', lambda e, n=n, S0=S0: e.matmul(psRHS[:], lhsT=Av[:, n, :], rhs=RV(S0), start=True, stop=False), [Abd, S0], [psRHS])
                P.op('pe', lambda e, sc=sc, qt=qt: e.matmul(psRHS[:], lhsT=sc[:, 1, :], rhs=qt[:, 3, :], start=False, stop=True), [sct, qtt], [psRHS])
                P.op('act', lambda e: e.activation(out=rhs_sb[:], in_=psRHS[:], func=AF.Copy), [psRHS], [rhs_sb])
                P.op('pe', lambda e, j=j: e.matmul(psU[:], lhsT=Xv[0][:, j, :], rhs=rhs_sb[:], start=True, stop=True), [Xt, rhs_sb], [psU])
                P.op('act', lambda e: e.activation(out=u_sb[:], in_=psU[:], func=AF.Copy), [psU], [u_sb])
                P.op('pe', lambda e, qt=qt: e.matmul(psSN[:], lhsT=qt[:, 1, :], rhs=u_sb[:], start=True, stop=False), [qtt, u_sb], [psSN])
                P.op('pe', lambda e, qt=qt: e.matmul(psSN[:], lhsT=qt[:, 2, :], rhs=qt[:, 3, :], start=False, stop=False), [qtt], [psSN])
                P.op('pe', lambda e, S0=S0: e.matmul(psSN[:], lhsT=identr[:], rhs=RV(S0), start=False, stop=True), [identr, S0], [psSN])
                P.op('act', lambda e, n=n, S1=S1: e.activation(out=RV(S1), in_=psSN[:], func=AF.Identity, scale=Gt[:, n * 64 + 63:n * 64 + 64]), [psSN, Gt], [S1])
                P.op('pe', lambda e, n=n, S0=S0: e.matmul(psYT[:], lhsT=RV(S0), rhs=Rv[:, n, :], start=True, stop=False), [S0, Rbd], [psYT])
                P.op('pe', lambda e, sc=sc: e.matmul(psYT[:], lhsT=u_sb[:], rhs=sc[:, 2, :], start=False, stop=False), [u_sb, sct], [psYT])
                P.op('pe', lambda e, sc=sc, qt=qt: e.matmul(psYT[:], lhsT=qt[:, 3, :], rhs=sc[:, 3, :], start=False, stop=True), [qtt, sct], [psYT])
                for hh in range(2):
                    hs_ = slice(hh * 64, hh * 64 + 64)
                    P.op('act', lambda e, hs_=hs_, n=n: e.activation(out=y32[hs_, n * 64:(n + 1) * 64], in_=psYT[hs_, hs_], func=AF.Copy), [psYT], [y32])
        dump('y_rwkv%d' % c, y32, y32[:])
        yr = cump
        yrv = RV(yr)
        P.op('act', lambda e: e.activation(out=yrv, in_=y32[:], func=AF.Copy), [y32], [yr])
        ps = nextps()
        P.op('pe', lambda e, ps=ps: e.matmul(ps[:], lhsT=bdones_r[:], rhs=yrv, start=True, stop=True), [bdones_r, yr], [ps])
        dd = Gt
        ddv = RV(dd)
        P.op('dve', lambda e, ps=ps: e.scalar_tensor_tensor(out=dd[:], in0=ps[:], scalar=-1.0 / 64, in1=y32[:], op0=MUL, op1=ADD), [ps, y32], [dd])
        sq = Ginv
        sqv = RV(sq)
        P.op('act', lambda e: e.activation(out=sqv, in_=dd[:], func=AF.Square), [dd], [sq])
        ps = nextps()
        P.op('pe', lambda e, ps=ps: e.matmul(ps[:], lhsT=bdones_r[:], rhs=sqv, start=True, stop=True), [bdones_r, sq], [ps])
        rs = Gp
        P.op('act', lambda e, ps=ps: e.activation(out=rs[:], in_=ps[:], func=AF.Ln, scale=1.0 / 64, bias=64e-5), [ps], [rs])
        P.op('act', lambda e: e.activation(out=rs[:], in_=rs[:], func=AF.Exp, scale=-0.5), [rs], [rs])
        P.op('dve', lambda e: e.tensor_tensor(out=dd[:], in0=dd[:], in1=rs[:], op=MUL), [dd, rs], [dd])
        P.op('act', lambda e: e.activation(out=dd[:], in_=dd[:], func=AF.Identity, scale=pcc('ln_w', c), bias=pcc('ln_b', c)), [dd, pc], [dd])
        P.op('pool', lambda e: e.tensor_tensor(out=bon[:], in0=bon[:], in1=v32[:], op=MUL), [bon, v32], [bon])
        P.op('pool', lambda e: e.tensor_tensor(out=dd[:], in0=dd[:], in1=bon[:], op=ADD), [dd, bon], [dd])
        P.op('dve', lambda e: e.tensor_tensor(out=merged[:, c, :], in0=dd[:], in1=gr32[:], op=MUL), [dd, gr32], [merged])

    if do_l0 and do_mlstm:
        causal = cload('causal')
        sel = P.sb("c_sel", [4, 4, 128])
        P.dma('sp', sel[:], din['sel'].ap(), writes=[sel])
        onesf = P.sb("onesf", [128, TT])
        P.op('pool', lambda e: e.memset(onesf[:], 1.0), [], [onesf])
        ones_r = P.sb("ones_r", [128, 128], SDT)
        P.op('dve', lambda e: e.tensor_copy(out=ones_r[:], in_=onesf[:, 0:128]), [onesf], [ones_r])
        m_tail = P.sb("m_tail", [128, 8, 3])
        P.op('pool', lambda e: e.memset(m_tail[:], 0.0), [], [m_tail])
        CT32 = P.sb("CT32", [128, 4, 2, 256])
        CTbf = P.sb("CTbf", [128, 4, 2, 256], BF16)
        nr32 = P.sb("nr32", [128, 4, 2, 128])
        nrbf = P.sb("nrbf", [128, 4, 2, 128], BF16)
        for t_ in (CT32, CTbf, nr32, nrbf):
            P.op('pool', lambda e, t_=t_: e.memset(t_[:], 0.0), [], [t_])
        mcar = P.sb("mcar", [4, 2])
        P.op('pool', lambda e: e.memset(mcar[:], 0.0), [], [mcar])
        MxE = P.sb("MxE", [4, 5])
        dec = P.sb("dec", [4, 4])
        smallb = P.sb("smallb", [128, 32])
        qe = P.sb("qe", [128, 2, 128], BF16)
        qw = P.sb("qw", [128, 2, 128], BF16)
        kg = P.sb("kg", [128, 2, 128], BF16)
        kgt = P.sb("kgt", [128, 256], BF16)
        st_bf = P.sb("st_bf", [128, 128], BF16)
        ddm = P.sb("ddm", [128, 128])
        recm = P.sb("recm", [128, 128])

    def mlstm(t0):
        qTt, kTt, vTt = [H[0], H[1]], [H[2], H[3]], [H[4], H[5]]
        ktt, vtt = [H[6], H[7]], [H[8], H[9]]
        xct = [H[10], H[11]]

        def fm(tl, c):
            return bfv(tl[c // 4]).rearrange("p (k t) -> p k t", k=4)[:, c % 4, :]

        def tokv(tl, tb):
            return bfv(tl[tb // 2]).rearrange("p (b d) -> p b d", b=2)[:, tb % 2, :]
        acc, xmb = G[9], G[10]
        xmb_ap = bfv(xmb)[:, 0:TT]
        gi_ps, gf_ps = PS[2], PS[3]
        for c in range(8):
            ps = proj_x('w0in', 25 + c)
            conv4(ps, c, m_tail, 'mcw', 'mcb', acc)
            P.op('act', lambda e, c=c: e.activation(out=fm(xct, c), in_=acc[:], func=AF.Silu), [acc], [xct[c // 4]])
            P.op('pool', lambda e: e.tensor_copy(out=xmb_ap, in_=ext[:, 3:TT + 3]), [ext], [xmb])
            for t_, (dst, src_ap, src_tb) in enumerate(((qTt, fm(xct, c), xct[c // 4]), (kTt, fm(xct, c), xct[c // 4]), (vTt, xmb_ap, xmb))):
                ps = nextps()
                P.op('pe', lambda e, ps=ps, t_=t_, src_ap=src_ap: e.matmul(ps[:], lhsT=wqkv_bf[:, t_, c, :], rhs=src_ap, start=True, stop=True),
                     [wqkv_bf, src_tb], [ps])
                P.op('act' if t_ != 1 else 'dve',
                     (lambda e, ps=ps, dst=dst: e.activation(out=fm(dst, c), in_=ps[:], func=AF.Copy)) if t_ != 1 else
                     (lambda e, ps=ps, dst=dst: e.tensor_copy(out=fm(dst, c), in_=ps[:])), [ps], [dst[c // 4]])
                j = t_ * 8 + c
                first, last = (c == 0 and t_ == 0), (c == 7 and t_ == 2)
                P.op('pe', lambda e, j=j, dst=dst, first=first, last=last: e.matmul(gi_ps[0:4, :], lhsT=wif_bf[:, 0, j, :], rhs=fm(dst, c), start=first, stop=last),
                     [wif_bf, dst[c // 4]], [gi_ps])
                P.op('pe', lambda e, j=j, dst=dst, first=first, last=last: e.matmul(gf_ps[0:4, :], lhsT=wif_bf[:, 1, j, :], rhs=fm(dst, c), start=first, stop=last),
                     [wif_bf, dst[c // 4]], [gf_ps])
            for t_, (dstl, src_ap, src_tb) in ((1, (ktt, fm(xct, c), xct[c // 4])), (2, (vtt, xmb_ap, xmb))):
                ps = nextps()
                for tb in range(NT):
                    P.op('pe', lambda e, ps=ps, tb=tb, t_=t_, src_ap=src_ap: e.matmul(ps[:, tb * 128:(tb + 1) * 128], lhsT=src_ap[:, tb * 128:(tb + 1) * 128], rhs=wqkv_bf[:, t_, c, :],
                                                                                      start=True, stop=True), [wqkv_bf, src_tb], [ps])
                for half in range(2):
                    dv = bfv(dstl[half]).rearrange("p (b d) -> p b d", b=2)[:, :, c * 128:(c + 1) * 128]
                    P.op('act' if half == 0 else 'dve',
                         (lambda e, ps=ps, dv=dv, half=half: e.activation(out=dv, in_=ps[:, half * 256:(half + 1) * 256].rearrange("p (b d) -> p b d", b=2), func=AF.Copy)) if half == 0 else
                         (lambda e, ps=ps, dv=dv, half=half: e.tensor_copy(out=dv, in_=ps[:, half * 256:(half + 1) * 256].rearrange("p (b d) -> p b d", b=2))),
                         [ps], [dstl[half]])
        rg_, re2, rwi, rem, rli, rFp, rag, rMx, rtmp = G[11], G[12], G[13], G[14], G[15], G[16], G[17], G[18], G[19]
        R4 = slice(0, 4)
        P.op('act', lambda e: e.activation(out=rli[R4, :], in_=gi_ps[R4, :], func=AF.Identity, bias=pc[R4, PC['bif_i']:PC['bif_i'] + 1]), [gi_ps, pc], [rli])
        P.op('act', lambda e: e.activation(out=rtmp[R4, :], in_=gf_ps[R4, :], func=AF.Exp, scale=-1.0, bias=pd[R4, PD['negbf']:PD['negbf'] + 1]), [gf_ps, pd], [rtmp])
        P.op('act', lambda e: e.activation(out=rtmp[R4, :], in_=rtmp[R4, :], func=AF.Ln, bias=1.0), [rtmp], [rtmp])
        P.op('dve', lambda e: e.tensor_tensor_scan(out=rFp[R4, :], data0=onesf[R4, :], data1=rtmp[R4, :], initial=mcar[:, 0:1], op0=MUL, op1=ADD),
             [onesf, rtmp, mcar], [rFp])
        P.op('pool', lambda e: e.tensor_tensor(out=rag[R4, :], in0=rli[R4, :], in1=rFp[R4, :], op=ADD), [rli, rFp], [rag])
        P.op('dve', lambda e: e.tensor_tensor_scan(out=rMx[R4, :], data0=rag[R4, :], data1=rag[R4, :], initial=mcar[:, 1:2], op0=MAX, op1=MAX),
             [rag, mcar], [rMx])
        P.op('pool', lambda e: e.tensor_copy(out=MxE[:, 0:1], in_=mcar[:, 1:2]), [mcar], [MxE])
        P.op('pool', lambda e: e.tensor_copy(out=MxE[:, 1:5], in_=rMx[R4, 127:TT:128]), [rMx], [MxE])
        P.op('pool', lambda e: e.tensor_copy(out=mcar[:, 0:1], in_=rFp[R4, TT - 1:TT]), [rFp], [mcar])
        P.op('pool', lambda e: e.tensor_copy(out=mcar[:, 1:2], in_=rMx[R4, TT - 1:TT]), [rMx], [mcar])
        v3 = lambda t_: t_[R4, :].rearrange("p (q t) -> p q t", q=NT)
        mend = MxE[:, 1:5].unsqueeze(2).broadcast_to([4, NT, 128])
        mprev = MxE[:, 0:4].unsqueeze(2).broadcast_to([4, NT, 128])
        P.op('dve', lambda e: e.tensor_tensor(out=v3(rg_), in0=v3(rag), in1=mend, op=SUB), [rag, MxE], [rg_])
        P.op('act', lambda e: e.activation(out=rg_[R4, :], in_=rg_[R4, :], func=AF.Exp), [rg_], [rg_])
        P.op('dve', lambda e: e.tensor_tensor(out=v3(re2), in0=mend, in1=v3(rMx), op=SUB), [rMx, MxE], [re2])
        P.op('act', lambda e: e.activation(out=re2[R4, :], in_=re2[R4, :], func=AF.Exp), [re2], [re2])
        P.op('dve', lambda e: e.tensor_tensor(out=v3(rwi), in0=mprev, in1=v3(rMx), op=SUB), [rMx, MxE], [rwi])
        P.op('act', lambda e: e.activation(out=rwi[R4, :], in_=rwi[R4, :], func=AF.Exp), [rwi], [rwi])
        P.op('dve', lambda e: e.tensor_tensor(out=rem[R4, :], in0=rFp[R4, :], in1=rMx[R4, :], op=SUB), [rFp, rMx], [rem])
        P.op('act', lambda e: e.activation(out=rem[R4, :], in_=rem[R4, :], func=AF.Exp), [rem], [rem])
        P.op('dve', lambda e: e.tensor_tensor(out=dec[:], in0=MxE[:, 0:4], in1=MxE[:, 1:5], op=SUB), [MxE], [dec])
        P.op('act', lambda e: e.activation(out=dec[:], in_=dec[:], func=AF.Exp), [dec], [dec])
        sp_ = PS[5]
        for q in range(NT):
            P.op('pe', lambda e, q=q: e.matmul(sp_[:, q * 4:(q + 1) * 4], lhsT=rg_[R4, q * 128:(q + 1) * 128], rhs=ident[0:4, 0:4], start=True, stop=True),
                 [rg_, ident], [sp_])
        for h in range(4):
            P.op('pe', lambda e, h=h: e.matmul(sp_[:, 16 + h * 4:16 + (h + 1) * 4], lhsT=sel[:, h, :], rhs=dec[:], start=True, stop=True), [sel, dec], [sp_])
        P.op('act', lambda e: e.activation(out=smallb[:], in_=sp_[:, 0:32], func=AF.Copy), [sp_], [smallb])
        for h in range(4):
            mlstm_head(h, qTt, kTt, ktt, vtt, xct, fm, tokv, (rg_, re2, rwi, rem))

    def mlstm_head(h, qTt, kTt, ktt, vtt, xct, fm, tokv, rows):
        bc = G[0:4]
        for i in range(4):
            ps = nextps()
            P.op('pe', lambda e, ps=ps, i=i: e.matmul(ps[:], lhsT=sel[:, h, :], rhs=rows[i][0:4, :], start=True, stop=True), [sel, rows[i]], [ps])
            P.op('act' if i % 2 == 0 else 'dve',
                 (lambda e, ps=ps, i=i: e.activation(out=bc[i][:], in_=ps[:], func=AF.Copy)) if i % 2 == 0 else
                 (lambda e, ps=ps, i=i: e.tensor_copy(out=bc[i][:], in_=ps[:])), [ps], [bc[i]])
        g_bc, e2_bc, wi_bc, em_bc = bc
        h32 = [G[4], G[5]]
        gms = [G[6], G[7]]
        for vc in range(2):
            ps = proj_x('w0in', 41 + 2 * h + vc)
            P.op('act', lambda e, ps=ps, vc=vc: e.activation(out=gms[vc][:], in_=ps[:], func=AF.Silu), [ps], [gms[vc]])
        qt_ = qTt[h // 2]
        kt_ = kTt[h // 2]
        qv = bfv(qt_).rearrange("p (k t) -> p k t", k=4)[:, 2 * (h % 2):2 * (h % 2) + 2, :]
        kv = bfv(kt_).rearrange("p (k t) -> p k t", k=4)[:, 2 * (h % 2):2 * (h % 2) + 2, :]
        CB, SB_, NB = PS[6], PS[7], PS[5]
        for q in range(NT):
            ts_ = slice(q * 128, (q + 1) * 128)
            bcast = lambda t_: t_[:, ts_].unsqueeze(1).broadcast_to([128, 2, 128])
            P.op('dve', lambda e: e.tensor_tensor(out=qe[:], in0=qv[:, :, ts_], in1=bcast(e2_bc), op=MUL), [qt_, e2_bc], [qe])
            P.op('pool', lambda e: e.tensor_tensor(out=qw[:], in0=qv[:, :, ts_], in1=bcast(wi_bc), op=MUL), [qt_, wi_bc], [qw])
            P.op('dve', lambda e: e.scalar_tensor_tensor(out=kg[:], in0=kv[:, :, ts_], scalar=0.0625, in1=bcast(g_bc), op0=MUL, op1=MUL), [kt_, g_bc], [kg])
            ktok = tokv(ktt, q)[:, h * 256:(h + 1) * 256]
            vtok = tokv(vtt, q)[:, h * 256:(h + 1) * 256]
            P.op('pool', lambda e: e.tensor_scalar(out=kgt[:], in0=ktok, scalar1=smallb[:, q * 4 + h:q * 4 + h + 1], scalar2=0.0625, op0=MUL, op1=MUL),
                 [ktt[q // 2], smallb], [kgt])
            for kc in range(2):
                P.op('pe', lambda e, kc=kc: e.matmul(CB[:, 0:128], lhsT=kg[:, kc, :], rhs=qe[:, kc, :], start=(kc == 0), stop=(kc == 1)), [kg, qe], [CB])
            P.op('dve', lambda e: e.tensor_tensor(out=st_bf[:], in0=CB[:, 0:128], in1=causal[:], op=MUL), [CB, causal], [st_bf])
            for vc in range(2):
                o_ = CB[:, 128 + vc * 128:256 + vc * 128]
                P.op('pe', lambda e, o_=o_, vc=vc: e.matmul(o_, lhsT=vtok[:, vc * 128:(vc + 1) * 128], rhs=st_bf[:], start=True, stop=False), [vtt[q // 2], st_bf], [CB])
                for kc in range(2):
                    P.op('pe', lambda e, o_=o_, vc=vc, kc=kc: e.matmul(o_, lhsT=CTbf[:, h, kc, vc * 128:(vc + 1) * 128], rhs=qw[:, kc, :], start=False, stop=(kc == 1)),
                         [CTbf, qw], [CB])
            o_ = CB[:, 384:512]
            P.op('pe', lambda e, o_=o_: e.matmul(o_, lhsT=ones_bf[:], rhs=st_bf[:], start=True, stop=False), [ones_bf, st_bf], [CB])
            for kc in range(2):
                P.op('pe', lambda e, o_=o_, kc=kc: e.matmul(o_, lhsT=nrbf[:, h, kc, :], rhs=qw[:, kc, :], start=False, stop=(kc == 1)), [nrbf, qw], [CB])
            P.op('act', lambda e: e.activation(out=ddm[:], in_=CB[:, 384:512], func=AF.Abs), [CB], [ddm])
            P.op('dve', lambda e: e.tensor_tensor(out=ddm[:], in0=ddm[:], in1=em_bc[:, ts_], op=MAX), [ddm, em_bc], [ddm])
            P.op('dve', lambda e: e.tensor_scalar(out=ddm[:], in0=ddm[:], scalar1=1e-6, scalar2=None, op0=ADD), [ddm], [ddm])
            P.op('dve', lambda e: e.reciprocal(out=recm[:], in_=ddm[:]), [ddm], [recm])
            for vc in range(2):
                P.op('dve', lambda e, vc=vc: e.tensor_tensor(out=h32[vc][:, ts_], in0=CB[:, 128 + vc * 128:256 + vc * 128], in1=recm[:], op=MUL), [CB, recm], [h32[vc]])
            for kc in range(2):
                P.op('pe', lambda e, kc=kc: e.matmul(SB_[:, kc * 256:(kc + 1) * 256], lhsT=kgt[:, kc * 128:(kc + 1) * 128], rhs=vtok, start=True, stop=True),
                     [kgt, vtt[q // 2]], [SB_])
                P.op('pe', lambda e, kc=kc: e.matmul(NB[:, 64 + kc * 128:64 + (kc + 1) * 128], lhsT=kgt[:, kc * 128:(kc + 1) * 128], rhs=ones_bf[:], start=True, stop=True),
                     [kgt, ones_bf], [NB])
            dcol = smallb[:, 16 + h * 4 + q:16 + h * 4 + q + 1]
            P.op('dve', lambda e: e.scalar_tensor_tensor(out=CT32[:, h], in0=CT32[:, h], scalar=dcol, in1=SB_[:].rearrange("p (k v) -> p k v", k=2), op0=MUL, op1=ADD),
                 [CT32, smallb, SB_], [CT32])
            P.op('act', lambda e: e.activation(out=CTbf[:, h], in_=CT32[:, h], func=AF.Copy), [CT32], [CTbf])
            P.op('dve', lambda e: e.scalar_tensor_tensor(out=nr32[:, h], in0=nr32[:, h], scalar=dcol, in1=NB[:, 64:320].rearrange("p (k v) -> p k v", k=2), op0=MUL, op1=ADD),
                 [nr32, smallb, NB], [nr32])
            P.op('pool', lambda e: e.tensor_copy(out=nrbf[:, h], in_=nr32[:, h]), [nr32], [nrbf])
        dump('h_m%d' % h, h32[0], h32[0][:])
        hr = [G[8], G[9]]
        ps = nextps()
        for vc in range(2):
            P.op('act', lambda e, vc=vc: e.activation(out=RV(hr[vc]), in_=h32[vc][:], func=AF.Copy), [h32[vc]], [hr[vc]])
            P.op('pe', lambda e, ps=ps, vc=vc: e.matmul(ps[:], lhsT=ones_r[:], rhs=RV(hr[vc]), start=(vc == 0), stop=(vc == 1)), [ones_r, hr[vc]], [ps])
        for vc in range(2):
            P.op('dve', lambda e, ps=ps, vc=vc: e.scalar_tensor_tensor(out=h32[vc][:], in0=ps[:], scalar=-1.0 / 256, in1=h32[vc][:], op0=MUL, op1=ADD), [ps, h32[vc]], [h32[vc]])
        ps2 = nextps()
        for vc in range(2):
            P.op('act', lambda e, vc=vc: e.activation(out=RV(hr[vc]), in_=h32[vc][:], func=AF.Square), [h32[vc]], [hr[vc]])
            P.op('pe', lambda e, ps2=ps2, vc=vc: e.matmul(ps2[:], lhsT=ones_r[:], rhs=RV(hr[vc]), start=(vc == 0), stop=(vc == 1)), [ones_r, hr[vc]], [ps2])
        rs = G[10]
        P.op('act', lambda e, ps2=ps2: e.activation(out=rs[:], in_=ps2[:], func=AF.Ln, scale=1.0 / 256, bias=1e-5), [ps2], [rs])
        P.op('act', lambda e: e.activation(out=rs[:], in_=rs[:], func=AF.Exp, scale=-0.5), [rs], [rs])
        for vc in range(2):
            ch = 2 * h + vc
            P.op('dve', lambda e, vc=vc: e.tensor_tensor(out=h32[vc][:], in0=h32[vc][:], in1=rs[:], op=MUL), [h32[vc], rs], [h32[vc]])
            P.op('act', lambda e, vc=vc, ch=ch: e.activation(out=h32[vc][:], in_=h32[vc][:], func=AF.Identity, scale=pcc('mnorm', ch)), [h32[vc], pc], [h32[vc]])
            P.op('dve', lambda e, vc=vc, ch=ch: e.scalar_tensor_tensor(out=h32[vc][:], in0=fm(xct, ch), scalar=pcc('mskip', ch), in1=h32[vc][:], op0=MUL, op1=ADD),
                 [xct[ch // 4], pc, h32[vc]], [h32[vc]])
            P.op('pool', lambda e, vc=vc, ch=ch: e.tensor_tensor(out=merged[:, 8 + ch, :], in0=h32[vc][:], in1=gms[vc][:], op=MUL), [h32[vc], gms[vc]], [merged])

    def layer0(t0):
        rmsnorm_x('mixn0')
        pstate['set'] = [0, 1]
        if do_rwkv:
            rwkv(t0)
        if do_mlstm:
            mlstm(t0)
        pstate['set'] = list(range(8))
        out_proj('w0out')

    if do_l0 and not (do_rwkv and do_mlstm):
        P.op('pool', lambda e: e.memset(merged[:], 0.0), [], [merged])

    for ti in range(ntiles):
        t0 = ti * TT
        for tb in range(NT):
            P.dma('sp', H[tb][:], x_d.ap()[t0 + tb * 128:t0 + (tb + 1) * 128, :], writes=[H[tb]])
        for kc in range(8):
            ps = nextps()
            for tb in range(NT):
                P.op('pe', lambda e, kc=kc, tb=tb, ps=ps: e.transpose(ps[:, tb * 128:(tb + 1) * 128], H[tb][:, kc * 128:(kc + 1) * 128], ident[:]),
                     [H[tb], ident], [ps])
            P.op('act' if kc % 2 == 0 else 'dve',
                 (lambda e, kc=kc, ps=ps: e.activation(out=hT[:, kc, :], in_=ps[:], func=AF.Copy)) if kc % 2 == 0 else
                 (lambda e, kc=kc, ps=ps: e.tensor_copy(out=hT[:, kc, :], in_=ps[:])), [ps], [hT])
        if do_l0:
            layer0(t0)
            ple(0, t0)
        if do_l1:
            layer1(t0)
            ple(1, t0)
        norm_stats()
        for k in range(8):
            of = G[k % 4]
            norm_apply(k, 'finn', of[:], of)
            ps = nextps()
            for tb in range(NT):
                P.op('pe', lambda e, tb=tb, ps=ps, of=of: e.transpose(ps[:, tb * 128:(tb + 1) * 128], of[:, tb * 128:(tb + 1) * 128], ident[:]),
                     [of, ident], [ps])
            for tb in range(NT):
                P.op('act' if k % 2 == 0 else 'dve',
                     (lambda e, k=k, ps=ps, tb=tb: e.activation(out=H[tb][:, k * 128:(k + 1) * 128], in_=ps[:, tb * 128:(tb + 1) * 128], func=AF.Copy)) if k % 2 == 0 else
                     (lambda e, k=k, ps=ps, tb=tb: e.tensor_copy(out=H[tb][:, k * 128:(k + 1) * 128], in_=ps[:, tb * 128:(tb + 1) * 128])),
                     [ps], [H[tb]])
        for tb in range(NT):
            P.dma('sp', o_d.ap()[t0 + tb * 128:t0 + (tb + 1) * 128, :], H[tb][:], reads=[H[tb]])
    P.finish()
    print("sbuf bytes/partition:", P.sbytes, "sems:", P.nsem, "instr:", {e: P.cnt[e] for e in ENGS})
    return nc


def kernel(**inputs):
    x = np.asarray(inputs['x'], np.float32)
    p = np.asarray(inputs['p'], np.float32)
    B, S, _ = x.shape
    shared = host_prep(inputs)
    nc = build_program(S)
    in_maps = []
    for b in range(B):
        m = dict(shared)
        m['x'] = np.ascontiguousarray(x[b])
        m['p'] = np.ascontiguousarray(p[:, b])
        in_maps.append(m)
    res = run_bass_kernel_spmd(nc, in_maps, core_ids=list(range(B)))
    return np.stack([np.asarray(r['out'], np.float32) for r in res.results], axis=0)
```

```python
import numpy as np
import concourse.bass as bass
import concourse.mybir as mybir
from concourse.bass_utils import run_bass_kernel_spmd
from contextlib import ExitStack

F32 = mybir.dt.float32
BF16 = mybir.dt.bfloat16
F32R = mybir.dt.float32r
SDT = F32R
AF = mybir.ActivationFunctionType
ALU = mybir.AluOpType
AX = mybir.AxisListType

ENGS = ('pe', 'act', 'dve', 'pool', 'sp')
EIDX = {e: i for i, e in enumerate(ENGS)}
EPOCH = 30000

D = 1024
TT = 512
NT = TT // 128
PLE = 256
DECAY_SCALE = 0.6065306597126334


class TB:
    __slots__ = ('name', 'h', 'lw', 'rd', 'dkey', 'root', 'psum')

    def __init__(self, name, h, root=None, psum=False):
        self.name = name
        self.h = h
        self.lw = None
        self.rd = {}
        self.dkey = None
        self.root = root if root is not None else self
        self.psum = psum or (root is not None and root.psum)

    def __getitem__(self, k):
        return self.h[k]


class _Rec:
    def __init__(self):
        self.call = None

    def __getattr__(self, name):
        def f(*a, **k):
            self.call = (name, a, k)
        return f


class Prog:
    def __init__(self, nc):
        self.nc = nc
        self.es = ExitStack()
        self.ops = {e: [] for e in ENGS}
        self.cnt = {e: 0 for e in ENGS}
        self.seen = {e: {} for e in ENGS}
        self.clk = {e: [] for e in ENGS}
        self.dcnt = {}
        self.dclk = {}
        self.sems = {}
        self.nsem = 0
        self.ndma = 0
        self.sbytes = 0

    def sb(self, name, shape, dt=F32):
        n = 1
        for s in shape[1:]:
            n *= s
        self.sbytes += n * (2 if dt == BF16 else 4)
        return TB(name, self.es.enter_context(self.nc.sbuf_tensor(name, list(shape), dt)))

    def ps(self, name, shape, dt=F32):
        return TB(name, self.es.enter_context(self.nc.psum_tensor(name, list(shape), dt)), psum=True)

    def dram(self, name, shape, dt, kind):
        return TB(name, self.nc.dram_tensor(name, list(shape), dt, kind=kind))

    def _sem(self, key):
        s = self.sems.get(key)
        if s is None:
            s = self.es.enter_context(self.nc.semaphore("s%d" % self.nsem))
            self.nsem += 1
            self.sems[key] = s
        return s

    def _semval(self, key, count):
        if key in EIDX:
            ep = (count - 1) // EPOCH
            return self._sem((key, ep)), count - ep * EPOCH
        return self._sem(key), 16 * count

    def _deps(self, e, reads, writes):
        seen = self.seen[e]
        need = {}

        def req(ev):
            if ev is None:
                return
            k, c = ev
            if k == 'pe' and e == 'pe':
                return
            if seen.get(k, 0) >= c:
                return
            if need.get(k, 0) < c:
                need[k] = c
        for t in reads:
            req(t.lw)
        for t in writes:
            req(t.lw)
            for k, c in t.rd.items():
                req((k, c))
        keys = list(need.keys())
        for k in keys:
            if k not in need:
                continue
            c = need[k]
            ck = self.clk[k][c - 1] if k in EIDX else self.dclk[(k, c)]
            for k2 in keys:
                if k2 != k and k2 in EIDX and k2 in need and ck[EIDX[k2]] >= need[k2]:
                    del need[k2]
        waits = []
        for k, c in need.items():
            waits.append(self._semval(k, c))
            ck = self.clk[k][c - 1] if k in EIDX else self.dclk[(k, c)]
            for e2, v in zip(ENGS, ck):
                if seen.get(e2, 0) < v:
                    seen[e2] = v
            if seen.get(k, 0) < c:
                seen[k] = c
        return waits

    def _snapshot(self, e):
        s = self.seen[e]
        return tuple(s.get(x, 0) for x in ENGS)

    def op(self, e, fn, reads=(), writes=()):
        writes = [t.root for t in writes] + [t.root for t in reads if t.psum]
        reads = [t.root for t in reads if not t.psum]
        rec = _Rec()
        fn(rec)
        fn = rec.call
        waits = self._deps(e, reads, writes)
        self.cnt[e] += 1
        c = self.cnt[e]
        snap = list(self._snapshot(e))
        snap[EIDX[e]] = c
        self.clk[e].append(tuple(snap))
        inc = self._semval(e, c)[0]
        self.ops[e].append((waits, fn, inc, 1))
        ev = (e, c)
        for t in reads:
            if t.rd.get(e, 0) < c:
                t.rd[e] = c
        for t in writes:
            t.lw = ev
            t.rd = {}
        return ev

    def dma(self, q, out, in_, reads=(), writes=(), key=None, **kw):
        if key is None:
            t0 = (list(writes) + list(reads))[0]
            if t0.dkey is None:
                t0.dkey = ('dma', self.ndma)
                self.ndma += 1
            key = t0.dkey
        reads = [t.root for t in reads]
        writes = [t.root for t in writes]
        waits = self._deps(q, reads, writes)
        self.dcnt[key] = self.dcnt.get(key, 0) + 1
        c = self.dcnt[key]
        self.dclk[(key, c)] = self._snapshot(q)
        sem = self._sem(key)
        self.ops[q].append((waits, ('dma_start', (), dict(out=out, in_=in_, **kw)), sem, 16))
        ev = (key, c)
        for t in reads:
            if t.rd.get(key, 0) < c:
                t.rd[key] = c
        for t in writes:
            t.lw = ev
            t.rd = {}
        return ev

    def finish(self):
        e = 'sp'
        tail = []
        for key, c in self.dcnt.items():
            tail.append(self._semval(key, c))
        for x in ENGS:
            if x != e and self.cnt[x] > 0:
                tail.append(self._semval(x, self.cnt[x]))
        blk = self.es.enter_context(self.nc.Block())
        engobj = {'pe': blk.tensor, 'act': blk.scalar, 'dve': blk.vector, 'pool': blk.gpsimd, 'sp': blk.sync}

        def make(ename):
            ops = self.ops[ename]

            def body(eng):
                for waits, fn, inc, n in ops:
                    for (s, v) in waits[1:]:
                        eng.wait_ge(s, v)
                    ins = getattr(eng, fn[0])(*fn[1], **fn[2])
                    if waits:
                        ins._wait_ge(waits[0][0], waits[0][1])
                    ins.then_inc(inc, n)
                if ename == e:
                    for (s, v) in tail:
                        eng.wait_ge(s, v)
            return body
        for ename in ENGS:
            if self.ops[ename] or ename == e:
                engobj[ename](make(ename))
        self.es.close()


PC_SPEC = [
    ('mixn0', 8), ('mixn1', 8), ('pen0', 8), ('pen1', 8), ('finn', 8),
    ('mu_r', 8), ('mu_k', 8), ('mu_v', 8), ('mu_l', 1),
    ('w0', 8), ('a0', 8), ('k_k', 8), ('k_a', 8), ('r_k', 8), ('ln_w', 8), ('ln_b', 8),
    ('mcw0', 8), ('mcw1', 8), ('mcw2', 8), ('mcw3', 8), ('mcb', 8), ('mnorm', 8), ('mskip', 8),
    ('bif_i', 1), ('bif_f', 1),
    ('ccw0', 16), ('ccw1', 16), ('ccw2', 16), ('ccw3', 16), ('ccb', 16), ('cbr', 16), ('cbi', 16), ('clam', 16),
]
PC = {}
_o = 0
for _n, _w in PC_SPEC:
    PC[_n] = _o
    _o += _w
NPC = _o
PD_SPEC = [('omu_r', 8), ('omu_k', 8), ('omu_v', 8), ('omu_l', 1), ('negbf', 1), ('csph', 16), ('cneg', 16), ('t0', 16), ('t1', 16)]
PD = {}
_o = 0
for _n, _w in PD_SPEC:
    PD[_n] = _o
    _o += _w
NPD = _o


def _cols(v):
    v = np.asarray(v, np.float32).reshape(-1)
    return np.ascontiguousarray(v.reshape(-1, 128).T)


def host_consts():
    c = {}
    c['ident'] = np.eye(128, dtype=np.float32)
    bd = np.zeros((128, 128), np.float32)
    bd[:64, :64] = 1
    bd[64:, 64:] = 1
    c['bdones'] = bd
    j = np.arange(128)[:, None]
    i = np.arange(128)[None, :]
    su = ((j < i) & ((j // 64) == (i // 64))).astype(np.float32)
    ui = ((j <= i) & ((j // 64) == (i // 64))).astype(np.float32)
    sl = ((j > i) & ((j // 64) == (i // 64))).astype(np.float32)
    on = np.ones((128, 128), np.float32)
    c['maskA'] = np.stack([su, su, ui, ui], axis=1)
    c['maskB'] = sl
    rm = np.ones((128, TT), np.float32)
    rm[:, ::64] = 0
    c['resetm'] = rm
    c['causal'] = (j <= i).astype(np.float32)
    sel = np.zeros((4, 4, 128), np.float32)
    for h in range(4):
        sel[h, h, :] = 1
    c['sel'] = sel
    return c


def host_prep(inp):
    g = {}
    f = lambda a: np.ascontiguousarray(np.asarray(a, np.float32))

    def chunked(w):
        K, M = w.shape
        return f(w.reshape(K // 128, 128, M // 128, 128).transpose(2, 1, 0, 3))
    g['w0in'] = chunked(f(inp['ab_w_in'])[0])
    g['w0out'] = chunked(f(inp['ab_w_out'])[0])
    g['w1in'] = chunked(f(inp['c_w_in'])[0])
    g['w1out'] = chunked(f(inp['c_w_out'])[0])
    g['pegate'] = np.stack([chunked(f(inp['pe_gate'])[i]) for i in range(2)])
    g['peup'] = np.stack([chunked(f(inp['pe_up'])[i]) for i in range(2)])
    wr = f(inp['c_wr'])[0]
    wi = f(inp['c_wi'])[0]
    g['wri'] = f(np.stack([wr, wi], axis=2))
    lup = np.zeros((128, 2, 1024), np.float32)
    lup[:64, 0, :] = f(inp['rwkv_w_up'])[0]
    lup[64:, 1, :] = f(inp['rwkv_a_up'])[0]
    g['lup'] = lup
    wbd = np.zeros((128, 3, 8, 128), np.float32)
    for t, nm in enumerate(['mlstm_wq', 'mlstm_wk', 'mlstm_wv']):
        w = f(inp[nm])[0]
        for c in range(8):
            for gg in range(32):
                wbd[4 * gg:4 * gg + 4, t, c, 4 * gg:4 * gg + 4] = w[c * 32 + gg]
    g['wqkv'] = wbd
    wif = f(inp['mlstm_w_if'])[0]
    g['wif'] = f(wif.reshape(24, 128, 2, 4).transpose(1, 2, 0, 3))
    pc = np.zeros((128, NPC), np.float32)

    def put(name, v):
        cc = _cols(v)
        pc[:, PC[name]:PC[name] + cc.shape[1]] = cc
    put('mixn0', f(inp['mix_norm'])[0]); put('mixn1', f(inp['mix_norm'])[1])
    put('pen0', f(inp['pe_norm'])[0]); put('pen1', f(inp['pe_norm'])[1])
    put('finn', f(inp['final_norm']))
    mu = f(inp['rwkv_mu'])[0]
    put('mu_r', mu[0]); put('mu_k', mu[1]); put('mu_v', mu[2])
    ml = f(inp['rwkv_mu_lora'])[0]
    put('mu_l', np.concatenate([ml[0], ml[1]]))
    put('w0', f(inp['rwkv_w0'])[0]); put('a0', f(inp['rwkv_a0'])[0])
    put('k_k', f(inp['rwkv_k_k'])[0]); put('k_a', f(inp['rwkv_k_a'])[0])
    put('r_k', f(inp['rwkv_r_k'])[0].reshape(-1))
    put('ln_w', f(inp['rwkv_ln_w'])[0]); put('ln_b', f(inp['rwkv_ln_b'])[0])
    mcw = f(inp['mlstm_conv_w'])[0]
    for j in range(4):
        put('mcw%d' % j, mcw[j])
    put('mcb', f(inp['mlstm_conv_b'])[0])
    put('mnorm', f(inp['mlstm_norm'])[0]); put('mskip', f(inp['mlstm_skip'])[0])
    bif = f(inp['mlstm_b_if'])[0]
    pc[0:4, PC['bif_i']] = bif[0:4]
    pc[0:4, PC['bif_f']] = bif[4:8]
    ccw = f(inp['c_conv_w'])[0]
    for j in range(4):
        put('ccw%d' % j, ccw[j])
    put('ccb', f(inp['c_conv_b'])[0]); put('cbr', f(inp['c_br'])[0]); put('cbi', f(inp['c_bi'])[0])
    put('clam', f(inp['c_lambda'])[0])
    g['pc'] = pc
    g.update(host_consts())
    return g


SHARED_SHAPES = {
    'w0in': [49, 128, 8, 128], 'w0out': [8, 128, 16, 128], 'w1in': [32, 128, 8, 128], 'w1out': [8, 128, 16, 128],
    'pegate': [2, 8, 128, 8, 128], 'peup': [2, 8, 128, 2, 128], 'wri': [16, 128, 2, 128], 'lup': [128, 2, 1024],
    'wqkv': [128, 3, 8, 128], 'wif': [128, 2, 24, 4], 'pc': [128, NPC],
    'ident': [128, 128], 'bdones': [128, 128], 'maskA': [128, 4, 128], 'maskB': [128, 128],
    'resetm': [128, TT], 'causal': [128, 128], 'sel': [4, 4, 128],
}


def build_program(S, do_l0=True, do_rwkv=True, do_mlstm=True, do_l1=True, dbg=None):
    nc = bass.Bass("TRN2", target_bir_lowering=False)
    P = Prog(nc)
    ntiles = S // TT
    din = {}
    for k, shp in SHARED_SHAPES.items():
        din[k] = nc.dram_tensor(k, shp, F32, kind="ExternalInput")
    x_d = nc.dram_tensor("x", [S, D], F32, kind="ExternalInput")
    p_d = nc.dram_tensor("p", [2, S, PLE], F32, kind="ExternalInput")
    o_d = nc.dram_tensor("out", [S, D], F32, kind="ExternalOutput")
    dbg_d = None
    dbg = dbg or []
    if dbg:
        dbg_d = nc.dram_tensor("dbg", [len(dbg), 128, TT], F32, kind="ExternalOutput")

    def dump(name, tb, ap):
        if name in dbg:
            P.dma('sp', dbg_d.ap()[dbg.index(name)], ap, reads=[tb], key=('dma', 'dbg'))

    MUL, ADD, SUB, MAX = ALU.mult, ALU.add, ALU.subtract, ALU.max

    def cload(name, dt=F32, rows=128):
        shp = SHARED_SHAPES[name]
        t = P.sb("c_" + name, shp, dt)
        P.dma('pool' if dt != F32 else 'sp', t[:], din[name].ap(), writes=[t])
        return t
    ident = cload('ident')
    pc = cload('pc')
    pd = P.sb("pd", [128, NPD])
    ones_bf = P.sb("ones_bf", [128, 128], BF16)
    P.op('pool', lambda e: e.memset(ones_bf[:], 1.0), [], [ones_bf])

    def pcc(name, c=0, n=1):
        return pc[:, PC[name] + c:PC[name] + c + n]

    def pdc(name, c=0, n=1):
        return pd[:, PD[name] + c:PD[name] + c + n]

    if do_l1:
        P.op('act', lambda e: e.activation(out=pdc('t0', 0, 16), in_=pcc('clam', 0, 16), func=AF.Exp, scale=-1.0), [pc], [pd])
        P.op('act', lambda e: e.activation(out=pdc('t1', 0, 16), in_=pdc('t0', 0, 16), func=AF.Ln, bias=1.0), [pd], [pd])
        P.op('dve', lambda e: e.tensor_scalar(out=pdc('csph', 0, 16), in0=pdc('t1', 0, 16), scalar1=4.0, scalar2=None, op0=MUL), [pd], [pd])
        P.op('dve', lambda e: e.tensor_scalar(out=pdc('cneg', 0, 16), in0=pdc('t1', 0, 16), scalar1=-8.0, scalar2=None, op0=MUL), [pd], [pd])
    if do_l0:
        for nm, w in (('r', 8), ('k', 8), ('v', 8), ('l', 1)):
            P.op('dve', lambda e, nm=nm, w=w: e.tensor_scalar(out=pdc('omu_' + nm, 0, w), in0=pcc('mu_' + nm, 0, w), scalar1=-1.0, scalar2=1.0, op0=MUL, op1=ADD),
                 [pc], [pd])
        P.op('dve', lambda e: e.tensor_scalar(out=pdc('negbf'), in0=pcc('bif_f'), scalar1=-1.0, scalar2=None, op0=MUL), [pc], [pd])

    scr = {}

    def mkscr(name, src_ap, shape, group=8):
        t = P.dram("scr_" + name, shape, BF16, "Internal")
        nm = shape[0]
        for j0 in range(0, nm, group):
            j1 = min(nm, j0 + group)
            P.dma('pool', t[j0:j1].rearrange("j p k m -> p j k m"), src_ap[j0:j1].rearrange("j p k m -> p j k m"),
                  writes=[t], key=('dma', 'scr_' + name))
        scr[name] = t
        return t
    if do_l0:
        lup_bf = cload('lup', BF16)
        wqkv_bf = cload('wqkv', BF16)
        wif_bf = cload('wif', BF16)
        mkscr('w0in', din['w0in'].ap(), [49, 128, 8, 128])
        mkscr('w0out', din['w0out'].ap(), [8, 128, 16, 128], group=4)
    for i in range(2):
        if (i == 0 and do_l0) or (i == 1 and do_l1):
            mkscr('pegate%d' % i, din['pegate'].ap()[i], [8, 128, 8, 128])
            mkscr('peup%d' % i, din['peup'].ap()[i], [8, 128, 2, 128])
    if do_l1:
        mkscr('w1in', din['w1in'].ap(), [32, 128, 8, 128])
        mkscr('wri', din['wri'].ap(), [16, 128, 2, 128], group=16)
        mkscr('w1out', din['w1out'].ap(), [8, 128, 16, 128], group=4)

    NWB = 5
    wring = [P.sb("wb%d" % i, [128, 16, 128], BF16) for i in range(NWB)]
    wstate = {'i': 0}

    def wld(name, j, nk):
        wb = wring[wstate['i'] % NWB]
        wstate['i'] += 1
        P.dma('sp', wb[:, 0:nk, :], scr[name][j], reads=[scr[name]], writes=[wb])
        return wb

    PS = [P.ps("ps%d" % i, [128, TT]) for i in range(8)]
    pstate = {'i': 0}

    pstate['set'] = list(range(8))

    def nextps():
        s_ = pstate['set']
        b = PS[s_[pstate['i'] % len(s_)]]
        pstate['i'] += 1
        return b

    hT = P.sb("hT", [128, 8, TT])
    xnT = P.sb("xnT", [128, 8, TT], BF16)
    merged = P.sb("merged", [128, 16, TT], BF16)
    ext = P.sb("ext", [128, TT + 3])
    NG, NH = 20, 12
    G = [P.sb("g%d" % i, [128, TT]) for i in range(NG)]
    H = [P.sb("h%d" % i, [128, 2 * TT]) for i in range(NH)]
    lnv, rstd = G[19], G[18]
    _alias = {}

    def RV(tb):
        if SDT == F32:
            return tb[:]
        a = _alias.get(tb.name)
        if a is None:
            ml = nc.lookup_mloc(tb.h)
            a = nc.alloc_sbuf_tensor_at(tb.name + "_r", [128, int(ml.dims[1]) // 4], F32R, offset=int(ml.addr))
            _alias[tb.name] = a
        return a[:]
    ntmp = [G[16], G[17]]

    def bfv(tb, n=None):
        v = tb[:].bitcast(BF16)
        return v

    if do_l1:
        l1_tail = P.sb("l1_tail", [128, 16, 3])
        l1_h = P.sb("l1_h", [128, 16])
        P.op('pool', lambda e: e.memset(l1_tail[:], 0.0), [], [l1_tail])
        P.op('pool', lambda e: e.memset(l1_h[:], 0.0), [], [l1_h])

    def norm_apply(k, gname, out_ap, out_tb):
        if k % 2 == 0:
            P.op('dve', lambda e: e.scalar_tensor_tensor(out=out_ap, in0=hT[:, k, :], scalar=pcc(gname, k), in1=rstd[:],
                                                         op0=MUL, op1=MUL), [hT, pc, rstd], [out_tb])
        else:
            nt = ntmp[(k // 2) % 2]
            P.op('act', lambda e: e.activation(out=nt[:], in_=hT[:, k, :], func=AF.Identity, scale=pcc(gname, k)), [hT, pc], [nt])
            P.op('pool', lambda e: e.tensor_tensor(out=out_ap, in0=nt[:], in1=rstd[:], op=MUL), [nt, rstd], [out_tb])

    def norm_stats():
        ps = nextps()
        for half in range(2):
            sq = H[4 + half]
            sqv = bfv(sq).rearrange("p (k t) -> p k t", k=4)
            P.op('act', lambda e, half=half, sqv=sqv: e.activation(out=sqv, in_=hT[:, 4 * half:4 * half + 4, :], func=AF.Square), [hT], [sq])
            for k in range(4):
                P.op('pe', lambda e, k=k, half=half, sqv=sqv: e.matmul(ps[:], lhsT=ones_bf[:], rhs=sqv[:, k, :], start=(half == 0 and k == 0), stop=(half == 1 and k == 3)),
                     [ones_bf, sq], [ps])
        P.op('act', lambda e: e.activation(out=lnv[:], in_=ps[:], func=AF.Ln, scale=1.0 / D, bias=1e-6), [ps], [lnv])
        P.op('act', lambda e: e.activation(out=rstd[:], in_=lnv[:], func=AF.Exp, scale=-0.5), [lnv], [rstd])

    def rmsnorm_x(gname):
        norm_stats()
        for k in range(8):
            norm_apply(k, gname, xnT[:, k, :], xnT)

    def proj(ps, wb, nk, rhs_tb, rhs_fn):
        for k in range(nk):
            P.op('pe', lambda e, k=k: e.matmul(ps[:], lhsT=wb[:, k, :], rhs=rhs_fn(k), start=(k == 0), stop=(k == nk - 1)),
                 [wb, rhs_tb], [ps])

    def proj_x(name, j):
        w = wld(name, j, 8)
        ps = nextps()
        proj(ps, w, 8, xnT, lambda k: xnT[:, k, :])
        return ps

    def ple(i, t0):
        ptok = H[0]
        ptv = ptok[:].rearrange("p (tb d) -> p tb d", tb=NT)
        pTt = G[12]
        pT = bfv(pTt).rearrange("p (k t) -> p k t", k=2)
        sgs, tmps = [G[0], G[1]], [G[2], G[3]]
        P.dma('sp', ptv, p_d.ap()[i, t0:t0 + TT, :].rearrange("(tb p) d -> p tb d", p=128), writes=[ptok])
        for kc in range(2):
            ps = nextps()
            for tb in range(NT):
                P.op('pe', lambda e, kc=kc, tb=tb, ps=ps: e.transpose(ps[:, tb * 128:(tb + 1) * 128], ptv[:, tb, kc * 128:(kc + 1) * 128], ident[:]),
                     [ptok, ident], [ps])
            P.op('act', lambda e, kc=kc, ps=ps: e.activation(out=pT[:, kc, :], in_=ps[:], func=AF.Copy), [ps], [pTt])
        rmsnorm_x('pen%d' % i)
        for m in range(8):
            sg, tmpa = sgs[m % 2], tmps[m % 2]
            wg = wld('pegate%d' % i, m, 8)
            wu = wld('peup%d' % i, m, 2)
            ps = nextps()
            proj(ps, wg, 8, xnT, lambda k: xnT[:, k, :])
            P.op('act', lambda e, ps=ps: e.activation(out=sg[:], in_=ps[:], func=AF.Sigmoid), [ps], [sg])
            ps2 = nextps()
            proj(ps2, wu, 2, pTt, lambda k: pT[:, k, :])
            P.op('dve', lambda e, ps2=ps2: e.tensor_tensor(out=tmpa[:], in0=ps2[:], in1=sg[:], op=MUL), [ps2, sg], [tmpa])
            P.op('pool', lambda e, m=m: e.tensor_tensor(out=hT[:, m, :], in0=hT[:, m, :], in1=tmpa[:], op=ADD), [hT, tmpa], [hT])

    def out_proj(name):
        for m in range(8):
            wo = wld(name, m, 16)
            ps = nextps()
            proj(ps, wo, 16, merged, lambda k: merged[:, k, :])
            P.op('dve', lambda e, m=m, ps=ps: e.tensor_tensor(out=hT[:, m, :], in0=hT[:, m, :], in1=ps[:], op=ADD), [hT, ps], [hT])

    def conv4(ps, c, tail_tb, wname, bname, acc):
        P.op('pool', lambda e: e.tensor_copy(out=ext[:, 0:3], in_=tail_tb[:, c, :]), [tail_tb], [ext])
        P.op('act', lambda e: e.activation(out=ext[:, 3:TT + 3], in_=ps[:], func=AF.Copy), [ps], [ext])
        P.op('act', lambda e: e.activation(out=acc[:], in_=ps[:], func=AF.Identity, scale=pcc(wname + '3', c), bias=pcc(bname, c)), [ps, pc], [acc])
        P.op('pool', lambda e: e.tensor_copy(out=tail_tb[:, c, :], in_=ext[:, TT:TT + 3]), [ext], [tail_tb])
        for j in range(3):
            P.op('dve', lambda e, j=j: e.scalar_tensor_tensor(out=acc[:], in0=ext[:, j:j + TT], scalar=pcc(wname + str(j), c), in1=acc[:],
                                                              op0=MUL, op1=ADD), [ext, pc, acc], [acc])

    def layer1(t0):
        rmsnorm_x('mixn1')
        for c in range(16):
            xc32, xcb, rg, ig, a2, th = G[6 * (c % 2):6 * (c % 2) + 6]
            xcb_ap = bfv(xcb)[:, 0:TT]
            ps = proj_x('w1in', c)
            conv4(ps, c, l1_tail, 'ccw', 'ccb', xc32)
            P.op('act', lambda e: e.activation(out=xcb_ap, in_=xc32[:], func=AF.Copy), [xc32], [xcb])
            wb = wld('wri', c, 2)
            ps_r = nextps()
            P.op('pe', lambda e: e.matmul(ps_r[:], lhsT=wb[:, 0, :], rhs=xcb_ap, start=True, stop=True), [wb, xcb], [ps_r])
            P.op('act', lambda e: e.activation(out=rg[:], in_=ps_r[:], func=AF.Sigmoid, bias=pcc('cbr', c)), [ps_r, pc], [rg])
            ps_i = nextps()
            P.op('pe', lambda e: e.matmul(ps_i[:], lhsT=wb[:, 1, :], rhs=xcb_ap, start=True, stop=True), [wb, xcb], [ps_i])
            P.op('act', lambda e: e.activation(out=ig[:], in_=ps_i[:], func=AF.Sigmoid, bias=pcc('cbi', c)), [ps_i, pc], [ig])
            P.op('act', lambda e: e.activation(out=a2[:], in_=rg[:], func=AF.Exp, scale=pdc('cneg', c)), [rg, pd], [a2])
            P.op('act', lambda e: e.activation(out=th[:], in_=rg[:], func=AF.Tanh, scale=pdc('csph', c)), [rg, pd], [th])
            P.op('dve', lambda e: e.scalar_tensor_tensor(out=th[:], in0=a2[:], scalar=1.0, in1=th[:], op0=ADD, op1=MUL), [a2, th], [th])
            P.op('dve', lambda e: e.tensor_scalar(out=a2[:], in0=th[:], scalar1=-1.0, scalar2=1.0, op0=MUL, op1=ADD), [th], [a2])
            P.op('dve', lambda e: e.scalar_tensor_tensor(out=th[:], in0=a2[:], scalar=1.0, in1=th[:], op0=ADD, op1=MUL), [a2, th], [th])
            P.op('act', lambda e: e.activation(out=th[:], in_=th[:], func=AF.Sqrt), [th], [th])
            P.op('pool', lambda e: e.tensor_tensor(out=ig[:], in0=xc32[:], in1=ig[:], op=MUL), [xc32, ig], [ig])
            P.op('pool', lambda e: e.tensor_tensor(out=ig[:], in0=ig[:], in1=th[:], op=MUL), [ig, th], [ig])
            P.op('dve', lambda e: e.tensor_tensor_scan(out=rg[:], data0=a2[:], data1=ig[:], initial=l1_h[:, c:c + 1], op0=MUL, op1=ADD),
                 [a2, ig, l1_h], [rg])
            P.op('pool', lambda e: e.tensor_copy(out=l1_h[:, c:c + 1], in_=rg[:, TT - 1:TT]), [rg], [l1_h])
            ps_g = proj_x('w1in', 16 + c)
            P.op('act', lambda e: e.activation(out=xc32[:], in_=ps_g[:], func=AF.Silu), [ps_g], [xc32])
            P.op('dve', lambda e: e.tensor_tensor(out=merged[:, c, :], in0=rg[:], in1=xc32[:], op=MUL), [rg, xc32], [merged])
        out_proj('w1out')

    if do_l0 and do_rwkv:
        identr = P.sb("identr", [128, 128], SDT)
        bdones = cload('bdones')
        bdones_bf = P.sb("bdones_bf", [128, 128], BF16)
        bdones_r = P.sb("bdones_r", [128, 128], SDT)
        maskA = cload('maskA')
        maskB = cload('maskB')
        resetm = cload('resetm')
        P.op('dve', lambda e: e.tensor_copy(out=identr[:], in_=ident[:]), [ident], [identr])
        P.op('dve', lambda e: e.tensor_copy(out=bdones_bf[:], in_=bdones[:]), [bdones], [bdones_bf])
        P.op('dve', lambda e: e.tensor_copy(out=bdones_r[:], in_=bdones[:]), [bdones], [bdones_r])
        rwc = P.sb("rwc", [128, 25])
        P.op('pool', lambda e: e.memset(rwc[:], 0.0), [], [rwc])
        lora_bf = P.sb("lora_bf", [128, TT], BF16)
        Sst = [[P.sb("S%d_%d" % (c, q), [128, 128], F32) for q in range(2)] for c in range(8)]
        for c in range(8):
            P.op('pool', lambda e, c=c: e.memset(Sst[c][0][:], 0.0), [], [Sst[c][0]])
        spar = [0] * 8
        rhs_sb = P.sb("rhs_sb", [128, 128], SDT)
        u_sb = P.sb("u_sb", [128, 128], SDT)
        psRHS = TB("psRHS", PS[7].h[:, 0:128], root=PS[7])
        psU = TB("psU", PS[7].h[:, 128:256], root=PS[7])
        psYT = TB("psYT", PS[7].h[:, 256:384], root=PS[7])
        psSN = TB("psSN", PS[7].h[:, 384:512], root=PS[7])

    def tshift(ps, dst, col, mu_ap, omu_ap):
        P.op('act', lambda e: e.activation(out=dst[:], in_=ps[:], func=AF.Identity, scale=omu_ap), [ps, pd], [dst])
        P.op('dve', lambda e: e.scalar_tensor_tensor(out=dst[:, 1:TT], in0=ps[:, 0:TT - 1], scalar=mu_ap, in1=dst[:, 1:TT], op0=MUL, op1=ADD),
             [ps, pc, dst], [dst])
        P.op('dve', lambda e: e.scalar_tensor_tensor(out=dst[:, 0:1], in0=rwc[:, col:col + 1], scalar=mu_ap, in1=dst[:, 0:1], op0=MUL, op1=ADD),
             [rwc, pc, dst], [dst])
        P.op('act', lambda e: e.activation(out=rwc[:, col:col + 1], in_=ps[:, TT - 1:TT], func=AF.Copy), [ps], [rwc])

    def r3(ap, n=8):
        return ap.rearrange("p (n t) -> p n t", n=n)

    def rwkv(t0):
        Rbd, Abd, Bbd, Kbd, Vbd = H[0:5]
        for i in range(5):
            P.op('pool', lambda e, i=i: e.memset(H[i][:], 0.0), [], [H[i]])
        bdv = [r3(RV(H[i])) for i in range(5)]
        ps = proj_x('w0in', 24)
        lora32 = G[16]
        tshift(ps, lora32, 24, pcc('mu_l'), pdc('omu_l'))
        P.op('act', lambda e: e.activation(out=lora_bf[0:64, :], in_=lora32[0:64, :], func=AF.Tanh), [lora32], [lora_bf])
        P.op('act', lambda e: e.activation(out=lora_bf[64:128, :], in_=lora32[64:128, :], func=AF.Copy), [lora32], [lora_bf])
        pend = []

        def pump(n):
            for _ in range(n):
                if pend:
                    pend.pop(0)()
        for th_ in rwkv_prep(0):
            th_()
        for c in range(8):
            rwkv_blockdiag(c, bdv)
            if c + 1 < 8:
                pend.extend(rwkv_prep(c + 1))
            rwkv_chunks(c, bdv, pump)
            rwkv_out(c)
            pump(len(pend))

    def rw_bufs(c):
        p_ = c % 2
        d = dict(r32=G[0], k32=G[1], v32=G[2], cump=G[6], Ginv=G[8], Gp=G[9], a32=G[10], kk32=G[11], kka=G[11],
                 rn=G[12], nkk=G[13], keff=G[15], t1=G[16])
        d.update(dict(gr32=(G[3], G[14])[p_], lw=(G[4], G[17])[p_], cum=(G[5], G[18])[p_], Gt=(G[7], G[19])[p_]))
        return d

    def rwkv_prep(c):
        B_ = rw_bufs(c)
        r32, k32, v32, gr32, lw, cum, cump, Gt, Ginv, Gp = [B_[k] for k in ('r32', 'k32', 'v32', 'gr32', 'lw', 'cum', 'cump', 'Gt', 'Ginv', 'Gp')]
        a32, kk32, kka, rn, nkk, keff, t1 = [B_[k] for k in ('a32', 'kk32', 'kka', 'rn', 'nkk', 'keff', 't1')]
        bon = cum
        T = []

        def projshift(dst, mch, nm, col):
            def f():
                ps = proj_x('w0in', mch)
                tshift(ps, dst, col, pcc('mu_' + nm, c), pdc('omu_' + nm, c))
            return f
        T.append(projshift(k32, 8 + c, 'k', 8 + c))

        def f_lw():
            ps = nextps()
            P.op('pe', lambda e: e.matmul(ps[:], lhsT=lup_bf[:, 0, c * 128:(c + 1) * 128], rhs=lora_bf[:], start=True, stop=True), [lup_bf, lora_bf], [ps])
            P.op('act', lambda e: e.activation(out=lw[:], in_=ps[:], func=AF.Sigmoid, bias=pcc('w0', c)), [ps, pc], [lw])
        T.append(f_lw)

        def f_a():
            ps = nextps()
            P.op('pe', lambda e: e.matmul(ps[:], lhsT=lup_bf[:, 1, c * 128:(c + 1) * 128], rhs=lora_bf[:], start=True, stop=True), [lup_bf, lora_bf], [ps])
            P.op('act', lambda e: e.activation(out=a32[:], in_=ps[:], func=AF.Sigmoid, bias=pcc('a0', c)), [ps, pc], [a32])
        T.append(f_a)
        T.append(lambda: P.op('dve', lambda e: e.tensor_tensor_scan(out=cum[:], data0=resetm[:], data1=lw[:], initial=0.0, op0=MUL, op1=ADD), [resetm, lw], [cum]))
        T.append(lambda: P.op('dve', lambda e: e.tensor_tensor(out=cump[:], in0=cum[:], in1=lw[:], op=SUB), [cum, lw], [cump]))
        T.append(lambda: P.op('act', lambda e: e.activation(out=Gt[:], in_=cum[:], func=AF.Exp, scale=-DECAY_SCALE), [cum], [Gt]))
        T.append(lambda: P.op('act', lambda e: e.activation(out=Ginv[:], in_=cum[:], func=AF.Exp, scale=DECAY_SCALE), [cum], [Ginv]))
        T.append(lambda: P.op('act', lambda e: e.activation(out=Gp[:], in_=cump[:], func=AF.Exp, scale=-DECAY_SCALE), [cump], [Gp]))
        T.append(lambda: P.op('dve', lambda e: e.tensor_scalar(out=kk32[:], in0=k32[:], scalar1=pcc('k_k', c), scalar2=None, op0=MUL), [k32, pc], [kk32]))
        sqk = bfv(t1)[:, 0:TT]
        T.append(lambda: P.op('act', lambda e: e.activation(out=sqk, in_=kk32[:], func=AF.Square), [kk32], [t1]))

        def f_rn():
            ps = nextps()
            P.op('pe', lambda e: e.matmul(ps[:], lhsT=bdones_bf[:], rhs=sqk, start=True, stop=True), [bdones_bf, t1], [ps])
            P.op('act', lambda e: e.activation(out=rn[:], in_=ps[:], func=AF.Ln, bias=1e-20), [ps], [rn])
            P.op('act', lambda e: e.activation(out=rn[:], in_=rn[:], func=AF.Exp, scale=-0.5), [rn], [rn])
        T.append(f_rn)
        T.append(lambda: P.op('dve', lambda e: e.scalar_tensor_tensor(out=nkk[:], in0=kk32[:], scalar=-1.0, in1=rn[:], op0=MUL, op1=MUL), [kk32, rn], [nkk]))
        T.append(lambda: P.op('dve', lambda e: e.scalar_tensor_tensor(out=kka[:], in0=nkk[:], scalar=-1.0, in1=a32[:], op0=MUL, op1=MUL), [nkk, a32], [kka]))
        T.append(lambda: P.op('pool', lambda e: e.tensor_scalar(out=t1[:], in0=a32[:], scalar1=-1.0, scalar2=pcc('k_a', c), op0=ADD, op1=MUL), [a32, pc], [t1]))
        T.append(lambda: P.op('dve', lambda e: e.scalar_tensor_tensor(out=keff[:], in0=t1[:], scalar=1.0, in1=k32[:], op0=ADD, op1=MUL), [t1, k32], [keff]))
        T.append(projshift(r32, c, 'r', c))
        rkr = bfv(t1)[:, 0:TT]
        T.append(lambda: P.op('dve', lambda e: e.scalar_tensor_tensor(out=rkr, in0=r32[:], scalar=pcc('r_k', c), in1=keff[:], op0=MUL, op1=MUL), [r32, pc, keff], [t1]))
        T.append(projshift(v32, 16 + c, 'v', 16 + c))

        def f_bon():
            ps = nextps()
            P.op('pe', lambda e: e.matmul(ps[:], lhsT=bdones_bf[:], rhs=rkr, start=True, stop=True), [bdones_bf, t1], [ps])
            P.op('dve', lambda e: e.tensor_tensor(out=bon[:], in0=ps[:], in1=v32[:], op=MUL), [ps, v32], [bon])
        T.append(f_bon)

        def f_gr():
            ps = proj_x('w0in', 33 + c)
            P.op('act', lambda e: e.activation(out=gr32[:], in_=ps[:], func=AF.Silu), [ps], [gr32])
        T.append(f_gr)
        return T

    def rwkv_blockdiag(c, bdv):
        B_ = rw_bufs(c)
        Rv, Av, Bv, Kv, Vv = bdv
        Rbd, Abd, Bbd, Kbd, Vbd = H[0:5]
        for hh in range(2):
            hs_ = slice(hh * 64, hh * 64 + 64)
            for i_, (dstv, dtb, a_, b_) in enumerate(((Rv, Rbd, B_['r32'], B_['Gt']), (Av, Abd, B_['nkk'], B_['Gp']), (Bv, Bbd, B_['kka'], B_['Ginv']), (Kv, Kbd, B_['keff'], B_['Ginv']))):
                P.op('dve' if (i_ + hh) % 2 == 0 else 'pool', lambda e, dstv=dstv, a_=a_, b_=b_, hs_=hs_: e.tensor_tensor(out=dstv[hs_, :, hs_], in0=r3(a_[hs_, :]), in1=r3(b_[hs_, :]), op=MUL),
                     [a_, b_], [dtb])
            P.op('act', lambda e, hs_=hs_: e.activation(out=Vv[hs_, :, hs_], in_=r3(B_['v32'][hs_, :]), func=AF.Copy), [B_['v32']], [Vbd])

    def rwkv_chunks(c, bdv, pump):
        B_ = rw_bufs(c)
        Gt, y32 = B_['Gt'], B_['lw']
        Rv, Av, Bv, Kv, Vv = bdv
        Rbd, Abd, Bbd, Kbd, Vbd = H[0:5]
        for hf in range(2):
            SCt = [H[5], H[6]]
            QTt = [H[7], H[8]]
            SCv = [RV(t).rearrange("p (j m k) -> p j m k", j=2, m=4) for t in SCt]
            QTv = [RV(t).rearrange("p (j m k) -> p j m k", j=2, m=4) for t in QTt]
            SCf = [t[:].rearrange("p (j m k) -> p j m k", j=2, m=4) for t in SCt]
            PQt = [H[9], H[10]]
            PQv = [[RV(PQt[b]).rearrange("p (q j m k) -> p q j m k", q=2, j=2, m=2)[:, par] for b in range(2)] for par in range(2)]
            Xt = H[11]
            Xv = [RV(Xt).rearrange("p (q j k) -> p q j k", q=2, j=4)[:, q] for q in range(2)]
            Xf = [Xt[:].rearrange("p (q j k) -> p q j k", q=2, j=4)[:, q] for q in range(2)]
            for j in range(4):
                n = hf * 4 + j
                sc = SCv[j // 2][:, j % 2]
                qt = QTv[j // 2][:, j % 2]
                A_, B2_ = PS[2], PS[3]
                for m, (l_, r_) in enumerate(((Bv, Av), (Kv, Av), (Bv, Rv), (Kv, Rv))):
                    P.op('pe', lambda e, m=m, l_=l_, r_=r_, n=n: e.matmul(A_[:, m * 128:(m + 1) * 128], lhsT=l_[:, n, :], rhs=r_[:, n, :], start=True, stop=True),
                         [Rbd, Abd, Bbd, Kbd], [A_])
                P.op('dve', lambda e, sc=sc: e.tensor_tensor(out=sc, in0=A_[:].rearrange("p (m k) -> p m k", m=4), in1=maskA[:], op=MUL),
                     [A_, maskA], [SCt[j // 2]])
                P.op('pe', lambda e, n=n: e.matmul(B2_[:, 0:128], lhsT=Av[:, n, :], rhs=Bv[:, n, :], start=True, stop=True), [Abd, Bbd], [B2_])
                for m, l_ in enumerate((Bv, Kv, Vv)):
                    P.op('pe', lambda e, m=m, l_=l_, n=n: e.matmul(B2_[:, (m + 1) * 128:(m + 2) * 128], lhsT=l_[:, n, :], rhs=identr[:], start=True, stop=True),
                         [Bbd, Kbd, Vbd, identr], [B2_])
                P.op('dve', lambda e, qt=qt: e.tensor_tensor(out=qt[:, 0, :], in0=B2_[:, 0:128], in1=maskB[:], op=MUL), [B2_, maskB], [QTt[j // 2]])
                P.op('act', lambda e, qt=qt: e.activation(out=qt[:, 1:4, :], in_=B2_[:, 128:512].rearrange("p (m k) -> p m k", m=3), func=AF.Copy),
                     [B2_], [QTt[j // 2]])
                pump(2)
            for j in range(4):
                P.op('dve', lambda e, j=j: e.tensor_tensor(out=Xv[1][:, j, :], in0=SCf[j // 2][:, j % 2, 0, :], in1=ident[:], op=ADD),
                     [SCt[j // 2], ident], [Xt])
            for lvl in range(1, 6):
                par = lvl % 2
                for b in range(2):
                    bank = PS[4 + b]
                    for jj in range(2):
                        j = 2 * b + jj
                        if lvl == 1:
                            Pp = SCv[j // 2][:, j % 2, 0, :]
                            Qp = QTv[j // 2][:, j % 2, 0, :]
                            rd = [SCt[j // 2], QTt[j // 2]]
                        else:
                            Pp = PQv[1 - par][b][:, jj, 0, :]
                            Qp = PQv[1 - par][b][:, jj, 1, :]
                            rd = [PQt[b]]
                        P.op('pe', lambda e, jj=jj, Pp=Pp, Qp=Qp, bank=bank: e.matmul(bank[:, (2 * jj) * 128:(2 * jj + 1) * 128], lhsT=Qp, rhs=Pp, start=True, stop=True), rd, [bank])
                        P.op('pe', lambda e, jj=jj, Pp=Pp, Qp=Qp, bank=bank: e.matmul(bank[:, (2 * jj + 1) * 128:(2 * jj + 2) * 128], lhsT=Pp, rhs=Qp, start=True, stop=True), rd, [bank])
                    if b == 0:
                        P.op('act', lambda e, b=b, bank=bank, par=par: e.activation(out=PQv[par][b], in_=bank[:].rearrange("p (j m k) -> p j m k", j=2, m=2), func=AF.Copy),
                             [bank], [PQt[b]])
                    else:
                        P.op('dve', lambda e, b=b, bank=bank, par=par: e.tensor_copy(out=PQv[par][b], in_=bank[:].rearrange("p (j m k) -> p j m k", j=2, m=2)),
                             [bank], [PQt[b]])
                XB = PS[6]
                for j in range(4):
                    Ql = PQv[par][j // 2][:, j % 2, 1, :]
                    P.op('pe', lambda e, j=j, Ql=Ql, par=par: e.matmul(XB[:, j * 128:(j + 1) * 128], lhsT=Ql, rhs=Xv[par][:, j, :], start=True, stop=True), [PQt[j // 2], Xt], [XB])
                P.op('dve', lambda e, par=par: e.tensor_tensor(out=Xv[1 - par], in0=XB[:].rearrange("p (j k) -> p j k", j=4), in1=Xf[par], op=ADD), [XB, Xt], [Xt])
                pump(1)
            for j in range(4):
                n = hf * 4 + j
                sc = SCv[j // 2][:, j % 2]
                qt = QTv[j // 2][:, j % 2]
                sct, qtt = SCt[j // 2], QTt[j // 2]
                S0 = Sst[c][spar[c]]
                S1 = Sst[c][1 - spar[c]]
                spar[c] = 1 - spar[c]
                P.op('pe', lambda e, n=n, S0=S0: e.matmul(psRHS[:], lhsT=Av[:, n, :], rhs=RV(S0), start=True, stop=False), [Abd, S0], [psRHS])
                P.op('pe', lambda e, sc=sc, qt=qt: e.matmul(psRHS[:], lhsT=sc[:, 1, :], rhs=qt[:, 3, :], start=False, stop=True), [sct, qtt], [psRHS])
                P.op('act', lambda e: e.activation(out=rhs_sb[:], in_=psRHS[:], func=AF.Copy), [psRHS], [rhs_sb])
                P.op('pe', lambda e, j=j: e.matmul(psU[:], lhsT=Xv[0][:, j, :], rhs=rhs_sb[:], start=True, stop=True), [Xt, rhs_sb], [psU])
                P.op('act', lambda e: e.activation(out=u_sb[:], in_=psU[:], func=AF.Copy), [psU], [u_sb])
                P.op('pe', lambda e, qt=qt: e.matmul(psSN[:], lhsT=qt[:, 1, :], rhs=u_sb[:], start=True, stop=False), [qtt, u_sb], [psSN])
                P.op('pe', lambda e, qt=qt: e.matmul(psSN[:], lhsT=qt[:, 2, :], rhs=qt[:, 3, :], start=False, stop=False), [qtt], [psSN])
                P.op('pe', lambda e, S0=S0: e.matmul(psSN[:], lhsT=identr[:], rhs=RV(S0), start=False, stop=True), [identr, S0], [psSN])
                P.op('act', lambda e, n=n, S1=S1: e.activation(out=RV(S1), in_=psSN[:], func=AF.Identity, scale=Gt[:, n * 64 + 63:n * 64 + 64]), [psSN, Gt], [S1])
                P.op('pe', lambda e, n=n, S0=S0: e.matmul(psYT[:], lhsT=RV(S0), rhs=Rv[:, n, :], start=True, stop=False), [S0, Rbd], [psYT])
                P.op('pe', lambda e, sc=sc: e.matmul(psYT[:], lhsT=u_sb[:], rhs=sc[:, 2, :], start=False, stop=False), [u_sb, sct], [psYT])
                P.op('pe', lambda e, sc=sc, qt=qt: e.matmul(psYT[:], lhsT=qt[:, 3, :], rhs=sc[:, 3, :], start=False, stop=True), [qtt, sct], [psYT])
                for hh in range(2):
                    hs_ = slice(hh * 64, hh * 64 + 64)
                    P.op('act', lambda e, hs_=hs_, n=n: e.activation(out=y32[hs_, n * 64:(n + 1) * 64], in_=psYT[hs_, hs_], func=AF.Copy), [psYT], [y32])
                pump(2)

    def rwkv_out(c):
        B_ = rw_bufs(c)
        y32, dd, bon, gr32 = B_['lw'], B_['Gt'], B_['cum'], B_['gr32']
        dump('y_rwkv%d' % c, y32, y32[:])
        yrv = RV(y32)
        P.op('act', lambda e: e.activation(out=yrv, in_=y32[:], func=AF.Copy), [y32], [y32])
        ps = nextps()
        P.op('pe', lambda e: e.matmul(ps[:], lhsT=bdones_r[:], rhs=yrv, start=True, stop=True), [bdones_r, y32], [ps])
        P.op('dve', lambda e: e.scalar_tensor_tensor(out=dd[:], in0=ps[:], scalar=-1.0 / 64, in1=y32[:], op0=MUL, op1=ADD), [ps, y32], [dd])
        sq, rs = H[5], H[6]
        sqv = RV(sq)[:, 0:TT]
        P.op('act', lambda e: e.activation(out=sqv, in_=dd[:], func=AF.Square), [dd], [sq])
        ps2 = nextps()
        P.op('pe', lambda e: e.matmul(ps2[:], lhsT=bdones_r[:], rhs=sqv, start=True, stop=True), [bdones_r, sq], [ps2])
        P.op('act', lambda e: e.activation(out=rs[:, 0:TT], in_=ps2[:], func=AF.Ln, scale=1.0 / 64, bias=64e-5), [ps2], [rs])
        P.op('act', lambda e: e.activation(out=rs[:, 0:TT], in_=rs[:, 0:TT], func=AF.Exp, scale=-0.5), [rs], [rs])
        P.op('dve', lambda e: e.tensor_tensor(out=dd[:], in0=dd[:], in1=rs[:, 0:TT], op=MUL), [dd, rs], [dd])
        P.op('act', lambda e: e.activation(out=dd[:], in_=dd[:], func=AF.Identity, scale=pcc('ln_w', c), bias=pcc('ln_b', c)), [dd, pc], [dd])
        P.op('pool', lambda e: e.tensor_tensor(out=dd[:], in0=dd[:], in1=bon[:], op=ADD), [dd, bon], [dd])
        P.op('dve', lambda e: e.tensor_tensor(out=merged[:, c, :], in0=dd[:], in1=gr32[:], op=MUL), [dd, gr32], [merged])

    if do_l0 and do_mlstm:
        causal = cload('causal')
        sel = P.sb("c_sel", [4, 4, 128])
        P.dma('sp', sel[:], din['sel'].ap(), writes=[sel])
        onesf = P.sb("onesf", [128, TT])
        P.op('pool', lambda e: e.memset(onesf[:], 1.0), [], [onesf])
        ones_r = P.sb("ones_r", [128, 128], SDT)
        P.op('dve', lambda e: e.tensor_copy(out=ones_r[:], in_=onesf[:, 0:128]), [onesf], [ones_r])
        m_tail = P.sb("m_tail", [128, 8, 3])
        P.op('pool', lambda e: e.memset(m_tail[:], 0.0), [], [m_tail])
        CT32 = P.sb("CT32", [128, 4, 2, 256])
        CTbf = P.sb("CTbf", [128, 4, 2, 256], BF16)
        nr32 = P.sb("nr32", [128, 4, 2, 128])
        nrbf = P.sb("nrbf", [128, 4, 2, 128], BF16)
        for t_ in (CT32, CTbf, nr32, nrbf):
            P.op('pool', lambda e, t_=t_: e.memset(t_[:], 0.0), [], [t_])
        mcar = P.sb("mcar", [4, 2])
        P.op('pool', lambda e: e.memset(mcar[:], 0.0), [], [mcar])
        MxE = P.sb("MxE", [4, 5])
        dec = P.sb("dec", [4, 4])
        smallb = P.sb("smallb", [128, 32])
        qe = P.sb("qe", [128, 2, 128], BF16)
        qw = P.sb("qw", [128, 2, 128], BF16)
        kg = P.sb("kg", [128, 2, 128], BF16)
        kgt = P.sb("kgt", [128, 256], BF16)
        st_bf = P.sb("st_bf", [128, 128], BF16)
        ddm = P.sb("ddm", [128, 128])
        recm = P.sb("recm", [128, 128])

    def mlstm(t0):
        qTt, kTt, vTt = [H[0], H[1]], [H[2], H[3]], [H[4], H[5]]
        ktt, vtt = [H[6], H[7]], [H[8], H[9]]
        xct = [H[10], H[11]]

        def fm(tl, c):
            return bfv(tl[c // 4]).rearrange("p (k t) -> p k t", k=4)[:, c % 4, :]

        def tokv(tl, tb):
            return bfv(tl[tb // 2]).rearrange("p (b d) -> p b d", b=2)[:, tb % 2, :]
        acc, xmb = G[9], G[10]
        xmb_ap = bfv(xmb)[:, 0:TT]
        gi_ps, gf_ps = PS[2], PS[3]
        for c in range(8):
            ps = proj_x('w0in', 25 + c)
            conv4(ps, c, m_tail, 'mcw', 'mcb', acc)
            P.op('act', lambda e, c=c: e.activation(out=fm(xct, c), in_=acc[:], func=AF.Silu), [acc], [xct[c // 4]])
            P.op('pool', lambda e: e.tensor_copy(out=xmb_ap, in_=ext[:, 3:TT + 3]), [ext], [xmb])
            for t_, (dst, src_ap, src_tb) in enumerate(((qTt, fm(xct, c), xct[c // 4]), (kTt, fm(xct, c), xct[c // 4]), (vTt, xmb_ap, xmb))):
                ps = nextps()
                P.op('pe', lambda e, ps=ps, t_=t_, src_ap=src_ap: e.matmul(ps[:], lhsT=wqkv_bf[:, t_, c, :], rhs=src_ap, start=True, stop=True),
                     [wqkv_bf, src_tb], [ps])
                P.op('act' if t_ != 1 else 'dve',
                     (lambda e, ps=ps, dst=dst: e.activation(out=fm(dst, c), in_=ps[:], func=AF.Copy)) if t_ != 1 else
                     (lambda e, ps=ps, dst=dst: e.tensor_copy(out=fm(dst, c), in_=ps[:])), [ps], [dst[c // 4]])
                j = t_ * 8 + c
                first, last = (c == 0 and t_ == 0), (c == 7 and t_ == 2)
                P.op('pe', lambda e, j=j, dst=dst, first=first, last=last: e.matmul(gi_ps[0:4, :], lhsT=wif_bf[:, 0, j, :], rhs=fm(dst, c), start=first, stop=last),
                     [wif_bf, dst[c // 4]], [gi_ps])
                P.op('pe', lambda e, j=j, dst=dst, first=first, last=last: e.matmul(gf_ps[0:4, :], lhsT=wif_bf[:, 1, j, :], rhs=fm(dst, c), start=first, stop=last),
                     [wif_bf, dst[c // 4]], [gf_ps])
            for t_, (dstl, src_ap, src_tb) in ((1, (ktt, fm(xct, c), xct[c // 4])), (2, (vtt, xmb_ap, xmb))):
                ps = nextps()
                for tb in range(NT):
                    P.op('pe', lambda e, ps=ps, tb=tb, t_=t_, src_ap=src_ap: e.matmul(ps[:, tb * 128:(tb + 1) * 128], lhsT=src_ap[:, tb * 128:(tb + 1) * 128], rhs=wqkv_bf[:, t_, c, :],
                                                                                      start=True, stop=True), [wqkv_bf, src_tb], [ps])
                for half in range(2):
                    dv = bfv(dstl[half]).rearrange("p (b d) -> p b d", b=2)[:, :, c * 128:(c + 1) * 128]
                    P.op('act' if half == 0 else 'dve',
                         (lambda e, ps=ps, dv=dv, half=half: e.activation(out=dv, in_=ps[:, half * 256:(half + 1) * 256].rearrange("p (b d) -> p b d", b=2), func=AF.Copy)) if half == 0 else
                         (lambda e, ps=ps, dv=dv, half=half: e.tensor_copy(out=dv, in_=ps[:, half * 256:(half + 1) * 256].rearrange("p (b d) -> p b d", b=2))),
                         [ps], [dstl[half]])
        rg_, re2, rwi, rem, rli, rFp, rag, rMx, rtmp = G[11], G[12], G[13], G[14], G[15], G[16], G[17], G[18], G[19]
        R4 = slice(0, 4)
        P.op('act', lambda e: e.activation(out=rli[R4, :], in_=gi_ps[R4, :], func=AF.Identity, bias=pc[R4, PC['bif_i']:PC['bif_i'] + 1]), [gi_ps, pc], [rli])
        P.op('act', lambda e: e.activation(out=rtmp[R4, :], in_=gf_ps[R4, :], func=AF.Exp, scale=-1.0, bias=pd[R4, PD['negbf']:PD['negbf'] + 1]), [gf_ps, pd], [rtmp])
        P.op('act', lambda e: e.activation(out=rtmp[R4, :], in_=rtmp[R4, :], func=AF.Ln, bias=1.0), [rtmp], [rtmp])
        P.op('dve', lambda e: e.tensor_tensor_scan(out=rFp[R4, :], data0=onesf[R4, :], data1=rtmp[R4, :], initial=mcar[:, 0:1], op0=MUL, op1=ADD),
             [onesf, rtmp, mcar], [rFp])
        P.op('pool', lambda e: e.tensor_tensor(out=rag[R4, :], in0=rli[R4, :], in1=rFp[R4, :], op=ADD), [rli, rFp], [rag])
        P.op('dve', lambda e: e.tensor_tensor_scan(out=rMx[R4, :], data0=rag[R4, :], data1=rag[R4, :], initial=mcar[:, 1:2], op0=MAX, op1=MAX),
             [rag, mcar], [rMx])
        P.op('pool', lambda e: e.tensor_copy(out=MxE[:, 0:1], in_=mcar[:, 1:2]), [mcar], [MxE])
        P.op('pool', lambda e: e.tensor_copy(out=MxE[:, 1:5], in_=rMx[R4, 127:TT:128]), [rMx], [MxE])
        P.op('pool', lambda e: e.tensor_copy(out=mcar[:, 0:1], in_=rFp[R4, TT - 1:TT]), [rFp], [mcar])
        P.op('pool', lambda e: e.tensor_copy(out=mcar[:, 1:2], in_=rMx[R4, TT - 1:TT]), [rMx], [mcar])
        v3 = lambda t_: t_[R4, :].rearrange("p (q t) -> p q t", q=NT)
        mend = MxE[:, 1:5].unsqueeze(2).broadcast_to([4, NT, 128])
        mprev = MxE[:, 0:4].unsqueeze(2).broadcast_to([4, NT, 128])
        P.op('dve', lambda e: e.tensor_tensor(out=v3(rg_), in0=v3(rag), in1=mend, op=SUB), [rag, MxE], [rg_])
        P.op('act', lambda e: e.activation(out=rg_[R4, :], in_=rg_[R4, :], func=AF.Exp), [rg_], [rg_])
        P.op('dve', lambda e: e.tensor_tensor(out=v3(re2), in0=mend, in1=v3(rMx), op=SUB), [rMx, MxE], [re2])
        P.op('act', lambda e: e.activation(out=re2[R4, :], in_=re2[R4, :], func=AF.Exp), [re2], [re2])
        P.op('dve', lambda e: e.tensor_tensor(out=v3(rwi), in0=mprev, in1=v3(rMx), op=SUB), [rMx, MxE], [rwi])
        P.op('act', lambda e: e.activation(out=rwi[R4, :], in_=rwi[R4, :], func=AF.Exp), [rwi], [rwi])
        P.op('dve', lambda e: e.tensor_tensor(out=rem[R4, :], in0=rFp[R4, :], in1=rMx[R4, :], op=SUB), [rFp, rMx], [rem])
        P.op('act', lambda e: e.activation(out=rem[R4, :], in_=rem[R4, :], func=AF.Exp), [rem], [rem])
        P.op('dve', lambda e: e.tensor_tensor(out=dec[:], in0=MxE[:, 0:4], in1=MxE[:, 1:5], op=SUB), [MxE], [dec])
        P.op('act', lambda e: e.activation(out=dec[:], in_=dec[:], func=AF.Exp), [dec], [dec])
        sp_ = PS[5]
        for q in range(NT):
            P.op('pe', lambda e, q=q: e.matmul(sp_[:, q * 4:(q + 1) * 4], lhsT=rg_[R4, q * 128:(q + 1) * 128], rhs=ident[0:4, 0:4], start=True, stop=True),
                 [rg_, ident], [sp_])
        for h in range(4):
            P.op('pe', lambda e, h=h: e.matmul(sp_[:, 16 + h * 4:16 + (h + 1) * 4], lhsT=sel[:, h, :], rhs=dec[:], start=True, stop=True), [sel, dec], [sp_])
        P.op('act', lambda e: e.activation(out=smallb[:], in_=sp_[:, 0:32], func=AF.Copy), [sp_], [smallb])
        for h in range(4):
            mlstm_head(h, qTt, kTt, ktt, vtt, xct, fm, tokv, (rg_, re2, rwi, rem))

    def mlstm_head(h, qTt, kTt, ktt, vtt, xct, fm, tokv, rows):
        bc = G[0:4]
        for i in range(4):
            ps = nextps()
            P.op('pe', lambda e, ps=ps, i=i: e.matmul(ps[:], lhsT=sel[:, h, :], rhs=rows[i][0:4, :], start=True, stop=True), [sel, rows[i]], [ps])
            P.op('act' if i % 2 == 0 else 'dve',
                 (lambda e, ps=ps, i=i: e.activation(out=bc[i][:], in_=ps[:], func=AF.Copy)) if i % 2 == 0 else
                 (lambda e, ps=ps, i=i: e.tensor_copy(out=bc[i][:], in_=ps[:])), [ps], [bc[i]])
        g_bc, e2_bc, wi_bc, em_bc = bc
        h32 = [G[4], G[5]]
        gms = [G[6], G[7]]
        for vc in range(2):
            ps = proj_x('w0in', 41 + 2 * h + vc)
            P.op('act', lambda e, ps=ps, vc=vc: e.activation(out=gms[vc][:], in_=ps[:], func=AF.Silu), [ps], [gms[vc]])
        qt_ = qTt[h // 2]
        kt_ = kTt[h // 2]
        qv = bfv(qt_).rearrange("p (k t) -> p k t", k=4)[:, 2 * (h % 2):2 * (h % 2) + 2, :]
        kv = bfv(kt_).rearrange("p (k t) -> p k t", k=4)[:, 2 * (h % 2):2 * (h % 2) + 2, :]
        CB, SB_, NB = PS[6], PS[7], PS[5]
        for q in range(NT):
            ts_ = slice(q * 128, (q + 1) * 128)
            bcast = lambda t_: t_[:, ts_].unsqueeze(1).broadcast_to([128, 2, 128])
            P.op('dve', lambda e: e.tensor_tensor(out=qe[:], in0=qv[:, :, ts_], in1=bcast(e2_bc), op=MUL), [qt_, e2_bc], [qe])
            P.op('pool', lambda e: e.tensor_tensor(out=qw[:], in0=qv[:, :, ts_], in1=bcast(wi_bc), op=MUL), [qt_, wi_bc], [qw])
            P.op('dve', lambda e: e.scalar_tensor_tensor(out=kg[:], in0=kv[:, :, ts_], scalar=0.0625, in1=bcast(g_bc), op0=MUL, op1=MUL), [kt_, g_bc], [kg])
            ktok = tokv(ktt, q)[:, h * 256:(h + 1) * 256]
            vtok = tokv(vtt, q)[:, h * 256:(h + 1) * 256]
            P.op('pool', lambda e: e.tensor_scalar(out=kgt[:], in0=ktok, scalar1=smallb[:, q * 4 + h:q * 4 + h + 1], scalar2=0.0625, op0=MUL, op1=MUL),
                 [ktt[q // 2], smallb], [kgt])
            for kc in range(2):
                P.op('pe', lambda e, kc=kc: e.matmul(CB[:, 0:128], lhsT=kg[:, kc, :], rhs=qe[:, kc, :], start=(kc == 0), stop=(kc == 1)), [kg, qe], [CB])
            P.op('dve', lambda e: e.tensor_tensor(out=st_bf[:], in0=CB[:, 0:128], in1=causal[:], op=MUL), [CB, causal], [st_bf])
            for vc in range(2):
                o_ = CB[:, 128 + vc * 128:256 + vc * 128]
                P.op('pe', lambda e, o_=o_, vc=vc: e.matmul(o_, lhsT=vtok[:, vc * 128:(vc + 1) * 128], rhs=st_bf[:], start=True, stop=False), [vtt[q // 2], st_bf], [CB])
                for kc in range(2):
                    P.op('pe', lambda e, o_=o_, vc=vc, kc=kc: e.matmul(o_, lhsT=CTbf[:, h, kc, vc * 128:(vc + 1) * 128], rhs=qw[:, kc, :], start=False, stop=(kc == 1)),
                         [CTbf, qw], [CB])
            o_ = CB[:, 384:512]
            P.op('pe', lambda e, o_=o_: e.matmul(o_, lhsT=ones_bf[:], rhs=st_bf[:], start=True, stop=False), [ones_bf, st_bf], [CB])
            for kc in range(2):
                P.op('pe', lambda e, o_=o_, kc=kc: e.matmul(o_, lhsT=nrbf[:, h, kc, :], rhs=qw[:, kc, :], start=False, stop=(kc == 1)), [nrbf, qw], [CB])
            P.op('act', lambda e: e.activation(out=ddm[:], in_=CB[:, 384:512], func=AF.Abs), [CB], [ddm])
            P.op('dve', lambda e: e.tensor_tensor(out=ddm[:], in0=ddm[:], in1=em_bc[:, ts_], op=MAX), [ddm, em_bc], [ddm])
            P.op('dve', lambda e: e.tensor_scalar(out=ddm[:], in0=ddm[:], scalar1=1e-6, scalar2=None, op0=ADD), [ddm], [ddm])
            P.op('dve', lambda e: e.reciprocal(out=recm[:], in_=ddm[:]), [ddm], [recm])
            for vc in range(2):
                P.op('dve', lambda e, vc=vc: e.tensor_tensor(out=h32[vc][:, ts_], in0=CB[:, 128 + vc * 128:256 + vc * 128], in1=recm[:], op=MUL), [CB, recm], [h32[vc]])
            for kc in range(2):
                P.op('pe', lambda e, kc=kc: e.matmul(SB_[:, kc * 256:(kc + 1) * 256], lhsT=kgt[:, kc * 128:(kc + 1) * 128], rhs=vtok, start=True, stop=True),
                     [kgt, vtt[q // 2]], [SB_])
                P.op('pe', lambda e, kc=kc: e.matmul(NB[:, 64 + kc * 128:64 + (kc + 1) * 128], lhsT=kgt[:, kc * 128:(kc + 1) * 128], rhs=ones_bf[:], start=True, stop=True),
                     [kgt, ones_bf], [NB])
            dcol = smallb[:, 16 + h * 4 + q:16 + h * 4 + q + 1]
            P.op('dve', lambda e: e.scalar_tensor_tensor(out=CT32[:, h], in0=CT32[:, h], scalar=dcol, in1=SB_[:].rearrange("p (k v) -> p k v", k=2), op0=MUL, op1=ADD),
                 [CT32, smallb, SB_], [CT32])
            P.op('act', lambda e: e.activation(out=CTbf[:, h], in_=CT32[:, h], func=AF.Copy), [CT32], [CTbf])
            P.op('dve', lambda e: e.scalar_tensor_tensor(out=nr32[:, h], in0=nr32[:, h], scalar=dcol, in1=NB[:, 64:320].rearrange("p (k v) -> p k v", k=2), op0=MUL, op1=ADD),
                 [nr32, smallb, NB], [nr32])
            P.op('pool', lambda e: e.tensor_copy(out=nrbf[:, h], in_=nr32[:, h]), [nr32], [nrbf])
        dump('h_m%d' % h, h32[0], h32[0][:])
        hr = [G[8], G[9]]
        ps = nextps()
        for vc in range(2):
            P.op('act', lambda e, vc=vc: e.activation(out=RV(hr[vc]), in_=h32[vc][:], func=AF.Copy), [h32[vc]], [hr[vc]])
            P.op('pe', lambda e, ps=ps, vc=vc: e.matmul(ps[:], lhsT=ones_r[:], rhs=RV(hr[vc]), start=(vc == 0), stop=(vc == 1)), [ones_r, hr[vc]], [ps])
        for vc in range(2):
            P.op('dve', lambda e, ps=ps, vc=vc: e.scalar_tensor_tensor(out=h32[vc][:], in0=ps[:], scalar=-1.0 / 256, in1=h32[vc][:], op0=MUL, op1=ADD), [ps, h32[vc]], [h32[vc]])
        ps2 = nextps()
        for vc in range(2):
            P.op('act', lambda e, vc=vc: e.activation(out=RV(hr[vc]), in_=h32[vc][:], func=AF.Square), [h32[vc]], [hr[vc]])
            P.op('pe', lambda e, ps2=ps2, vc=vc: e.matmul(ps2[:], lhsT=ones_r[:], rhs=RV(hr[vc]), start=(vc == 0), stop=(vc == 1)), [ones_r, hr[vc]], [ps2])
        rs = G[10]
        P.op('act', lambda e, ps2=ps2: e.activation(out=rs[:], in_=ps2[:], func=AF.Ln, scale=1.0 / 256, bias=1e-5), [ps2], [rs])
        P.op('act', lambda e: e.activation(out=rs[:], in_=rs[:], func=AF.Exp, scale=-0.5), [rs], [rs])
        for vc in range(2):
            ch = 2 * h + vc
            P.op('dve', lambda e, vc=vc: e.tensor_tensor(out=h32[vc][:], in0=h32[vc][:], in1=rs[:], op=MUL), [h32[vc], rs], [h32[vc]])
            P.op('act', lambda e, vc=vc, ch=ch: e.activation(out=h32[vc][:], in_=h32[vc][:], func=AF.Identity, scale=pcc('mnorm', ch)), [h32[vc], pc], [h32[vc]])
            P.op('dve', lambda e, vc=vc, ch=ch: e.scalar_tensor_tensor(out=h32[vc][:], in0=fm(xct, ch), scalar=pcc('mskip', ch), in1=h32[vc][:], op0=MUL, op1=ADD),
                 [xct[ch // 4], pc, h32[vc]], [h32[vc]])
            P.op('pool', lambda e, vc=vc, ch=ch: e.tensor_tensor(out=merged[:, 8 + ch, :], in0=h32[vc][:], in1=gms[vc][:], op=MUL), [h32[vc], gms[vc]], [merged])

    def layer0(t0):
        rmsnorm_x('mixn0')
        pstate['set'] = [0, 1]
        if do_rwkv:
            rwkv(t0)
        if do_mlstm:
            mlstm(t0)
        pstate['set'] = list(range(8))
        out_proj('w0out')

    if do_l0 and not (do_rwkv and do_mlstm):
        P.op('pool', lambda e: e.memset(merged[:], 0.0), [], [merged])

    for ti in range(ntiles):
        t0 = ti * TT
        for tb in range(NT):
            P.dma('sp', H[tb][:], x_d.ap()[t0 + tb * 128:t0 + (tb + 1) * 128, :], writes=[H[tb]])
        for kc in range(8):
            ps = nextps()
            for tb in range(NT):
                P.op('pe', lambda e, kc=kc, tb=tb, ps=ps: e.transpose(ps[:, tb * 128:(tb + 1) * 128], H[tb][:, kc * 128:(kc + 1) * 128], ident[:]),
                     [H[tb], ident], [ps])
            P.op('act' if kc % 2 == 0 else 'dve',
                 (lambda e, kc=kc, ps=ps: e.activation(out=hT[:, kc, :], in_=ps[:], func=AF.Copy)) if kc % 2 == 0 else
                 (lambda e, kc=kc, ps=ps: e.tensor_copy(out=hT[:, kc, :], in_=ps[:])), [ps], [hT])
        if do_l0:
            layer0(t0)
            ple(0, t0)
        if do_l1:
            layer1(t0)
            ple(1, t0)
        norm_stats()
        for k in range(8):
            of = G[k % 4]
            norm_apply(k, 'finn', of[:], of)
            ps = nextps()
            for tb in range(NT):
                P.op('pe', lambda e, tb=tb, ps=ps, of=of: e.transpose(ps[:, tb * 128:(tb + 1) * 128], of[:, tb * 128:(tb + 1) * 128], ident[:]),
                     [of, ident], [ps])
            for tb in range(NT):
                P.op('act' if k % 2 == 0 else 'dve',
                     (lambda e, k=k, ps=ps, tb=tb: e.activation(out=H[tb][:, k * 128:(k + 1) * 128], in_=ps[:, tb * 128:(tb + 1) * 128], func=AF.Copy)) if k % 2 == 0 else
                     (lambda e, k=k, ps=ps, tb=tb: e.tensor_copy(out=H[tb][:, k * 128:(k + 1) * 128], in_=ps[:, tb * 128:(tb + 1) * 128])),
                     [ps], [H[tb]])
        for tb in range(NT):
            P.dma('sp', o_d.ap()[t0 + tb * 128:t0 + (tb + 1) * 128, :], H[tb][:], reads=[H[tb]])
    P.finish()
    print("sbuf bytes/partition:", P.sbytes, "sems:", P.nsem, "instr:", {e: P.cnt[e] for e in ENGS})
    return nc


def kernel(**inputs):
    x = np.asarray(inputs['x'], np.float32)
    p = np.asarray(inputs['p'], np.float32)
    B, S, _ = x.shape
    shared = host_prep(inputs)
    nc = build_program(S)
    in_maps = []
    for b in range(B):
        m = dict(shared)
        m['x'] = np.ascontiguousarray(x[b])
        m['p'] = np.ascontiguousarray(p[:, b])
        in_maps.append(m)
    res = run_bass_kernel_spmd(nc, in_maps, core_ids=list(range(B)))
    return np.stack([np.asarray(r['out'], np.float32) for r in res.results], axis=0)
```

```python
import numpy as np
import concourse.bass as bass
import concourse.mybir as mybir
from concourse.bass_utils import run_bass_kernel_spmd
from contextlib import ExitStack

F32 = mybir.dt.float32
BF16 = mybir.dt.bfloat16
F32R = mybir.dt.float32r
SDT = F32R
AF = mybir.ActivationFunctionType
ALU = mybir.AluOpType
AX = mybir.AxisListType

ENGS = ('pe', 'act', 'dve', 'pool', 'sp')
EIDX = {e: i for i, e in enumerate(ENGS)}
EPOCH = 30000

D = 1024
TT = 512
NT = TT // 128
PLE = 256
DECAY_SCALE = 0.6065306597126334


class TB:
    __slots__ = ('name', 'h', 'lw', 'rd', 'dkey', 'root', 'psum')

    def __init__(self, name, h, root=None, psum=False):
        self.name = name
        self.h = h
        self.lw = None
        self.rd = {}
        self.dkey = None
        self.root = root if root is not None else self
        self.psum = psum or (root is not None and root.psum)

    def __getitem__(self, k):
        return self.h[k]


class _Rec:
    def __init__(self):
        self.call = None

    def __getattr__(self, name):
        def f(*a, **k):
            self.call = (name, a, k)
        return f


class Prog:
    def __init__(self, nc):
        self.nc = nc
        self.es = ExitStack()
        self.ops = {e: [] for e in ENGS}
        self.cnt = {e: 0 for e in ENGS}
        self.seen = {e: {} for e in ENGS}
        self.clk = {e: [] for e in ENGS}
        self.dcnt = {}
        self.dclk = {}
        self.sems = {}
        self.nsem = 0
        self.ndma = 0
        self.sbytes = 0

    def sb(self, name, shape, dt=F32):
        n = 1
        for s in shape[1:]:
            n *= s
        self.sbytes += n * (2 if dt == BF16 else 4)
        return TB(name, self.es.enter_context(self.nc.sbuf_tensor(name, list(shape), dt)))

    def ps(self, name, shape, dt=F32):
        return TB(name, self.es.enter_context(self.nc.psum_tensor(name, list(shape), dt)), psum=True)

    def dram(self, name, shape, dt, kind):
        return TB(name, self.nc.dram_tensor(name, list(shape), dt, kind=kind))

    def _sem(self, key):
        s = self.sems.get(key)
        if s is None:
            s = self.es.enter_context(self.nc.semaphore("s%d" % self.nsem))
            self.nsem += 1
            self.sems[key] = s
        return s

    def _semval(self, key, count):
        if key in EIDX:
            ep = (count - 1) // EPOCH
            return self._sem((key, ep)), count - ep * EPOCH
        return self._sem(key), 16 * count

    def _deps(self, e, reads, writes):
        seen = self.seen[e]
        need = {}

        def req(ev):
            if ev is None:
                return
            k, c = ev
            if k == 'pe' and e == 'pe':
                return
            if seen.get(k, 0) >= c:
                return
            if need.get(k, 0) < c:
                need[k] = c
        for t in reads:
            req(t.lw)
        for t in writes:
            req(t.lw)
            for k, c in t.rd.items():
                req((k, c))
        keys = list(need.keys())
        for k in keys:
            if k not in need:
                continue
            c = need[k]
            ck = self.clk[k][c - 1] if k in EIDX else self.dclk[(k, c)]
            for k2 in keys:
                if k2 != k and k2 in EIDX and k2 in need and ck[EIDX[k2]] >= need[k2]:
                    del need[k2]
        waits = []
        for k, c in need.items():
            waits.append(self._semval(k, c))
            ck = self.clk[k][c - 1] if k in EIDX else self.dclk[(k, c)]
            for e2, v in zip(ENGS, ck):
                if seen.get(e2, 0) < v:
                    seen[e2] = v
            if seen.get(k, 0) < c:
                seen[k] = c
        return waits

    def _snapshot(self, e):
        s = self.seen[e]
        return tuple(s.get(x, 0) for x in ENGS)

    def op(self, e, fn, reads=(), writes=()):
        writes = [t.root for t in writes] + [t.root for t in reads if t.psum]
        reads = [t.root for t in reads if not t.psum]
        rec = _Rec()
        fn(rec)
        fn = rec.call
        waits = self._deps(e, reads, writes)
        self.cnt[e] += 1
        c = self.cnt[e]
        snap = list(self._snapshot(e))
        snap[EIDX[e]] = c
        self.clk[e].append(tuple(snap))
        inc = self._semval(e, c)[0]
        self.ops[e].append((waits, fn, inc, 1))
        ev = (e, c)
        for t in reads:
            if t.rd.get(e, 0) < c:
                t.rd[e] = c
        for t in writes:
            t.lw = ev
            t.rd = {}
        return ev

    def dma(self, q, out, in_, reads=(), writes=(), key=None, **kw):
        if key is None:
            t0 = (list(writes) + list(reads))[0]
            if t0.dkey is None:
                t0.dkey = ('dma', self.ndma)
                self.ndma += 1
            key = t0.dkey
        reads = [t.root for t in reads]
        writes = [t.root for t in writes]
        waits = self._deps(q, reads, writes)
        self.dcnt[key] = self.dcnt.get(key, 0) + 1
        c = self.dcnt[key]
        self.dclk[(key, c)] = self._snapshot(q)
        sem = self._sem(key)
        self.ops[q].append((waits, ('dma_start', (), dict(out=out, in_=in_, **kw)), sem, 16))
        ev = (key, c)
        for t in reads:
            if t.rd.get(key, 0) < c:
                t.rd[key] = c
        for t in writes:
            t.lw = ev
            t.rd = {}
        return ev

    def finish(self):
        e = 'sp'
        tail = []
        for key, c in self.dcnt.items():
            tail.append(self._semval(key, c))
        for x in ENGS:
            if x != e and self.cnt[x] > 0:
                tail.append(self._semval(x, self.cnt[x]))
        blk = self.es.enter_context(self.nc.Block())
        engobj = {'pe': blk.tensor, 'act': blk.scalar, 'dve': blk.vector, 'pool': blk.gpsimd, 'sp': blk.sync}

        def make(ename):
            ops = self.ops[ename]

            def body(eng):
                for waits, fn, inc, n in ops:
                    for (s, v) in waits[1:]:
                        eng.wait_ge(s, v)
                    ins = getattr(eng, fn[0])(*fn[1], **fn[2])
                    if waits:
                        ins._wait_ge(waits[0][0], waits[0][1])
                    ins.then_inc(inc, n)
                if ename == e:
                    for (s, v) in tail:
                        eng.wait_ge(s, v)
            return body
        for ename in ENGS:
            if self.ops[ename] or ename == e:
                engobj[ename](make(ename))
        self.es.close()


PC_SPEC = [
    ('mixn0', 8), ('mixn1', 8), ('pen0', 8), ('pen1', 8), ('finn', 8),
    ('mu_r', 8), ('mu_k', 8), ('mu_v', 8), ('mu_l', 1),
    ('w0', 8), ('a0', 8), ('k_k', 8), ('k_a', 8), ('r_k', 8), ('ln_w', 8), ('ln_b', 8),
    ('mcw0', 8), ('mcw1', 8), ('mcw2', 8), ('mcw3', 8), ('mcb', 8), ('mnorm', 8), ('mskip', 8),
    ('bif_i', 1), ('bif_f', 1),
    ('ccw0', 16), ('ccw1', 16), ('ccw2', 16), ('ccw3', 16), ('ccb', 16), ('cbr', 16), ('cbi', 16), ('clam', 16),
]
PC = {}
_o = 0
for _n, _w in PC_SPEC:
    PC[_n] = _o
    _o += _w
NPC = _o
PD_SPEC = [('omu_r', 8), ('omu_k', 8), ('omu_v', 8), ('omu_l', 1), ('negbf', 1), ('csph', 16), ('cneg', 16), ('t0', 16), ('t1', 16)]
PD = {}
_o = 0
for _n, _w in PD_SPEC:
    PD[_n] = _o
    _o += _w
NPD = _o


def _cols(v):
    v = np.asarray(v, np.float32).reshape(-1)
    return np.ascontiguousarray(v.reshape(-1, 128).T)


def host_consts():
    c = {}
    c['ident'] = np.eye(128, dtype=np.float32)
    bd = np.zeros((128, 128), np.float32)
    bd[:64, :64] = 1
    bd[64:, 64:] = 1
    c['bdones'] = bd
    j = np.arange(128)[:, None]
    i = np.arange(128)[None, :]
    su = ((j < i) & ((j // 64) == (i // 64))).astype(np.float32)
    ui = ((j <= i) & ((j // 64) == (i // 64))).astype(np.float32)
    sl = ((j > i) & ((j // 64) == (i // 64))).astype(np.float32)
    on = np.ones((128, 128), np.float32)
    c['maskA'] = np.stack([su, su, ui, ui], axis=1)
    c['maskB'] = sl
    rm = np.ones((128, TT), np.float32)
    rm[:, ::64] = 0
    c['resetm'] = rm
    c['causal'] = (j <= i).astype(np.float32)
    sel = np.zeros((4, 4, 128), np.float32)
    for h in range(4):
        sel[h, h, :] = 1
    c['sel'] = sel
    return c


def host_prep(inp):
    g = {}
    f = lambda a: np.ascontiguousarray(np.asarray(a, np.float32))

    def chunked(w):
        K, M = w.shape
        return f(w.reshape(K // 128, 128, M // 128, 128).transpose(2, 1, 0, 3))
    g['w0in'] = chunked(f(inp['ab_w_in'])[0])
    g['w0out'] = chunked(f(inp['ab_w_out'])[0])
    g['w1in'] = chunked(f(inp['c_w_in'])[0])
    g['w1out'] = chunked(f(inp['c_w_out'])[0])
    g['pegate'] = np.stack([chunked(f(inp['pe_gate'])[i]) for i in range(2)])
    g['peup'] = np.stack([chunked(f(inp['pe_up'])[i]) for i in range(2)])
    wr = f(inp['c_wr'])[0]
    wi = f(inp['c_wi'])[0]
    g['wri'] = f(np.stack([wr, wi], axis=2))
    lup = np.zeros((128, 2, 1024), np.float32)
    lup[:64, 0, :] = f(inp['rwkv_w_up'])[0]
    lup[64:, 1, :] = f(inp['rwkv_a_up'])[0]
    g['lup'] = lup
    wbd = np.zeros((128, 3, 8, 128), np.float32)
    for t, nm in enumerate(['mlstm_wq', 'mlstm_wk', 'mlstm_wv']):
        w = f(inp[nm])[0]
        for c in range(8):
            for gg in range(32):
                wbd[4 * gg:4 * gg + 4, t, c, 4 * gg:4 * gg + 4] = w[c * 32 + gg]
    g['wqkv'] = wbd
    wif = f(inp['mlstm_w_if'])[0]
    g['wif'] = f(wif.reshape(24, 128, 2, 4).transpose(1, 2, 0, 3))
    pc = np.zeros((128, NPC), np.float32)

    def put(name, v):
        cc = _cols(v)
        pc[:, PC[name]:PC[name] + cc.shape[1]] = cc
    put('mixn0', f(inp['mix_norm'])[0]); put('mixn1', f(inp['mix_norm'])[1])
    put('pen0', f(inp['pe_norm'])[0]); put('pen1', f(inp['pe_norm'])[1])
    put('finn', f(inp['final_norm']))
    mu = f(inp['rwkv_mu'])[0]
    put('mu_r', mu[0]); put('mu_k', mu[1]); put('mu_v', mu[2])
    ml = f(inp['rwkv_mu_lora'])[0]
    put('mu_l', np.concatenate([ml[0], ml[1]]))
    put('w0', f(inp['rwkv_w0'])[0]); put('a0', f(inp['rwkv_a0'])[0])
    put('k_k', f(inp['rwkv_k_k'])[0]); put('k_a', f(inp['rwkv_k_a'])[0])
    put('r_k', f(inp['rwkv_r_k'])[0].reshape(-1))
    put('ln_w', f(inp['rwkv_ln_w'])[0]); put('ln_b', f(inp['rwkv_ln_b'])[0])
    mcw = f(inp['mlstm_conv_w'])[0]
    for j in range(4):
        put('mcw%d' % j, mcw[j])
    put('mcb', f(inp['mlstm_conv_b'])[0])
    put('mnorm', f(inp['mlstm_norm'])[0]); put('mskip', f(inp['mlstm_skip'])[0])
    bif = f(inp['mlstm_b_if'])[0]
    pc[0:4, PC['bif_i']] = bif[0:4]
    pc[0:4, PC['bif_f']] = bif[4:8]
    ccw = f(inp['c_conv_w'])[0]
    for j in range(4):
        put('ccw%d' % j, ccw[j])
    put('ccb', f(inp['c_conv_b'])[0]); put('cbr', f(inp['c_br'])[0]); put('cbi', f(inp['c_bi'])[0])
    put('clam', f(inp['c_lambda'])[0])
    g['pc'] = pc
    g.update(host_consts())
    return g


SHARED_SHAPES = {
    'w0in': [49, 128, 8, 128], 'w0out': [8, 128, 16, 128], 'w1in': [32, 128, 8, 128], 'w1out': [8, 128, 16, 128],
    'pegate': [2, 8, 128, 8, 128], 'peup': [2, 8, 128, 2, 128], 'wri': [16, 128, 2, 128], 'lup': [128, 2, 1024],
    'wqkv': [128, 3, 8, 128], 'wif': [128, 2, 24, 4], 'pc': [128, NPC],
    'ident': [128, 128], 'bdones': [128, 128], 'maskA': [128, 4, 128], 'maskB': [128, 128],
    'resetm': [128, TT], 'causal': [128, 128], 'sel': [4, 4, 128],
}


def build_program(S, do_l0=True, do_rwkv=True, do_mlstm=True, do_l1=True, dbg=None):
    nc = bass.Bass("TRN2", target_bir_lowering=False)
    P = Prog(nc)
    ntiles = S // TT
    din = {}
    for k, shp in SHARED_SHAPES.items():
        din[k] = nc.dram_tensor(k, shp, F32, kind="ExternalInput")
    x_d = nc.dram_tensor("x", [S, D], F32, kind="ExternalInput")
    p_d = nc.dram_tensor("p", [2, S, PLE], F32, kind="ExternalInput")
    o_d = nc.dram_tensor("out", [S, D], F32, kind="ExternalOutput")
    dbg_d = None
    dbg = dbg or []
    if dbg:
        dbg_d = nc.dram_tensor("dbg", [len(dbg), 128, TT], F32, kind="ExternalOutput")

    def dump(name, tb, ap):
        if name in dbg:
            P.dma('sp', dbg_d.ap()[dbg.index(name)], ap, reads=[tb], key=('dma', 'dbg'))

    MUL, ADD, SUB, MAX = ALU.mult, ALU.add, ALU.subtract, ALU.max

    def cload(name, dt=F32, rows=128):
        shp = SHARED_SHAPES[name]
        t = P.sb("c_" + name, shp, dt)
        P.dma('pool' if dt != F32 else 'sp', t[:], din[name].ap(), writes=[t])
        return t
    ident = cload('ident')
    pc = cload('pc')
    pd = P.sb("pd", [128, NPD])
    ones_bf = P.sb("ones_bf", [128, 128], BF16)
    P.op('pool', lambda e: e.memset(ones_bf[:], 1.0), [], [ones_bf])

    def pcc(name, c=0, n=1):
        return pc[:, PC[name] + c:PC[name] + c + n]

    def pdc(name, c=0, n=1):
        return pd[:, PD[name] + c:PD[name] + c + n]

    if do_l1:
        P.op('act', lambda e: e.activation(out=pdc('t0', 0, 16), in_=pcc('clam', 0, 16), func=AF.Exp, scale=-1.0), [pc], [pd])
        P.op('act', lambda e: e.activation(out=pdc('t1', 0, 16), in_=pdc('t0', 0, 16), func=AF.Ln, bias=1.0), [pd], [pd])
        P.op('dve', lambda e: e.tensor_scalar(out=pdc('csph', 0, 16), in0=pdc('t1', 0, 16), scalar1=4.0, scalar2=None, op0=MUL), [pd], [pd])
        P.op('dve', lambda e: e.tensor_scalar(out=pdc('cneg', 0, 16), in0=pdc('t1', 0, 16), scalar1=-8.0, scalar2=None, op0=MUL), [pd], [pd])
    if do_l0:
        for nm, w in (('r', 8), ('k', 8), ('v', 8), ('l', 1)):
            P.op('dve', lambda e, nm=nm, w=w: e.tensor_scalar(out=pdc('omu_' + nm, 0, w), in0=pcc('mu_' + nm, 0, w), scalar1=-1.0, scalar2=1.0, op0=MUL, op1=ADD),
                 [pc], [pd])
        P.op('dve', lambda e: e.tensor_scalar(out=pdc('negbf'), in0=pcc('bif_f'), scalar1=-1.0, scalar2=None, op0=MUL), [pc], [pd])

    scr = {}

    def mkscr(name, src_ap, shape, group=8):
        t = P.dram("scr_" + name, shape, BF16, "Internal")
        nm = shape[0]
        for j0 in range(0, nm, group):
            j1 = min(nm, j0 + group)
            P.dma('pool', t[j0:j1].rearrange("j p k m -> p j k m"), src_ap[j0:j1].rearrange("j p k m -> p j k m"),
                  writes=[t], key=('dma', 'scr_' + name))
        scr[name] = t
        return t
    if do_l0:
        lup_bf = cload('lup', BF16)
        wqkv_bf = cload('wqkv', BF16)
        wif_bf = cload('wif', BF16)
        mkscr('w0in', din['w0in'].ap(), [49, 128, 8, 128])
        mkscr('w0out', din['w0out'].ap(), [8, 128, 16, 128], group=4)
    for i in range(2):
        if (i == 0 and do_l0) or (i == 1 and do_l1):
            mkscr('pegate%d' % i, din['pegate'].ap()[i], [8, 128, 8, 128])
            mkscr('peup%d' % i, din['peup'].ap()[i], [8, 128, 2, 128])
    if do_l1:
        mkscr('w1in', din['w1in'].ap(), [32, 128, 8, 128])
        mkscr('wri', din['wri'].ap(), [16, 128, 2, 128], group=16)
        mkscr('w1out', din['w1out'].ap(), [8, 128, 16, 128], group=4)

    NWB = 5
    wring = [P.sb("wb%d" % i, [128, 16, 128], BF16) for i in range(NWB)]
    wstate = {'i': 0}

    def wld(name, j, nk):
        wb = wring[wstate['i'] % NWB]
        wstate['i'] += 1
        P.dma('sp', wb[:, 0:nk, :], scr[name][j], reads=[scr[name]], writes=[wb])
        return wb

    PS = [P.ps("ps%d" % i, [128, TT]) for i in range(8)]
    pstate = {'i': 0}

    pstate['set'] = list(range(8))

    def nextps():
        s_ = pstate['set']
        b = PS[s_[pstate['i'] % len(s_)]]
        pstate['i'] += 1
        return b

    hT = P.sb("hT", [128, 8, TT])
    xnT = P.sb("xnT", [128, 8, TT], BF16)
    merged = P.sb("merged", [128, 16, TT], BF16)
    ext = P.sb("ext", [128, TT + 3])
    NG, NH = 20, 12
    G = [P.sb("g%d" % i, [128, TT]) for i in range(NG)]
    H = [P.sb("h%d" % i, [128, 2 * TT]) for i in range(NH)]
    lnv, rstd = G[19], G[18]
    _alias = {}

    def RV(tb):
        if SDT == F32:
            return tb[:]
        a = _alias.get(tb.name)
        if a is None:
            ml = nc.lookup_mloc(tb.h)
            a = nc.alloc_sbuf_tensor_at(tb.name + "_r", [128, int(ml.dims[1]) // 4], F32R, offset=int(ml.addr))
            _alias[tb.name] = a
        return a[:]
    ntmp = [G[16], G[17]]

    def bfv(tb, n=None):
        v = tb[:].bitcast(BF16)
        return v

    if do_l1:
        l1_tail = P.sb("l1_tail", [128, 16, 3])
        l1_h = P.sb("l1_h", [128, 16])
        P.op('pool', lambda e: e.memset(l1_tail[:], 0.0), [], [l1_tail])
        P.op('pool', lambda e: e.memset(l1_h[:], 0.0), [], [l1_h])

    def norm_apply(k, gname, out_ap, out_tb):
        if k % 2 == 0:
            P.op('dve', lambda e: e.scalar_tensor_tensor(out=out_ap, in0=hT[:, k, :], scalar=pcc(gname, k), in1=rstd[:],
                                                         op0=MUL, op1=MUL), [hT, pc, rstd], [out_tb])
        else:
            nt = ntmp[(k // 2) % 2]
            P.op('act', lambda e: e.activation(out=nt[:], in_=hT[:, k, :], func=AF.Identity, scale=pcc(gname, k)), [hT, pc], [nt])
            P.op('pool', lambda e: e.tensor_tensor(out=out_ap, in0=nt[:], in1=rstd[:], op=MUL), [nt, rstd], [out_tb])

    def norm_stats():
        ps = nextps()
        for half in range(2):
            sq = H[4 + half]
            sqv = bfv(sq).rearrange("p (k t) -> p k t", k=4)
            P.op('act', lambda e, half=half, sqv=sqv: e.activation(out=sqv, in_=hT[:, 4 * half:4 * half + 4, :], func=AF.Square), [hT], [sq])
            for k in range(4):
                P.op('pe', lambda e, k=k, half=half, sqv=sqv: e.matmul(ps[:], lhsT=ones_bf[:], rhs=sqv[:, k, :], start=(half == 0 and k == 0), stop=(half == 1 and k == 3)),
                     [ones_bf, sq], [ps])
        P.op('act', lambda e: e.activation(out=lnv[:], in_=ps[:], func=AF.Ln, scale=1.0 / D, bias=1e-6), [ps], [lnv])
        P.op('act', lambda e: e.activation(out=rstd[:], in_=lnv[:], func=AF.Exp, scale=-0.5), [lnv], [rstd])

    def rmsnorm_x(gname):
        norm_stats()
        for k in range(8):
            norm_apply(k, gname, xnT[:, k, :], xnT)

    def proj(ps, wb, nk, rhs_tb, rhs_fn):
        for k in range(nk):
            P.op('pe', lambda e, k=k: e.matmul(ps[:], lhsT=wb[:, k, :], rhs=rhs_fn(k), start=(k == 0), stop=(k == nk - 1)),
                 [wb, rhs_tb], [ps])

    def proj_x(name, j):
        w = wld(name, j, 8)
        ps = nextps()
        proj(ps, w, 8, xnT, lambda k: xnT[:, k, :])
        return ps

    def ple(i, t0):
        ptok = H[0]
        ptv = ptok[:].rearrange("p (tb d) -> p tb d", tb=NT)
        pTt = G[12]
        pT = bfv(pTt).rearrange("p (k t) -> p k t", k=2)
        sgs, tmps = [G[0], G[1]], [G[2], G[3]]
        P.dma('sp', ptv, p_d.ap()[i, t0:t0 + TT, :].rearrange("(tb p) d -> p tb d", p=128), writes=[ptok])
        for kc in range(2):
            ps = nextps()
            for tb in range(NT):
                P.op('pe', lambda e, kc=kc, tb=tb, ps=ps: e.transpose(ps[:, tb * 128:(tb + 1) * 128], ptv[:, tb, kc * 128:(kc + 1) * 128], ident[:]),
                     [ptok, ident], [ps])
            P.op('act', lambda e, kc=kc, ps=ps: e.activation(out=pT[:, kc, :], in_=ps[:], func=AF.Copy), [ps], [pTt])
        rmsnorm_x('pen%d' % i)
        for m in range(8):
            sg, tmpa = sgs[m % 2], tmps[m % 2]
            wg = wld('pegate%d' % i, m, 8)
            wu = wld('peup%d' % i, m, 2)
            ps = nextps()
            proj(ps, wg, 8, xnT, lambda k: xnT[:, k, :])
            P.op('act', lambda e, ps=ps: e.activation(out=sg[:], in_=ps[:], func=AF.Sigmoid), [ps], [sg])
            ps2 = nextps()
            proj(ps2, wu, 2, pTt, lambda k: pT[:, k, :])
            P.op('dve', lambda e, ps2=ps2: e.tensor_tensor(out=tmpa[:], in0=ps2[:], in1=sg[:], op=MUL), [ps2, sg], [tmpa])
            P.op('pool', lambda e, m=m: e.tensor_tensor(out=hT[:, m, :], in0=hT[:, m, :], in1=tmpa[:], op=ADD), [hT, tmpa], [hT])

    def out_proj(name):
        for m in range(8):
            wo = wld(name, m, 16)
            ps = nextps()
            proj(ps, wo, 16, merged, lambda k: merged[:, k, :])
            P.op('dve', lambda e, m=m, ps=ps: e.tensor_tensor(out=hT[:, m, :], in0=hT[:, m, :], in1=ps[:], op=ADD), [hT, ps], [hT])

    def conv4(ps, c, tail_tb, wname, bname, acc):
        P.op('pool', lambda e: e.tensor_copy(out=ext[:, 0:3], in_=tail_tb[:, c, :]), [tail_tb], [ext])
        P.op('act', lambda e: e.activation(out=ext[:, 3:TT + 3], in_=ps[:], func=AF.Copy), [ps], [ext])
        P.op('act', lambda e: e.activation(out=acc[:], in_=ps[:], func=AF.Identity, scale=pcc(wname + '3', c), bias=pcc(bname, c)), [ps, pc], [acc])
        P.op('pool', lambda e: e.tensor_copy(out=tail_tb[:, c, :], in_=ext[:, TT:TT + 3]), [ext], [tail_tb])
        for j in range(3):
            P.op('dve', lambda e, j=j: e.scalar_tensor_tensor(out=acc[:], in0=ext[:, j:j + TT], scalar=pcc(wname + str(j), c), in1=acc[:],
                                                              op0=MUL, op1=ADD), [ext, pc, acc], [acc])

    def layer1(t0):
        rmsnorm_x('mixn1')
        for c in range(16):
            xc32, xcb, rg, ig, a2, th = G[6 * (c % 2):6 * (c % 2) + 6]
            xcb_ap = bfv(xcb)[:, 0:TT]
            ps = proj_x('w1in', c)
            conv4(ps, c, l1_tail, 'ccw', 'ccb', xc32)
            P.op('act', lambda e: e.activation(out=xcb_ap, in_=xc32[:], func=AF.Copy), [xc32], [xcb])
            wb = wld('wri', c, 2)
            ps_r = nextps()
            P.op('pe', lambda e: e.matmul(ps_r[:], lhsT=wb[:, 0, :], rhs=xcb_ap, start=True, stop=True), [wb, xcb], [ps_r])
            P.op('act', lambda e: e.activation(out=rg[:], in_=ps_r[:], func=AF.Sigmoid, bias=pcc('cbr', c)), [ps_r, pc], [rg])
            ps_i = nextps()
            P.op('pe', lambda e: e.matmul(ps_i[:], lhsT=wb[:, 1, :], rhs=xcb_ap, start=True, stop=True), [wb, xcb], [ps_i])
            P.op('act', lambda e: e.activation(out=ig[:], in_=ps_i[:], func=AF.Sigmoid, bias=pcc('cbi', c)), [ps_i, pc], [ig])
            P.op('act', lambda e: e.activation(out=a2[:], in_=rg[:], func=AF.Exp, scale=pdc('cneg', c)), [rg, pd], [a2])
            P.op('act', lambda e: e.activation(out=th[:], in_=rg[:], func=AF.Tanh, scale=pdc('csph', c)), [rg, pd], [th])
            P.op('dve', lambda e: e.scalar_tensor_tensor(out=th[:], in0=a2[:], scalar=1.0, in1=th[:], op0=ADD, op1=MUL), [a2, th], [th])
            P.op('dve', lambda e: e.tensor_scalar(out=a2[:], in0=th[:], scalar1=-1.0, scalar2=1.0, op0=MUL, op1=ADD), [th], [a2])
            P.op('dve', lambda e: e.scalar_tensor_tensor(out=th[:], in0=a2[:], scalar=1.0, in1=th[:], op0=ADD, op1=MUL), [a2, th], [th])
            P.op('act', lambda e: e.activation(out=th[:], in_=th[:], func=AF.Sqrt), [th], [th])
            P.op('pool', lambda e: e.tensor_tensor(out=ig[:], in0=xc32[:], in1=ig[:], op=MUL), [xc32, ig], [ig])
            P.op('pool', lambda e: e.tensor_tensor(out=ig[:], in0=ig[:], in1=th[:], op=MUL), [ig, th], [ig])
            P.op('dve', lambda e: e.tensor_tensor_scan(out=rg[:], data0=a2[:], data1=ig[:], initial=l1_h[:, c:c + 1], op0=MUL, op1=ADD),
                 [a2, ig, l1_h], [rg])
            P.op('pool', lambda e: e.tensor_copy(out=l1_h[:, c:c + 1], in_=rg[:, TT - 1:TT]), [rg], [l1_h])
            ps_g = proj_x('w1in', 16 + c)
            P.op('act', lambda e: e.activation(out=xc32[:], in_=ps_g[:], func=AF.Silu), [ps_g], [xc32])
            P.op('dve', lambda e: e.tensor_tensor(out=merged[:, c, :], in0=rg[:], in1=xc32[:], op=MUL), [rg, xc32], [merged])
        out_proj('w1out')

    if do_l0 and do_rwkv:
        identr = P.sb("identr", [128, 128], SDT)
        bdones = cload('bdones')
        bdones_bf = P.sb("bdones_bf", [128, 128], BF16)
        bdones_r = P.sb("bdones_r", [128, 128], SDT)
        maskA = cload('maskA')
        maskB = cload('maskB')
        resetm = cload('resetm')
        P.op('dve', lambda e: e.tensor_copy(out=identr[:], in_=ident[:]), [ident], [identr])
        P.op('dve', lambda e: e.tensor_copy(out=bdones_bf[:], in_=bdones[:]), [bdones], [bdones_bf])
        P.op('dve', lambda e: e.tensor_copy(out=bdones_r[:], in_=bdones[:]), [bdones], [bdones_r])
        rwc = P.sb("rwc", [128, 25])
        P.op('pool', lambda e: e.memset(rwc[:], 0.0), [], [rwc])
        lora_bf = P.sb("lora_bf", [128, TT], BF16)
        Sst = [[P.sb("S%d_%d" % (c, q), [128, 128], F32) for q in range(2)] for c in range(8)]
        for c in range(8):
            P.op('pool', lambda e, c=c: e.memset(Sst[c][0][:], 0.0), [], [Sst[c][0]])
        spar = [0] * 8
        rhs_sb = P.sb("rhs_sb", [128, 128], SDT)
        u_sb = P.sb("u_sb", [128, 128], SDT)
        psRHS = TB("psRHS", PS[7].h[:, 0:128], root=PS[7])
        psU = TB("psU", PS[7].h[:, 128:256], root=PS[7])
        psYT = TB("psYT", PS[7].h[:, 256:384], root=PS[7])
        psSN = TB("psSN", PS[7].h[:, 384:512], root=PS[7])

    def tshift(ps, dst, col, mu_ap, omu_ap):
        P.op('act', lambda e: e.activation(out=dst[:], in_=ps[:], func=AF.Identity, scale=omu_ap), [ps, pd], [dst])
        P.op('dve', lambda e: e.scalar_tensor_tensor(out=dst[:, 1:TT], in0=ps[:, 0:TT - 1], scalar=mu_ap, in1=dst[:, 1:TT], op0=MUL, op1=ADD),
             [ps, pc, dst], [dst])
        P.op('dve', lambda e: e.scalar_tensor_tensor(out=dst[:, 0:1], in0=rwc[:, col:col + 1], scalar=mu_ap, in1=dst[:, 0:1], op0=MUL, op1=ADD),
             [rwc, pc, dst], [dst])
        P.op('act', lambda e: e.activation(out=rwc[:, col:col + 1], in_=ps[:, TT - 1:TT], func=AF.Copy), [ps], [rwc])

    def r3(ap, n=8):
        return ap.rearrange("p (n t) -> p n t", n=n)

    def rwkv(t0):
        Rbd, Abd, Bbd, Kbd, Vbd = H[0:5]
        for i in range(5):
            P.op('pool', lambda e, i=i: e.memset(H[i][:], 0.0), [], [H[i]])
        bdv = [r3(RV(H[i])) for i in range(5)]
        ps = proj_x('w0in', 24)
        lora32 = G[16]
        tshift(ps, lora32, 24, pcc('mu_l'), pdc('omu_l'))
        P.op('act', lambda e: e.activation(out=lora_bf[0:64, :], in_=lora32[0:64, :], func=AF.Tanh), [lora32], [lora_bf])
        P.op('act', lambda e: e.activation(out=lora_bf[64:128, :], in_=lora32[64:128, :], func=AF.Copy), [lora32], [lora_bf])
        pend = []

        def pump(n):
            for _ in range(n):
                if pend:
                    pend.pop(0)()
        for th_ in rwkv_prep(0):
            th_()
        rwkv_blockdiag(0, bdv)
        for c in range(8):
            if c + 1 < 8:
                pend.extend(rwkv_prep(c + 1))
            rwkv_chunks(c, bdv, pump)
            pump(len(pend))
            if c + 1 < 8:
                rwkv_blockdiag(c + 1, bdv)
            rwkv_out(c)

    def rw_bufs(c):
        p_ = c % 2
        d = dict(r32=G[0], k32=G[1], v32=G[2], cump=G[6], Ginv=G[8], Gp=G[9], a32=G[10], kk32=G[11], kka=G[11],
                 rn=G[12], nkk=G[13], keff=G[15], t1=G[16])
        d.update(dict(gr32=(G[3], G[14])[p_], lw=(G[4], G[17])[p_], cum=(G[5], G[18])[p_], Gt=(G[7], G[19])[p_]))
        return d

    def rwkv_prep(c):
        B_ = rw_bufs(c)
        r32, k32, v32, gr32, lw, cum, cump, Gt, Ginv, Gp = [B_[k] for k in ('r32', 'k32', 'v32', 'gr32', 'lw', 'cum', 'cump', 'Gt', 'Ginv', 'Gp')]
        a32, kk32, kka, rn, nkk, keff, t1 = [B_[k] for k in ('a32', 'kk32', 'kka', 'rn', 'nkk', 'keff', 't1')]
        bon = cum
        T = []

        def projshift(dst, mch, nm, col):
            def f():
                ps = proj_x('w0in', mch)
                tshift(ps, dst, col, pcc('mu_' + nm, c), pdc('omu_' + nm, c))
            return f
        T.append(projshift(k32, 8 + c, 'k', 8 + c))

        def f_lw():
            ps = nextps()
            P.op('pe', lambda e: e.matmul(ps[:], lhsT=lup_bf[:, 0, c * 128:(c + 1) * 128], rhs=lora_bf[:], start=True, stop=True), [lup_bf, lora_bf], [ps])
            P.op('act', lambda e: e.activation(out=lw[:], in_=ps[:], func=AF.Sigmoid, bias=pcc('w0', c)), [ps, pc], [lw])
        T.append(f_lw)

        def f_a():
            ps = nextps()
            P.op('pe', lambda e: e.matmul(ps[:], lhsT=lup_bf[:, 1, c * 128:(c + 1) * 128], rhs=lora_bf[:], start=True, stop=True), [lup_bf, lora_bf], [ps])
            P.op('act', lambda e: e.activation(out=a32[:], in_=ps[:], func=AF.Sigmoid, bias=pcc('a0', c)), [ps, pc], [a32])
        T.append(f_a)
        T.append(lambda: P.op('dve', lambda e: e.tensor_tensor_scan(out=cum[:], data0=resetm[:], data1=lw[:], initial=0.0, op0=MUL, op1=ADD), [resetm, lw], [cum]))
        T.append(lambda: P.op('dve', lambda e: e.tensor_tensor(out=cump[:], in0=cum[:], in1=lw[:], op=SUB), [cum, lw], [cump]))
        T.append(lambda: P.op('act', lambda e: e.activation(out=Gt[:], in_=cum[:], func=AF.Exp, scale=-DECAY_SCALE), [cum], [Gt]))
        T.append(lambda: P.op('act', lambda e: e.activation(out=Ginv[:], in_=cum[:], func=AF.Exp, scale=DECAY_SCALE), [cum], [Ginv]))
        T.append(lambda: P.op('act', lambda e: e.activation(out=Gp[:], in_=cump[:], func=AF.Exp, scale=-DECAY_SCALE), [cump], [Gp]))
        T.append(lambda: P.op('dve', lambda e: e.tensor_scalar(out=kk32[:], in0=k32[:], scalar1=pcc('k_k', c), scalar2=None, op0=MUL), [k32, pc], [kk32]))
        sqk = bfv(t1)[:, 0:TT]
        T.append(lambda: P.op('act', lambda e: e.activation(out=sqk, in_=kk32[:], func=AF.Square), [kk32], [t1]))

        def f_rn():
            ps = nextps()
            P.op('pe', lambda e: e.matmul(ps[:], lhsT=bdones_bf[:], rhs=sqk, start=True, stop=True), [bdones_bf, t1], [ps])
            P.op('act', lambda e: e.activation(out=rn[:], in_=ps[:], func=AF.Ln, bias=1e-20), [ps], [rn])
            P.op('act', lambda e: e.activation(out=rn[:], in_=rn[:], func=AF.Exp, scale=-0.5), [rn], [rn])
        T.append(f_rn)
        T.append(lambda: P.op('dve', lambda e: e.scalar_tensor_tensor(out=nkk[:], in0=kk32[:], scalar=-1.0, in1=rn[:], op0=MUL, op1=MUL), [kk32, rn], [nkk]))
        T.append(lambda: P.op('dve', lambda e: e.scalar_tensor_tensor(out=kka[:], in0=nkk[:], scalar=-1.0, in1=a32[:], op0=MUL, op1=MUL), [nkk, a32], [kka]))
        T.append(lambda: P.op('pool', lambda e: e.tensor_scalar(out=t1[:], in0=a32[:], scalar1=-1.0, scalar2=pcc('k_a', c), op0=ADD, op1=MUL), [a32, pc], [t1]))
        T.append(lambda: P.op('dve', lambda e: e.scalar_tensor_tensor(out=keff[:], in0=t1[:], scalar=1.0, in1=k32[:], op0=ADD, op1=MUL), [t1, k32], [keff]))
        T.append(projshift(r32, c, 'r', c))
        rkr = bfv(t1)[:, 0:TT]
        T.append(lambda: P.op('dve', lambda e: e.scalar_tensor_tensor(out=rkr, in0=r32[:], scalar=pcc('r_k', c), in1=keff[:], op0=MUL, op1=MUL), [r32, pc, keff], [t1]))
        T.append(projshift(v32, 16 + c, 'v', 16 + c))

        def f_bon():
            ps = nextps()
            P.op('pe', lambda e: e.matmul(ps[:], lhsT=bdones_bf[:], rhs=rkr, start=True, stop=True), [bdones_bf, t1], [ps])
            P.op('dve', lambda e: e.tensor_tensor(out=bon[:], in0=ps[:], in1=v32[:], op=MUL), [ps, v32], [bon])
        T.append(f_bon)

        def f_gr():
            ps = proj_x('w0in', 33 + c)
            P.op('act', lambda e: e.activation(out=gr32[:], in_=ps[:], func=AF.Silu), [ps], [gr32])
        T.append(f_gr)
        return T

    def rwkv_blockdiag(c, bdv):
        B_ = rw_bufs(c)
        Rv, Av, Bv, Kv, Vv = bdv
        Rbd, Abd, Bbd, Kbd, Vbd = H[0:5]
        for hh in range(2):
            hs_ = slice(hh * 64, hh * 64 + 64)
            for i_, (dstv, dtb, a_, b_) in enumerate(((Rv, Rbd, B_['r32'], B_['Gt']), (Av, Abd, B_['nkk'], B_['Gp']), (Bv, Bbd, B_['kka'], B_['Ginv']), (Kv, Kbd, B_['keff'], B_['Ginv']))):
                P.op('dve' if (i_ + hh) % 2 == 0 else 'pool', lambda e, dstv=dstv, a_=a_, b_=b_, hs_=hs_: e.tensor_tensor(out=dstv[hs_, :, hs_], in0=r3(a_[hs_, :]), in1=r3(b_[hs_, :]), op=MUL),
                     [a_, b_], [dtb])
            P.op('act', lambda e, hs_=hs_: e.activation(out=Vv[hs_, :, hs_], in_=r3(B_['v32'][hs_, :]), func=AF.Copy), [B_['v32']], [Vbd])

    def rwkv_chunks(c, bdv, pump):
        B_ = rw_bufs(c)
        Gt, y32 = B_['Gt'], B_['lw']
        Rv, Av, Bv, Kv, Vv = bdv
        Rbd, Abd, Bbd, Kbd, Vbd = H[0:5]
        for hf in range(2):
            SCt = [H[5], H[6]]
            QTt = [H[7], H[8]]
            SCv = [RV(t).rearrange("p (j m k) -> p j m k", j=2, m=4) for t in SCt]
            QTv = [RV(t).rearrange("p (j m k) -> p j m k", j=2, m=4) for t in QTt]
            SCf = [t[:].rearrange("p (j m k) -> p j m k", j=2, m=4) for t in SCt]
            PQt = [H[9], H[10]]
            PQv = [[RV(PQt[b]).rearrange("p (q j m k) -> p q j m k", q=2, j=2, m=2)[:, par] for b in range(2)] for par in range(2)]
            Xt = H[11]
            Xv = [RV(Xt).rearrange("p (q j k) -> p q j k", q=2, j=4)[:, q] for q in range(2)]
            Xf = [Xt[:].rearrange("p (q j k) -> p q j k", q=2, j=4)[:, q] for q in range(2)]
            for j in range(4):
                n = hf * 4 + j
                sc = SCv[j // 2][:, j % 2]
                qt = QTv[j // 2][:, j % 2]
                A_, B2_ = PS[2], PS[3]
                for m, (l_, r_) in enumerate(((Bv, Av), (Kv, Av), (Bv, Rv), (Kv, Rv))):
                    P.op('pe', lambda e, m=m, l_=l_, r_=r_, n=n: e.matmul(A_[:, m * 128:(m + 1) * 128], lhsT=l_[:, n, :], rhs=r_[:, n, :], start=True, stop=True),
                         [Rbd, Abd, Bbd, Kbd], [A_])
                P.op('dve', lambda e, sc=sc: e.tensor_tensor(out=sc, in0=A_[:].rearrange("p (m k) -> p m k", m=4), in1=maskA[:], op=MUL),
                     [A_, maskA], [SCt[j // 2]])
                P.op('pe', lambda e, n=n: e.matmul(B2_[:, 0:128], lhsT=Av[:, n, :], rhs=Bv[:, n, :], start=True, stop=True), [Abd, Bbd], [B2_])
                for m, l_ in enumerate((Bv, Kv, Vv)):
                    P.op('pe', lambda e, m=m, l_=l_, n=n: e.matmul(B2_[:, (m + 1) * 128:(m + 2) * 128], lhsT=l_[:, n, :], rhs=identr[:], start=True, stop=True),
                         [Bbd, Kbd, Vbd, identr], [B2_])
                P.op('dve', lambda e, qt=qt: e.tensor_tensor(out=qt[:, 0, :], in0=B2_[:, 0:128], in1=maskB[:], op=MUL), [B2_, maskB], [QTt[j // 2]])
                P.op('act', lambda e, qt=qt: e.activation(out=qt[:, 1:4, :], in_=B2_[:, 128:512].rearrange("p (m k) -> p m k", m=3), func=AF.Copy),
                     [B2_], [QTt[j // 2]])
                pump(2)
            for j in range(4):
                P.op('dve', lambda e, j=j: e.tensor_tensor(out=Xv[1][:, j, :], in0=SCf[j // 2][:, j % 2, 0, :], in1=ident[:], op=ADD),
                     [SCt[j // 2], ident], [Xt])
            for lvl in range(1, 6):
                par = lvl % 2
                for b in range(2):
                    bank = PS[4 + b]
                    for jj in range(2):
                        j = 2 * b + jj
                        if lvl == 1:
                            Pp = SCv[j // 2][:, j % 2, 0, :]
                            Qp = QTv[j // 2][:, j % 2, 0, :]
                            rd = [SCt[j // 2], QTt[j // 2]]
                        else:
                            Pp = PQv[1 - par][b][:, jj, 0, :]
                            Qp = PQv[1 - par][b][:, jj, 1, :]
                            rd = [PQt[b]]
                        P.op('pe', lambda e, jj=jj, Pp=Pp, Qp=Qp, bank=bank: e.matmul(bank[:, (2 * jj) * 128:(2 * jj + 1) * 128], lhsT=Qp, rhs=Pp, start=True, stop=True), rd, [bank])
                        P.op('pe', lambda e, jj=jj, Pp=Pp, Qp=Qp, bank=bank: e.matmul(bank[:, (2 * jj + 1) * 128:(2 * jj + 2) * 128], lhsT=Pp, rhs=Qp, start=True, stop=True), rd, [bank])
                    if b == 0:
                        P.op('act', lambda e, b=b, bank=bank, par=par: e.activation(out=PQv[par][b], in_=bank[:].rearrange("p (j m k) -> p j m k", j=2, m=2), func=AF.Copy),
                             [bank], [PQt[b]])
                    else:
                        P.op('dve', lambda e, b=b, bank=bank, par=par: e.tensor_copy(out=PQv[par][b], in_=bank[:].rearrange("p (j m k) -> p j m k", j=2, m=2)),
                             [bank], [PQt[b]])
                XB = PS[6]
                for j in range(4):
                    Ql = PQv[par][j // 2][:, j % 2, 1, :]
                    P.op('pe', lambda e, j=j, Ql=Ql, par=par: e.matmul(XB[:, j * 128:(j + 1) * 128], lhsT=Ql, rhs=Xv[par][:, j, :], start=True, stop=True), [PQt[j // 2], Xt], [XB])
                P.op('dve', lambda e, par=par: e.tensor_tensor(out=Xv[1 - par], in0=XB[:].rearrange("p (j k) -> p j k", j=4), in1=Xf[par], op=ADD), [XB, Xt], [Xt])
                pump(1)
            for j in range(4):
                n = hf * 4 + j
                sc = SCv[j // 2][:, j % 2]
                qt = QTv[j // 2][:, j % 2]
                sct, qtt = SCt[j // 2], QTt[j // 2]
                S0 = Sst[c][spar[c]]
                S1 = Sst[c][1 - spar[c]]
                spar[c] = 1 - spar[c]
                P.op('pe', lambda e, n=n, S0=S0: e.matmul(psRHS[:], lhsT=Av[:, n, :], rhs=RV(S0), start=True, stop=False), [Abd, S0], [psRHS])
                P.op('pe', lambda e, sc=sc, qt=qt: e.matmul(psRHS[:], lhsT=sc[:, 1, :], rhs=qt[:, 3, :], start=False, stop=True), [sct, qtt], [psRHS])
                P.op('act', lambda e: e.activation(out=rhs_sb[:], in_=psRHS[:], func=AF.Copy), [psRHS], [rhs_sb])
                P.op('pe', lambda e, j=j: e.matmul(psU[:], lhsT=Xv[0][:, j, :], rhs=rhs_sb[:], start=True, stop=True), [Xt, rhs_sb], [psU])
                P.op('act', lambda e: e.activation(out=u_sb[:], in_=psU[:], func=AF.Copy), [psU], [u_sb])
                P.op('pe', lambda e, qt=qt: e.matmul(psSN[:], lhsT=qt[:, 1, :], rhs=u_sb[:], start=True, stop=False), [qtt, u_sb], [psSN])
                P.op('pe', lambda e, qt=qt: e.matmul(psSN[:], lhsT=qt[:, 2, :], rhs=qt[:, 3, :], start=False, stop=False), [qtt], [psSN])
                P.op('pe', lambda e, S0=S0: e.matmul(psSN[:], lhsT=identr[:], rhs=RV(S0), start=False, stop=True), [identr, S0], [psSN])
                P.op('act', lambda e, n=n, S1=S1: e.activation(out=RV(S1), in_=psSN[:], func=AF.Identity, scale=Gt[:, n * 64 + 63:n * 64 + 64]), [psSN, Gt], [S1])
                P.op('pe', lambda e, n=n, S0=S0: e.matmul(psYT[:], lhsT=RV(S0), rhs=Rv[:, n, :], start=True, stop=False), [S0, Rbd], [psYT])
                P.op('pe', lambda e, sc=sc: e.matmul(psYT[:], lhsT=u_sb[:], rhs=sc[:, 2, :], start=False, stop=False), [u_sb, sct], [psYT])
                P.op('pe', lambda e, sc=sc, qt=qt: e.matmul(psYT[:], lhsT=qt[:, 3, :], rhs=sc[:, 3, :], start=False, stop=True), [qtt, sct], [psYT])
                for hh in range(2):
                    hs_ = slice(hh * 64, hh * 64 + 64)
                    P.op('act', lambda e, hs_=hs_, n=n: e.activation(out=y32[hs_, n * 64:(n + 1) * 64], in_=psYT[hs_, hs_], func=AF.Copy), [psYT], [y32])
                pump(2)

    def rwkv_out(c):
        B_ = rw_bufs(c)
        y32, dd, bon, gr32 = B_['lw'], B_['Gt'], B_['cum'], B_['gr32']
        dump('y_rwkv%d' % c, y32, y32[:])
        yrv = RV(y32)
        P.op('act', lambda e: e.activation(out=yrv, in_=y32[:], func=AF.Copy), [y32], [y32])
        ps = nextps()
        P.op('pe', lambda e: e.matmul(ps[:], lhsT=bdones_r[:], rhs=yrv, start=True, stop=True), [bdones_r, y32], [ps])
        P.op('dve', lambda e: e.scalar_tensor_tensor(out=dd[:], in0=ps[:], scalar=-1.0 / 64, in1=y32[:], op0=MUL, op1=ADD), [ps, y32], [dd])
        sq, rs = H[5], H[6]
        sqv = RV(sq)[:, 0:TT]
        P.op('act', lambda e: e.activation(out=sqv, in_=dd[:], func=AF.Square), [dd], [sq])
        ps2 = nextps()
        P.op('pe', lambda e: e.matmul(ps2[:], lhsT=bdones_r[:], rhs=sqv, start=True, stop=True), [bdones_r, sq], [ps2])
        P.op('act', lambda e: e.activation(out=rs[:, 0:TT], in_=ps2[:], func=AF.Ln, scale=1.0 / 64, bias=64e-5), [ps2], [rs])
        P.op('act', lambda e: e.activation(out=rs[:, 0:TT], in_=rs[:, 0:TT], func=AF.Exp, scale=-0.5), [rs], [rs])
        P.op('dve', lambda e: e.tensor_tensor(out=dd[:], in0=dd[:], in1=rs[:, 0:TT], op=MUL), [dd, rs], [dd])
        P.op('act', lambda e: e.activation(out=dd[:], in_=dd[:], func=AF.Identity, scale=pcc('ln_w', c), bias=pcc('ln_b', c)), [dd, pc], [dd])
        P.op('pool', lambda e: e.tensor_tensor(out=dd[:], in0=dd[:], in1=bon[:], op=ADD), [dd, bon], [dd])
        P.op('dve', lambda e: e.tensor_tensor(out=merged[:, c, :], in0=dd[:], in1=gr32[:], op=MUL), [dd, gr32], [merged])

    if do_l0 and do_mlstm:
        causal = cload('causal')
        sel = P.sb("c_sel", [4, 4, 128])
        P.dma('sp', sel[:], din['sel'].ap(), writes=[sel])
        onesf = P.sb("onesf", [128, TT])
        P.op('pool', lambda e: e.memset(onesf[:], 1.0), [], [onesf])
        ones_r = P.sb("ones_r", [128, 128], SDT)
        P.op('dve', lambda e: e.tensor_copy(out=ones_r[:], in_=onesf[:, 0:128]), [onesf], [ones_r])
        m_tail = P.sb("m_tail", [128, 8, 3])
        P.op('pool', lambda e: e.memset(m_tail[:], 0.0), [], [m_tail])
        CT32 = P.sb("CT32", [128, 4, 2, 256])
        CTbf = P.sb("CTbf", [128, 4, 2, 256], BF16)
        nr32 = P.sb("nr32", [128, 4, 2, 128])
        nrbf = P.sb("nrbf", [128, 4, 2, 128], BF16)
        for t_ in (CT32, CTbf, nr32, nrbf):
            P.op('pool', lambda e, t_=t_: e.memset(t_[:], 0.0), [], [t_])
        mcar = P.sb("mcar", [4, 2])
        P.op('pool', lambda e: e.memset(mcar[:], 0.0), [], [mcar])
        MxE = P.sb("MxE", [4, 5])
        dec = P.sb("dec", [4, 4])
        smallb = P.sb("smallb", [128, 32])
        qe = P.sb("qe", [128, 2, 128], BF16)
        qw = P.sb("qw", [128, 2, 128], BF16)
        kg = P.sb("kg", [128, 2, 128], BF16)
        kgt = P.sb("kgt", [128, 256], BF16)
        st_bf = P.sb("st_bf", [128, 128], BF16)
        ddm = P.sb("ddm", [128, 128])
        recm = P.sb("recm", [128, 128])

    def mlstm(t0):
        qTt, kTt, vTt = [H[0], H[1]], [H[2], H[3]], [H[4], H[5]]
        ktt, vtt = [H[6], H[7]], [H[8], H[9]]
        xct = [H[10], H[11]]

        def fm(tl, c):
            return bfv(tl[c // 4]).rearrange("p (k t) -> p k t", k=4)[:, c % 4, :]

        def tokv(tl, tb):
            return bfv(tl[tb // 2]).rearrange("p (b d) -> p b d", b=2)[:, tb % 2, :]
        acc, xmb = G[9], G[10]
        xmb_ap = bfv(xmb)[:, 0:TT]
        gi_ps, gf_ps = PS[2], PS[3]
        for c in range(8):
            ps = proj_x('w0in', 25 + c)
            conv4(ps, c, m_tail, 'mcw', 'mcb', acc)
            P.op('act', lambda e, c=c: e.activation(out=fm(xct, c), in_=acc[:], func=AF.Silu), [acc], [xct[c // 4]])
            P.op('pool', lambda e: e.tensor_copy(out=xmb_ap, in_=ext[:, 3:TT + 3]), [ext], [xmb])
            for t_, (dst, src_ap, src_tb) in enumerate(((qTt, fm(xct, c), xct[c // 4]), (kTt, fm(xct, c), xct[c // 4]), (vTt, xmb_ap, xmb))):
                ps = nextps()
                P.op('pe', lambda e, ps=ps, t_=t_, src_ap=src_ap: e.matmul(ps[:], lhsT=wqkv_bf[:, t_, c, :], rhs=src_ap, start=True, stop=True),
                     [wqkv_bf, src_tb], [ps])
                P.op('act' if t_ != 1 else 'dve',
                     (lambda e, ps=ps, dst=dst: e.activation(out=fm(dst, c), in_=ps[:], func=AF.Copy)) if t_ != 1 else
                     (lambda e, ps=ps, dst=dst: e.tensor_copy(out=fm(dst, c), in_=ps[:])), [ps], [dst[c // 4]])
                j = t_ * 8 + c
                first, last = (c == 0 and t_ == 0), (c == 7 and t_ == 2)
                P.op('pe', lambda e, j=j, dst=dst, first=first, last=last: e.matmul(gi_ps[0:4, :], lhsT=wif_bf[:, 0, j, :], rhs=fm(dst, c), start=first, stop=last),
                     [wif_bf, dst[c // 4]], [gi_ps])
                P.op('pe', lambda e, j=j, dst=dst, first=first, last=last: e.matmul(gf_ps[0:4, :], lhsT=wif_bf[:, 1, j, :], rhs=fm(dst, c), start=first, stop=last),
                     [wif_bf, dst[c // 4]], [gf_ps])
            for t_, (dstl, src_ap, src_tb) in ((1, (ktt, fm(xct, c), xct[c // 4])), (2, (vtt, xmb_ap, xmb))):
                ps = nextps()
                for tb in range(NT):
                    P.op('pe', lambda e, ps=ps, tb=tb, t_=t_, src_ap=src_ap: e.matmul(ps[:, tb * 128:(tb + 1) * 128], lhsT=src_ap[:, tb * 128:(tb + 1) * 128], rhs=wqkv_bf[:, t_, c, :],
                                                                                      start=True, stop=True), [wqkv_bf, src_tb], [ps])
                for half in range(2):
                    dv = bfv(dstl[half]).rearrange("p (b d) -> p b d", b=2)[:, :, c * 128:(c + 1) * 128]
                    P.op('act' if half == 0 else 'dve',
                         (lambda e, ps=ps, dv=dv, half=half: e.activation(out=dv, in_=ps[:, half * 256:(half + 1) * 256].rearrange("p (b d) -> p b d", b=2), func=AF.Copy)) if half == 0 else
                         (lambda e, ps=ps, dv=dv, half=half: e.tensor_copy(out=dv, in_=ps[:, half * 256:(half + 1) * 256].rearrange("p (b d) -> p b d", b=2))),
                         [ps], [dstl[half]])
        rg_, re2, rwi, rem, rli, rFp, rag, rMx, rtmp = G[11], G[12], G[13], G[14], G[15], G[16], G[17], G[18], G[19]
        R4 = slice(0, 4)
        P.op('act', lambda e: e.activation(out=rli[R4, :], in_=gi_ps[R4, :], func=AF.Identity, bias=pc[R4, PC['bif_i']:PC['bif_i'] + 1]), [gi_ps, pc], [rli])
        P.op('act', lambda e: e.activation(out=rtmp[R4, :], in_=gf_ps[R4, :], func=AF.Exp, scale=-1.0, bias=pd[R4, PD['negbf']:PD['negbf'] + 1]), [gf_ps, pd], [rtmp])
        P.op('act', lambda e: e.activation(out=rtmp[R4, :], in_=rtmp[R4, :], func=AF.Ln, bias=1.0), [rtmp], [rtmp])
        P.op('dve', lambda e: e.tensor_tensor_scan(out=rFp[R4, :], data0=onesf[R4, :], data1=rtmp[R4, :], initial=mcar[:, 0:1], op0=MUL, op1=ADD),
             [onesf, rtmp, mcar], [rFp])
        P.op('pool', lambda e: e.tensor_tensor(out=rag[R4, :], in0=rli[R4, :], in1=rFp[R4, :], op=ADD), [rli, rFp], [rag])
        P.op('dve', lambda e: e.tensor_tensor_scan(out=rMx[R4, :], data0=rag[R4, :], data1=rag[R4, :], initial=mcar[:, 1:2], op0=MAX, op1=MAX),
             [rag, mcar], [rMx])
        P.op('pool', lambda e: e.tensor_copy(out=MxE[:, 0:1], in_=mcar[:, 1:2]), [mcar], [MxE])
        P.op('pool', lambda e: e.tensor_copy(out=MxE[:, 1:5], in_=rMx[R4, 127:TT:128]), [rMx], [MxE])
        P.op('pool', lambda e: e.tensor_copy(out=mcar[:, 0:1], in_=rFp[R4, TT - 1:TT]), [rFp], [mcar])
        P.op('pool', lambda e: e.tensor_copy(out=mcar[:, 1:2], in_=rMx[R4, TT - 1:TT]), [rMx], [mcar])
        v3 = lambda t_: t_[R4, :].rearrange("p (q t) -> p q t", q=NT)
        mend = MxE[:, 1:5].unsqueeze(2).broadcast_to([4, NT, 128])
        mprev = MxE[:, 0:4].unsqueeze(2).broadcast_to([4, NT, 128])
        P.op('dve', lambda e: e.tensor_tensor(out=v3(rg_), in0=v3(rag), in1=mend, op=SUB), [rag, MxE], [rg_])
        P.op('act', lambda e: e.activation(out=rg_[R4, :], in_=rg_[R4, :], func=AF.Exp), [rg_], [rg_])
        P.op('dve', lambda e: e.tensor_tensor(out=v3(re2), in0=mend, in1=v3(rMx), op=SUB), [rMx, MxE], [re2])
        P.op('act', lambda e: e.activation(out=re2[R4, :], in_=re2[R4, :], func=AF.Exp), [re2], [re2])
        P.op('dve', lambda e: e.tensor_tensor(out=v3(rwi), in0=mprev, in1=v3(rMx), op=SUB), [rMx, MxE], [rwi])
        P.op('act', lambda e: e.activation(out=rwi[R4, :], in_=rwi[R4, :], func=AF.Exp), [rwi], [rwi])
        P.op('dve', lambda e: e.tensor_tensor(out=rem[R4, :], in0=rFp[R4, :], in1=rMx[R4, :], op=SUB), [rFp, rMx], [rem])
        P.op('act', lambda e: e.activation(out=rem[R4, :], in_=rem[R4, :], func=AF.Exp), [rem], [rem])
        P.op('dve', lambda e: e.tensor_tensor(out=dec[:], in0=MxE[:, 0:4], in1=MxE[:, 1:5], op=SUB), [MxE], [dec])
        P.op('act', lambda e: e.activation(out=dec[:], in_=dec[:], func=AF.Exp), [dec], [dec])
        sp_ = PS[5]
        for q in range(NT):
            P.op('pe', lambda e, q=q: e.matmul(sp_[:, q * 4:(q + 1) * 4], lhsT=rg_[R4, q * 128:(q + 1) * 128], rhs=ident[0:4, 0:4], start=True, stop=True),
                 [rg_, ident], [sp_])
        for h in range(4):
            P.op('pe', lambda e, h=h: e.matmul(sp_[:, 16 + h * 4:16 + (h + 1) * 4], lhsT=sel[:, h, :], rhs=dec[:], start=True, stop=True), [sel, dec], [sp_])
        P.op('act', lambda e: e.activation(out=smallb[:], in_=sp_[:, 0:32], func=AF.Copy), [sp_], [smallb])
        for h in range(4):
            mlstm_head(h, qTt, kTt, ktt, vtt, xct, fm, tokv, (rg_, re2, rwi, rem))

    def mlstm_head(h, qTt, kTt, ktt, vtt, xct, fm, tokv, rows):
        bc = G[0:4]
        for i in range(4):
            ps = nextps()
            P.op('pe', lambda e, ps=ps, i=i: e.matmul(ps[:], lhsT=sel[:, h, :], rhs=rows[i][0:4, :], start=True, stop=True), [sel, rows[i]], [ps])
            P.op('act' if i % 2 == 0 else 'dve',
                 (lambda e, ps=ps, i=i: e.activation(out=bc[i][:], in_=ps[:], func=AF.Copy)) if i % 2 == 0 else
                 (lambda e, ps=ps, i=i: e.tensor_copy(out=bc[i][:], in_=ps[:])), [ps], [bc[i]])
        g_bc, e2_bc, wi_bc, em_bc = bc
        h32 = [G[4], G[5]]
        gms = [G[6], G[7]]
        for vc in range(2):
            ps = proj_x('w0in', 41 + 2 * h + vc)
            P.op('act', lambda e, ps=ps, vc=vc: e.activation(out=gms[vc][:], in_=ps[:], func=AF.Silu), [ps], [gms[vc]])
        qt_ = qTt[h // 2]
        kt_ = kTt[h // 2]
        qv = bfv(qt_).rearrange("p (k t) -> p k t", k=4)[:, 2 * (h % 2):2 * (h % 2) + 2, :]
        kv = bfv(kt_).rearrange("p (k t) -> p k t", k=4)[:, 2 * (h % 2):2 * (h % 2) + 2, :]
        CB, SB_, NB = PS[6], PS[7], PS[5]
        for q in range(NT):
            ts_ = slice(q * 128, (q + 1) * 128)
            bcast = lambda t_: t_[:, ts_].unsqueeze(1).broadcast_to([128, 2, 128])
            P.op('dve', lambda e: e.tensor_tensor(out=qe[:], in0=qv[:, :, ts_], in1=bcast(e2_bc), op=MUL), [qt_, e2_bc], [qe])
            P.op('pool', lambda e: e.tensor_tensor(out=qw[:], in0=qv[:, :, ts_], in1=bcast(wi_bc), op=MUL), [qt_, wi_bc], [qw])
            P.op('dve', lambda e: e.scalar_tensor_tensor(out=kg[:], in0=kv[:, :, ts_], scalar=0.0625, in1=bcast(g_bc), op0=MUL, op1=MUL), [kt_, g_bc], [kg])
            ktok = tokv(ktt, q)[:, h * 256:(h + 1) * 256]
            vtok = tokv(vtt, q)[:, h * 256:(h + 1) * 256]
            P.op('pool', lambda e: e.tensor_scalar(out=kgt[:], in0=ktok, scalar1=smallb[:, q * 4 + h:q * 4 + h + 1], scalar2=0.0625, op0=MUL, op1=MUL),
                 [ktt[q // 2], smallb], [kgt])
            for kc in range(2):
                P.op('pe', lambda e, kc=kc: e.matmul(CB[:, 0:128], lhsT=kg[:, kc, :], rhs=qe[:, kc, :], start=(kc == 0), stop=(kc == 1)), [kg, qe], [CB])
            P.op('dve', lambda e: e.tensor_tensor(out=st_bf[:], in0=CB[:, 0:128], in1=causal[:], op=MUL), [CB, causal], [st_bf])
            for vc in range(2):
                o_ = CB[:, 128 + vc * 128:256 + vc * 128]
                P.op('pe', lambda e, o_=o_, vc=vc: e.matmul(o_, lhsT=vtok[:, vc * 128:(vc + 1) * 128], rhs=st_bf[:], start=True, stop=False), [vtt[q // 2], st_bf], [CB])
                for kc in range(2):
                    P.op('pe', lambda e, o_=o_, vc=vc, kc=kc: e.matmul(o_, lhsT=CTbf[:, h, kc, vc * 128:(vc + 1) * 128], rhs=qw[:, kc, :], start=False, stop=(kc == 1)),
                         [CTbf, qw], [CB])
            o_ = CB[:, 384:512]
            P.op('pe', lambda e, o_=o_: e.matmul(o_, lhsT=ones_bf[:], rhs=st_bf[:], start=True, stop=False), [ones_bf, st_bf], [CB])
            for kc in range(2):
                P.op('pe', lambda e, o_=o_, kc=kc: e.matmul(o_, lhsT=nrbf[:, h, kc, :], rhs=qw[:, kc, :], start=False, stop=(kc == 1)), [nrbf, qw], [CB])
            P.op('act', lambda e: e.activation(out=ddm[:], in_=CB[:, 384:512], func=AF.Abs), [CB], [ddm])
            P.op('dve', lambda e: e.tensor_tensor(out=ddm[:], in0=ddm[:], in1=em_bc[:, ts_], op=MAX), [ddm, em_bc], [ddm])
            P.op('dve', lambda e: e.tensor_scalar(out=ddm[:], in0=ddm[:], scalar1=1e-6, scalar2=None, op0=ADD), [ddm], [ddm])
            P.op('dve', lambda e: e.reciprocal(out=recm[:], in_=ddm[:]), [ddm], [recm])
            for vc in range(2):
                P.op('dve', lambda e, vc=vc: e.tensor_tensor(out=h32[vc][:, ts_], in0=CB[:, 128 + vc * 128:256 + vc * 128], in1=recm[:], op=MUL), [CB, recm], [h32[vc]])
            for kc in range(2):
                P.op('pe', lambda e, kc=kc: e.matmul(SB_[:, kc * 256:(kc + 1) * 256], lhsT=kgt[:, kc * 128:(kc + 1) * 128], rhs=vtok, start=True, stop=True),
                     [kgt, vtt[q // 2]], [SB_])
                P.op('pe', lambda e, kc=kc: e.matmul(NB[:, 64 + kc * 128:64 + (kc + 1) * 128], lhsT=kgt[:, kc * 128:(kc + 1) * 128], rhs=ones_bf[:], start=True, stop=True),
                     [kgt, ones_bf], [NB])
            dcol = smallb[:, 16 + h * 4 + q:16 + h * 4 + q + 1]
            P.op('dve', lambda e: e.scalar_tensor_tensor(out=CT32[:, h], in0=CT32[:, h], scalar=dcol, in1=SB_[:].rearrange("p (k v) -> p k v", k=2), op0=MUL, op1=ADD),
                 [CT32, smallb, SB_], [CT32])
            P.op('act', lambda e: e.activation(out=CTbf[:, h], in_=CT32[:, h], func=AF.Copy), [CT32], [CTbf])
            P.op('dve', lambda e: e.scalar_tensor_tensor(out=nr32[:, h], in0=nr32[:, h], scalar=dcol, in1=NB[:, 64:320].rearrange("p (k v) -> p k v", k=2), op0=MUL, op1=ADD),
                 [nr32, smallb, NB], [nr32])
            P.op('pool', lambda e: e.tensor_copy(out=nrbf[:, h], in_=nr32[:, h]), [nr32], [nrbf])
        dump('h_m%d' % h, h32[0], h32[0][:])
        hr = [G[8], G[9]]
        ps = nextps()
        for vc in range(2):
            P.op('act', lambda e, vc=vc: e.activation(out=RV(hr[vc]), in_=h32[vc][:], func=AF.Copy), [h32[vc]], [hr[vc]])
            P.op('pe', lambda e, ps=ps, vc=vc: e.matmul(ps[:], lhsT=ones_r[:], rhs=RV(hr[vc]), start=(vc == 0), stop=(vc == 1)), [ones_r, hr[vc]], [ps])
        for vc in range(2):
            P.op('dve', lambda e, ps=ps, vc=vc: e.scalar_tensor_tensor(out=h32[vc][:], in0=ps[:], scalar=-1.0 / 256, in1=h32[vc][:], op0=MUL, op1=ADD), [ps, h32[vc]], [h32[vc]])
        ps2 = nextps()
        for vc in range(2):
            P.op('act', lambda e, vc=vc: e.activation(out=RV(hr[vc]), in_=h32[vc][:], func=AF.Square), [h32[vc]], [hr[vc]])
            P.op('pe', lambda e, ps2=ps2, vc=vc: e.matmul(ps2[:], lhsT=ones_r[:], rhs=RV(hr[vc]), start=(vc == 0), stop=(vc == 1)), [ones_r, hr[vc]], [ps2])
        rs = G[10]
        P.op('act', lambda e, ps2=ps2: e.activation(out=rs[:], in_=ps2[:], func=AF.Ln, scale=1.0 / 256, bias=1e-5), [ps2], [rs])
        P.op('act', lambda e: e.activation(out=rs[:], in_=rs[:], func=AF.Exp, scale=-0.5), [rs], [rs])
        for vc in range(2):
            ch = 2 * h + vc
            P.op('dve', lambda e, vc=vc: e.tensor_tensor(out=h32[vc][:], in0=h32[vc][:], in1=rs[:], op=MUL), [h32[vc], rs], [h32[vc]])
            P.op('act', lambda e, vc=vc, ch=ch: e.activation(out=h32[vc][:], in_=h32[vc][:], func=AF.Identity, scale=pcc('mnorm', ch)), [h32[vc], pc], [h32[vc]])
            P.op('dve', lambda e, vc=vc, ch=ch: e.scalar_tensor_tensor(out=h32[vc][:], in0=fm(xct, ch), scalar=pcc('mskip', ch), in1=h32[vc][:], op0=MUL, op1=ADD),
                 [xct[ch // 4], pc, h32[vc]], [h32[vc]])
            P.op('pool', lambda e, vc=vc, ch=ch: e.tensor_tensor(out=merged[:, 8 + ch, :], in0=h32[vc][:], in1=gms[vc][:], op=MUL), [h32[vc], gms[vc]], [merged])

    def layer0(t0):
        rmsnorm_x('mixn0')
        pstate['set'] = [0, 1]
        if do_rwkv:
            rwkv(t0)
        if do_mlstm:
            mlstm(t0)
        pstate['set'] = list(range(8))
        out_proj('w0out')

    if do_l0 and not (do_rwkv and do_mlstm):
        P.op('pool', lambda e: e.memset(merged[:], 0.0), [], [merged])

    for ti in range(ntiles):
        t0 = ti * TT
        for tb in range(NT):
            P.dma('sp', H[tb][:], x_d.ap()[t0 + tb * 128:t0 + (tb + 1) * 128, :], writes=[H[tb]])
        for kc in range(8):
            ps = nextps()
            for tb in range(NT):
                P.op('pe', lambda e, kc=kc, tb=tb, ps=ps: e.transpose(ps[:, tb * 128:(tb + 1) * 128], H[tb][:, kc * 128:(kc + 1) * 128], ident[:]),
                     [H[tb], ident], [ps])
            P.op('act' if kc % 2 == 0 else 'dve',
                 (lambda e, kc=kc, ps=ps: e.activation(out=hT[:, kc, :], in_=ps[:], func=AF.Copy)) if kc % 2 == 0 else
                 (lambda e, kc=kc, ps=ps: e.tensor_copy(out=hT[:, kc, :], in_=ps[:])), [ps], [hT])
        if do_l0:
            layer0(t0)
            ple(0, t0)
        if do_l1:
            layer1(t0)
            ple(1, t0)
        norm_stats()
        for k in range(8):
            of = G[k % 4]
            norm_apply(k, 'finn', of[:], of)
            ps = nextps()
            for tb in range(NT):
                P.op('pe', lambda e, tb=tb, ps=ps, of=of: e.transpose(ps[:, tb * 128:(tb + 1) * 128], of[:, tb * 128:(tb + 1) * 128], ident[:]),
                     [of, ident], [ps])
            for tb in range(NT):
                P.op('act' if k % 2 == 0 else 'dve',
                     (lambda e, k=k, ps=ps, tb=tb: e.activation(out=H[tb][:, k * 128:(k + 1) * 128], in_=ps[:, tb * 128:(tb + 1) * 128], func=AF.Copy)) if k % 2 == 0 else
                     (lambda e, k=k, ps=ps, tb=tb: e.tensor_copy(out=H[tb][:, k * 128:(k + 1) * 128], in_=ps[:, tb * 128:(tb + 1) * 128])),
                     [ps], [H[tb]])
        for tb in range(NT):
            P.dma('sp', o_d.ap()[t0 + tb * 128:t0 + (tb + 1) * 128, :], H[tb][:], reads=[H[tb]])
    P.finish()
    print("sbuf bytes/partition:", P.sbytes, "sems:", P.nsem, "instr:", {e: P.cnt[e] for e in ENGS})
    return nc


def kernel(**inputs):
    x = np.asarray(inputs['x'], np.float32)
    p = np.asarray(inputs['p'], np.float32)
    B, S, _ = x.shape
    shared = host_prep(inputs)
    nc = build_program(S)
    in_maps = []
    for b in range(B):
        m = dict(shared)
        m['x'] = np.ascontiguousarray(x[b])
        m['p'] = np.ascontiguousarray(p[:, b])
        in_maps.append(m)
    res = run_bass_kernel_spmd(nc, in_maps, core_ids=list(range(B)))
    return np.stack([np.asarray(r['out'], np.float32) for r in res.results], axis=0)
```

```python
import numpy as np
import concourse.bass as bass
import concourse.mybir as mybir
from concourse.bass_utils import run_bass_kernel_spmd
from contextlib import ExitStack

F32 = mybir.dt.float32
BF16 = mybir.dt.bfloat16
F32R = mybir.dt.float32r
SDT = F32R
AF = mybir.ActivationFunctionType
ALU = mybir.AluOpType
AX = mybir.AxisListType

ENGS = ('pe', 'act', 'dve', 'pool', 'sp')
EIDX = {e: i for i, e in enumerate(ENGS)}
EPOCH = 30000

D = 1024
TT = 512
NT = TT // 128
PLE = 256
DECAY_SCALE = 0.6065306597126334


class TB:
    __slots__ = ('name', 'h', 'lw', 'rd', 'dkey', 'root', 'psum')

    def __init__(self, name, h, root=None, psum=False):
        self.name = name
        self.h = h
        self.lw = None
        self.rd = {}
        self.dkey = None
        self.root = root if root is not None else self
        self.psum = psum or (root is not None and root.psum)

    def __getitem__(self, k):
        return self.h[k]


class _Rec:
    def __init__(self):
        self.call = None

    def __getattr__(self, name):
        def f(*a, **k):
            self.call = (name, a, k)
        return f


class Prog:
    def __init__(self, nc):
        self.nc = nc
        self.es = ExitStack()
        self.ops = {e: [] for e in ENGS}
        self.cnt = {e: 0 for e in ENGS}
        self.seen = {e: {} for e in ENGS}
        self.clk = {e: [] for e in ENGS}
        self.dcnt = {}
        self.dclk = {}
        self.sems = {}
        self.nsem = 0
        self.ndma = 0
        self.sbytes = 0

    def sb(self, name, shape, dt=F32):
        n = 1
        for s in shape[1:]:
            n *= s
        self.sbytes += n * (2 if dt == BF16 else 4)
        return TB(name, self.es.enter_context(self.nc.sbuf_tensor(name, list(shape), dt)))

    def ps(self, name, shape, dt=F32):
        return TB(name, self.es.enter_context(self.nc.psum_tensor(name, list(shape), dt)), psum=True)

    def dram(self, name, shape, dt, kind):
        return TB(name, self.nc.dram_tensor(name, list(shape), dt, kind=kind))

    def _sem(self, key):
        s = self.sems.get(key)
        if s is None:
            s = self.es.enter_context(self.nc.semaphore("s%d" % self.nsem))
            self.nsem += 1
            self.sems[key] = s
        return s

    def _semval(self, key, count):
        if key in EIDX:
            ep = (count - 1) // EPOCH
            return self._sem((key, ep)), count - ep * EPOCH
        return self._sem(key), 16 * count

    def _deps(self, e, reads, writes):
        seen = self.seen[e]
        need = {}

        def req(ev):
            if ev is None:
                return
            k, c = ev
            if k == 'pe' and e == 'pe':
                return
            if seen.get(k, 0) >= c:
                return
            if need.get(k, 0) < c:
                need[k] = c
        for t in reads:
            req(t.lw)
        for t in writes:
            req(t.lw)
            for k, c in t.rd.items():
                req((k, c))
        keys = list(need.keys())
        for k in keys:
            if k not in need:
                continue
            c = need[k]
            ck = self.clk[k][c - 1] if k in EIDX else self.dclk[(k, c)]
            for k2 in keys:
                if k2 != k and k2 in EIDX and k2 in need and ck[EIDX[k2]] >= need[k2]:
                    del need[k2]
        waits = []
        for k, c in need.items():
            waits.append(self._semval(k, c))
            ck = self.clk[k][c - 1] if k in EIDX else self.dclk[(k, c)]
            for e2, v in zip(ENGS, ck):
                if seen.get(e2, 0) < v:
                    seen[e2] = v
            if seen.get(k, 0) < c:
                seen[k] = c
        return waits

    def _snapshot(self, e):
        s = self.seen[e]
        return tuple(s.get(x, 0) for x in ENGS)

    def op(self, e, fn, reads=(), writes=()):
        writes = [t.root for t in writes] + [t.root for t in reads if t.psum]
        reads = [t.root for t in reads if not t.psum]
        rec = _Rec()
        fn(rec)
        fn = rec.call
        waits = self._deps(e, reads, writes)
        self.cnt[e] += 1
        c = self.cnt[e]
        snap = list(self._snapshot(e))
        snap[EIDX[e]] = c
        self.clk[e].append(tuple(snap))
        inc = self._semval(e, c)[0]
        self.ops[e].append((waits, fn, inc, 1))
        ev = (e, c)
        for t in reads:
            if t.rd.get(e, 0) < c:
                t.rd[e] = c
        for t in writes:
            t.lw = ev
            t.rd = {}
        return ev

    def dma(self, q, out, in_, reads=(), writes=(), key=None, **kw):
        if key is None:
            t0 = (list(writes) + list(reads))[0]
            if t0.dkey is None:
                t0.dkey = ('dma', self.ndma)
                self.ndma += 1
            key = t0.dkey
        reads = [t.root for t in reads]
        writes = [t.root for t in writes]
        waits = self._deps(q, reads, writes)
        self.dcnt[key] = self.dcnt.get(key, 0) + 1
        c = self.dcnt[key]
        self.dclk[(key, c)] = self._snapshot(q)
        sem = self._sem(key)
        self.ops[q].append((waits, ('dma_start', (), dict(out=out, in_=in_, **kw)), sem, 16))
        ev = (key, c)
        for t in reads:
            if t.rd.get(key, 0) < c:
                t.rd[key] = c
        for t in writes:
            t.lw = ev
            t.rd = {}
        return ev

    def finish(self):
        e = 'sp'
        tail = []
        for key, c in self.dcnt.items():
            tail.append(self._semval(key, c))
        for x in ENGS:
            if x != e and self.cnt[x] > 0:
                tail.append(self._semval(x, self.cnt[x]))
        blk = self.es.enter_context(self.nc.Block())
        engobj = {'pe': blk.tensor, 'act': blk.scalar, 'dve': blk.vector, 'pool': blk.gpsimd, 'sp': blk.sync}

        def make(ename):
            ops = self.ops[ename]

            def body(eng):
                for waits, fn, inc, n in ops:
                    for (s, v) in waits[1:]:
                        eng.wait_ge(s, v)
                    ins = getattr(eng, fn[0])(*fn[1], **fn[2])
                    if waits:
                        ins._wait_ge(waits[0][0], waits[0][1])
                    ins.then_inc(inc, n)
                if ename == e:
                    for (s, v) in tail:
                        eng.wait_ge(s, v)
            return body
        for ename in ENGS:
            if self.ops[ename] or ename == e:
                engobj[ename](make(ename))
        self.es.close()


PC_SPEC = [
    ('mixn0', 8), ('mixn1', 8), ('pen0', 8), ('pen1', 8), ('finn', 8),
    ('mu_r', 8), ('mu_k', 8), ('mu_v', 8), ('mu_l', 1),
    ('w0', 8), ('a0', 8), ('k_k', 8), ('k_a', 8), ('r_k', 8), ('ln_w', 8), ('ln_b', 8),
    ('mcw0', 8), ('mcw1', 8), ('mcw2', 8), ('mcw3', 8), ('mcb', 8), ('mnorm', 8), ('mskip', 8),
    ('bif_i', 1), ('bif_f', 1),
    ('ccw0', 16), ('ccw1', 16), ('ccw2', 16), ('ccw3', 16), ('ccb', 16), ('cbr', 16), ('cbi', 16), ('clam', 16),
]
PC = {}
_o = 0
for _n, _w in PC_SPEC:
    PC[_n] = _o
    _o += _w
NPC = _o
PD_SPEC = [('omu_r', 8), ('omu_k', 8), ('omu_v', 8), ('omu_l', 1), ('negbf', 1), ('csph', 16), ('cneg', 16), ('t0', 16), ('t1', 16)]
PD = {}
_o = 0
for _n, _w in PD_SPEC:
    PD[_n] = _o
    _o += _w
NPD = _o


def _cols(v):
    v = np.asarray(v, np.float32).reshape(-1)
    return np.ascontiguousarray(v.reshape(-1, 128).T)


def host_consts():
    c = {}
    c['ident'] = np.eye(128, dtype=np.float32)
    bd = np.zeros((128, 128), np.float32)
    bd[:64, :64] = 1
    bd[64:, 64:] = 1
    c['bdones'] = bd
    j = np.arange(128)[:, None]
    i = np.arange(128)[None, :]
    su = ((j < i) & ((j // 64) == (i // 64))).astype(np.float32)
    ui = ((j <= i) & ((j // 64) == (i // 64))).astype(np.float32)
    sl = ((j > i) & ((j // 64) == (i // 64))).astype(np.float32)
    on = np.ones((128, 128), np.float32)
    c['maskA'] = np.stack([su, su, ui, ui], axis=1)
    c['maskB'] = sl
    rm = np.ones((128, TT), np.float32)
    rm[:, ::64] = 0
    c['resetm'] = rm
    c['causal'] = (j <= i).astype(np.float32)
    sel = np.zeros((4, 4, 128), np.float32)
    for h in range(4):
        sel[h, h, :] = 1
    c['sel'] = sel
    return c


def host_prep(inp):
    g = {}
    f = lambda a: np.ascontiguousarray(np.asarray(a, np.float32))

    def chunked(w):
        K, M = w.shape
        return f(w.reshape(K // 128, 128, M // 128, 128).transpose(2, 1, 0, 3))
    g['w0in'] = chunked(f(inp['ab_w_in'])[0])
    g['w0out'] = chunked(f(inp['ab_w_out'])[0])
    g['w1in'] = chunked(f(inp['c_w_in'])[0])
    g['w1out'] = chunked(f(inp['c_w_out'])[0])
    g['pegate'] = np.stack([chunked(f(inp['pe_gate'])[i]) for i in range(2)])
    g['peup'] = np.stack([chunked(f(inp['pe_up'])[i]) for i in range(2)])
    wr = f(inp['c_wr'])[0]
    wi = f(inp['c_wi'])[0]
    g['wri'] = f(np.stack([wr, wi], axis=2))
    lup = np.zeros((128, 2, 1024), np.float32)
    lup[:64, 0, :] = f(inp['rwkv_w_up'])[0]
    lup[64:, 1, :] = f(inp['rwkv_a_up'])[0]
    g['lup'] = lup
    wbd = np.zeros((128, 3, 8, 128), np.float32)
    for t, nm in enumerate(['mlstm_wq', 'mlstm_wk', 'mlstm_wv']):
        w = f(inp[nm])[0]
        for c in range(8):
            for gg in range(32):
                wbd[4 * gg:4 * gg + 4, t, c, 4 * gg:4 * gg + 4] = w[c * 32 + gg]
    g['wqkv'] = wbd
    wif = f(inp['mlstm_w_if'])[0]
    g['wif'] = f(wif.reshape(24, 128, 2, 4).transpose(1, 2, 0, 3))
    pc = np.zeros((128, NPC), np.float32)

    def put(name, v):
        cc = _cols(v)
        pc[:, PC[name]:PC[name] + cc.shape[1]] = cc
    put('mixn0', f(inp['mix_norm'])[0]); put('mixn1', f(inp['mix_norm'])[1])
    put('pen0', f(inp['pe_norm'])[0]); put('pen1', f(inp['pe_norm'])[1])
    put('finn', f(inp['final_norm']))
    mu = f(inp['rwkv_mu'])[0]
    put('mu_r', mu[0]); put('mu_k', mu[1]); put('mu_v', mu[2])
    ml = f(inp['rwkv_mu_lora'])[0]
    put('mu_l', np.concatenate([ml[0], ml[1]]))
    put('w0', f(inp['rwkv_w0'])[0]); put('a0', f(inp['rwkv_a0'])[0])
    put('k_k', f(inp['rwkv_k_k'])[0]); put('k_a', f(inp['rwkv_k_a'])[0])
    put('r_k', f(inp['rwkv_r_k'])[0].reshape(-1))
    put('ln_w', f(inp['rwkv_ln_w'])[0]); put('ln_b', f(inp['rwkv_ln_b'])[0])
    mcw = f(inp['mlstm_conv_w'])[0]
    for j in range(4):
        put('mcw%d' % j, mcw[j])
    put('mcb', f(inp['mlstm_conv_b'])[0])
    put('mnorm', f(inp['mlstm_norm'])[0]); put('mskip', f(inp['mlstm_skip'])[0])
    bif = f(inp['mlstm_b_if'])[0]
    pc[0:4, PC['bif_i']] = bif[0:4]
    pc[0:4, PC['bif_f']] = bif[4:8]
    ccw = f(inp['c_conv_w'])[0]
    for j in range(4):
        put('ccw%d' % j, ccw[j])
    put('ccb', f(inp['c_conv_b'])[0]); put('cbr', f(inp['c_br'])[0]); put('cbi', f(inp['c_bi'])[0])
    put('clam', f(inp['c_lambda'])[0])
    g['pc'] = pc
    g.update(host_consts())
    return g


SHARED_SHAPES = {
    'w0in': [49, 128, 8, 128], 'w0out': [8, 128, 16, 128], 'w1in': [32, 128, 8, 128], 'w1out': [8, 128, 16, 128],
    'pegate': [2, 8, 128, 8, 128], 'peup': [2, 8, 128, 2, 128], 'wri': [16, 128, 2, 128], 'lup': [128, 2, 1024],
    'wqkv': [128, 3, 8, 128], 'wif': [128, 2, 24, 4], 'pc': [128, NPC],
    'ident': [128, 128], 'bdones': [128, 128], 'maskA': [128, 4, 128], 'maskB': [128, 128],
    'resetm': [128, TT], 'causal': [128, 128], 'sel': [4, 4, 128],
}


def build_program(S, do_l0=True, do_rwkv=True, do_mlstm=True, do_l1=True, dbg=None):
    nc = bass.Bass("TRN2", target_bir_lowering=False)
    P = Prog(nc)
    ntiles = S // TT
    din = {}
    for k, shp in SHARED_SHAPES.items():
        din[k] = nc.dram_tensor(k, shp, F32, kind="ExternalInput")
    x_d = nc.dram_tensor("x", [S, D], F32, kind="ExternalInput")
    p_d = nc.dram_tensor("p", [2, S, PLE], F32, kind="ExternalInput")
    o_d = nc.dram_tensor("out", [S, D], F32, kind="ExternalOutput")
    dbg_d = None
    dbg = dbg or []
    if dbg:
        dbg_d = nc.dram_tensor("dbg", [len(dbg), 128, TT], F32, kind="ExternalOutput")

    def dump(name, tb, ap):
        if name in dbg:
            P.dma('sp', dbg_d.ap()[dbg.index(name)], ap, reads=[tb], key=('dma', 'dbg'))

    MUL, ADD, SUB, MAX = ALU.mult, ALU.add, ALU.subtract, ALU.max

    def cload(name, dt=F32, rows=128):
        shp = SHARED_SHAPES[name]
        t = P.sb("c_" + name, shp, dt)
        P.dma('pool' if dt != F32 else 'sp', t[:], din[name].ap(), writes=[t])
        return t
    ident = cload('ident')
    pc = cload('pc')
    pd = P.sb("pd", [128, NPD])
    ones_bf = P.sb("ones_bf", [128, 128], BF16)
    P.op('pool', lambda e: e.memset(ones_bf[:], 1.0), [], [ones_bf])

    def pcc(name, c=0, n=1):
        return pc[:, PC[name] + c:PC[name] + c + n]

    def pdc(name, c=0, n=1):
        return pd[:, PD[name] + c:PD[name] + c + n]

    if do_l1:
        P.op('act', lambda e: e.activation(out=pdc('t0', 0, 16), in_=pcc('clam', 0, 16), func=AF.Exp, scale=-1.0), [pc], [pd])
        P.op('act', lambda e: e.activation(out=pdc('t1', 0, 16), in_=pdc('t0', 0, 16), func=AF.Ln, bias=1.0), [pd], [pd])
        P.op('dve', lambda e: e.tensor_scalar(out=pdc('csph', 0, 16), in0=pdc('t1', 0, 16), scalar1=4.0, scalar2=None, op0=MUL), [pd], [pd])
        P.op('dve', lambda e: e.tensor_scalar(out=pdc('cneg', 0, 16), in0=pdc('t1', 0, 16), scalar1=-8.0, scalar2=None, op0=MUL), [pd], [pd])
    if do_l0:
        for nm, w in (('r', 8), ('k', 8), ('v', 8), ('l', 1)):
            P.op('dve', lambda e, nm=nm, w=w: e.tensor_scalar(out=pdc('omu_' + nm, 0, w), in0=pcc('mu_' + nm, 0, w), scalar1=-1.0, scalar2=1.0, op0=MUL, op1=ADD),
                 [pc], [pd])
        P.op('dve', lambda e: e.tensor_scalar(out=pdc('negbf'), in0=pcc('bif_f'), scalar1=-1.0, scalar2=None, op0=MUL), [pc], [pd])

    scr = {}

    def mkscr(name, src_ap, shape, group=8):
        t = P.dram("scr_" + name, shape, BF16, "Internal")
        nm = shape[0]
        for j0 in range(0, nm, group):
            j1 = min(nm, j0 + group)
            P.dma('pool', t[j0:j1].rearrange("j p k m -> p j k m"), src_ap[j0:j1].rearrange("j p k m -> p j k m"),
                  writes=[t], key=('dma', 'scr_' + name))
        scr[name] = t
        return t
    if do_l0:
        lup_bf = cload('lup', BF16)
        wqkv_bf = cload('wqkv', BF16)
        wif_bf = cload('wif', BF16)
        mkscr('w0in', din['w0in'].ap(), [49, 128, 8, 128])
        mkscr('w0out', din['w0out'].ap(), [8, 128, 16, 128], group=4)
    for i in range(2):
        if (i == 0 and do_l0) or (i == 1 and do_l1):
            mkscr('pegate%d' % i, din['pegate'].ap()[i], [8, 128, 8, 128])
            mkscr('peup%d' % i, din['peup'].ap()[i], [8, 128, 2, 128])
    if do_l1:
        mkscr('w1in', din['w1in'].ap(), [32, 128, 8, 128])
        mkscr('wri', din['wri'].ap(), [16, 128, 2, 128], group=16)
        mkscr('w1out', din['w1out'].ap(), [8, 128, 16, 128], group=4)

    NWB = 5
    wring = [P.sb("wb%d" % i, [128, 16, 128], BF16) for i in range(NWB)]
    wstate = {'i': 0}

    def wld(name, j, nk):
        wb = wring[wstate['i'] % NWB]
        wstate['i'] += 1
        P.dma('sp', wb[:, 0:nk, :], scr[name][j], reads=[scr[name]], writes=[wb])
        return wb

    PS = [P.ps("ps%d" % i, [128, TT]) for i in range(8)]
    pstate = {'i': 0}

    pstate['set'] = list(range(8))

    def nextps():
        s_ = pstate['set']
        b = PS[s_[pstate['i'] % len(s_)]]
        pstate['i'] += 1
        return b

    hT = P.sb("hT", [128, 8, TT])
    xnT = P.sb("xnT", [128, 8, TT], BF16)
    merged = P.sb("merged", [128, 16, TT], BF16)
    ext = P.sb("ext", [128, TT + 3])
    NG, NH = 20, 12
    G = [P.sb("g%d" % i, [128, TT]) for i in range(NG)]
    H = [P.sb("h%d" % i, [128, 2 * TT]) for i in range(NH)]
    lnv, rstd = G[19], G[18]
    _alias = {}

    def RV(tb):
        if SDT == F32:
            return tb[:]
        a = _alias.get(tb.name)
        if a is None:
            ml = nc.lookup_mloc(tb.h)
            a = nc.alloc_sbuf_tensor_at(tb.name + "_r", [128, int(ml.dims[1]) // 4], F32R, offset=int(ml.addr))
            _alias[tb.name] = a
        return a[:]
    ntmp = [G[16], G[17]]

    def bfv(tb, n=None):
        v = tb[:].bitcast(BF16)
        return v

    if do_l1:
        l1_tail = P.sb("l1_tail", [128, 16, 3])
        l1_h = P.sb("l1_h", [128, 16])
        P.op('pool', lambda e: e.memset(l1_tail[:], 0.0), [], [l1_tail])
        P.op('pool', lambda e: e.memset(l1_h[:], 0.0), [], [l1_h])

    def norm_apply(k, gname, out_ap, out_tb):
        if k % 2 == 0:
            P.op('dve', lambda e: e.scalar_tensor_tensor(out=out_ap, in0=hT[:, k, :], scalar=pcc(gname, k), in1=rstd[:],
                                                         op0=MUL, op1=MUL), [hT, pc, rstd], [out_tb])
        else:
            nt = ntmp[(k // 2) % 2]
            P.op('act', lambda e: e.activation(out=nt[:], in_=hT[:, k, :], func=AF.Identity, scale=pcc(gname, k)), [hT, pc], [nt])
            P.op('pool', lambda e: e.tensor_tensor(out=out_ap, in0=nt[:], in1=rstd[:], op=MUL), [nt, rstd], [out_tb])

    def norm_stats():
        ps = nextps()
        for half in range(2):
            sq = H[4 + half]
            sqv = bfv(sq).rearrange("p (k t) -> p k t", k=4)
            P.op('act', lambda e, half=half, sqv=sqv: e.activation(out=sqv, in_=hT[:, 4 * half:4 * half + 4, :], func=AF.Square), [hT], [sq])
            for k in range(4):
                P.op('pe', lambda e, k=k, half=half, sqv=sqv: e.matmul(ps[:], lhsT=ones_bf[:], rhs=sqv[:, k, :], start=(half == 0 and k == 0), stop=(half == 1 and k == 3)),
                     [ones_bf, sq], [ps])
        P.op('act', lambda e: e.activation(out=lnv[:], in_=ps[:], func=AF.Ln, scale=1.0 / D, bias=1e-6), [ps], [lnv])
        P.op('act', lambda e: e.activation(out=rstd[:], in_=lnv[:], func=AF.Exp, scale=-0.5), [lnv], [rstd])

    def rmsnorm_x(gname):
        norm_stats()
        for k in range(8):
            norm_apply(k, gname, xnT[:, k, :], xnT)

    def proj(ps, wb, nk, rhs_tb, rhs_fn):
        for k in range(nk):
            P.op('pe', lambda e, k=k: e.matmul(ps[:], lhsT=wb[:, k, :], rhs=rhs_fn(k), start=(k == 0), stop=(k == nk - 1)),
                 [wb, rhs_tb], [ps])

    def proj_x(name, j):
        w = wld(name, j, 8)
        ps = nextps()
        proj(ps, w, 8, xnT, lambda k: xnT[:, k, :])
        return ps

    def ple(i, t0):
        ptok = H[0]
        ptv = ptok[:].rearrange("p (tb d) -> p tb d", tb=NT)
        pTt = G[12]
        pT = bfv(pTt).rearrange("p (k t) -> p k t", k=2)
        sgs, tmps = [G[0], G[1]], [G[2], G[3]]
        P.dma('sp', ptv, p_d.ap()[i, t0:t0 + TT, :].rearrange("(tb p) d -> p tb d", p=128), writes=[ptok])
        for kc in range(2):
            ps = nextps()
            for tb in range(NT):
                P.op('pe', lambda e, kc=kc, tb=tb, ps=ps: e.transpose(ps[:, tb * 128:(tb + 1) * 128], ptv[:, tb, kc * 128:(kc + 1) * 128], ident[:]),
                     [ptok, ident], [ps])
            P.op('act', lambda e, kc=kc, ps=ps: e.activation(out=pT[:, kc, :], in_=ps[:], func=AF.Copy), [ps], [pTt])
        rmsnorm_x('pen%d' % i)
        for m in range(8):
            sg, tmpa = sgs[m % 2], tmps[m % 2]
            wg = wld('pegate%d' % i, m, 8)
            wu = wld('peup%d' % i, m, 2)
            ps = nextps()
            proj(ps, wg, 8, xnT, lambda k: xnT[:, k, :])
            P.op('act', lambda e, ps=ps: e.activation(out=sg[:], in_=ps[:], func=AF.Sigmoid), [ps], [sg])
            ps2 = nextps()
            proj(ps2, wu, 2, pTt, lambda k: pT[:, k, :])
            P.op('dve', lambda e, ps2=ps2: e.tensor_tensor(out=tmpa[:], in0=ps2[:], in1=sg[:], op=MUL), [ps2, sg], [tmpa])
            P.op('pool', lambda e, m=m: e.tensor_tensor(out=hT[:, m, :], in0=hT[:, m, :], in1=tmpa[:], op=ADD), [hT, tmpa], [hT])

    def out_proj(name):
        for m in range(8):
            wo = wld(name, m, 16)
            ps = nextps()
            proj(ps, wo, 16, merged, lambda k: merged[:, k, :])
            P.op('dve', lambda e, m=m, ps=ps: e.tensor_tensor(out=hT[:, m, :], in0=hT[:, m, :], in1=ps[:], op=ADD), [hT, ps], [hT])

    def conv4(ps, c, tail_tb, wname, bname, acc):
        P.op('pool', lambda e: e.tensor_copy(out=ext[:, 0:3], in_=tail_tb[:, c, :]), [tail_tb], [ext])
        P.op('act', lambda e: e.activation(out=ext[:, 3:TT + 3], in_=ps[:], func=AF.Copy), [ps], [ext])
        P.op('act', lambda e: e.activation(out=acc[:], in_=ps[:], func=AF.Identity, scale=pcc(wname + '3', c), bias=pcc(bname, c)), [ps, pc], [acc])
        P.op('pool', lambda e: e.tensor_copy(out=tail_tb[:, c, :], in_=ext[:, TT:TT + 3]), [ext], [tail_tb])
        for j in range(3):
            P.op('dve', lambda e, j=j: e.scalar_tensor_tensor(out=acc[:], in0=ext[:, j:j + TT], scalar=pcc(wname + str(j), c), in1=acc[:],
                                                              op0=MUL, op1=ADD), [ext, pc, acc], [acc])

    def layer1(t0):
        rmsnorm_x('mixn1')
        for c in range(16):
            xc32, xcb, rg, ig, a2, th = G[6 * (c % 2):6 * (c % 2) + 6]
            xcb_ap = bfv(xcb)[:, 0:TT]
            ps = proj_x('w1in', c)
            conv4(ps, c, l1_tail, 'ccw', 'ccb', xc32)
            P.op('act', lambda e: e.activation(out=xcb_ap, in_=xc32[:], func=AF.Copy), [xc32], [xcb])
            wb = wld('wri', c, 2)
            ps_r = nextps()
            P.op('pe', lambda e: e.matmul(ps_r[:], lhsT=wb[:, 0, :], rhs=xcb_ap, start=True, stop=True), [wb, xcb], [ps_r])
            P.op('act', lambda e: e.activation(out=rg[:], in_=ps_r[:], func=AF.Sigmoid, bias=pcc('cbr', c)), [ps_r, pc], [rg])
            ps_i = nextps()
            P.op('pe', lambda e: e.matmul(ps_i[:], lhsT=wb[:, 1, :], rhs=xcb_ap, start=True, stop=True), [wb, xcb], [ps_i])
            P.op('act', lambda e: e.activation(out=ig[:], in_=ps_i[:], func=AF.Sigmoid, bias=pcc('cbi', c)), [ps_i, pc], [ig])
            P.op('act', lambda e: e.activation(out=a2[:], in_=rg[:], func=AF.Exp, scale=pdc('cneg', c)), [rg, pd], [a2])
            P.op('act', lambda e: e.activation(out=th[:], in_=rg[:], func=AF.Tanh, scale=pdc('csph', c)), [rg, pd], [th])
            P.op('dve', lambda e: e.scalar_tensor_tensor(out=th[:], in0=a2[:], scalar=1.0, in1=th[:], op0=ADD, op1=MUL), [a2, th], [th])
            P.op('dve', lambda e: e.tensor_scalar(out=a2[:], in0=th[:], scalar1=-1.0, scalar2=1.0, op0=MUL, op1=ADD), [th], [a2])
            P.op('dve', lambda e: e.scalar_tensor_tensor(out=th[:], in0=a2[:], scalar=1.0, in1=th[:], op0=ADD, op1=MUL), [a2, th], [th])
            P.op('act', lambda e: e.activation(out=th[:], in_=th[:], func=AF.Sqrt), [th], [th])
            P.op('pool', lambda e: e.tensor_tensor(out=ig[:], in0=xc32[:], in1=ig[:], op=MUL), [xc32, ig], [ig])
            P.op('pool', lambda e: e.tensor_tensor(out=ig[:], in0=ig[:], in1=th[:], op=MUL), [ig, th], [ig])
            P.op('dve', lambda e: e.tensor_tensor_scan(out=rg[:], data0=a2[:], data1=ig[:], initial=l1_h[:, c:c + 1], op0=MUL, op1=ADD),
                 [a2, ig, l1_h], [rg])
            P.op('pool', lambda e: e.tensor_copy(out=l1_h[:, c:c + 1], in_=rg[:, TT - 1:TT]), [rg], [l1_h])
            ps_g = proj_x('w1in', 16 + c)
            P.op('act', lambda e: e.activation(out=xc32[:], in_=ps_g[:], func=AF.Silu), [ps_g], [xc32])
            P.op('dve', lambda e: e.tensor_tensor(out=merged[:, c, :], in0=rg[:], in1=xc32[:], op=MUL), [rg, xc32], [merged])
        out_proj('w1out')

    if do_l0 and do_rwkv:
        identr = P.sb("identr", [128, 128], SDT)
        bdones = cload('bdones')
        bdones_bf = P.sb("bdones_bf", [128, 128], BF16)
        bdones_r = P.sb("bdones_r", [128, 128], SDT)
        maskA = cload('maskA')
        maskB = cload('maskB')
        resetm = cload('resetm')
        P.op('dve', lambda e: e.tensor_copy(out=identr[:], in_=ident[:]), [ident], [identr])
        P.op('dve', lambda e: e.tensor_copy(out=bdones_bf[:], in_=bdones[:]), [bdones], [bdones_bf])
        P.op('dve', lambda e: e.tensor_copy(out=bdones_r[:], in_=bdones[:]), [bdones], [bdones_r])
        rwc = P.sb("rwc", [128, 25])
        P.op('pool', lambda e: e.memset(rwc[:], 0.0), [], [rwc])
        lora_bf = P.sb("lora_bf", [128, TT], BF16)
        Sst = [[P.sb("S%d_%d" % (c, q), [128, 128], F32) for q in range(2)] for c in range(8)]
        for c in range(8):
            P.op('pool', lambda e, c=c: e.memset(Sst[c][0][:], 0.0), [], [Sst[c][0]])
        spar = [0] * 8
        rhs_sb = P.sb("rhs_sb", [128, 128], SDT)
        u_sb = P.sb("u_sb", [128, 128], SDT)
        psRHS = TB("psRHS", PS[7].h[:, 0:128], root=PS[7])
        psU = TB("psU", PS[7].h[:, 128:256], root=PS[7])
        psYT = TB("psYT", PS[6].h[:, 0:128], root=PS[6])
        psSN = TB("psSN", PS[7].h[:, 384:512], root=PS[7])

    def tshift(ps, dst, col, mu_ap, omu_ap):
        P.op('act', lambda e: e.activation(out=dst[:], in_=ps[:], func=AF.Identity, scale=omu_ap), [ps, pd], [dst])
        P.op('dve', lambda e: e.scalar_tensor_tensor(out=dst[:, 1:TT], in0=ps[:, 0:TT - 1], scalar=mu_ap, in1=dst[:, 1:TT], op0=MUL, op1=ADD),
             [ps, pc, dst], [dst])
        P.op('dve', lambda e: e.scalar_tensor_tensor(out=dst[:, 0:1], in0=rwc[:, col:col + 1], scalar=mu_ap, in1=dst[:, 0:1], op0=MUL, op1=ADD),
             [rwc, pc, dst], [dst])
        P.op('act', lambda e: e.activation(out=rwc[:, col:col + 1], in_=ps[:, TT - 1:TT], func=AF.Copy), [ps], [rwc])

    def r3(ap, n=8):
        return ap.rearrange("p (n t) -> p n t", n=n)

    def rwkv(t0):
        Rbd, Abd, Bbd, Kbd, Vbd = H[0:5]
        for i in range(5):
            P.op('pool', lambda e, i=i: e.memset(H[i][:], 0.0), [], [H[i]])
        bdv = [r3(RV(H[i])) for i in range(5)]
        ps = proj_x('w0in', 24)
        lora32 = G[16]
        tshift(ps, lora32, 24, pcc('mu_l'), pdc('omu_l'))
        P.op('act', lambda e: e.activation(out=lora_bf[0:64, :], in_=lora32[0:64, :], func=AF.Tanh), [lora32], [lora_bf])
        P.op('act', lambda e: e.activation(out=lora_bf[64:128, :], in_=lora32[64:128, :], func=AF.Copy), [lora32], [lora_bf])
        pend = []

        def pump(n):
            for _ in range(n):
                if pend:
                    pend.pop(0)()
        for th_ in rwkv_prep(0):
            th_()
        rwkv_blockdiag(0, bdv)
        for c in range(8):
            if c + 1 < 8:
                pend.extend(rwkv_prep(c + 1))
            rwkv_chunks(c, bdv, pump)
            pump(len(pend))
            if c + 1 < 8:
                rwkv_blockdiag(c + 1, bdv)
            rwkv_out(c)

    def rw_bufs(c):
        p_ = c % 2
        d = dict(r32=G[0], k32=G[1], v32=G[2], cump=G[6], Ginv=G[8], Gp=G[9], a32=G[10], kk32=G[11], kka=G[11],
                 rn=G[12], nkk=G[13], keff=G[15], t1=G[16])
        d.update(dict(gr32=(G[3], G[14])[p_], lw=(G[4], G[17])[p_], cum=(G[5], G[18])[p_], Gt=(G[7], G[19])[p_]))
        return d

    def rwkv_prep(c):
        B_ = rw_bufs(c)
        r32, k32, v32, gr32, lw, cum, cump, Gt, Ginv, Gp = [B_[k] for k in ('r32', 'k32', 'v32', 'gr32', 'lw', 'cum', 'cump', 'Gt', 'Ginv', 'Gp')]
        a32, kk32, kka, rn, nkk, keff, t1 = [B_[k] for k in ('a32', 'kk32', 'kka', 'rn', 'nkk', 'keff', 't1')]
        bon = cum
        T = []

        def projshift(dst, mch, nm, col):
            def f():
                ps = proj_x('w0in', mch)
                tshift(ps, dst, col, pcc('mu_' + nm, c), pdc('omu_' + nm, c))
            return f
        T.append(projshift(k32, 8 + c, 'k', 8 + c))

        def f_lw():
            ps = nextps()
            P.op('pe', lambda e: e.matmul(ps[:], lhsT=lup_bf[:, 0, c * 128:(c + 1) * 128], rhs=lora_bf[:], start=True, stop=True), [lup_bf, lora_bf], [ps])
            P.op('act', lambda e: e.activation(out=lw[:], in_=ps[:], func=AF.Sigmoid, bias=pcc('w0', c)), [ps, pc], [lw])
        T.append(f_lw)

        def f_a():
            ps = nextps()
            P.op('pe', lambda e: e.matmul(ps[:], lhsT=lup_bf[:, 1, c * 128:(c + 1) * 128], rhs=lora_bf[:], start=True, stop=True), [lup_bf, lora_bf], [ps])
            P.op('act', lambda e: e.activation(out=a32[:], in_=ps[:], func=AF.Sigmoid, bias=pcc('a0', c)), [ps, pc], [a32])
        T.append(f_a)
        T.append(lambda: P.op('dve', lambda e: e.tensor_tensor_scan(out=cum[:], data0=resetm[:], data1=lw[:], initial=0.0, op0=MUL, op1=ADD), [resetm, lw], [cum]))
        T.append(lambda: P.op('dve', lambda e: e.tensor_tensor(out=cump[:], in0=cum[:], in1=lw[:], op=SUB), [cum, lw], [cump]))
        T.append(lambda: P.op('act', lambda e: e.activation(out=Gt[:], in_=cum[:], func=AF.Exp, scale=-DECAY_SCALE), [cum], [Gt]))
        T.append(lambda: P.op('act', lambda e: e.activation(out=Ginv[:], in_=cum[:], func=AF.Exp, scale=DECAY_SCALE), [cum], [Ginv]))
        T.append(lambda: P.op('act', lambda e: e.activation(out=Gp[:], in_=cump[:], func=AF.Exp, scale=-DECAY_SCALE), [cump], [Gp]))
        T.append(lambda: P.op('dve', lambda e: e.tensor_scalar(out=kk32[:], in0=k32[:], scalar1=pcc('k_k', c), scalar2=None, op0=MUL), [k32, pc], [kk32]))
        sqk = bfv(t1)[:, 0:TT]
        T.append(lambda: P.op('act', lambda e: e.activation(out=sqk, in_=kk32[:], func=AF.Square), [kk32], [t1]))

        def f_rn():
            ps = nextps()
            P.op('pe', lambda e: e.matmul(ps[:], lhsT=bdones_bf[:], rhs=sqk, start=True, stop=True), [bdones_bf, t1], [ps])
            P.op('act', lambda e: e.activation(out=rn[:], in_=ps[:], func=AF.Ln, bias=1e-20), [ps], [rn])
            P.op('act', lambda e: e.activation(out=rn[:], in_=rn[:], func=AF.Exp, scale=-0.5), [rn], [rn])
        T.append(f_rn)
        T.append(lambda: P.op('dve', lambda e: e.scalar_tensor_tensor(out=nkk[:], in0=kk32[:], scalar=-1.0, in1=rn[:], op0=MUL, op1=MUL), [kk32, rn], [nkk]))
        T.append(lambda: P.op('dve', lambda e: e.scalar_tensor_tensor(out=kka[:], in0=nkk[:], scalar=-1.0, in1=a32[:], op0=MUL, op1=MUL), [nkk, a32], [kka]))
        T.append(lambda: P.op('pool', lambda e: e.tensor_scalar(out=t1[:], in0=a32[:], scalar1=-1.0, scalar2=pcc('k_a', c), op0=ADD, op1=MUL), [a32, pc], [t1]))
        T.append(lambda: P.op('dve', lambda e: e.scalar_tensor_tensor(out=keff[:], in0=t1[:], scalar=1.0, in1=k32[:], op0=ADD, op1=MUL), [t1, k32], [keff]))
        T.append(projshift(r32, c, 'r', c))
        rkr = bfv(t1)[:, 0:TT]
        T.append(lambda: P.op('dve', lambda e: e.scalar_tensor_tensor(out=rkr, in0=r32[:], scalar=pcc('r_k', c), in1=keff[:], op0=MUL, op1=MUL), [r32, pc, keff], [t1]))
        T.append(projshift(v32, 16 + c, 'v', 16 + c))

        def f_bon():
            ps = nextps()
            P.op('pe', lambda e: e.matmul(ps[:], lhsT=bdones_bf[:], rhs=rkr, start=True, stop=True), [bdones_bf, t1], [ps])
            P.op('dve', lambda e: e.tensor_tensor(out=bon[:], in0=ps[:], in1=v32[:], op=MUL), [ps, v32], [bon])
        T.append(f_bon)

        def f_gr():
            ps = proj_x('w0in', 33 + c)
            P.op('act', lambda e: e.activation(out=gr32[:], in_=ps[:], func=AF.Silu), [ps], [gr32])
        T.append(f_gr)
        return T

    def rwkv_blockdiag(c, bdv):
        B_ = rw_bufs(c)
        Rv, Av, Bv, Kv, Vv = bdv
        Rbd, Abd, Bbd, Kbd, Vbd = H[0:5]
        for hh in range(2):
            hs_ = slice(hh * 64, hh * 64 + 64)
            for i_, (dstv, dtb, a_, b_) in enumerate(((Rv, Rbd, B_['r32'], B_['Gt']), (Av, Abd, B_['nkk'], B_['Gp']), (Bv, Bbd, B_['kka'], B_['Ginv']), (Kv, Kbd, B_['keff'], B_['Ginv']))):
                P.op('dve' if (i_ + hh) % 2 == 0 else 'pool', lambda e, dstv=dstv, a_=a_, b_=b_, hs_=hs_: e.tensor_tensor(out=dstv[hs_, :, hs_], in0=r3(a_[hs_, :]), in1=r3(b_[hs_, :]), op=MUL),
                     [a_, b_], [dtb])
            P.op('act', lambda e, hs_=hs_: e.activation(out=Vv[hs_, :, hs_], in_=r3(B_['v32'][hs_, :]), func=AF.Copy), [B_['v32']], [Vbd])

    def rwkv_chunks(c, bdv, pump):
        B_ = rw_bufs(c)
        Gt, y32 = B_['Gt'], B_['lw']
        Rv, Av, Bv, Kv, Vv = bdv
        Rbd, Abd, Bbd, Kbd, Vbd = H[0:5]
        for hf in range(2):
            SCt = [H[5], H[6]]
            QTt = [H[7], H[8]]
            SCv = [RV(t).rearrange("p (j m k) -> p j m k", j=2, m=4) for t in SCt]
            QTv = [RV(t).rearrange("p (j m k) -> p j m k", j=2, m=4) for t in QTt]
            SCf = [t[:].rearrange("p (j m k) -> p j m k", j=2, m=4) for t in SCt]
            PQt = [H[9], H[10]]
            PQv = [[RV(PQt[b]).rearrange("p (q j m k) -> p q j m k", q=2, j=2, m=2)[:, par] for b in range(2)] for par in range(2)]
            Xt = H[11]
            Xv = [RV(Xt).rearrange("p (q j k) -> p q j k", q=2, j=4)[:, q] for q in range(2)]
            Xf = [Xt[:].rearrange("p (q j k) -> p q j k", q=2, j=4)[:, q] for q in range(2)]
            for j in range(4):
                n = hf * 4 + j
                sc = SCv[j // 2][:, j % 2]
                qt = QTv[j // 2][:, j % 2]
                A_, B2_ = PS[2], PS[3]
                for m, (l_, r_) in enumerate(((Bv, Av), (Kv, Av), (Bv, Rv), (Kv, Rv))):
                    P.op('pe', lambda e, m=m, l_=l_, r_=r_, n=n: e.matmul(A_[:, m * 128:(m + 1) * 128], lhsT=l_[:, n, :], rhs=r_[:, n, :], start=True, stop=True),
                         [Rbd, Abd, Bbd, Kbd], [A_])
                P.op('dve', lambda e, sc=sc: e.tensor_tensor(out=sc, in0=A_[:].rearrange("p (m k) -> p m k", m=4), in1=maskA[:], op=MUL),
                     [A_, maskA], [SCt[j // 2]])
                P.op('pe', lambda e, n=n: e.matmul(B2_[:, 0:128], lhsT=Av[:, n, :], rhs=Bv[:, n, :], start=True, stop=True), [Abd, Bbd], [B2_])
                for m, l_ in enumerate((Bv, Kv, Vv)):
                    P.op('pe', lambda e, m=m, l_=l_, n=n: e.matmul(B2_[:, (m + 1) * 128:(m + 2) * 128], lhsT=l_[:, n, :], rhs=identr[:], start=True, stop=True),
                         [Bbd, Kbd, Vbd, identr], [B2_])
                P.op('dve', lambda e, qt=qt: e.tensor_tensor(out=qt[:, 0, :], in0=B2_[:, 0:128], in1=maskB[:], op=MUL), [B2_, maskB], [QTt[j // 2]])
                P.op('act', lambda e, qt=qt: e.activation(out=qt[:, 1:4, :], in_=B2_[:, 128:512].rearrange("p (m k) -> p m k", m=3), func=AF.Copy),
                     [B2_], [QTt[j // 2]])
                pump(2)
            for j in range(4):
                P.op('dve', lambda e, j=j: e.tensor_tensor(out=Xv[1][:, j, :], in0=SCf[j // 2][:, j % 2, 0, :], in1=ident[:], op=ADD),
                     [SCt[j // 2], ident], [Xt])
            for lvl in range(1, 6):
                par = lvl % 2
                for b in range(2):
                    bank = PS[4 + b]
                    for jj in range(2):
                        j = 2 * b + jj
                        if lvl == 1:
                            Pp = SCv[j // 2][:, j % 2, 0, :]
                            Qp = QTv[j // 2][:, j % 2, 0, :]
                            rd = [SCt[j // 2], QTt[j // 2]]
                        else:
                            Pp = PQv[1 - par][b][:, jj, 0, :]
                            Qp = PQv[1 - par][b][:, jj, 1, :]
                            rd = [PQt[b]]
                        P.op('pe', lambda e, jj=jj, Pp=Pp, Qp=Qp, bank=bank: e.matmul(bank[:, (2 * jj) * 128:(2 * jj + 1) * 128], lhsT=Qp, rhs=Pp, start=True, stop=True), rd, [bank])
                        P.op('pe', lambda e, jj=jj, Pp=Pp, Qp=Qp, bank=bank: e.matmul(bank[:, (2 * jj + 1) * 128:(2 * jj + 2) * 128], lhsT=Pp, rhs=Qp, start=True, stop=True), rd, [bank])
                    if b == 0:
                        P.op('act', lambda e, b=b, bank=bank, par=par: e.activation(out=PQv[par][b], in_=bank[:].rearrange("p (j m k) -> p j m k", j=2, m=2), func=AF.Copy),
                             [bank], [PQt[b]])
                    else:
                        P.op('dve', lambda e, b=b, bank=bank, par=par: e.tensor_copy(out=PQv[par][b], in_=bank[:].rearrange("p (j m k) -> p j m k", j=2, m=2)),
                             [bank], [PQt[b]])
                XB = PS[6]
                for j in range(4):
                    Ql = PQv[par][j // 2][:, j % 2, 1, :]
                    P.op('pe', lambda e, j=j, Ql=Ql, par=par: e.matmul(XB[:, j * 128:(j + 1) * 128], lhsT=Ql, rhs=Xv[par][:, j, :], start=True, stop=True), [PQt[j // 2], Xt], [XB])
                P.op('dve', lambda e, par=par: e.tensor_tensor(out=Xv[1 - par], in0=XB[:].rearrange("p (j k) -> p j k", j=4), in1=Xf[par], op=ADD), [XB, Xt], [Xt])
                pump(1)
            for j in range(4):
                n = hf * 4 + j
                sc = SCv[j // 2][:, j % 2]
                qt = QTv[j // 2][:, j % 2]
                sct, qtt = SCt[j // 2], QTt[j // 2]
                S0 = Sst[c][spar[c]]
                S1 = Sst[c][1 - spar[c]]
                spar[c] = 1 - spar[c]
                P.op('pe', lambda e, n=n, S0=S0: e.matmul(psRHS[:], lhsT=Av[:, n, :], rhs=RV(S0), start=True, stop=False), [Abd, S0], [psRHS])
                P.op('pe', lambda e, sc=sc, qt=qt: e.matmul(psRHS[:], lhsT=sc[:, 1, :], rhs=qt[:, 3, :], start=False, stop=True), [sct, qtt], [psRHS])
                P.op('dve', lambda e: e.tensor_copy(out=rhs_sb[:], in_=psRHS[:]), [psRHS], [rhs_sb])
                P.op('pe', lambda e, j=j: e.matmul(psU[:], lhsT=Xv[0][:, j, :], rhs=rhs_sb[:], start=True, stop=True), [Xt, rhs_sb], [psU])
                P.op('dve', lambda e: e.tensor_copy(out=u_sb[:], in_=psU[:]), [psU], [u_sb])
                P.op('pe', lambda e, qt=qt: e.matmul(psSN[:], lhsT=qt[:, 1, :], rhs=u_sb[:], start=True, stop=False), [qtt, u_sb], [psSN])
                P.op('pe', lambda e, qt=qt: e.matmul(psSN[:], lhsT=qt[:, 2, :], rhs=qt[:, 3, :], start=False, stop=False), [qtt], [psSN])
                P.op('pe', lambda e, S0=S0: e.matmul(psSN[:], lhsT=identr[:], rhs=RV(S0), start=False, stop=True), [identr, S0], [psSN])
                P.op('dve', lambda e, n=n, S1=S1: e.tensor_scalar(out=RV(S1), in0=psSN[:], scalar1=Gt[:, n * 64 + 63:n * 64 + 64], scalar2=None, op0=MUL), [psSN, Gt], [S1])
                P.op('pe', lambda e, n=n, S0=S0: e.matmul(psYT[:], lhsT=RV(S0), rhs=Rv[:, n, :], start=True, stop=False), [S0, Rbd], [psYT])
                P.op('pe', lambda e, sc=sc: e.matmul(psYT[:], lhsT=u_sb[:], rhs=sc[:, 2, :], start=False, stop=False), [u_sb, sct], [psYT])
                P.op('pe', lambda e, sc=sc, qt=qt: e.matmul(psYT[:], lhsT=qt[:, 3, :], rhs=sc[:, 3, :], start=False, stop=True), [qtt, sct], [psYT])
                for hh in range(2):
                    hs_ = slice(hh * 64, hh * 64 + 64)
                    P.op('act', lambda e, hs_=hs_, n=n: e.activation(out=y32[hs_, n * 64:(n + 1) * 64], in_=psYT[hs_, hs_], func=AF.Copy), [psYT], [y32])
                pump(2)

    def rwkv_out(c):
        B_ = rw_bufs(c)
        y32, dd, bon, gr32 = B_['lw'], B_['Gt'], B_['cum'], B_['gr32']
        dump('y_rwkv%d' % c, y32, y32[:])
        yrv = RV(y32)
        P.op('act', lambda e: e.activation(out=yrv, in_=y32[:], func=AF.Copy), [y32], [y32])
        ps = nextps()
        P.op('pe', lambda e: e.matmul(ps[:], lhsT=bdones_r[:], rhs=yrv, start=True, stop=True), [bdones_r, y32], [ps])
        P.op('dve', lambda e: e.scalar_tensor_tensor(out=dd[:], in0=ps[:], scalar=-1.0 / 64, in1=y32[:], op0=MUL, op1=ADD), [ps, y32], [dd])
        sq, rs = H[5], H[6]
        sqv = RV(sq)[:, 0:TT]
        P.op('act', lambda e: e.activation(out=sqv, in_=dd[:], func=AF.Square), [dd], [sq])
        ps2 = nextps()
        P.op('pe', lambda e: e.matmul(ps2[:], lhsT=bdones_r[:], rhs=sqv, start=True, stop=True), [bdones_r, sq], [ps2])
        P.op('act', lambda e: e.activation(out=rs[:, 0:TT], in_=ps2[:], func=AF.Ln, scale=1.0 / 64, bias=64e-5), [ps2], [rs])
        P.op('act', lambda e: e.activation(out=rs[:, 0:TT], in_=rs[:, 0:TT], func=AF.Exp, scale=-0.5), [rs], [rs])
        P.op('dve', lambda e: e.tensor_tensor(out=dd[:], in0=dd[:], in1=rs[:, 0:TT], op=MUL), [dd, rs], [dd])
        P.op('act', lambda e: e.activation(out=dd[:], in_=dd[:], func=AF.Identity, scale=pcc('ln_w', c), bias=pcc('ln_b', c)), [dd, pc], [dd])
        P.op('pool', lambda e: e.tensor_tensor(out=dd[:], in0=dd[:], in1=bon[:], op=ADD), [dd, bon], [dd])
        P.op('dve', lambda e: e.tensor_tensor(out=merged[:, c, :], in0=dd[:], in1=gr32[:], op=MUL), [dd, gr32], [merged])

    if do_l0 and do_mlstm:
        causal = cload('causal')
        sel = P.sb("c_sel", [4, 4, 128])
        P.dma('sp', sel[:], din['sel'].ap(), writes=[sel])
        onesf = P.sb("onesf", [128, TT])
        P.op('pool', lambda e: e.memset(onesf[:], 1.0), [], [onesf])
        ones_r = P.sb("ones_r", [128, 128], SDT)
        P.op('dve', lambda e: e.tensor_copy(out=ones_r[:], in_=onesf[:, 0:128]), [onesf], [ones_r])
        m_tail = P.sb("m_tail", [128, 8, 3])
        P.op('pool', lambda e: e.memset(m_tail[:], 0.0), [], [m_tail])
        CT32 = P.sb("CT32", [128, 4, 2, 256])
        CTbf = P.sb("CTbf", [128, 4, 2, 256], BF16)
        nr32 = P.sb("nr32", [128, 4, 2, 128])
        nrbf = P.sb("nrbf", [128, 4, 2, 128], BF16)
        for t_ in (CT32, CTbf, nr32, nrbf):
            P.op('pool', lambda e, t_=t_: e.memset(t_[:], 0.0), [], [t_])
        mcar = P.sb("mcar", [4, 2])
        P.op('pool', lambda e: e.memset(mcar[:], 0.0), [], [mcar])
        MxE = P.sb("MxE", [4, 5])
        dec = P.sb("dec", [4, 4])
        smallb = P.sb("smallb", [128, 32])
        qe = P.sb("qe", [128, 2, 128], BF16)
        qw = P.sb("qw", [128, 2, 128], BF16)
        kg = P.sb("kg", [128, 2, 128], BF16)
        kgt = P.sb("kgt", [128, 256], BF16)
        st_bf = P.sb("st_bf", [128, 128], BF16)
        ddm = P.sb("ddm", [128, 128])
        recm = P.sb("recm", [128, 128])

    def mlstm(t0):
        qTt, kTt, vTt = [H[0], H[1]], [H[2], H[3]], [H[4], H[5]]
        ktt, vtt = [H[6], H[7]], [H[8], H[9]]
        xct = [H[10], H[11]]

        def fm(tl, c):
            return bfv(tl[c // 4]).rearrange("p (k t) -> p k t", k=4)[:, c % 4, :]

        def tokv(tl, tb):
            return bfv(tl[tb // 2]).rearrange("p (b d) -> p b d", b=2)[:, tb % 2, :]
        acc, xmb = G[9], G[10]
        xmb_ap = bfv(xmb)[:, 0:TT]
        gi_ps, gf_ps = PS[2], PS[3]
        for c in range(8):
            ps = proj_x('w0in', 25 + c)
            conv4(ps, c, m_tail, 'mcw', 'mcb', acc)
            P.op('act', lambda e, c=c: e.activation(out=fm(xct, c), in_=acc[:], func=AF.Silu), [acc], [xct[c // 4]])
            P.op('pool', lambda e: e.tensor_copy(out=xmb_ap, in_=ext[:, 3:TT + 3]), [ext], [xmb])
            for t_, (dst, src_ap, src_tb) in enumerate(((qTt, fm(xct, c), xct[c // 4]), (kTt, fm(xct, c), xct[c // 4]), (vTt, xmb_ap, xmb))):
                ps = nextps()
                P.op('pe', lambda e, ps=ps, t_=t_, src_ap=src_ap: e.matmul(ps[:], lhsT=wqkv_bf[:, t_, c, :], rhs=src_ap, start=True, stop=True),
                     [wqkv_bf, src_tb], [ps])
                P.op('act' if t_ != 1 else 'dve',
                     (lambda e, ps=ps, dst=dst: e.activation(out=fm(dst, c), in_=ps[:], func=AF.Copy)) if t_ != 1 else
                     (lambda e, ps=ps, dst=dst: e.tensor_copy(out=fm(dst, c), in_=ps[:])), [ps], [dst[c // 4]])
                j = t_ * 8 + c
                first, last = (c == 0 and t_ == 0), (c == 7 and t_ == 2)
                P.op('pe', lambda e, j=j, dst=dst, first=first, last=last: e.matmul(gi_ps[0:4, :], lhsT=wif_bf[:, 0, j, :], rhs=fm(dst, c), start=first, stop=last),
                     [wif_bf, dst[c // 4]], [gi_ps])
                P.op('pe', lambda e, j=j, dst=dst, first=first, last=last: e.matmul(gf_ps[0:4, :], lhsT=wif_bf[:, 1, j, :], rhs=fm(dst, c), start=first, stop=last),
                     [wif_bf, dst[c // 4]], [gf_ps])
            for t_, (dstl, src_ap, src_tb) in ((1, (ktt, fm(xct, c), xct[c // 4])), (2, (vtt, xmb_ap, xmb))):
                ps = nextps()
                for tb in range(NT):
                    P.op('pe', lambda e, ps=ps, tb=tb, t_=t_, src_ap=src_ap: e.matmul(ps[:, tb * 128:(tb + 1) * 128], lhsT=src_ap[:, tb * 128:(tb + 1) * 128], rhs=wqkv_bf[:, t_, c, :],
                                                                                      start=True, stop=True), [wqkv_bf, src_tb], [ps])
                for half in range(2):
                    dv = bfv(dstl[half]).rearrange("p (b d) -> p b d", b=2)[:, :, c * 128:(c + 1) * 128]
                    P.op('act' if half == 0 else 'dve',
                         (lambda e, ps=ps, dv=dv, half=half: e.activation(out=dv, in_=ps[:, half * 256:(half + 1) * 256].rearrange("p (b d) -> p b d", b=2), func=AF.Copy)) if half == 0 else
                         (lambda e, ps=ps, dv=dv, half=half: e.tensor_copy(out=dv, in_=ps[:, half * 256:(half + 1) * 256].rearrange("p (b d) -> p b d", b=2))),
                         [ps], [dstl[half]])
        rg_, re2, rwi, rem, rli, rFp, rag, rMx, rtmp = G[11], G[12], G[13], G[14], G[15], G[16], G[17], G[18], G[19]
        R4 = slice(0, 4)
        P.op('act', lambda e: e.activation(out=rli[R4, :], in_=gi_ps[R4, :], func=AF.Identity, bias=pc[R4, PC['bif_i']:PC['bif_i'] + 1]), [gi_ps, pc], [rli])
        P.op('act', lambda e: e.activation(out=rtmp[R4, :], in_=gf_ps[R4, :], func=AF.Exp, scale=-1.0, bias=pd[R4, PD['negbf']:PD['negbf'] + 1]), [gf_ps, pd], [rtmp])
        P.op('act', lambda e: e.activation(out=rtmp[R4, :], in_=rtmp[R4, :], func=AF.Ln, bias=1.0), [rtmp], [rtmp])
        P.op('dve', lambda e: e.tensor_tensor_scan(out=rFp[R4, :], data0=onesf[R4, :], data1=rtmp[R4, :], initial=mcar[:, 0:1], op0=MUL, op1=ADD),
             [onesf, rtmp, mcar], [rFp])
        P.op('pool', lambda e: e.tensor_tensor(out=rag[R4, :], in0=rli[R4, :], in1=rFp[R4, :], op=ADD), [rli, rFp], [rag])
        P.op('dve', lambda e: e.tensor_tensor_scan(out=rMx[R4, :], data0=rag[R4, :], data1=rag[R4, :], initial=mcar[:, 1:2], op0=MAX, op1=MAX),
             [rag, mcar], [rMx])
        P.op('pool', lambda e: e.tensor_copy(out=MxE[:, 0:1], in_=mcar[:, 1:2]), [mcar], [MxE])
        P.op('pool', lambda e: e.tensor_copy(out=MxE[:, 1:5], in_=rMx[R4, 127:TT:128]), [rMx], [MxE])
        P.op('pool', lambda e: e.tensor_copy(out=mcar[:, 0:1], in_=rFp[R4, TT - 1:TT]), [rFp], [mcar])
        P.op('pool', lambda e: e.tensor_copy(out=mcar[:, 1:2], in_=rMx[R4, TT - 1:TT]), [rMx], [mcar])
        v3 = lambda t_: t_[R4, :].rearrange("p (q t) -> p q t", q=NT)
        mend = MxE[:, 1:5].unsqueeze(2).broadcast_to([4, NT, 128])
        mprev = MxE[:, 0:4].unsqueeze(2).broadcast_to([4, NT, 128])
        P.op('dve', lambda e: e.tensor_tensor(out=v3(rg_), in0=v3(rag), in1=mend, op=SUB), [rag, MxE], [rg_])
        P.op('act', lambda e: e.activation(out=rg_[R4, :], in_=rg_[R4, :], func=AF.Exp), [rg_], [rg_])
        P.op('dve', lambda e: e.tensor_tensor(out=v3(re2), in0=mend, in1=v3(rMx), op=SUB), [rMx, MxE], [re2])
        P.op('act', lambda e: e.activation(out=re2[R4, :], in_=re2[R4, :], func=AF.Exp), [re2], [re2])
        P.op('dve', lambda e: e.tensor_tensor(out=v3(rwi), in0=mprev, in1=v3(rMx), op=SUB), [rMx, MxE], [rwi])
        P.op('act', lambda e: e.activation(out=rwi[R4, :], in_=rwi[R4, :], func=AF.Exp), [rwi], [rwi])
        P.op('dve', lambda e: e.tensor_tensor(out=rem[R4, :], in0=rFp[R4, :], in1=rMx[R4, :], op=SUB), [rFp, rMx], [rem])
        P.op('act', lambda e: e.activation(out=rem[R4, :], in_=rem[R4, :], func=AF.Exp), [rem], [rem])
        P.op('dve', lambda e: e.tensor_tensor(out=dec[:], in0=MxE[:, 0:4], in1=MxE[:, 1:5], op=SUB), [MxE], [dec])
        P.op('act', lambda e: e.activation(out=dec[:], in_=dec[:], func=AF.Exp), [dec], [dec])
        sp_ = PS[5]
        for q in range(NT):
            P.op('pe', lambda e, q=q: e.matmul(sp_[:, q * 4:(q + 1) * 4], lhsT=rg_[R4, q * 128:(q + 1) * 128], rhs=ident[0:4, 0:4], start=True, stop=True),
                 [rg_, ident], [sp_])
        for h in range(4):
            P.op('pe', lambda e, h=h: e.matmul(sp_[:, 16 + h * 4:16 + (h + 1) * 4], lhsT=sel[:, h, :], rhs=dec[:], start=True, stop=True), [sel, dec], [sp_])
        P.op('act', lambda e: e.activation(out=smallb[:], in_=sp_[:, 0:32], func=AF.Copy), [sp_], [smallb])
        for h in range(4):
            mlstm_head(h, qTt, kTt, ktt, vtt, xct, fm, tokv, (rg_, re2, rwi, rem))

    def mlstm_head(h, qTt, kTt, ktt, vtt, xct, fm, tokv, rows):
        bc = G[0:4]
        for i in range(4):
            ps = nextps()
            P.op('pe', lambda e, ps=ps, i=i: e.matmul(ps[:], lhsT=sel[:, h, :], rhs=rows[i][0:4, :], start=True, stop=True), [sel, rows[i]], [ps])
            P.op('act' if i % 2 == 0 else 'dve',
                 (lambda e, ps=ps, i=i: e.activation(out=bc[i][:], in_=ps[:], func=AF.Copy)) if i % 2 == 0 else
                 (lambda e, ps=ps, i=i: e.tensor_copy(out=bc[i][:], in_=ps[:])), [ps], [bc[i]])
        g_bc, e2_bc, wi_bc, em_bc = bc
        h32 = [G[4], G[5]]
        gms = [G[6], G[7]]
        for vc in range(2):
            ps = proj_x('w0in', 41 + 2 * h + vc)
            P.op('act', lambda e, ps=ps, vc=vc: e.activation(out=gms[vc][:], in_=ps[:], func=AF.Silu), [ps], [gms[vc]])
        qt_ = qTt[h // 2]
        kt_ = kTt[h // 2]
        qv = bfv(qt_).rearrange("p (k t) -> p k t", k=4)[:, 2 * (h % 2):2 * (h % 2) + 2, :]
        kv = bfv(kt_).rearrange("p (k t) -> p k t", k=4)[:, 2 * (h % 2):2 * (h % 2) + 2, :]
        CB, SB_, NB = PS[6], PS[7], PS[5]
        for q in range(NT):
            ts_ = slice(q * 128, (q + 1) * 128)
            bcast = lambda t_: t_[:, ts_].unsqueeze(1).broadcast_to([128, 2, 128])
            P.op('dve', lambda e: e.tensor_tensor(out=qe[:], in0=qv[:, :, ts_], in1=bcast(e2_bc), op=MUL), [qt_, e2_bc], [qe])
            P.op('pool', lambda e: e.tensor_tensor(out=qw[:], in0=qv[:, :, ts_], in1=bcast(wi_bc), op=MUL), [qt_, wi_bc], [qw])
            P.op('dve', lambda e: e.scalar_tensor_tensor(out=kg[:], in0=kv[:, :, ts_], scalar=0.0625, in1=bcast(g_bc), op0=MUL, op1=MUL), [kt_, g_bc], [kg])
            ktok = tokv(ktt, q)[:, h * 256:(h + 1) * 256]
            vtok = tokv(vtt, q)[:, h * 256:(h + 1) * 256]
            P.op('pool', lambda e: e.tensor_scalar(out=kgt[:], in0=ktok, scalar1=smallb[:, q * 4 + h:q * 4 + h + 1], scalar2=0.0625, op0=MUL, op1=MUL),
                 [ktt[q // 2], smallb], [kgt])
            for kc in range(2):
                P.op('pe', lambda e, kc=kc: e.matmul(CB[:, 0:128], lhsT=kg[:, kc, :], rhs=qe[:, kc, :], start=(kc == 0), stop=(kc == 1)), [kg, qe], [CB])
            P.op('dve', lambda e: e.tensor_tensor(out=st_bf[:], in0=CB[:, 0:128], in1=causal[:], op=MUL), [CB, causal], [st_bf])
            for vc in range(2):
                o_ = CB[:, 128 + vc * 128:256 + vc * 128]
                P.op('pe', lambda e, o_=o_, vc=vc: e.matmul(o_, lhsT=vtok[:, vc * 128:(vc + 1) * 128], rhs=st_bf[:], start=True, stop=False), [vtt[q // 2], st_bf], [CB])
                for kc in range(2):
                    P.op('pe', lambda e, o_=o_, vc=vc, kc=kc: e.matmul(o_, lhsT=CTbf[:, h, kc, vc * 128:(vc + 1) * 128], rhs=qw[:, kc, :], start=False, stop=(kc == 1)),
                         [CTbf, qw], [CB])
            o_ = CB[:, 384:512]
            P.op('pe', lambda e, o_=o_: e.matmul(o_, lhsT=ones_bf[:], rhs=st_bf[:], start=True, stop=False), [ones_bf, st_bf], [CB])
            for kc in range(2):
                P.op('pe', lambda e, o_=o_, kc=kc: e.matmul(o_, lhsT=nrbf[:, h, kc, :], rhs=qw[:, kc, :], start=False, stop=(kc == 1)), [nrbf, qw], [CB])
            P.op('act', lambda e: e.activation(out=ddm[:], in_=CB[:, 384:512], func=AF.Abs), [CB], [ddm])
            P.op('dve', lambda e: e.tensor_tensor(out=ddm[:], in0=ddm[:], in1=em_bc[:, ts_], op=MAX), [ddm, em_bc], [ddm])
            P.op('dve', lambda e: e.tensor_scalar(out=ddm[:], in0=ddm[:], scalar1=1e-6, scalar2=None, op0=ADD), [ddm], [ddm])
            P.op('dve', lambda e: e.reciprocal(out=recm[:], in_=ddm[:]), [ddm], [recm])
            for vc in range(2):
                P.op('dve', lambda e, vc=vc: e.tensor_tensor(out=h32[vc][:, ts_], in0=CB[:, 128 + vc * 128:256 + vc * 128], in1=recm[:], op=MUL), [CB, recm], [h32[vc]])
            for kc in range(2):
                P.op('pe', lambda e, kc=kc: e.matmul(SB_[:, kc * 256:(kc + 1) * 256], lhsT=kgt[:, kc * 128:(kc + 1) * 128], rhs=vtok, start=True, stop=True),
                     [kgt, vtt[q // 2]], [SB_])
                P.op('pe', lambda e, kc=kc: e.matmul(NB[:, 64 + kc * 128:64 + (kc + 1) * 128], lhsT=kgt[:, kc * 128:(kc + 1) * 128], rhs=ones_bf[:], start=True, stop=True),
                     [kgt, ones_bf], [NB])
            dcol = smallb[:, 16 + h * 4 + q:16 + h * 4 + q + 1]
            P.op('dve', lambda e: e.scalar_tensor_tensor(out=CT32[:, h], in0=CT32[:, h], scalar=dcol, in1=SB_[:].rearrange("p (k v) -> p k v", k=2), op0=MUL, op1=ADD),
                 [CT32, smallb, SB_], [CT32])
            P.op('act', lambda e: e.activation(out=CTbf[:, h], in_=CT32[:, h], func=AF.Copy), [CT32], [CTbf])
            P.op('dve', lambda e: e.scalar_tensor_tensor(out=nr32[:, h], in0=nr32[:, h], scalar=dcol, in1=NB[:, 64:320].rearrange("p (k v) -> p k v", k=2), op0=MUL, op1=ADD),
                 [nr32, smallb, NB], [nr32])
            P.op('pool', lambda e: e.tensor_copy(out=nrbf[:, h], in_=nr32[:, h]), [nr32], [nrbf])
        dump('h_m%d' % h, h32[0], h32[0][:])
        hr = [G[8], G[9]]
        ps = nextps()
        for vc in range(2):
            P.op('act', lambda e, vc=vc: e.activation(out=RV(hr[vc]), in_=h32[vc][:], func=AF.Copy), [h32[vc]], [hr[vc]])
            P.op('pe', lambda e, ps=ps, vc=vc: e.matmul(ps[:], lhsT=ones_r[:], rhs=RV(hr[vc]), start=(vc == 0), stop=(vc == 1)), [ones_r, hr[vc]], [ps])
        for vc in range(2):
            P.op('dve', lambda e, ps=ps, vc=vc: e.scalar_tensor_tensor(out=h32[vc][:], in0=ps[:], scalar=-1.0 / 256, in1=h32[vc][:], op0=MUL, op1=ADD), [ps, h32[vc]], [h32[vc]])
        ps2 = nextps()
        for vc in range(2):
            P.op('act', lambda e, vc=vc: e.activation(out=RV(hr[vc]), in_=h32[vc][:], func=AF.Square), [h32[vc]], [hr[vc]])
            P.op('pe', lambda e, ps2=ps2, vc=vc: e.matmul(ps2[:], lhsT=ones_r[:], rhs=RV(hr[vc]), start=(vc == 0), stop=(vc == 1)), [ones_r, hr[vc]], [ps2])
        rs = G[10]
        P.op('act', lambda e, ps2=ps2: e.activation(out=rs[:], in_=ps2[:], func=AF.Ln, scale=1.0 / 256, bias=1e-5), [ps2], [rs])
        P.op('act', lambda e: e.activation(out=rs[:], in_=rs[:], func=AF.Exp, scale=-0.5), [rs], [rs])
        for vc in range(2):
            ch = 2 * h + vc
            P.op('dve', lambda e, vc=vc: e.tensor_tensor(out=h32[vc][:], in0=h32[vc][:], in1=rs[:], op=MUL), [h32[vc], rs], [h32[vc]])
            P.op('act', lambda e, vc=vc, ch=ch: e.activation(out=h32[vc][:], in_=h32[vc][:], func=AF.Identity, scale=pcc('mnorm', ch)), [h32[vc], pc], [h32[vc]])
            P.op('dve', lambda e, vc=vc, ch=ch: e.scalar_tensor_tensor(out=h32[vc][:], in0=fm(xct, ch), scalar=pcc('mskip', ch), in1=h32[vc][:], op0=MUL, op1=ADD),
                 [xct[ch // 4], pc, h32[vc]], [h32[vc]])
            P.op('pool', lambda e, vc=vc, ch=ch: e.tensor_tensor(out=merged[:, 8 + ch, :], in0=h32[vc][:], in1=gms[vc][:], op=MUL), [h32[vc], gms[vc]], [merged])

    def layer0(t0):
        rmsnorm_x('mixn0')
        pstate['set'] = [0, 1]
        if do_rwkv:
            rwkv(t0)
        if do_mlstm:
            mlstm(t0)
        pstate['set'] = list(range(8))
        out_proj('w0out')

    if do_l0 and not (do_rwkv and do_mlstm):
        P.op('pool', lambda e: e.memset(merged[:], 0.0), [], [merged])

    for ti in range(ntiles):
        t0 = ti * TT
        for tb in range(NT):
            P.dma('sp', H[tb][:], x_d.ap()[t0 + tb * 128:t0 + (tb + 1) * 128, :], writes=[H[tb]])
        for kc in range(8):
            ps = nextps()
            for tb in range(NT):
                P.op('pe', lambda e, kc=kc, tb=tb, ps=ps: e.transpose(ps[:, tb * 128:(tb + 1) * 128], H[tb][:, kc * 128:(kc + 1) * 128], ident[:]),
                     [H[tb], ident], [ps])
            P.op('act' if kc % 2 == 0 else 'dve',
                 (lambda e, kc=kc, ps=ps: e.activation(out=hT[:, kc, :], in_=ps[:], func=AF.Copy)) if kc % 2 == 0 else
                 (lambda e, kc=kc, ps=ps: e.tensor_copy(out=hT[:, kc, :], in_=ps[:])), [ps], [hT])
        if do_l0:
            layer0(t0)
            ple(0, t0)
        if do_l1:
            layer1(t0)
            ple(1, t0)
        norm_stats()
        for k in range(8):
            of = G[k % 4]
            norm_apply(k, 'finn', of[:], of)
            ps = nextps()
            for tb in range(NT):
                P.op('pe', lambda e, tb=tb, ps=ps, of=of: e.transpose(ps[:, tb * 128:(tb + 1) * 128], of[:, tb * 128:(tb + 1) * 128], ident[:]),
                     [of, ident], [ps])
            for tb in range(NT):
                P.op('act' if k % 2 == 0 else 'dve',
                     (lambda e, k=k, ps=ps, tb=tb: e.activation(out=H[tb][:, k * 128:(k + 1) * 128], in_=ps[:, tb * 128:(tb + 1) * 128], func=AF.Copy)) if k % 2 == 0 else
                     (lambda e, k=k, ps=ps, tb=tb: e.tensor_copy(out=H[tb][:, k * 128:(k + 1) * 128], in_=ps[:, tb * 128:(tb + 1) * 128])),
                     [ps], [H[tb]])
        for tb in range(NT):
            P.dma('sp', o_d.ap()[t0 + tb * 128:t0 + (tb + 1) * 128, :], H[tb][:], reads=[H[tb]])
    P.finish()
    print("sbuf bytes/partition:", P.sbytes, "sems:", P.nsem, "instr:", {e: P.cnt[e] for e in ENGS})
    return nc


def kernel(**inputs):
    x = np.asarray(inputs['x'], np.float32)
    p = np.asarray(inputs['p'], np.float32)
    B, S, _ = x.shape
    shared = host_prep(inputs)
    nc = build_program(S)
    in_maps = []
    for b in range(B):
        m = dict(shared)
        m['x'] = np.ascontiguousarray(x[b])
        m['p'] = np.ascontiguousarray(p[:, b])
        in_maps.append(m)
    res = run_bass_kernel_spmd(nc, in_maps, core_ids=list(range(B)))
    return np.stack([np.asarray(r['out'], np.float32) for r in res.results], axis=0)
```

```python
import numpy as np
import concourse.bass as bass
import concourse.mybir as mybir
from concourse.bass_utils import run_bass_kernel_spmd
from contextlib import ExitStack

F32 = mybir.dt.float32
BF16 = mybir.dt.bfloat16
F32R = mybir.dt.float32r
SDT = F32R
AF = mybir.ActivationFunctionType
ALU = mybir.AluOpType
AX = mybir.AxisListType

ENGS = ('pe', 'act', 'dve', 'pool', 'sp')
EIDX = {e: i for i, e in enumerate(ENGS)}
EPOCH = 30000

D = 1024
TT = 512
NT = TT // 128
PLE = 256
DECAY_SCALE = 0.6065306597126334


class TB:
    __slots__ = ('name', 'h', 'lw', 'rd', 'dkey', 'root', 'psum')

    def __init__(self, name, h, root=None, psum=False):
        self.name = name
        self.h = h
        self.lw = None
        self.rd = {}
        self.dkey = None
        self.root = root if root is not None else self
        self.psum = psum or (root is not None and root.psum)

    def __getitem__(self, k):
        return self.h[k]


class _Rec:
    def __init__(self):
        self.call = None

    def __getattr__(self, name):
        def f(*a, **k):
            self.call = (name, a, k)
        return f


class Prog:
    def __init__(self, nc):
        self.nc = nc
        self.es = ExitStack()
        self.ops = {e: [] for e in ENGS}
        self.cnt = {e: 0 for e in ENGS}
        self.seen = {e: {} for e in ENGS}
        self.clk = {e: [] for e in ENGS}
        self.dcnt = {}
        self.dclk = {}
        self.sems = {}
        self.nsem = 0
        self.ndma = 0
        self.sbytes = 0

    def sb(self, name, shape, dt=F32):
        n = 1
        for s in shape[1:]:
            n *= s
        self.sbytes += n * (2 if dt == BF16 else 4)
        return TB(name, self.es.enter_context(self.nc.sbuf_tensor(name, list(shape), dt)))

    def ps(self, name, shape, dt=F32):
        return TB(name, self.es.enter_context(self.nc.psum_tensor(name, list(shape), dt)), psum=True)

    def dram(self, name, shape, dt, kind):
        return TB(name, self.nc.dram_tensor(name, list(shape), dt, kind=kind))

    def _sem(self, key):
        s = self.sems.get(key)
        if s is None:
            s = self.es.enter_context(self.nc.semaphore("s%d" % self.nsem))
            self.nsem += 1
            self.sems[key] = s
        return s

    def _semval(self, key, count):
        if key in EIDX:
            ep = (count - 1) // EPOCH
            return self._sem((key, ep)), count - ep * EPOCH
        return self._sem(key), 16 * count

    def _deps(self, e, reads, writes):
        seen = self.seen[e]
        need = {}

        def req(ev):
            if ev is None:
                return
            k, c = ev
            if k == 'pe' and e == 'pe':
                return
            if seen.get(k, 0) >= c:
                return
            if need.get(k, 0) < c:
                need[k] = c
        for t in reads:
            req(t.lw)
        for t in writes:
            req(t.lw)
            for k, c in t.rd.items():
                req((k, c))
        keys = list(need.keys())
        for k in keys:
            if k not in need:
                continue
            c = need[k]
            ck = self.clk[k][c - 1] if k in EIDX else self.dclk[(k, c)]
            for k2 in keys:
                if k2 != k and k2 in EIDX and k2 in need and ck[EIDX[k2]] >= need[k2]:
                    del need[k2]
        waits = []
        for k, c in need.items():
            waits.append(self._semval(k, c))
            ck = self.clk[k][c - 1] if k in EIDX else self.dclk[(k, c)]
            for e2, v in zip(ENGS, ck):
                if seen.get(e2, 0) < v:
                    seen[e2] = v
            if seen.get(k, 0) < c:
                seen[k] = c
        return waits

    def _snapshot(self, e):
        s = self.seen[e]
        return tuple(s.get(x, 0) for x in ENGS)

    def op(self, e, fn, reads=(), writes=()):
        writes = [t.root for t in writes] + [t.root for t in reads if t.psum]
        reads = [t.root for t in reads if not t.psum]
        rec = _Rec()
        fn(rec)
        fn = rec.call
        waits = self._deps(e, reads, writes)
        self.cnt[e] += 1
        c = self.cnt[e]
        snap = list(self._snapshot(e))
        snap[EIDX[e]] = c
        self.clk[e].append(tuple(snap))
        inc = self._semval(e, c)[0]
        self.ops[e].append((waits, fn, inc, 1))
        ev = (e, c)
        for t in reads:
            if t.rd.get(e, 0) < c:
                t.rd[e] = c
        for t in writes:
            t.lw = ev
            t.rd = {}
        return ev

    def dma(self, q, out, in_, reads=(), writes=(), key=None, **kw):
        if key is None:
            t0 = (list(writes) + list(reads))[0]
            if t0.dkey is None:
                t0.dkey = ('dma', self.ndma)
                self.ndma += 1
            key = t0.dkey
        reads = [t.root for t in reads]
        writes = [t.root for t in writes]
        waits = self._deps(q, reads, writes)
        self.dcnt[key] = self.dcnt.get(key, 0) + 1
        c = self.dcnt[key]
        self.dclk[(key, c)] = self._snapshot(q)
        sem = self._sem(key)
        self.ops[q].append((waits, ('dma_start', (), dict(out=out, in_=in_, **kw)), sem, 16))
        ev = (key, c)
        for t in reads:
            if t.rd.get(key, 0) < c:
                t.rd[key] = c
        for t in writes:
            t.lw = ev
            t.rd = {}
        return ev

    def finish(self):
        e = 'sp'
        tail = []
        for key, c in self.dcnt.items():
            tail.append(self._semval(key, c))
        for x in ENGS:
            if x != e and self.cnt[x] > 0:
                tail.append(self._semval(x, self.cnt[x]))
        blk = self.es.enter_context(self.nc.Block())
        engobj = {'pe': blk.tensor, 'act': blk.scalar, 'dve': blk.vector, 'pool': blk.gpsimd, 'sp': blk.sync}

        def make(ename):
            ops = self.ops[ename]

            def body(eng):
                for waits, fn, inc, n in ops:
                    for (s, v) in waits[1:]:
                        eng.wait_ge(s, v)
                    ins = getattr(eng, fn[0])(*fn[1], **fn[2])
                    if waits:
                        ins._wait_ge(waits[0][0], waits[0][1])
                    ins.then_inc(inc, n)
                if ename == e:
                    for (s, v) in tail:
                        eng.wait_ge(s, v)
            return body
        for ename in ENGS:
            if self.ops[ename] or ename == e:
                engobj[ename](make(ename))
        self.es.close()


PC_SPEC = [
    ('mixn0', 8), ('mixn1', 8), ('pen0', 8), ('pen1', 8), ('finn', 8),
    ('mu_r', 8), ('mu_k', 8), ('mu_v', 8), ('mu_l', 1),
    ('w0', 8), ('a0', 8), ('k_k', 8), ('k_a', 8), ('r_k', 8), ('ln_w', 8), ('ln_b', 8),
    ('mcw0', 8), ('mcw1', 8), ('mcw2', 8), ('mcw3', 8), ('mcb', 8), ('mnorm', 8), ('mskip', 8),
    ('bif_i', 1), ('bif_f', 1),
    ('ccw0', 16), ('ccw1', 16), ('ccw2', 16), ('ccw3', 16), ('ccb', 16), ('cbr', 16), ('cbi', 16), ('clam', 16),
]
PC = {}
_o = 0
for _n, _w in PC_SPEC:
    PC[_n] = _o
    _o += _w
NPC = _o
PD_SPEC = [('omu_r', 8), ('omu_k', 8), ('omu_v', 8), ('omu_l', 1), ('negbf', 1), ('csph', 16), ('cneg', 16), ('t0', 16), ('t1', 16)]
PD = {}
_o = 0
for _n, _w in PD_SPEC:
    PD[_n] = _o
    _o += _w
NPD = _o


def _cols(v):
    v = np.asarray(v, np.float32).reshape(-1)
    return np.ascontiguousarray(v.reshape(-1, 128).T)


def host_consts():
    c = {}
    c['ident'] = np.eye(128, dtype=np.float32)
    bd = np.zeros((128, 128), np.float32)
    bd[:64, :64] = 1
    bd[64:, 64:] = 1
    c['bdones'] = bd
    j = np.arange(128)[:, None]
    i = np.arange(128)[None, :]
    su = ((j < i) & ((j // 64) == (i // 64))).astype(np.float32)
    ui = ((j <= i) & ((j // 64) == (i // 64))).astype(np.float32)
    sl = ((j > i) & ((j // 64) == (i // 64))).astype(np.float32)
    on = np.ones((128, 128), np.float32)
    c['maskA'] = np.stack([su, su, ui, ui], axis=1)
    c['maskB'] = sl
    rm = np.ones((128, TT), np.float32)
    rm[:, ::64] = 0
    c['resetm'] = rm
    c['causal'] = (j <= i).astype(np.float32)
    sel = np.zeros((4, 4, 128), np.float32)
    for h in range(4):
        sel[h, h, :] = 1
    c['sel'] = sel
    return c


def host_prep(inp):
    g = {}
    f = lambda a: np.ascontiguousarray(np.asarray(a, np.float32))

    def chunked(w):
        K, M = w.shape
        return f(w.reshape(K // 128, 128, M // 128, 128).transpose(2, 1, 0, 3))
    g['w0in'] = chunked(f(inp['ab_w_in'])[0])
    g['w0out'] = chunked(f(inp['ab_w_out'])[0])
    g['w1in'] = chunked(f(inp['c_w_in'])[0])
    g['w1out'] = chunked(f(inp['c_w_out'])[0])
    g['pegate'] = np.stack([chunked(f(inp['pe_gate'])[i]) for i in range(2)])
    g['peup'] = np.stack([chunked(f(inp['pe_up'])[i]) for i in range(2)])
    wr = f(inp['c_wr'])[0]
    wi = f(inp['c_wi'])[0]
    g['wri'] = f(np.stack([wr, wi], axis=2))
    lup = np.zeros((128, 2, 1024), np.float32)
    lup[:64, 0, :] = f(inp['rwkv_w_up'])[0]
    lup[64:, 1, :] = f(inp['rwkv_a_up'])[0]
    g['lup'] = lup
    wbd = np.zeros((128, 3, 8, 128), np.float32)
    for t, nm in enumerate(['mlstm_wq', 'mlstm_wk', 'mlstm_wv']):
        w = f(inp[nm])[0]
        for c in range(8):
            for gg in range(32):
                wbd[4 * gg:4 * gg + 4, t, c, 4 * gg:4 * gg + 4] = w[c * 32 + gg]
    g['wqkv'] = wbd
    wif = f(inp['mlstm_w_if'])[0]
    g['wif'] = f(wif.reshape(24, 128, 2, 4).transpose(1, 2, 0, 3))
    pc = np.zeros((128, NPC), np.float32)

    def put(name, v):
        cc = _cols(v)
        pc[:, PC[name]:PC[name] + cc.shape[1]] = cc
    put('mixn0', f(inp['mix_norm'])[0]); put('mixn1', f(inp['mix_norm'])[1])
    put('pen0', f(inp['pe_norm'])[0]); put('pen1', f(inp['pe_norm'])[1])
    put('finn', f(inp['final_norm']))
    mu = f(inp['rwkv_mu'])[0]
    put('mu_r', mu[0]); put('mu_k', mu[1]); put('mu_v', mu[2])
    ml = f(inp['rwkv_mu_lora'])[0]
    put('mu_l', np.concatenate([ml[0], ml[1]]))
    put('w0', f(inp['rwkv_w0'])[0]); put('a0', f(inp['rwkv_a0'])[0])
    put('k_k', f(inp['rwkv_k_k'])[0]); put('k_a', f(inp['rwkv_k_a'])[0])
    put('r_k', f(inp['rwkv_r_k'])[0].reshape(-1))
    put('ln_w', f(inp['rwkv_ln_w'])[0]); put('ln_b', f(inp['rwkv_ln_b'])[0])
    mcw = f(inp['mlstm_conv_w'])[0]
    for j in range(4):
        put('mcw%d' % j, mcw[j])
    put('mcb', f(inp['mlstm_conv_b'])[0])
    put('mnorm', f(inp['mlstm_norm'])[0]); put('mskip', f(inp['mlstm_skip'])[0])
    bif = f(inp['mlstm_b_if'])[0]
    pc[0:4, PC['bif_i']] = bif[0:4]
    pc[0:4, PC['bif_f']] = bif[4:8]
    ccw = f(inp['c_conv_w'])[0]
    for j in range(4):
        put('ccw%d' % j, ccw[j])
    put('ccb', f(inp['c_conv_b'])[0]); put('cbr', f(inp['c_br'])[0]); put('cbi', f(inp['c_bi'])[0])
    put('clam', f(inp['c_lambda'])[0])
    g['pc'] = pc
    g.update(host_consts())
    return g


SHARED_SHAPES = {
    'w0in': [49, 128, 8, 128], 'w0out': [8, 128, 16, 128], 'w1in': [32, 128, 8, 128], 'w1out': [8, 128, 16, 128],
    'pegate': [2, 8, 128, 8, 128], 'peup': [2, 8, 128, 2, 128], 'wri': [16, 128, 2, 128], 'lup': [128, 2, 1024],
    'wqkv': [128, 3, 8, 128], 'wif': [128, 2, 24, 4], 'pc': [128, NPC],
    'ident': [128, 128], 'bdones': [128, 128], 'maskA': [128, 4, 128], 'maskB': [128, 128],
    'resetm': [128, TT], 'causal': [128, 128], 'sel': [4, 4, 128],
}


def build_program(S, do_l0=True, do_rwkv=True, do_mlstm=True, do_l1=True, dbg=None):
    nc = bass.Bass("TRN2", target_bir_lowering=False)
    P = Prog(nc)
    ntiles = S // TT
    din = {}
    for k, shp in SHARED_SHAPES.items():
        din[k] = nc.dram_tensor(k, shp, F32, kind="ExternalInput")
    x_d = nc.dram_tensor("x", [S, D], F32, kind="ExternalInput")
    p_d = nc.dram_tensor("p", [2, S, PLE], F32, kind="ExternalInput")
    o_d = nc.dram_tensor("out", [S, D], F32, kind="ExternalOutput")
    dbg_d = None
    dbg = dbg or []
    if dbg:
        dbg_d = nc.dram_tensor("dbg", [len(dbg), 128, TT], F32, kind="ExternalOutput")

    def dump(name, tb, ap):
        if name in dbg:
            P.dma('sp', dbg_d.ap()[dbg.index(name)], ap, reads=[tb], key=('dma', 'dbg'))

    MUL, ADD, SUB, MAX = ALU.mult, ALU.add, ALU.subtract, ALU.max

    def cload(name, dt=F32, rows=128):
        shp = SHARED_SHAPES[name]
        t = P.sb("c_" + name, shp, dt)
        P.dma('pool' if dt != F32 else 'sp', t[:], din[name].ap(), writes=[t])
        return t
    ident = cload('ident')
    pc = cload('pc')
    pd = P.sb("pd", [128, NPD])
    ones_bf = P.sb("ones_bf", [128, 128], BF16)
    P.op('pool', lambda e: e.memset(ones_bf[:], 1.0), [], [ones_bf])

    def pcc(name, c=0, n=1):
        return pc[:, PC[name] + c:PC[name] + c + n]

    def pdc(name, c=0, n=1):
        return pd[:, PD[name] + c:PD[name] + c + n]

    if do_l1:
        P.op('act', lambda e: e.activation(out=pdc('t0', 0, 16), in_=pcc('clam', 0, 16), func=AF.Exp, scale=-1.0), [pc], [pd])
        P.op('act', lambda e: e.activation(out=pdc('t1', 0, 16), in_=pdc('t0', 0, 16), func=AF.Ln, bias=1.0), [pd], [pd])
        P.op('dve', lambda e: e.tensor_scalar(out=pdc('csph', 0, 16), in0=pdc('t1', 0, 16), scalar1=4.0, scalar2=None, op0=MUL), [pd], [pd])
        P.op('dve', lambda e: e.tensor_scalar(out=pdc('cneg', 0, 16), in0=pdc('t1', 0, 16), scalar1=-8.0, scalar2=None, op0=MUL), [pd], [pd])
    if do_l0:
        for nm, w in (('r', 8), ('k', 8), ('v', 8), ('l', 1)):
            P.op('dve', lambda e, nm=nm, w=w: e.tensor_scalar(out=pdc('omu_' + nm, 0, w), in0=pcc('mu_' + nm, 0, w), scalar1=-1.0, scalar2=1.0, op0=MUL, op1=ADD),
                 [pc], [pd])
        P.op('dve', lambda e: e.tensor_scalar(out=pdc('negbf'), in0=pcc('bif_f'), scalar1=-1.0, scalar2=None, op0=MUL), [pc], [pd])

    scr = {}

    def mkscr(name, src_ap, shape, group=8):
        t = P.dram("scr_" + name, shape, BF16, "Internal")
        nm = shape[0]
        for j0 in range(0, nm, group):
            j1 = min(nm, j0 + group)
            P.dma('pool', t[j0:j1].rearrange("j p k m -> p j k m"), src_ap[j0:j1].rearrange("j p k m -> p j k m"),
                  writes=[t], key=('dma', 'scr_' + name))
        scr[name] = t
        return t
    if do_l0:
        lup_bf = cload('lup', BF16)
        wqkv_bf = cload('wqkv', BF16)
        wif_bf = cload('wif', BF16)
        mkscr('w0in', din['w0in'].ap(), [49, 128, 8, 128])
        mkscr('w0out', din['w0out'].ap(), [8, 128, 16, 128], group=4)
    for i in range(2):
        if (i == 0 and do_l0) or (i == 1 and do_l1):
            mkscr('pegate%d' % i, din['pegate'].ap()[i], [8, 128, 8, 128])
            mkscr('peup%d' % i, din['peup'].ap()[i], [8, 128, 2, 128])
    if do_l1:
        mkscr('w1in', din['w1in'].ap(), [32, 128, 8, 128])
        mkscr('wri', din['wri'].ap(), [16, 128, 2, 128], group=16)
        mkscr('w1out', din['w1out'].ap(), [8, 128, 16, 128], group=4)

    NWB = 5
    wring = [P.sb("wb%d" % i, [128, 16, 128], BF16) for i in range(NWB)]
    wstate = {'i': 0}

    def wld(name, j, nk):
        wb = wring[wstate['i'] % NWB]
        wstate['i'] += 1
        P.dma('sp', wb[:, 0:nk, :], scr[name][j], reads=[scr[name]], writes=[wb])
        return wb

    PS = [P.ps("ps%d" % i, [128, TT]) for i in range(8)]
    pstate = {'i': 0}

    pstate['set'] = list(range(8))

    def nextps():
        s_ = pstate['set']
        b = PS[s_[pstate['i'] % len(s_)]]
        pstate['i'] += 1
        return b

    hT = P.sb("hT", [128, 8, TT])
    xnT = P.sb("xnT", [128, 8, TT], BF16)
    merged = P.sb("merged", [128, 16, TT], BF16)
    ext = P.sb("ext", [128, TT + 3])
    NG, NH = 20, 12
    G = [P.sb("g%d" % i, [128, TT]) for i in range(NG)]
    H = [P.sb("h%d" % i, [128, 2 * TT]) for i in range(NH)]
    lnv, rstd = G[19], G[18]
    _alias = {}

    def RV(tb):
        if SDT == F32:
            return tb[:]
        a = _alias.get(tb.name)
        if a is None:
            ml = nc.lookup_mloc(tb.h)
            a = nc.alloc_sbuf_tensor_at(tb.name + "_r", [128, int(ml.dims[1]) // 4], F32R, offset=int(ml.addr))
            _alias[tb.name] = a
        return a[:]
    ntmp = [G[16], G[17]]

    def bfv(tb, n=None):
        v = tb[:].bitcast(BF16)
        return v

    if do_l1:
        l1_tail = P.sb("l1_tail", [128, 16, 3])
        l1_h = P.sb("l1_h", [128, 16])
        P.op('pool', lambda e: e.memset(l1_tail[:], 0.0), [], [l1_tail])
        P.op('pool', lambda e: e.memset(l1_h[:], 0.0), [], [l1_h])

    def norm_apply(k, gname, out_ap, out_tb):
        if k % 2 == 0:
            P.op('dve', lambda e: e.scalar_tensor_tensor(out=out_ap, in0=hT[:, k, :], scalar=pcc(gname, k), in1=rstd[:],
                                                         op0=MUL, op1=MUL), [hT, pc, rstd], [out_tb])
        else:
            nt = ntmp[(k // 2) % 2]
            P.op('act', lambda e: e.activation(out=nt[:], in_=hT[:, k, :], func=AF.Identity, scale=pcc(gname, k)), [hT, pc], [nt])
            P.op('pool', lambda e: e.tensor_tensor(out=out_ap, in0=nt[:], in1=rstd[:], op=MUL), [nt, rstd], [out_tb])

    def norm_stats():
        ps = nextps()
        for half in range(2):
            sq = H[4 + half]
            sqv = bfv(sq).rearrange("p (k t) -> p k t", k=4)
            P.op('act', lambda e, half=half, sqv=sqv: e.activation(out=sqv, in_=hT[:, 4 * half:4 * half + 4, :], func=AF.Square), [hT], [sq])
            for k in range(4):
                P.op('pe', lambda e, k=k, half=half, sqv=sqv: e.matmul(ps[:], lhsT=ones_bf[:], rhs=sqv[:, k, :], start=(half == 0 and k == 0), stop=(half == 1 and k == 3)),
                     [ones_bf, sq], [ps])
        P.op('act', lambda e: e.activation(out=lnv[:], in_=ps[:], func=AF.Ln, scale=1.0 / D, bias=1e-6), [ps], [lnv])
        P.op('act', lambda e: e.activation(out=rstd[:], in_=lnv[:], func=AF.Exp, scale=-0.5), [lnv], [rstd])

    def rmsnorm_x(gname):
        norm_stats()
        for k in range(8):
            norm_apply(k, gname, xnT[:, k, :], xnT)

    def proj(ps, wb, nk, rhs_tb, rhs_fn):
        for k in range(nk):
            P.op('pe', lambda e, k=k: e.matmul(ps[:], lhsT=wb[:, k, :], rhs=rhs_fn(k), start=(k == 0), stop=(k == nk - 1)),
                 [wb, rhs_tb], [ps])

    def proj_x(name, j):
        w = wld(name, j, 8)
        ps = nextps()
        proj(ps, w, 8, xnT, lambda k: xnT[:, k, :])
        return ps

    def ple(i, t0):
        ptok = H[0]
        ptv = ptok[:].rearrange("p (tb d) -> p tb d", tb=NT)
        pTt = G[12]
        pT = bfv(pTt).rearrange("p (k t) -> p k t", k=2)
        sgs, tmps = [G[0], G[1]], [G[2], G[3]]
        P.dma('sp', ptv, p_d.ap()[i, t0:t0 + TT, :].rearrange("(tb p) d -> p tb d", p=128), writes=[ptok])
        for kc in range(2):
            ps = nextps()
            for tb in range(NT):
                P.op('pe', lambda e, kc=kc, tb=tb, ps=ps: e.transpose(ps[:, tb * 128:(tb + 1) * 128], ptv[:, tb, kc * 128:(kc + 1) * 128], ident[:]),
                     [ptok, ident], [ps])
            P.op('act', lambda e, kc=kc, ps=ps: e.activation(out=pT[:, kc, :], in_=ps[:], func=AF.Copy), [ps], [pTt])
        rmsnorm_x('pen%d' % i)
        for m in range(8):
            sg, tmpa = sgs[m % 2], tmps[m % 2]
            wg = wld('pegate%d' % i, m, 8)
            wu = wld('peup%d' % i, m, 2)
            ps = nextps()
            proj(ps, wg, 8, xnT, lambda k: xnT[:, k, :])
            P.op('act', lambda e, ps=ps: e.activation(out=sg[:], in_=ps[:], func=AF.Sigmoid), [ps], [sg])
            ps2 = nextps()
            proj(ps2, wu, 2, pTt, lambda k: pT[:, k, :])
            P.op('dve', lambda e, ps2=ps2: e.tensor_tensor(out=tmpa[:], in0=ps2[:], in1=sg[:], op=MUL), [ps2, sg], [tmpa])
            P.op('pool', lambda e, m=m: e.tensor_tensor(out=hT[:, m, :], in0=hT[:, m, :], in1=tmpa[:], op=ADD), [hT, tmpa], [hT])

    def out_proj(name):
        for m in range(8):
            wo = wld(name, m, 16)
            ps = nextps()
            proj(ps, wo, 16, merged, lambda k: merged[:, k, :])
            P.op('dve', lambda e, m=m, ps=ps: e.tensor_tensor(out=hT[:, m, :], in0=hT[:, m, :], in1=ps[:], op=ADD), [hT, ps], [hT])

    def conv4(ps, c, tail_tb, wname, bname, acc):
        P.op('pool', lambda e: e.tensor_copy(out=ext[:, 0:3], in_=tail_tb[:, c, :]), [tail_tb], [ext])
        P.op('act', lambda e: e.activation(out=ext[:, 3:TT + 3], in_=ps[:], func=AF.Copy), [ps], [ext])
        P.op('act', lambda e: e.activation(out=acc[:], in_=ps[:], func=AF.Identity, scale=pcc(wname + '3', c), bias=pcc(bname, c)), [ps, pc], [acc])
        P.op('pool', lambda e: e.tensor_copy(out=tail_tb[:, c, :], in_=ext[:, TT:TT + 3]), [ext], [tail_tb])
        for j in range(3):
            P.op('dve', lambda e, j=j: e.scalar_tensor_tensor(out=acc[:], in0=ext[:, j:j + TT], scalar=pcc(wname + str(j), c), in1=acc[:],
                                                              op0=MUL, op1=ADD), [ext, pc, acc], [acc])

    def layer1(t0):
        rmsnorm_x('mixn1')
        for g_ in range(4):
            bufs = []
            for k in range(4):
                s0, s1, s2 = H[3 * k], H[3 * k + 1], H[3 * k + 2]
                bufs.append(dict(xc32=(s0, s0[:, 0:TT]), xcb=(s0, s0[:, TT:2 * TT].bitcast(BF16)[:, 0:TT]),
                                 rg=(s1, s1[:, 0:TT]), ig=(s1, s1[:, TT:2 * TT]), a2=(s2, s2[:, 0:TT]), th=(s2, s2[:, TT:2 * TT])))
            cs = [4 * g_ + k for k in range(4)]
            for k, c in enumerate(cs):
                xt, xa = bufs[k]['xc32']
                bt, ba = bufs[k]['xcb']
                ps = proj_x('w1in', c)
                P.op('pool', lambda e: e.tensor_copy(out=ext[:, 0:3], in_=l1_tail[:, c, :]), [l1_tail], [ext])
                P.op('act', lambda e: e.activation(out=ext[:, 3:TT + 3], in_=ps[:], func=AF.Copy), [ps], [ext])
                P.op('act', lambda e: e.activation(out=xa, in_=ps[:], func=AF.Identity, scale=pcc('ccw3', c), bias=pcc('ccb', c)), [ps, pc], [xt])
                P.op('pool', lambda e: e.tensor_copy(out=l1_tail[:, c, :], in_=ext[:, TT:TT + 3]), [ext], [l1_tail])
                for j in range(3):
                    P.op('dve', lambda e, j=j: e.scalar_tensor_tensor(out=xa, in0=ext[:, j:j + TT], scalar=pcc('ccw%d' % j, c), in1=xa, op0=MUL, op1=ADD), [ext, pc, xt], [xt])
                P.op('act', lambda e: e.activation(out=ba, in_=xa, func=AF.Copy), [xt], [bt])
            for k, c in enumerate(cs):
                bt, ba = bufs[k]['xcb']
                rt, ra = bufs[k]['rg']
                it, ia = bufs[k]['ig']
                wb = wld('wri', c, 2)
                ps_r = nextps()
                P.op('pe', lambda e: e.matmul(ps_r[:], lhsT=wb[:, 0, :], rhs=ba, start=True, stop=True), [wb, bt], [ps_r])
                ps_i = nextps()
                P.op('pe', lambda e: e.matmul(ps_i[:], lhsT=wb[:, 1, :], rhs=ba, start=True, stop=True), [wb, bt], [ps_i])
                P.op('act', lambda e: e.activation(out=ra, in_=ps_r[:], func=AF.Sigmoid, bias=pcc('cbr', c)), [ps_r, pc], [rt])
                P.op('act', lambda e: e.activation(out=ia, in_=ps_i[:], func=AF.Sigmoid, bias=pcc('cbi', c)), [ps_i, pc], [it])
            for k, c in enumerate(cs):
                rt, ra = bufs[k]['rg']
                at, aa_ = bufs[k]['a2']
                P.op('act', lambda e: e.activation(out=aa_, in_=ra, func=AF.Exp, scale=pdc('cneg', c)), [rt, pd], [at])
            for k, c in enumerate(cs):
                rt, ra = bufs[k]['rg']
                tt_, ta = bufs[k]['th']
                P.op('act', lambda e: e.activation(out=ta, in_=ra, func=AF.Tanh, scale=pdc('csph', c)), [rt, pd], [tt_])
            for k, c in enumerate(cs):
                at, aa_ = bufs[k]['a2']
                tt_, ta = bufs[k]['th']
                P.op('dve', lambda e: e.scalar_tensor_tensor(out=ta, in0=aa_, scalar=1.0, in1=ta, op0=ADD, op1=MUL), [at, tt_], [tt_])
                P.op('dve', lambda e: e.tensor_scalar(out=aa_, in0=ta, scalar1=-1.0, scalar2=1.0, op0=MUL, op1=ADD), [tt_], [at])
                P.op('dve', lambda e: e.scalar_tensor_tensor(out=ta, in0=aa_, scalar=1.0, in1=ta, op0=ADD, op1=MUL), [at, tt_], [tt_])
            for k, c in enumerate(cs):
                tt_, ta = bufs[k]['th']
                P.op('act', lambda e: e.activation(out=ta, in_=ta, func=AF.Sqrt), [tt_], [tt_])
            for k, c in enumerate(cs):
                xt, xa = bufs[k]['xc32']
                rt, ra = bufs[k]['rg']
                it, ia = bufs[k]['ig']
                at, aa_ = bufs[k]['a2']
                tt_, ta = bufs[k]['th']
                P.op('pool', lambda e: e.tensor_tensor(out=ia, in0=xa, in1=ia, op=MUL), [xt, it], [it])
                P.op('pool', lambda e: e.tensor_tensor(out=ia, in0=ia, in1=ta, op=MUL), [it, tt_], [it])
                P.op('dve', lambda e: e.tensor_tensor_scan(out=ra, data0=aa_, data1=ia, initial=l1_h[:, c:c + 1], op0=MUL, op1=ADD), [at, it, l1_h], [rt])
                P.op('pool', lambda e: e.tensor_copy(out=l1_h[:, c:c + 1], in_=ra[:, TT - 1:TT]), [rt], [l1_h])
            for k, c in enumerate(cs):
                xt, xa = bufs[k]['xc32']
                rt, ra = bufs[k]['rg']
                ps_g = proj_x('w1in', 16 + c)
                P.op('act', lambda e: e.activation(out=xa, in_=ps_g[:], func=AF.Silu), [ps_g], [xt])
                P.op('dve', lambda e: e.tensor_tensor(out=merged[:, c, :], in0=ra, in1=xa, op=MUL), [rt, xt], [merged])
        out_proj('w1out')

    if do_l0 and do_rwkv:
        identr = P.sb("identr", [128, 128], SDT)
        bdones = cload('bdones')
        bdones_bf = P.sb("bdones_bf", [128, 128], BF16)
        bdones_r = P.sb("bdones_r", [128, 128], SDT)
        maskA = cload('maskA')
        maskB = cload('maskB')
        resetm = cload('resetm')
        P.op('dve', lambda e: e.tensor_copy(out=identr[:], in_=ident[:]), [ident], [identr])
        P.op('dve', lambda e: e.tensor_copy(out=bdones_bf[:], in_=bdones[:]), [bdones], [bdones_bf])
        P.op('dve', lambda e: e.tensor_copy(out=bdones_r[:], in_=bdones[:]), [bdones], [bdones_r])
        rwc = P.sb("rwc", [128, 25])
        P.op('pool', lambda e: e.memset(rwc[:], 0.0), [], [rwc])
        lora_bf = P.sb("lora_bf", [128, TT], BF16)
        Sst = [[P.sb("S%d_%d" % (c, q), [128, 128], F32) for q in range(2)] for c in range(8)]
        for c in range(8):
            P.op('pool', lambda e, c=c: e.memset(Sst[c][0][:], 0.0), [], [Sst[c][0]])
        spar = [0] * 8
        rhs_sb = P.sb("rhs_sb", [128, 128], SDT)
        u_sb = P.sb("u_sb", [128, 128], SDT)
        psRHS = TB("psRHS", PS[7].h[:, 0:128], root=PS[7])
        psU = TB("psU", PS[7].h[:, 128:256], root=PS[7])
        psYT = TB("psYT", PS[6].h[:, 0:128], root=PS[6])
        psSN = TB("psSN", PS[7].h[:, 384:512], root=PS[7])

    def tshift(ps, dst, col, mu_ap, omu_ap):
        P.op('act', lambda e: e.activation(out=dst[:], in_=ps[:], func=AF.Identity, scale=omu_ap), [ps, pd], [dst])
        P.op('dve', lambda e: e.scalar_tensor_tensor(out=dst[:, 1:TT], in0=ps[:, 0:TT - 1], scalar=mu_ap, in1=dst[:, 1:TT], op0=MUL, op1=ADD),
             [ps, pc, dst], [dst])
        P.op('dve', lambda e: e.scalar_tensor_tensor(out=dst[:, 0:1], in0=rwc[:, col:col + 1], scalar=mu_ap, in1=dst[:, 0:1], op0=MUL, op1=ADD),
             [rwc, pc, dst], [dst])
        P.op('act', lambda e: e.activation(out=rwc[:, col:col + 1], in_=ps[:, TT - 1:TT], func=AF.Copy), [ps], [rwc])

    def r3(ap, n=8):
        return ap.rearrange("p (n t) -> p n t", n=n)

    def rwkv(t0):
        Rbd, Abd, Bbd, Kbd, Vbd = H[0:5]
        for i in range(5):
            P.op('pool', lambda e, i=i: e.memset(H[i][:], 0.0), [], [H[i]])
        bdv = [r3(RV(H[i])) for i in range(5)]
        ps = proj_x('w0in', 24)
        lora32 = G[16]
        tshift(ps, lora32, 24, pcc('mu_l'), pdc('omu_l'))
        P.op('act', lambda e: e.activation(out=lora_bf[0:64, :], in_=lora32[0:64, :], func=AF.Tanh), [lora32], [lora_bf])
        P.op('act', lambda e: e.activation(out=lora_bf[64:128, :], in_=lora32[64:128, :], func=AF.Copy), [lora32], [lora_bf])
        pend = []

        def pump(n):
            for _ in range(n):
                if pend:
                    pend.pop(0)()
        for th_ in rwkv_prep(0):
            th_()
        rwkv_blockdiag(0, bdv)
        for c in range(8):
            if c + 1 < 8:
                pend.extend(rwkv_prep(c + 1))
            rwkv_chunks(c, bdv, pump)
            pump(len(pend))
            if c + 1 < 8:
                rwkv_blockdiag(c + 1, bdv)
            rwkv_out(c)

    def rw_bufs(c):
        p_ = c % 2
        d = dict(r32=G[0], k32=G[1], v32=G[2], cump=G[6], Ginv=G[8], Gp=G[9], a32=G[10], kk32=G[11], kka=G[11],
                 rn=G[12], nkk=G[13], keff=G[15], t1=G[16])
        d.update(dict(gr32=(G[3], G[14])[p_], lw=(G[4], G[17])[p_], cum=(G[5], G[18])[p_], Gt=(G[7], G[19])[p_]))
        return d

    def rwkv_prep(c):
        B_ = rw_bufs(c)
        r32, k32, v32, gr32, lw, cum, cump, Gt, Ginv, Gp = [B_[k] for k in ('r32', 'k32', 'v32', 'gr32', 'lw', 'cum', 'cump', 'Gt', 'Ginv', 'Gp')]
        a32, kk32, kka, rn, nkk, keff, t1 = [B_[k] for k in ('a32', 'kk32', 'kka', 'rn', 'nkk', 'keff', 't1')]
        bon = cum
        T = []

        def projshift(dst, mch, nm, col):
            def f():
                ps = proj_x('w0in', mch)
                tshift(ps, dst, col, pcc('mu_' + nm, c), pdc('omu_' + nm, c))
            return f
        T.append(projshift(k32, 8 + c, 'k', 8 + c))

        def f_lw():
            ps = nextps()
            P.op('pe', lambda e: e.matmul(ps[:], lhsT=lup_bf[:, 0, c * 128:(c + 1) * 128], rhs=lora_bf[:], start=True, stop=True), [lup_bf, lora_bf], [ps])
            P.op('act', lambda e: e.activation(out=lw[:], in_=ps[:], func=AF.Sigmoid, bias=pcc('w0', c)), [ps, pc], [lw])
        T.append(f_lw)

        def f_a():
            ps = nextps()
            P.op('pe', lambda e: e.matmul(ps[:], lhsT=lup_bf[:, 1, c * 128:(c + 1) * 128], rhs=lora_bf[:], start=True, stop=True), [lup_bf, lora_bf], [ps])
            P.op('act', lambda e: e.activation(out=a32[:], in_=ps[:], func=AF.Sigmoid, bias=pcc('a0', c)), [ps, pc], [a32])
        T.append(f_a)
        T.append(lambda: P.op('dve', lambda e: e.tensor_tensor_scan(out=cum[:], data0=resetm[:], data1=lw[:], initial=0.0, op0=MUL, op1=ADD), [resetm, lw], [cum]))
        T.append(lambda: P.op('dve', lambda e: e.tensor_tensor(out=cump[:], in0=cum[:], in1=lw[:], op=SUB), [cum, lw], [cump]))
        T.append(lambda: P.op('act', lambda e: e.activation(out=Gt[:], in_=cum[:], func=AF.Exp, scale=-DECAY_SCALE), [cum], [Gt]))
        T.append(lambda: P.op('act', lambda e: e.activation(out=Ginv[:], in_=cum[:], func=AF.Exp, scale=DECAY_SCALE), [cum], [Ginv]))
        T.append(lambda: P.op('act', lambda e: e.activation(out=Gp[:], in_=cump[:], func=AF.Exp, scale=-DECAY_SCALE), [cump], [Gp]))
        T.append(lambda: P.op('dve', lambda e: e.tensor_scalar(out=kk32[:], in0=k32[:], scalar1=pcc('k_k', c), scalar2=None, op0=MUL), [k32, pc], [kk32]))
        sqk = bfv(t1)[:, 0:TT]
        T.append(lambda: P.op('act', lambda e: e.activation(out=sqk, in_=kk32[:], func=AF.Square), [kk32], [t1]))

        def f_rn():
            ps = nextps()
            P.op('pe', lambda e: e.matmul(ps[:], lhsT=bdones_bf[:], rhs=sqk, start=True, stop=True), [bdones_bf, t1], [ps])
            P.op('act', lambda e: e.activation(out=rn[:], in_=ps[:], func=AF.Ln, bias=1e-20), [ps], [rn])
            P.op('act', lambda e: e.activation(out=rn[:], in_=rn[:], func=AF.Exp, scale=-0.5), [rn], [rn])
        T.append(f_rn)
        T.append(lambda: P.op('dve', lambda e: e.scalar_tensor_tensor(out=nkk[:], in0=kk32[:], scalar=-1.0, in1=rn[:], op0=MUL, op1=MUL), [kk32, rn], [nkk]))
        T.append(lambda: P.op('dve', lambda e: e.scalar_tensor_tensor(out=kka[:], in0=nkk[:], scalar=-1.0, in1=a32[:], op0=MUL, op1=MUL), [nkk, a32], [kka]))
        T.append(lambda: P.op('pool', lambda e: e.tensor_scalar(out=t1[:], in0=a32[:], scalar1=-1.0, scalar2=pcc('k_a', c), op0=ADD, op1=MUL), [a32, pc], [t1]))
        T.append(lambda: P.op('dve', lambda e: e.scalar_tensor_tensor(out=keff[:], in0=t1[:], scalar=1.0, in1=k32[:], op0=ADD, op1=MUL), [t1, k32], [keff]))
        T.append(projshift(r32, c, 'r', c))
        rkr = bfv(t1)[:, 0:TT]
        T.append(lambda: P.op('dve', lambda e: e.scalar_tensor_tensor(out=rkr, in0=r32[:], scalar=pcc('r_k', c), in1=keff[:], op0=MUL, op1=MUL), [r32, pc, keff], [t1]))
        T.append(projshift(v32, 16 + c, 'v', 16 + c))

        def f_bon():
            ps = nextps()
            P.op('pe', lambda e: e.matmul(ps[:], lhsT=bdones_bf[:], rhs=rkr, start=True, stop=True), [bdones_bf, t1], [ps])
            P.op('dve', lambda e: e.tensor_tensor(out=bon[:], in0=ps[:], in1=v32[:], op=MUL), [ps, v32], [bon])
        T.append(f_bon)

        def f_gr():
            ps = proj_x('w0in', 33 + c)
            P.op('act', lambda e: e.activation(out=gr32[:], in_=ps[:], func=AF.Silu), [ps], [gr32])
        T.append(f_gr)
        return T

    def rwkv_blockdiag(c, bdv):
        B_ = rw_bufs(c)
        Rv, Av, Bv, Kv, Vv = bdv
        Rbd, Abd, Bbd, Kbd, Vbd = H[0:5]
        for hh in range(2):
            hs_ = slice(hh * 64, hh * 64 + 64)
            for i_, (dstv, dtb, a_, b_) in enumerate(((Rv, Rbd, B_['r32'], B_['Gt']), (Av, Abd, B_['nkk'], B_['Gp']), (Bv, Bbd, B_['kka'], B_['Ginv']), (Kv, Kbd, B_['keff'], B_['Ginv']))):
                P.op('dve' if (i_ + hh) % 2 == 0 else 'pool', lambda e, dstv=dstv, a_=a_, b_=b_, hs_=hs_: e.tensor_tensor(out=dstv[hs_, :, hs_], in0=r3(a_[hs_, :]), in1=r3(b_[hs_, :]), op=MUL),
                     [a_, b_], [dtb])
            P.op('act', lambda e, hs_=hs_: e.activation(out=Vv[hs_, :, hs_], in_=r3(B_['v32'][hs_, :]), func=AF.Copy), [B_['v32']], [Vbd])

    def rwkv_chunks(c, bdv, pump):
        B_ = rw_bufs(c)
        Gt, y32 = B_['Gt'], B_['lw']
        Rv, Av, Bv, Kv, Vv = bdv
        Rbd, Abd, Bbd, Kbd, Vbd = H[0:5]
        for hf in range(2):
            SCt = [H[5], H[6]]
            QTt = [H[7], H[8]]
            SCv = [RV(t).rearrange("p (j m k) -> p j m k", j=2, m=4) for t in SCt]
            QTv = [RV(t).rearrange("p (j m k) -> p j m k", j=2, m=4) for t in QTt]
            SCf = [t[:].rearrange("p (j m k) -> p j m k", j=2, m=4) for t in SCt]
            PQt = [H[9], H[10]]
            PQv = [[RV(PQt[b]).rearrange("p (q j m k) -> p q j m k", q=2, j=2, m=2)[:, par] for b in range(2)] for par in range(2)]
            Xt = H[11]
            Xv = [RV(Xt).rearrange("p (q j k) -> p q j k", q=2, j=4)[:, q] for q in range(2)]
            Xf = [Xt[:].rearrange("p (q j k) -> p q j k", q=2, j=4)[:, q] for q in range(2)]
            for j in range(4):
                n = hf * 4 + j
                sc = SCv[j // 2][:, j % 2]
                qt = QTv[j // 2][:, j % 2]
                A_, B2_ = PS[2], PS[3]
                for m, (l_, r_) in enumerate(((Bv, Av), (Kv, Av), (Bv, Rv), (Kv, Rv))):
                    P.op('pe', lambda e, m=m, l_=l_, r_=r_, n=n: e.matmul(A_[:, m * 128:(m + 1) * 128], lhsT=l_[:, n, :], rhs=r_[:, n, :], start=True, stop=True),
                         [Rbd, Abd, Bbd, Kbd], [A_])
                P.op('dve', lambda e, sc=sc: e.tensor_tensor(out=sc, in0=A_[:].rearrange("p (m k) -> p m k", m=4), in1=maskA[:], op=MUL),
                     [A_, maskA], [SCt[j // 2]])
                P.op('pe', lambda e, n=n: e.matmul(B2_[:, 0:128], lhsT=Av[:, n, :], rhs=Bv[:, n, :], start=True, stop=True), [Abd, Bbd], [B2_])
                for m, l_ in enumerate((Bv, Kv, Vv)):
                    P.op('pe', lambda e, m=m, l_=l_, n=n: e.matmul(B2_[:, (m + 1) * 128:(m + 2) * 128], lhsT=l_[:, n, :], rhs=identr[:], start=True, stop=True),
                         [Bbd, Kbd, Vbd, identr], [B2_])
                P.op('dve', lambda e, qt=qt: e.tensor_tensor(out=qt[:, 0, :], in0=B2_[:, 0:128], in1=maskB[:], op=MUL), [B2_, maskB], [QTt[j // 2]])
                P.op('act', lambda e, qt=qt: e.activation(out=qt[:, 1:4, :], in_=B2_[:, 128:512].rearrange("p (m k) -> p m k", m=3), func=AF.Copy),
                     [B2_], [QTt[j // 2]])
                pump(2)
            for j in range(4):
                P.op('dve', lambda e, j=j: e.tensor_tensor(out=Xv[1][:, j, :], in0=SCf[j // 2][:, j % 2, 0, :], in1=ident[:], op=ADD),
                     [SCt[j // 2], ident], [Xt])
            for lvl in range(1, 6):
                par = lvl % 2
                for b in range(2):
                    bank = PS[4 + b]
                    for jj in range(2):
                        j = 2 * b + jj
                        if lvl == 1:
                            Pp = SCv[j // 2][:, j % 2, 0, :]
                            Qp = QTv[j // 2][:, j % 2, 0, :]
                            rd = [SCt[j // 2], QTt[j // 2]]
                        else:
                            Pp = PQv[1 - par][b][:, jj, 0, :]
                            Qp = PQv[1 - par][b][:, jj, 1, :]
                            rd = [PQt[b]]
                        P.op('pe', lambda e, jj=jj, Pp=Pp, Qp=Qp, bank=bank: e.matmul(bank[:, (2 * jj) * 128:(2 * jj + 1) * 128], lhsT=Qp, rhs=Pp, start=True, stop=True), rd, [bank])
                        P.op('pe', lambda e, jj=jj, Pp=Pp, Qp=Qp, bank=bank: e.matmul(bank[:, (2 * jj + 1) * 128:(2 * jj + 2) * 128], lhsT=Pp, rhs=Qp, start=True, stop=True), rd, [bank])
                    if b == 0:
                        P.op('act', lambda e, b=b, bank=bank, par=par: e.activation(out=PQv[par][b], in_=bank[:].rearrange("p (j m k) -> p j m k", j=2, m=2), func=AF.Copy),
                             [bank], [PQt[b]])
                    else:
                        P.op('dve', lambda e, b=b, bank=bank, par=par: e.tensor_copy(out=PQv[par][b], in_=bank[:].rearrange("p (j m k) -> p j m k", j=2, m=2)),
                             [bank], [PQt[b]])
                XB = PS[6]
                for j in range(4):
                    Ql = PQv[par][j // 2][:, j % 2, 1, :]
                    P.op('pe', lambda e, j=j, Ql=Ql, par=par: e.matmul(XB[:, j * 128:(j + 1) * 128], lhsT=Ql, rhs=Xv[par][:, j, :], start=True, stop=True), [PQt[j // 2], Xt], [XB])
                P.op('dve', lambda e, par=par: e.tensor_tensor(out=Xv[1 - par], in0=XB[:].rearrange("p (j k) -> p j k", j=4), in1=Xf[par], op=ADD), [XB, Xt], [Xt])
                pump(1)
            for j in range(4):
                n = hf * 4 + j
                sc = SCv[j // 2][:, j % 2]
                qt = QTv[j // 2][:, j % 2]
                sct, qtt = SCt[j // 2], QTt[j // 2]
                S0 = Sst[c][spar[c]]
                S1 = Sst[c][1 - spar[c]]
                spar[c] = 1 - spar[c]
                P.op('pe', lambda e, n=n, S0=S0: e.matmul(psRHS[:], lhsT=Av[:, n, :], rhs=RV(S0), start=True, stop=False), [Abd, S0], [psRHS])
                P.op('pe', lambda e, sc=sc, qt=qt: e.matmul(psRHS[:], lhsT=sc[:, 1, :], rhs=qt[:, 3, :], start=False, stop=True), [sct, qtt], [psRHS])
                P.op('dve', lambda e: e.tensor_copy(out=rhs_sb[:], in_=psRHS[:]), [psRHS], [rhs_sb])
                P.op('pe', lambda e, j=j: e.matmul(psU[:], lhsT=Xv[0][:, j, :], rhs=rhs_sb[:], start=True, stop=True), [Xt, rhs_sb], [psU])
                P.op('dve', lambda e: e.tensor_copy(out=u_sb[:], in_=psU[:]), [psU], [u_sb])
                P.op('pe', lambda e, qt=qt: e.matmul(psSN[:], lhsT=qt[:, 1, :], rhs=u_sb[:], start=True, stop=False), [qtt, u_sb], [psSN])
                P.op('pe', lambda e, qt=qt: e.matmul(psSN[:], lhsT=qt[:, 2, :], rhs=qt[:, 3, :], start=False, stop=False), [qtt], [psSN])
                P.op('pe', lambda e, S0=S0: e.matmul(psSN[:], lhsT=identr[:], rhs=RV(S0), start=False, stop=True), [identr, S0], [psSN])
                P.op('dve', lambda e, n=n, S1=S1: e.tensor_scalar(out=RV(S1), in0=psSN[:], scalar1=Gt[:, n * 64 + 63:n * 64 + 64], scalar2=None, op0=MUL), [psSN, Gt], [S1])
                P.op('pe', lambda e, n=n, S0=S0: e.matmul(psYT[:], lhsT=RV(S0), rhs=Rv[:, n, :], start=True, stop=False), [S0, Rbd], [psYT])
                P.op('pe', lambda e, sc=sc: e.matmul(psYT[:], lhsT=u_sb[:], rhs=sc[:, 2, :], start=False, stop=False), [u_sb, sct], [psYT])
                P.op('pe', lambda e, sc=sc, qt=qt: e.matmul(psYT[:], lhsT=qt[:, 3, :], rhs=sc[:, 3, :], start=False, stop=True), [qtt, sct], [psYT])
                for hh in range(2):
                    hs_ = slice(hh * 64, hh * 64 + 64)
                    P.op('act', lambda e, hs_=hs_, n=n: e.activation(out=y32[hs_, n * 64:(n + 1) * 64], in_=psYT[hs_, hs_], func=AF.Copy), [psYT], [y32])
                pump(2)

    def rwkv_out(c):
        B_ = rw_bufs(c)
        y32, dd, bon, gr32 = B_['lw'], B_['Gt'], B_['cum'], B_['gr32']
        dump('y_rwkv%d' % c, y32, y32[:])
        yrv = RV(y32)
        P.op('act', lambda e: e.activation(out=yrv, in_=y32[:], func=AF.Copy), [y32], [y32])
        ps = nextps()
        P.op('pe', lambda e: e.matmul(ps[:], lhsT=bdones_r[:], rhs=yrv, start=True, stop=True), [bdones_r, y32], [ps])
        P.op('dve', lambda e: e.scalar_tensor_tensor(out=dd[:], in0=ps[:], scalar=-1.0 / 64, in1=y32[:], op0=MUL, op1=ADD), [ps, y32], [dd])
        sq, rs = H[5], H[6]
        sqv = RV(sq)[:, 0:TT]
        P.op('act', lambda e: e.activation(out=sqv, in_=dd[:], func=AF.Square), [dd], [sq])
        ps2 = nextps()
        P.op('pe', lambda e: e.matmul(ps2[:], lhsT=bdones_r[:], rhs=sqv, start=True, stop=True), [bdones_r, sq], [ps2])
        P.op('act', lambda e: e.activation(out=rs[:, 0:TT], in_=ps2[:], func=AF.Ln, scale=1.0 / 64, bias=64e-5), [ps2], [rs])
        P.op('act', lambda e: e.activation(out=rs[:, 0:TT], in_=rs[:, 0:TT], func=AF.Exp, scale=-0.5), [rs], [rs])
        P.op('dve', lambda e: e.tensor_tensor(out=dd[:], in0=dd[:], in1=rs[:, 0:TT], op=MUL), [dd, rs], [dd])
        P.op('act', lambda e: e.activation(out=dd[:], in_=dd[:], func=AF.Identity, scale=pcc('ln_w', c), bias=pcc('ln_b', c)), [dd, pc], [dd])
        P.op('pool', lambda e: e.tensor_tensor(out=dd[:], in0=dd[:], in1=bon[:], op=ADD), [dd, bon], [dd])
        P.op('dve', lambda e: e.tensor_tensor(out=merged[:, c, :], in0=dd[:], in1=gr32[:], op=MUL), [dd, gr32], [merged])

    if do_l0 and do_mlstm:
        causal = cload('causal')
        sel = P.sb("c_sel", [4, 4, 128])
        P.dma('sp', sel[:], din['sel'].ap(), writes=[sel])
        onesf = P.sb("onesf", [128, TT])
        P.op('pool', lambda e: e.memset(onesf[:], 1.0), [], [onesf])
        ones_r = P.sb("ones_r", [128, 128], SDT)
        P.op('dve', lambda e: e.tensor_copy(out=ones_r[:], in_=onesf[:, 0:128]), [onesf], [ones_r])
        m_tail = P.sb("m_tail", [128, 8, 3])
        P.op('pool', lambda e: e.memset(m_tail[:], 0.0), [], [m_tail])
        CT32 = P.sb("CT32", [128, 4, 2, 256])
        CTbf = P.sb("CTbf", [128, 4, 2, 256], BF16)
        nr32 = P.sb("nr32", [128, 4, 2, 128])
        nrbf = P.sb("nrbf", [128, 4, 2, 128], BF16)
        for t_ in (CT32, CTbf, nr32, nrbf):
            P.op('pool', lambda e, t_=t_: e.memset(t_[:], 0.0), [], [t_])
        mcar = P.sb("mcar", [4, 2])
        P.op('pool', lambda e: e.memset(mcar[:], 0.0), [], [mcar])
        MxE = P.sb("MxE", [4, 5])
        dec = P.sb("dec", [4, 4])
        smallb = P.sb("smallb", [128, 32])
        qe = P.sb("qe", [128, 2, 128], BF16)
        qw = P.sb("qw", [128, 2, 128], BF16)
        kg = P.sb("kg", [128, 2, 128], BF16)
        kgt = P.sb("kgt", [128, 256], BF16)
        st_bf = P.sb("st_bf", [128, 128], BF16)
        ddm = P.sb("ddm", [128, 128])
        recm = P.sb("recm", [128, 128])

    def mlstm(t0):
        qTt, kTt, vTt = [H[0], H[1]], [H[2], H[3]], [H[4], H[5]]
        ktt, vtt = [H[6], H[7]], [H[8], H[9]]
        xct = [H[10], H[11]]

        def fm(tl, c):
            return bfv(tl[c // 4]).rearrange("p (k t) -> p k t", k=4)[:, c % 4, :]

        def tokv(tl, tb):
            return bfv(tl[tb // 2]).rearrange("p (b d) -> p b d", b=2)[:, tb % 2, :]
        acc, xmb = G[9], G[10]
        xmb_ap = bfv(xmb)[:, 0:TT]
        gi_ps, gf_ps = PS[2], PS[3]
        for c in range(8):
            ps = proj_x('w0in', 25 + c)
            conv4(ps, c, m_tail, 'mcw', 'mcb', acc)
            P.op('act', lambda e, c=c: e.activation(out=fm(xct, c), in_=acc[:], func=AF.Silu), [acc], [xct[c // 4]])
            P.op('pool', lambda e: e.tensor_copy(out=xmb_ap, in_=ext[:, 3:TT + 3]), [ext], [xmb])
            for t_, (dst, src_ap, src_tb) in enumerate(((qTt, fm(xct, c), xct[c // 4]), (kTt, fm(xct, c), xct[c // 4]), (vTt, xmb_ap, xmb))):
                ps = nextps()
                P.op('pe', lambda e, ps=ps, t_=t_, src_ap=src_ap: e.matmul(ps[:], lhsT=wqkv_bf[:, t_, c, :], rhs=src_ap, start=True, stop=True),
                     [wqkv_bf, src_tb], [ps])
                P.op('act' if t_ != 1 else 'dve',
                     (lambda e, ps=ps, dst=dst: e.activation(out=fm(dst, c), in_=ps[:], func=AF.Copy)) if t_ != 1 else
                     (lambda e, ps=ps, dst=dst: e.tensor_copy(out=fm(dst, c), in_=ps[:])), [ps], [dst[c // 4]])
                j = t_ * 8 + c
                first, last = (c == 0 and t_ == 0), (c == 7 and t_ == 2)
                P.op('pe', lambda e, j=j, dst=dst, first=first, last=last: e.matmul(gi_ps[0:4, :], lhsT=wif_bf[:, 0, j, :], rhs=fm(dst, c), start=first, stop=last),
                     [wif_bf, dst[c // 4]], [gi_ps])
                P.op('pe', lambda e, j=j, dst=dst, first=first, last=last: e.matmul(gf_ps[0:4, :], lhsT=wif_bf[:, 1, j, :], rhs=fm(dst, c), start=first, stop=last),
                     [wif_bf, dst[c // 4]], [gf_ps])
            for t_, (dstl, src_ap, src_tb) in ((1, (ktt, fm(xct, c), xct[c // 4])), (2, (vtt, xmb_ap, xmb))):
                ps = nextps()
                for tb in range(NT):
                    P.op('pe', lambda e, ps=ps, tb=tb, t_=t_, src_ap=src_ap: e.matmul(ps[:, tb * 128:(tb + 1) * 128], lhsT=src_ap[:, tb * 128:(tb + 1) * 128], rhs=wqkv_bf[:, t_, c, :],
                                                                                      start=True, stop=True), [wqkv_bf, src_tb], [ps])
                for half in range(2):
                    dv = bfv(dstl[half]).rearrange("p (b d) -> p b d", b=2)[:, :, c * 128:(c + 1) * 128]
                    P.op('act' if half == 0 else 'dve',
                         (lambda e, ps=ps, dv=dv, half=half: e.activation(out=dv, in_=ps[:, half * 256:(half + 1) * 256].rearrange("p (b d) -> p b d", b=2), func=AF.Copy)) if half == 0 else
                         (lambda e, ps=ps, dv=dv, half=half: e.tensor_copy(out=dv, in_=ps[:, half * 256:(half + 1) * 256].rearrange("p (b d) -> p b d", b=2))),
                         [ps], [dstl[half]])
        rg_, re2, rwi, rem, rli, rFp, rag, rMx, rtmp = G[11], G[12], G[13], G[14], G[15], G[16], G[17], G[18], G[19]
        R4 = slice(0, 4)
        P.op('act', lambda e: e.activation(out=rli[R4, :], in_=gi_ps[R4, :], func=AF.Identity, bias=pc[R4, PC['bif_i']:PC['bif_i'] + 1]), [gi_ps, pc], [rli])
        P.op('act', lambda e: e.activation(out=rtmp[R4, :], in_=gf_ps[R4, :], func=AF.Exp, scale=-1.0, bias=pd[R4, PD['negbf']:PD['negbf'] + 1]), [gf_ps, pd], [rtmp])
        P.op('act', lambda e: e.activation(out=rtmp[R4, :], in_=rtmp[R4, :], func=AF.Ln, bias=1.0), [rtmp], [rtmp])
        P.op('dve', lambda e: e.tensor_tensor_scan(out=rFp[R4, :], data0=onesf[R4, :], data1=rtmp[R4, :], initial=mcar[:, 0:1], op0=MUL, op1=ADD),
             [onesf, rtmp, mcar], [rFp])
        P.op('pool', lambda e: e.tensor_tensor(out=rag[R4, :], in0=rli[R4, :], in1=rFp[R4, :], op=ADD), [rli, rFp], [rag])
        P.op('dve', lambda e: e.tensor_tensor_scan(out=rMx[R4, :], data0=rag[R4, :], data1=rag[R4, :], initial=mcar[:, 1:2], op0=MAX, op1=MAX),
             [rag, mcar], [rMx])
        P.op('pool', lambda e: e.tensor_copy(out=MxE[:, 0:1], in_=mcar[:, 1:2]), [mcar], [MxE])
        P.op('pool', lambda e: e.tensor_copy(out=MxE[:, 1:5], in_=rMx[R4, 127:TT:128]), [rMx], [MxE])
        P.op('pool', lambda e: e.tensor_copy(out=mcar[:, 0:1], in_=rFp[R4, TT - 1:TT]), [rFp], [mcar])
        P.op('pool', lambda e: e.tensor_copy(out=mcar[:, 1:2], in_=rMx[R4, TT - 1:TT]), [rMx], [mcar])
        v3 = lambda t_: t_[R4, :].rearrange("p (q t) -> p q t", q=NT)
        mend = MxE[:, 1:5].unsqueeze(2).broadcast_to([4, NT, 128])
        mprev = MxE[:, 0:4].unsqueeze(2).broadcast_to([4, NT, 128])
        P.op('dve', lambda e: e.tensor_tensor(out=v3(rg_), in0=v3(rag), in1=mend, op=SUB), [rag, MxE], [rg_])
        P.op('act', lambda e: e.activation(out=rg_[R4, :], in_=rg_[R4, :], func=AF.Exp), [rg_], [rg_])
        P.op('dve', lambda e: e.tensor_tensor(out=v3(re2), in0=mend, in1=v3(rMx), op=SUB), [rMx, MxE], [re2])
        P.op('act', lambda e: e.activation(out=re2[R4, :], in_=re2[R4, :], func=AF.Exp), [re2], [re2])
        P.op('dve', lambda e: e.tensor_tensor(out=v3(rwi), in0=mprev, in1=v3(rMx), op=SUB), [rMx, MxE], [rwi])
        P.op('act', lambda e: e.activation(out=rwi[R4, :], in_=rwi[R4, :], func=AF.Exp), [rwi], [rwi])
        P.op('dve', lambda e: e.tensor_tensor(out=rem[R4, :], in0=rFp[R4, :], in1=rMx[R4, :], op=SUB), [rFp, rMx], [rem])
        P.op('act', lambda e: e.activation(out=rem[R4, :], in_=rem[R4, :], func=AF.Exp), [rem], [rem])
        P.op('dve', lambda e: e.tensor_tensor(out=dec[:], in0=MxE[:, 0:4], in1=MxE[:, 1:5], op=SUB), [MxE], [dec])
        P.op('act', lambda e: e.activation(out=dec[:], in_=dec[:], func=AF.Exp), [dec], [dec])
        sp_ = PS[5]
        for q in range(NT):
            P.op('pe', lambda e, q=q: e.matmul(sp_[:, q * 4:(q + 1) * 4], lhsT=rg_[R4, q * 128:(q + 1) * 128], rhs=ident[0:4, 0:4], start=True, stop=True),
                 [rg_, ident], [sp_])
        for h in range(4):
            P.op('pe', lambda e, h=h: e.matmul(sp_[:, 16 + h * 4:16 + (h + 1) * 4], lhsT=sel[:, h, :], rhs=dec[:], start=True, stop=True), [sel, dec], [sp_])
        P.op('act', lambda e: e.activation(out=smallb[:], in_=sp_[:, 0:32], func=AF.Copy), [sp_], [smallb])
        for h in range(4):
            mlstm_head(h, qTt, kTt, ktt, vtt, xct, fm, tokv, (rg_, re2, rwi, rem))

    def mlstm_head(h, qTt, kTt, ktt, vtt, xct, fm, tokv, rows):
        bc = G[0:4]
        for i in range(4):
            ps = nextps()
            P.op('pe', lambda e, ps=ps, i=i: e.matmul(ps[:], lhsT=sel[:, h, :], rhs=rows[i][0:4, :], start=True, stop=True), [sel, rows[i]], [ps])
            P.op('act' if i % 2 == 0 else 'dve',
                 (lambda e, ps=ps, i=i: e.activation(out=bc[i][:], in_=ps[:], func=AF.Copy)) if i % 2 == 0 else
                 (lambda e, ps=ps, i=i: e.tensor_copy(out=bc[i][:], in_=ps[:])), [ps], [bc[i]])
        g_bc, e2_bc, wi_bc, em_bc = bc
        h32 = [G[4], G[5]]
        gms = [G[6], G[7]]
        for vc in range(2):
            ps = proj_x('w0in', 41 + 2 * h + vc)
            P.op('act', lambda e, ps=ps, vc=vc: e.activation(out=gms[vc][:], in_=ps[:], func=AF.Silu), [ps], [gms[vc]])
        qt_ = qTt[h // 2]
        kt_ = kTt[h // 2]
        qv = bfv(qt_).rearrange("p (k t) -> p k t", k=4)[:, 2 * (h % 2):2 * (h % 2) + 2, :]
        kv = bfv(kt_).rearrange("p (k t) -> p k t", k=4)[:, 2 * (h % 2):2 * (h % 2) + 2, :]
        CB, SB_, NB = PS[6], PS[7], PS[5]
        for q in range(NT):
            ts_ = slice(q * 128, (q + 1) * 128)
            bcast = lambda t_: t_[:, ts_].unsqueeze(1).broadcast_to([128, 2, 128])
            P.op('dve', lambda e: e.tensor_tensor(out=qe[:], in0=qv[:, :, ts_], in1=bcast(e2_bc), op=MUL), [qt_, e2_bc], [qe])
            P.op('pool', lambda e: e.tensor_tensor(out=qw[:], in0=qv[:, :, ts_], in1=bcast(wi_bc), op=MUL), [qt_, wi_bc], [qw])
            P.op('dve', lambda e: e.scalar_tensor_tensor(out=kg[:], in0=kv[:, :, ts_], scalar=0.0625, in1=bcast(g_bc), op0=MUL, op1=MUL), [kt_, g_bc], [kg])
            ktok = tokv(ktt, q)[:, h * 256:(h + 1) * 256]
            vtok = tokv(vtt, q)[:, h * 256:(h + 1) * 256]
            P.op('pool', lambda e: e.tensor_scalar(out=kgt[:], in0=ktok, scalar1=smallb[:, q * 4 + h:q * 4 + h + 1], scalar2=0.0625, op0=MUL, op1=MUL),
                 [ktt[q // 2], smallb], [kgt])
            for kc in range(2):
                P.op('pe', lambda e, kc=kc: e.matmul(SB_[:, kc * 256:(kc + 1) * 256], lhsT=kgt[:, kc * 128:(kc + 1) * 128], rhs=vtok, start=True, stop=True),
                     [kgt, vtt[q // 2]], [SB_])
                P.op('pe', lambda e, kc=kc: e.matmul(NB[:, 64 + kc * 128:64 + (kc + 1) * 128], lhsT=kgt[:, kc * 128:(kc + 1) * 128], rhs=ones_bf[:], start=True, stop=True),
                     [kgt, ones_bf], [NB])
            for kc in range(2):
                P.op('pe', lambda e, kc=kc: e.matmul(CB[:, 0:128], lhsT=kg[:, kc, :], rhs=qe[:, kc, :], start=(kc == 0), stop=(kc == 1)), [kg, qe], [CB])
            P.op('dve', lambda e: e.tensor_tensor(out=st_bf[:], in0=CB[:, 0:128], in1=causal[:], op=MUL), [CB, causal], [st_bf])
            for vc in range(2):
                o_ = CB[:, 128 + vc * 128:256 + vc * 128]
                P.op('pe', lambda e, o_=o_, vc=vc: e.matmul(o_, lhsT=vtok[:, vc * 128:(vc + 1) * 128], rhs=st_bf[:], start=True, stop=False), [vtt[q // 2], st_bf], [CB])
                for kc in range(2):
                    P.op('pe', lambda e, o_=o_, vc=vc, kc=kc: e.matmul(o_, lhsT=CTbf[:, h, kc, vc * 128:(vc + 1) * 128], rhs=qw[:, kc, :], start=False, stop=(kc == 1)),
                         [CTbf, qw], [CB])
            o_ = CB[:, 384:512]
            P.op('pe', lambda e, o_=o_: e.matmul(o_, lhsT=ones_bf[:], rhs=st_bf[:], start=True, stop=False), [ones_bf, st_bf], [CB])
            for kc in range(2):
                P.op('pe', lambda e, o_=o_, kc=kc: e.matmul(o_, lhsT=nrbf[:, h, kc, :], rhs=qw[:, kc, :], start=False, stop=(kc == 1)), [nrbf, qw], [CB])
            dcol = smallb[:, 16 + h * 4 + q:16 + h * 4 + q + 1]
            P.op('dve', lambda e: e.scalar_tensor_tensor(out=CT32[:, h], in0=CT32[:, h], scalar=dcol, in1=SB_[:].rearrange("p (k v) -> p k v", k=2), op0=MUL, op1=ADD),
                 [CT32, smallb, SB_], [CT32])
            P.op('act', lambda e: e.activation(out=CTbf[:, h], in_=CT32[:, h], func=AF.Copy), [CT32], [CTbf])
            P.op('dve', lambda e: e.scalar_tensor_tensor(out=nr32[:, h], in0=nr32[:, h], scalar=dcol, in1=NB[:, 64:320].rearrange("p (k v) -> p k v", k=2), op0=MUL, op1=ADD),
                 [nr32, smallb, NB], [nr32])
            P.op('pool', lambda e: e.tensor_copy(out=nrbf[:, h], in_=nr32[:, h]), [nr32], [nrbf])
            P.op('act', lambda e: e.activation(out=ddm[:], in_=CB[:, 384:512], func=AF.Abs), [CB], [ddm])
            P.op('dve', lambda e: e.tensor_tensor(out=ddm[:], in0=ddm[:], in1=em_bc[:, ts_], op=MAX), [ddm, em_bc], [ddm])
            P.op('dve', lambda e: e.tensor_scalar(out=ddm[:], in0=ddm[:], scalar1=1e-6, scalar2=None, op0=ADD), [ddm], [ddm])
            P.op('dve', lambda e: e.reciprocal(out=recm[:], in_=ddm[:]), [ddm], [recm])
            for vc in range(2):
                P.op('dve', lambda e, vc=vc: e.tensor_tensor(out=h32[vc][:, ts_], in0=CB[:, 128 + vc * 128:256 + vc * 128], in1=recm[:], op=MUL), [CB, recm], [h32[vc]])
        dump('h_m%d' % h, h32[0], h32[0][:])
        hr = [G[8], G[9]]
        ps = nextps()
        for vc in range(2):
            P.op('act', lambda e, vc=vc: e.activation(out=RV(hr[vc]), in_=h32[vc][:], func=AF.Copy), [h32[vc]], [hr[vc]])
            P.op('pe', lambda e, ps=ps, vc=vc: e.matmul(ps[:], lhsT=ones_r[:], rhs=RV(hr[vc]), start=(vc == 0), stop=(vc == 1)), [ones_r, hr[vc]], [ps])
        for vc in range(2):
            P.op('dve', lambda e, ps=ps, vc=vc: e.scalar_tensor_tensor(out=h32[vc][:], in0=ps[:], scalar=-1.0 / 256, in1=h32[vc][:], op0=MUL, op1=ADD), [ps, h32[vc]], [h32[vc]])
        ps2 = nextps()
        for vc in range(2):
            P.op('act', lambda e, vc=vc: e.activation(out=RV(hr[vc]), in_=h32[vc][:], func=AF.Square), [h32[vc]], [hr[vc]])
            P.op('pe', lambda e, ps2=ps2, vc=vc: e.matmul(ps2[:], lhsT=ones_r[:], rhs=RV(hr[vc]), start=(vc == 0), stop=(vc == 1)), [ones_r, hr[vc]], [ps2])
        rs = G[10]
        P.op('act', lambda e, ps2=ps2: e.activation(out=rs[:], in_=ps2[:], func=AF.Ln, scale=1.0 / 256, bias=1e-5), [ps2], [rs])
        P.op('act', lambda e: e.activation(out=rs[:], in_=rs[:], func=AF.Exp, scale=-0.5), [rs], [rs])
        for vc in range(2):
            ch = 2 * h + vc
            P.op('dve', lambda e, vc=vc: e.tensor_tensor(out=h32[vc][:], in0=h32[vc][:], in1=rs[:], op=MUL), [h32[vc], rs], [h32[vc]])
            P.op('act', lambda e, vc=vc, ch=ch: e.activation(out=h32[vc][:], in_=h32[vc][:], func=AF.Identity, scale=pcc('mnorm', ch)), [h32[vc], pc], [h32[vc]])
            P.op('dve', lambda e, vc=vc, ch=ch: e.scalar_tensor_tensor(out=h32[vc][:], in0=fm(xct, ch), scalar=pcc('mskip', ch), in1=h32[vc][:], op0=MUL, op1=ADD),
                 [xct[ch // 4], pc, h32[vc]], [h32[vc]])
            P.op('pool', lambda e, vc=vc, ch=ch: e.tensor_tensor(out=merged[:, 8 + ch, :], in0=h32[vc][:], in1=gms[vc][:], op=MUL), [h32[vc], gms[vc]], [merged])

    def layer0(t0):
        rmsnorm_x('mixn0')
        pstate['set'] = [0, 1]
        if do_rwkv:
            rwkv(t0)
        if do_mlstm:
            mlstm(t0)
        pstate['set'] = list(range(8))
        out_proj('w0out')

    if do_l0 and not (do_rwkv and do_mlstm):
        P.op('pool', lambda e: e.memset(merged[:], 0.0), [], [merged])

    for ti in range(ntiles):
        t0 = ti * TT
        for tb in range(NT):
            P.dma('sp', H[tb][:], x_d.ap()[t0 + tb * 128:t0 + (tb + 1) * 128, :], writes=[H[tb]])
        for kc in range(8):
            ps = nextps()
            for tb in range(NT):
                P.op('pe', lambda e, kc=kc, tb=tb, ps=ps: e.transpose(ps[:, tb * 128:(tb + 1) * 128], H[tb][:, kc * 128:(kc + 1) * 128], ident[:]),
                     [H[tb], ident], [ps])
            P.op('act' if kc % 2 == 0 else 'dve',
                 (lambda e, kc=kc, ps=ps: e.activation(out=hT[:, kc, :], in_=ps[:], func=AF.Copy)) if kc % 2 == 0 else
                 (lambda e, kc=kc, ps=ps: e.tensor_copy(out=hT[:, kc, :], in_=ps[:])), [ps], [hT])
        if do_l0:
            layer0(t0)
            ple(0, t0)
        if do_l1:
            layer1(t0)
            ple(1, t0)
        norm_stats()
        for k in range(8):
            of = G[k % 4]
            norm_apply(k, 'finn', of[:], of)
            ps = nextps()
            for tb in range(NT):
                P.op('pe', lambda e, tb=tb, ps=ps, of=of: e.transpose(ps[:, tb * 128:(tb + 1) * 128], of[:, tb * 128:(tb + 1) * 128], ident[:]),
                     [of, ident], [ps])
            for tb in range(NT):
                P.op('act' if k % 2 == 0 else 'dve',
                     (lambda e, k=k, ps=ps, tb=tb: e.activation(out=H[tb][:, k * 128:(k + 1) * 128], in_=ps[:, tb * 128:(tb + 1) * 128], func=AF.Copy)) if k % 2 == 0 else
                     (lambda e, k=k, ps=ps, tb=tb: e.tensor_copy(out=H[tb][:, k * 128:(k + 1) * 128], in_=ps[:, tb * 128:(tb + 1) * 128])),
                     [ps], [H[tb]])
        for tb in range(NT):
            P.dma('sp', o_d.ap()[t0 + tb * 128:t0 + (tb + 1) * 128, :], H[tb][:], reads=[H[tb]])
    P.finish()
    print("sbuf bytes/partition:", P.sbytes, "sems:", P.nsem, "instr:", {e: P.cnt[e] for e in ENGS})
    return nc


def kernel(**inputs):
    x = np.asarray(inputs['x'], np.float32)
    p = np.asarray(inputs['p'], np.float32)
    B, S, _ = x.shape
    shared = host_prep(inputs)
    nc = build_program(S)
    in_maps = []
    for b in range(B):
        m = dict(shared)
        m['x'] = np.ascontiguousarray(x[b])
        m['p'] = np.ascontiguousarray(p[:, b])
        in_maps.append(m)
    res = run_bass_kernel_spmd(nc, in_maps, core_ids=list(range(B)))
    return np.stack([np.asarray(r['out'], np.float32) for r in res.results], axis=0)
```

```python
import numpy as np
import concourse.bass as bass
import concourse.mybir as mybir
from concourse.bass_utils import run_bass_kernel_spmd
from contextlib import ExitStack

F32 = mybir.dt.float32
BF16 = mybir.dt.bfloat16
F32R = mybir.dt.float32r
SDT = F32R
AF = mybir.ActivationFunctionType
ALU = mybir.AluOpType
AX = mybir.AxisListType

ENGS = ('pe', 'act', 'dve', 'pool', 'sp')
EIDX = {e: i for i, e in enumerate(ENGS)}
EPOCH = 30000

D = 1024
TT = 512
NT = TT // 128
PLE = 256
DECAY_SCALE = 0.6065306597126334


class TB:
    __slots__ = ('name', 'h', 'lw', 'rd', 'dkey', 'root', 'psum')

    def __init__(self, name, h, root=None, psum=False):
        self.name = name
        self.h = h
        self.lw = None
        self.rd = {}
        self.dkey = None
        self.root = root if root is not None else self
        self.psum = psum or (root is not None and root.psum)

    def __getitem__(self, k):
        return self.h[k]


class _Rec:
    def __init__(self):
        self.call = None

    def __getattr__(self, name):
        def f(*a, **k):
            self.call = (name, a, k)
        return f


class Prog:
    def __init__(self, nc):
        self.nc = nc
        self.es = ExitStack()
        self.ops = {e: [] for e in ENGS}
        self.cnt = {e: 0 for e in ENGS}
        self.seen = {e: {} for e in ENGS}
        self.clk = {e: [] for e in ENGS}
        self.dcnt = {}
        self.dclk = {}
        self.sems = {}
        self.nsem = 0
        self.ndma = 0
        self.sbytes = 0

    def sb(self, name, shape, dt=F32):
        n = 1
        for s in shape[1:]:
            n *= s
        self.sbytes += n * (2 if dt == BF16 else 4)
        return TB(name, self.es.enter_context(self.nc.sbuf_tensor(name, list(shape), dt)))

    def ps(self, name, shape, dt=F32):
        return TB(name, self.es.enter_context(self.nc.psum_tensor(name, list(shape), dt)), psum=True)

    def dram(self, name, shape, dt, kind):
        return TB(name, self.nc.dram_tensor(name, list(shape), dt, kind=kind))

    def _sem(self, key):
        s = self.sems.get(key)
        if s is None:
            s = self.es.enter_context(self.nc.semaphore("s%d" % self.nsem))
            self.nsem += 1
            self.sems[key] = s
        return s

    def _semval(self, key, count):
        if key in EIDX:
            ep = (count - 1) // EPOCH
            return self._sem((key, ep)), count - ep * EPOCH
        return self._sem(key), 16 * count

    def _deps(self, e, reads, writes):
        seen = self.seen[e]
        need = {}

        def req(ev):
            if ev is None:
                return
            k, c = ev
            if k == 'pe' and e == 'pe':
                return
            if seen.get(k, 0) >= c:
                return
            if need.get(k, 0) < c:
                need[k] = c
        for t in reads:
            req(t.lw)
        for t in writes:
            req(t.lw)
            for k, c in t.rd.items():
                req((k, c))
        keys = list(need.keys())
        for k in keys:
            if k not in need:
                continue
            c = need[k]
            ck = self.clk[k][c - 1] if k in EIDX else self.dclk[(k, c)]
            for k2 in keys:
                if k2 != k and k2 in EIDX and k2 in need and ck[EIDX[k2]] >= need[k2]:
                    del need[k2]
        waits = []
        for k, c in need.items():
            waits.append(self._semval(k, c))
            ck = self.clk[k][c - 1] if k in EIDX else self.dclk[(k, c)]
            for e2, v in zip(ENGS, ck):
                if seen.get(e2, 0) < v:
                    seen[e2] = v
            if seen.get(k, 0) < c:
                seen[k] = c
        return waits

    def _snapshot(self, e):
        s = self.seen[e]
        return tuple(s.get(x, 0) for x in ENGS)

    def op(self, e, fn, reads=(), writes=()):
        writes = [t.root for t in writes] + [t.root for t in reads if t.psum]
        reads = [t.root for t in reads if not t.psum]
        rec = _Rec()
        fn(rec)
        fn = rec.call
        waits = self._deps(e, reads, writes)
        self.cnt[e] += 1
        c = self.cnt[e]
        snap = list(self._snapshot(e))
        snap[EIDX[e]] = c
        self.clk[e].append(tuple(snap))
        inc = self._semval(e, c)[0]
        self.ops[e].append((waits, fn, inc, 1))
        ev = (e, c)
        for t in reads:
            if t.rd.get(e, 0) < c:
                t.rd[e] = c
        for t in writes:
            t.lw = ev
            t.rd = {}
        return ev

    def dma(self, q, out, in_, reads=(), writes=(), key=None, **kw):
        if key is None:
            t0 = (list(writes) + list(reads))[0]
            if t0.dkey is None:
                t0.dkey = ('dma', self.ndma)
                self.ndma += 1
            key = t0.dkey
        reads = [t.root for t in reads]
        writes = [t.root for t in writes]
        waits = self._deps(q, reads, writes)
        self.dcnt[key] = self.dcnt.get(key, 0) + 1
        c = self.dcnt[key]
        self.dclk[(key, c)] = self._snapshot(q)
        sem = self._sem(key)
        self.ops[q].append((waits, ('dma_start', (), dict(out=out, in_=in_, **kw)), sem, 16))
        ev = (key, c)
        for t in reads:
            if t.rd.get(key, 0) < c:
                t.rd[key] = c
        for t in writes:
            t.lw = ev
            t.rd = {}
        return ev

    def finish(self):
        e = 'sp'
        tail = []
        for key, c in self.dcnt.items():
            tail.append(self._semval(key, c))
        for x in ENGS:
            if x != e and self.cnt[x] > 0:
                tail.append(self._semval(x, self.cnt[x]))
        blk = self.es.enter_context(self.nc.Block())
        engobj = {'pe': blk.tensor, 'act': blk.scalar, 'dve': blk.vector, 'pool': blk.gpsimd, 'sp': blk.sync}

        def make(ename):
            ops = self.ops[ename]

            def body(eng):
                for waits, fn, inc, n in ops:
                    for (s, v) in waits[1:]:
                        eng.wait_ge(s, v)
                    ins = getattr(eng, fn[0])(*fn[1], **fn[2])
                    if waits:
                        ins._wait_ge(waits[0][0], waits[0][1])
                    ins.then_inc(inc, n)
                if ename == e:
                    for (s, v) in tail:
                        eng.wait_ge(s, v)
            return body
        for ename in ENGS:
            if self.ops[ename] or ename == e:
                engobj[ename](make(ename))
        self.es.close()


PC_SPEC = [
    ('mixn0', 8), ('mixn1', 8), ('pen0', 8), ('pen1', 8), ('finn', 8),
    ('mu_r', 8), ('mu_k', 8), ('mu_v', 8), ('mu_l', 1),
    ('w0', 8), ('a0', 8), ('k_k', 8), ('k_a', 8), ('r_k', 8), ('ln_w', 8), ('ln_b', 8),
    ('mcw0', 8), ('mcw1', 8), ('mcw2', 8), ('mcw3', 8), ('mcb', 8), ('mnorm', 8), ('mskip', 8),
    ('bif_i', 1), ('bif_f', 1),
    ('ccw0', 16), ('ccw1', 16), ('ccw2', 16), ('ccw3', 16), ('ccb', 16), ('cbr', 16), ('cbi', 16), ('clam', 16),
]
PC = {}
_o = 0
for _n, _w in PC_SPEC:
    PC[_n] = _o
    _o += _w
NPC = _o
PD_SPEC = [('omu_r', 8), ('omu_k', 8), ('omu_v', 8), ('omu_l', 1), ('negbf', 1), ('csph', 16), ('cneg', 16), ('t0', 16), ('t1', 16)]
PD = {}
_o = 0
for _n, _w in PD_SPEC:
    PD[_n] = _o
    _o += _w
NPD = _o


def _cols(v):
    v = np.asarray(v, np.float32).reshape(-1)
    return np.ascontiguousarray(v.reshape(-1, 128).T)


def host_consts():
    c = {}
    c['ident'] = np.eye(128, dtype=np.float32)
    bd = np.zeros((128, 128), np.float32)
    bd[:64, :64] = 1
    bd[64:, 64:] = 1
    c['bdones'] = bd
    j = np.arange(128)[:, None]
    i = np.arange(128)[None, :]
    su = ((j < i) & ((j // 64) == (i // 64))).astype(np.float32)
    ui = ((j <= i) & ((j // 64) == (i // 64))).astype(np.float32)
    sl = ((j > i) & ((j // 64) == (i // 64))).astype(np.float32)
    on = np.ones((128, 128), np.float32)
    c['maskA'] = np.stack([su, su, ui, ui], axis=1)
    c['maskB'] = sl
    rm = np.ones((128, TT), np.float32)
    rm[:, ::64] = 0
    c['resetm'] = rm
    c['causal'] = (j <= i).astype(np.float32)
    sel = np.zeros((4, 4, 128), np.float32)
    for h in range(4):
        sel[h, h, :] = 1
    c['sel'] = sel
    return c


def host_prep(inp):
    g = {}
    f = lambda a: np.ascontiguousarray(np.asarray(a, np.float32))

    def chunked(w):
        K, M = w.shape
        return f(w.reshape(K // 128, 128, M // 128, 128).transpose(2, 1, 0, 3))
    g['w0in'] = chunked(f(inp['ab_w_in'])[0])
    g['w0out'] = chunked(f(inp['ab_w_out'])[0])
    g['w1in'] = chunked(f(inp['c_w_in'])[0])
    g['w1out'] = chunked(f(inp['c_w_out'])[0])
    g['pegate'] = np.stack([chunked(f(inp['pe_gate'])[i]) for i in range(2)])
    g['peup'] = np.stack([chunked(f(inp['pe_up'])[i]) for i in range(2)])
    wr = f(inp['c_wr'])[0]
    wi = f(inp['c_wi'])[0]
    g['wri'] = f(np.stack([wr, wi], axis=2))
    lup = np.zeros((128, 2, 1024), np.float32)
    lup[:64, 0, :] = f(inp['rwkv_w_up'])[0]
    lup[64:, 1, :] = f(inp['rwkv_a_up'])[0]
    g['lup'] = lup
    wbd = np.zeros((128, 3, 8, 128), np.float32)
    for t, nm in enumerate(['mlstm_wq', 'mlstm_wk', 'mlstm_wv']):
        w = f(inp[nm])[0]
        for c in range(8):
            for gg in range(32):
                wbd[4 * gg:4 * gg + 4, t, c, 4 * gg:4 * gg + 4] = w[c * 32 + gg]
    g['wqkv'] = wbd
    wif = f(inp['mlstm_w_if'])[0]
    g['wif'] = f(wif.reshape(24, 128, 2, 4).transpose(1, 2, 0, 3))
    pc = np.zeros((128, NPC), np.float32)

    def put(name, v):
        cc = _cols(v)
        pc[:, PC[name]:PC[name] + cc.shape[1]] = cc
    put('mixn0', f(inp['mix_norm'])[0]); put('mixn1', f(inp['mix_norm'])[1])
    put('pen0', f(inp['pe_norm'])[0]); put('pen1', f(inp['pe_norm'])[1])
    put('finn', f(inp['final_norm']))
    mu = f(inp['rwkv_mu'])[0]
    put('mu_r', mu[0]); put('mu_k', mu[1]); put('mu_v', mu[2])
    ml = f(inp['rwkv_mu_lora'])[0]
    put('mu_l', np.concatenate([ml[0], ml[1]]))
    put('w0', f(inp['rwkv_w0'])[0]); put('a0', f(inp['rwkv_a0'])[0])
    put('k_k', f(inp['rwkv_k_k'])[0]); put('k_a', f(inp['rwkv_k_a'])[0])
    put('r_k', f(inp['rwkv_r_k'])[0].reshape(-1))
    put('ln_w', f(inp['rwkv_ln_w'])[0]); put('ln_b', f(inp['rwkv_ln_b'])[0])
    mcw = f(inp['mlstm_conv_w'])[0]
    for j in range(4):
        put('mcw%d' % j, mcw[j])
    put('mcb', f(inp['mlstm_conv_b'])[0])
    put('mnorm', f(inp['mlstm_norm'])[0]); put('mskip', f(inp['mlstm_skip'])[0])
    bif = f(inp['mlstm_b_if'])[0]
    pc[0:4, PC['bif_i']] = bif[0:4]
    pc[0:4, PC['bif_f']] = bif[4:8]
    ccw = f(inp['c_conv_w'])[0]
    for j in range(4):
        put('ccw%d' % j, ccw[j])
    put('ccb', f(inp['c_conv_b'])[0]); put('cbr', f(inp['c_br'])[0]); put('cbi', f(inp['c_bi'])[0])
    put('clam', f(inp['c_lambda'])[0])
    g['pc'] = pc
    g.update(host_consts())
    return g


SHARED_SHAPES = {
    'w0in': [49, 128, 8, 128], 'w0out': [8, 128, 16, 128], 'w1in': [32, 128, 8, 128], 'w1out': [8, 128, 16, 128],
    'pegate': [2, 8, 128, 8, 128], 'peup': [2, 8, 128, 2, 128], 'wri': [16, 128, 2, 128], 'lup': [128, 2, 1024],
    'wqkv': [128, 3, 8, 128], 'wif': [128, 2, 24, 4], 'pc': [128, NPC],
    'ident': [128, 128], 'bdones': [128, 128], 'maskA': [128, 4, 128], 'maskB': [128, 128],
    'resetm': [128, TT], 'causal': [128, 128], 'sel': [4, 4, 128],
}


def build_program(S, do_l0=True, do_rwkv=True, do_mlstm=True, do_l1=True, dbg=None):
    nc = bass.Bass("TRN2", target_bir_lowering=False)
    P = Prog(nc)
    ntiles = S // TT
    din = {}
    for k, shp in SHARED_SHAPES.items():
        din[k] = nc.dram_tensor(k, shp, F32, kind="ExternalInput")
    x_d = nc.dram_tensor("x", [S, D], F32, kind="ExternalInput")
    p_d = nc.dram_tensor("p", [2, S, PLE], F32, kind="ExternalInput")
    o_d = nc.dram_tensor("out", [S, D], F32, kind="ExternalOutput")
    dbg_d = None
    dbg = dbg or []
    if dbg:
        dbg_d = nc.dram_tensor("dbg", [len(dbg), 128, TT], F32, kind="ExternalOutput")

    def dump(name, tb, ap):
        if name in dbg:
            P.dma('sp', dbg_d.ap()[dbg.index(name)], ap, reads=[tb], key=('dma', 'dbg'))

    MUL, ADD, SUB, MAX = ALU.mult, ALU.add, ALU.subtract, ALU.max

    def cload(name, dt=F32, rows=128):
        shp = SHARED_SHAPES[name]
        t = P.sb("c_" + name, shp, dt)
        P.dma('pool' if dt != F32 else 'sp', t[:], din[name].ap(), writes=[t])
        return t
    ident = cload('ident')
    pc = cload('pc')
    pd = P.sb("pd", [128, NPD])
    ones_bf = P.sb("ones_bf", [128, 128], BF16)
    P.op('pool', lambda e: e.memset(ones_bf[:], 1.0), [], [ones_bf])

    def pcc(name, c=0, n=1):
        return pc[:, PC[name] + c:PC[name] + c + n]

    def pdc(name, c=0, n=1):
        return pd[:, PD[name] + c:PD[name] + c + n]

    if do_l1:
        P.op('act', lambda e: e.activation(out=pdc('t0', 0, 16), in_=pcc('clam', 0, 16), func=AF.Exp, scale=-1.0), [pc], [pd])
        P.op('act', lambda e: e.activation(out=pdc('t1', 0, 16), in_=pdc('t0', 0, 16), func=AF.Ln, bias=1.0), [pd], [pd])
        P.op('dve', lambda e: e.tensor_scalar(out=pdc('csph', 0, 16), in0=pdc('t1', 0, 16), scalar1=4.0, scalar2=None, op0=MUL), [pd], [pd])
        P.op('dve', lambda e: e.tensor_scalar(out=pdc('cneg', 0, 16), in0=pdc('t1', 0, 16), scalar1=-8.0, scalar2=None, op0=MUL), [pd], [pd])
    if do_l0:
        for nm, w in (('r', 8), ('k', 8), ('v', 8), ('l', 1)):
            P.op('dve', lambda e, nm=nm, w=w: e.tensor_scalar(out=pdc('omu_' + nm, 0, w), in0=pcc('mu_' + nm, 0, w), scalar1=-1.0, scalar2=1.0, op0=MUL, op1=ADD),
                 [pc], [pd])
        P.op('dve', lambda e: e.tensor_scalar(out=pdc('negbf'), in0=pcc('bif_f'), scalar1=-1.0, scalar2=None, op0=MUL), [pc], [pd])

    scr = {}

    def mkscr(name, src_ap, shape, group=8):
        t = P.dram("scr_" + name, shape, BF16, "Internal")
        nm = shape[0]
        for j0 in range(0, nm, group):
            j1 = min(nm, j0 + group)
            P.dma('pool', t[j0:j1].rearrange("j p k m -> p j k m"), src_ap[j0:j1].rearrange("j p k m -> p j k m"),
                  writes=[t], key=('dma', 'scr_' + name))
        scr[name] = t
        return t
    if do_l0:
        lup_bf = cload('lup', BF16)
        wqkv_bf = cload('wqkv', BF16)
        wif_bf = cload('wif', BF16)
        mkscr('w0in', din['w0in'].ap(), [49, 128, 8, 128])
        mkscr('w0out', din['w0out'].ap(), [8, 128, 16, 128], group=4)
    for i in range(2):
        if (i == 0 and do_l0) or (i == 1 and do_l1):
            mkscr('pegate%d' % i, din['pegate'].ap()[i], [8, 128, 8, 128])
            mkscr('peup%d' % i, din['peup'].ap()[i], [8, 128, 2, 128])
    if do_l1:
        mkscr('w1in', din['w1in'].ap(), [32, 128, 8, 128])
        mkscr('wri', din['wri'].ap(), [16, 128, 2, 128], group=16)
        mkscr('w1out', din['w1out'].ap(), [8, 128, 16, 128], group=4)

    NWB = 4
    wring = [P.sb("wb%d" % i, [128, 16, 128], BF16) for i in range(NWB)]
    wstate = {'i': 0}

    def wld(name, j, nk):
        wb = wring[wstate['i'] % NWB]
        wstate['i'] += 1
        P.dma('sp', wb[:, 0:nk, :], scr[name][j], reads=[scr[name]], writes=[wb])
        return wb

    PS = [P.ps("ps%d" % i, [128, TT]) for i in range(8)]
    pstate = {'i': 0}

    pstate['set'] = list(range(8))

    def nextps():
        s_ = pstate['set']
        b = PS[s_[pstate['i'] % len(s_)]]
        pstate['i'] += 1
        return b

    hT = P.sb("hT", [128, 8, TT])
    xnT = P.sb("xnT", [128, 8, TT], BF16)
    merged = P.sb("merged", [128, 16, TT], BF16)
    ext = P.sb("ext", [128, TT + 3])
    exts = [ext, P.sb("ext2", [128, TT + 3])] if (do_l0 and do_mlstm) else [ext, ext]
    NG, NH = 20, 12
    G = [P.sb("g%d" % i, [128, TT]) for i in range(NG)]
    H = [P.sb("h%d" % i, [128, 2 * TT]) for i in range(NH)]
    lnv, rstd = G[19], G[18]
    _alias = {}

    def RV(tb):
        if SDT == F32:
            return tb[:]
        a = _alias.get(tb.name)
        if a is None:
            ml = nc.lookup_mloc(tb.h)
            a = nc.alloc_sbuf_tensor_at(tb.name + "_r", [128, int(ml.dims[1]) // 4], F32R, offset=int(ml.addr))
            _alias[tb.name] = a
        return a[:]
    ntmp = [G[16], G[17]]

    def bfv(tb, n=None):
        v = tb[:].bitcast(BF16)
        return v

    if do_l1:
        l1_tail = P.sb("l1_tail", [128, 16, 3])
        l1_h = P.sb("l1_h", [128, 16])
        P.op('pool', lambda e: e.memset(l1_tail[:], 0.0), [], [l1_tail])
        P.op('pool', lambda e: e.memset(l1_h[:], 0.0), [], [l1_h])

    def norm_apply(k, gname, out_ap, out_tb):
        if k % 2 == 0:
            P.op('dve', lambda e: e.scalar_tensor_tensor(out=out_ap, in0=hT[:, k, :], scalar=pcc(gname, k), in1=rstd[:],
                                                         op0=MUL, op1=MUL), [hT, pc, rstd], [out_tb])
        else:
            nt = ntmp[(k // 2) % 2]
            P.op('act', lambda e: e.activation(out=nt[:], in_=hT[:, k, :], func=AF.Identity, scale=pcc(gname, k)), [hT, pc], [nt])
            P.op('pool', lambda e: e.tensor_tensor(out=out_ap, in0=nt[:], in1=rstd[:], op=MUL), [nt, rstd], [out_tb])

    def norm_stats():
        ps = nextps()
        for half in range(2):
            sq = H[4 + half]
            sqv = bfv(sq).rearrange("p (k t) -> p k t", k=4)
            P.op('act', lambda e, half=half, sqv=sqv: e.activation(out=sqv, in_=hT[:, 4 * half:4 * half + 4, :], func=AF.Square), [hT], [sq])
            for k in range(4):
                P.op('pe', lambda e, k=k, half=half, sqv=sqv: e.matmul(ps[:], lhsT=ones_bf[:], rhs=sqv[:, k, :], start=(half == 0 and k == 0), stop=(half == 1 and k == 3)),
                     [ones_bf, sq], [ps])
        P.op('act', lambda e: e.activation(out=lnv[:], in_=ps[:], func=AF.Ln, scale=1.0 / D, bias=1e-6), [ps], [lnv])
        P.op('act', lambda e: e.activation(out=rstd[:], in_=lnv[:], func=AF.Exp, scale=-0.5), [lnv], [rstd])

    def rmsnorm_x(gname):
        norm_stats()
        for k in range(8):
            norm_apply(k, gname, xnT[:, k, :], xnT)

    def proj(ps, wb, nk, rhs_tb, rhs_fn):
        for k in range(nk):
            P.op('pe', lambda e, k=k: e.matmul(ps[:], lhsT=wb[:, k, :], rhs=rhs_fn(k), start=(k == 0), stop=(k == nk - 1)),
                 [wb, rhs_tb], [ps])

    def proj_x(name, j):
        w = wld(name, j, 8)
        ps = nextps()
        proj(ps, w, 8, xnT, lambda k: xnT[:, k, :])
        return ps

    def ple(i, t0):
        ptok = H[0]
        ptv = ptok[:].rearrange("p (tb d) -> p tb d", tb=NT)
        pTt = G[12]
        pT = bfv(pTt).rearrange("p (k t) -> p k t", k=2)
        sgs, tmps = [G[0], G[1]], [G[2], G[3]]
        P.dma('sp', ptv, p_d.ap()[i, t0:t0 + TT, :].rearrange("(tb p) d -> p tb d", p=128), writes=[ptok])
        for kc in range(2):
            ps = nextps()
            for tb in range(NT):
                P.op('pe', lambda e, kc=kc, tb=tb, ps=ps: e.transpose(ps[:, tb * 128:(tb + 1) * 128], ptv[:, tb, kc * 128:(kc + 1) * 128], ident[:]),
                     [ptok, ident], [ps])
            P.op('act', lambda e, kc=kc, ps=ps: e.activation(out=pT[:, kc, :], in_=ps[:], func=AF.Copy), [ps], [pTt])
        rmsnorm_x('pen%d' % i)
        for m in range(8):
            sg, tmpa = sgs[m % 2], tmps[m % 2]
            wg = wld('pegate%d' % i, m, 8)
            wu = wld('peup%d' % i, m, 2)
            ps = nextps()
            proj(ps, wg, 8, xnT, lambda k: xnT[:, k, :])
            P.op('act', lambda e, ps=ps: e.activation(out=sg[:], in_=ps[:], func=AF.Sigmoid), [ps], [sg])
            ps2 = nextps()
            proj(ps2, wu, 2, pTt, lambda k: pT[:, k, :])
            P.op('dve', lambda e, ps2=ps2: e.tensor_tensor(out=tmpa[:], in0=ps2[:], in1=sg[:], op=MUL), [ps2, sg], [tmpa])
            P.op('pool', lambda e, m=m: e.tensor_tensor(out=hT[:, m, :], in0=hT[:, m, :], in1=tmpa[:], op=ADD), [hT, tmpa], [hT])

    def out_proj(name):
        for m in range(8):
            wo = wld(name, m, 16)
            ps = nextps()
            proj(ps, wo, 16, merged, lambda k: merged[:, k, :])
            P.op('dve', lambda e, m=m, ps=ps: e.tensor_tensor(out=hT[:, m, :], in0=hT[:, m, :], in1=ps[:], op=ADD), [hT, ps], [hT])

    def conv4(ps, c, tail_tb, wname, bname, acc, ext=ext):
        P.op('pool', lambda e: e.tensor_copy(out=ext[:, 0:3], in_=tail_tb[:, c, :]), [tail_tb], [ext])
        P.op('act', lambda e: e.activation(out=ext[:, 3:TT + 3], in_=ps[:], func=AF.Copy), [ps], [ext])
        P.op('act', lambda e: e.activation(out=acc[:], in_=ps[:], func=AF.Identity, scale=pcc(wname + '3', c), bias=pcc(bname, c)), [ps, pc], [acc])
        P.op('pool', lambda e: e.tensor_copy(out=tail_tb[:, c, :], in_=ext[:, TT:TT + 3]), [ext], [tail_tb])
        for j in range(3):
            P.op('dve', lambda e, j=j: e.scalar_tensor_tensor(out=acc[:], in0=ext[:, j:j + TT], scalar=pcc(wname + str(j), c), in1=acc[:],
                                                              op0=MUL, op1=ADD), [ext, pc, acc], [acc])

    def layer1(t0):
        rmsnorm_x('mixn1')
        for g_ in range(4):
            bufs = []
            for k in range(4):
                s0, s1, s2 = H[3 * k], H[3 * k + 1], H[3 * k + 2]
                bufs.append(dict(xc32=(s0, s0[:, 0:TT]), xcb=(s0, s0[:, TT:2 * TT].bitcast(BF16)[:, 0:TT]),
                                 rg=(s1, s1[:, 0:TT]), ig=(s1, s1[:, TT:2 * TT]), a2=(s2, s2[:, 0:TT]), th=(s2, s2[:, TT:2 * TT])))
            cs = [4 * g_ + k for k in range(4)]
            for k, c in enumerate(cs):
                xt, xa = bufs[k]['xc32']
                bt, ba = bufs[k]['xcb']
                ext = exts[k % 2]
                ps = proj_x('w1in', c)
                P.op('pool', lambda e: e.tensor_copy(out=ext[:, 0:3], in_=l1_tail[:, c, :]), [l1_tail], [ext])
                P.op('act', lambda e: e.activation(out=ext[:, 3:TT + 3], in_=ps[:], func=AF.Copy), [ps], [ext])
                P.op('act', lambda e: e.activation(out=xa, in_=ps[:], func=AF.Identity, scale=pcc('ccw3', c), bias=pcc('ccb', c)), [ps, pc], [xt])
                P.op('pool', lambda e: e.tensor_copy(out=l1_tail[:, c, :], in_=ext[:, TT:TT + 3]), [ext], [l1_tail])
                for j in range(3):
                    P.op('dve', lambda e, j=j: e.scalar_tensor_tensor(out=xa, in0=ext[:, j:j + TT], scalar=pcc('ccw%d' % j, c), in1=xa, op0=MUL, op1=ADD), [ext, pc, xt], [xt])
                P.op('act', lambda e: e.activation(out=ba, in_=xa, func=AF.Copy), [xt], [bt])
            for k, c in enumerate(cs):
                bt, ba = bufs[k]['xcb']
                rt, ra = bufs[k]['rg']
                it, ia = bufs[k]['ig']
                wb = wld('wri', c, 2)
                ps_r = nextps()
                P.op('pe', lambda e: e.matmul(ps_r[:], lhsT=wb[:, 0, :], rhs=ba, start=True, stop=True), [wb, bt], [ps_r])
                ps_i = nextps()
                P.op('pe', lambda e: e.matmul(ps_i[:], lhsT=wb[:, 1, :], rhs=ba, start=True, stop=True), [wb, bt], [ps_i])
                P.op('act', lambda e: e.activation(out=ra, in_=ps_r[:], func=AF.Sigmoid, bias=pcc('cbr', c)), [ps_r, pc], [rt])
                P.op('act', lambda e: e.activation(out=ia, in_=ps_i[:], func=AF.Sigmoid, bias=pcc('cbi', c)), [ps_i, pc], [it])
            for k, c in enumerate(cs):
                rt, ra = bufs[k]['rg']
                at, aa_ = bufs[k]['a2']
                P.op('act', lambda e: e.activation(out=aa_, in_=ra, func=AF.Exp, scale=pdc('cneg', c)), [rt, pd], [at])
            for k, c in enumerate(cs):
                rt, ra = bufs[k]['rg']
                tt_, ta = bufs[k]['th']
                P.op('act', lambda e: e.activation(out=ta, in_=ra, func=AF.Tanh, scale=pdc('csph', c)), [rt, pd], [tt_])
            for k, c in enumerate(cs):
                at, aa_ = bufs[k]['a2']
                tt_, ta = bufs[k]['th']
                P.op('dve', lambda e: e.scalar_tensor_tensor(out=ta, in0=aa_, scalar=1.0, in1=ta, op0=ADD, op1=MUL), [at, tt_], [tt_])
                P.op('dve', lambda e: e.tensor_scalar(out=aa_, in0=ta, scalar1=-1.0, scalar2=1.0, op0=MUL, op1=ADD), [tt_], [at])
                P.op('dve', lambda e: e.scalar_tensor_tensor(out=ta, in0=aa_, scalar=1.0, in1=ta, op0=ADD, op1=MUL), [at, tt_], [tt_])
            for k, c in enumerate(cs):
                tt_, ta = bufs[k]['th']
                P.op('act', lambda e: e.activation(out=ta, in_=ta, func=AF.Sqrt), [tt_], [tt_])
            for k, c in enumerate(cs):
                xt, xa = bufs[k]['xc32']
                rt, ra = bufs[k]['rg']
                it, ia = bufs[k]['ig']
                at, aa_ = bufs[k]['a2']
                tt_, ta = bufs[k]['th']
                P.op('pool', lambda e: e.tensor_tensor(out=ia, in0=xa, in1=ia, op=MUL), [xt, it], [it])
                P.op('pool', lambda e: e.tensor_tensor(out=ia, in0=ia, in1=ta, op=MUL), [it, tt_], [it])
                P.op('dve', lambda e: e.tensor_tensor_scan(out=ra, data0=aa_, data1=ia, initial=l1_h[:, c:c + 1], op0=MUL, op1=ADD), [at, it, l1_h], [rt])
                P.op('pool', lambda e: e.tensor_copy(out=l1_h[:, c:c + 1], in_=ra[:, TT - 1:TT]), [rt], [l1_h])
            for k, c in enumerate(cs):
                xt, xa = bufs[k]['xc32']
                rt, ra = bufs[k]['rg']
                ps_g = proj_x('w1in', 16 + c)
                P.op('act', lambda e: e.activation(out=xa, in_=ps_g[:], func=AF.Silu), [ps_g], [xt])
                P.op('dve', lambda e: e.tensor_tensor(out=merged[:, c, :], in0=ra, in1=xa, op=MUL), [rt, xt], [merged])
        out_proj('w1out')

    if do_l0 and do_rwkv:
        identr = P.sb("identr", [128, 128], SDT)
        bdones = cload('bdones')
        bdones_bf = P.sb("bdones_bf", [128, 128], BF16)
        bdones_r = P.sb("bdones_r", [128, 128], SDT)
        maskA = cload('maskA')
        maskB = cload('maskB')
        resetm = cload('resetm')
        P.op('dve', lambda e: e.tensor_copy(out=identr[:], in_=ident[:]), [ident], [identr])
        P.op('dve', lambda e: e.tensor_copy(out=bdones_bf[:], in_=bdones[:]), [bdones], [bdones_bf])
        P.op('dve', lambda e: e.tensor_copy(out=bdones_r[:], in_=bdones[:]), [bdones], [bdones_r])
        rwc = P.sb("rwc", [128, 25])
        P.op('pool', lambda e: e.memset(rwc[:], 0.0), [], [rwc])
        lora_bf = P.sb("lora_bf", [128, TT], BF16)
        Sst = [[P.sb("S%d_%d" % (c, q), [128, 128], F32) for q in range(2)] for c in range(8)]
        for c in range(8):
            P.op('pool', lambda e, c=c: e.memset(Sst[c][0][:], 0.0), [], [Sst[c][0]])
        spar = [0] * 8
        rhs_sb = P.sb("rhs_sb", [128, 128], SDT)
        u_sb = P.sb("u_sb", [128, 128], SDT)
        psRHS = TB("psRHS", PS[7].h[:, 0:128], root=PS[7])
        psU = TB("psU", PS[7].h[:, 128:256], root=PS[7])
        psYT = TB("psYT", PS[6].h[:, 0:128], root=PS[6])
        psSN = TB("psSN", PS[7].h[:, 384:512], root=PS[7])

    def tshift(ps, dst, col, mu_ap, omu_ap):
        P.op('act', lambda e: e.activation(out=dst[:], in_=ps[:], func=AF.Identity, scale=omu_ap), [ps, pd], [dst])
        P.op('dve', lambda e: e.scalar_tensor_tensor(out=dst[:, 1:TT], in0=ps[:, 0:TT - 1], scalar=mu_ap, in1=dst[:, 1:TT], op0=MUL, op1=ADD),
             [ps, pc, dst], [dst])
        P.op('dve', lambda e: e.scalar_tensor_tensor(out=dst[:, 0:1], in0=rwc[:, col:col + 1], scalar=mu_ap, in1=dst[:, 0:1], op0=MUL, op1=ADD),
             [rwc, pc, dst], [dst])
        P.op('act', lambda e: e.activation(out=rwc[:, col:col + 1], in_=ps[:, TT - 1:TT], func=AF.Copy), [ps], [rwc])

    def r3(ap, n=8):
        return ap.rearrange("p (n t) -> p n t", n=n)

    def rwkv(t0):
        Rbd, Abd, Bbd, Kbd, Vbd = H[0:5]
        for i in range(5):
            P.op('pool', lambda e, i=i: e.memset(H[i][:], 0.0), [], [H[i]])
        bdv = [r3(RV(H[i])) for i in range(5)]
        ps = proj_x('w0in', 24)
        lora32 = G[16]
        tshift(ps, lora32, 24, pcc('mu_l'), pdc('omu_l'))
        P.op('act', lambda e: e.activation(out=lora_bf[0:64, :], in_=lora32[0:64, :], func=AF.Tanh), [lora32], [lora_bf])
        P.op('act', lambda e: e.activation(out=lora_bf[64:128, :], in_=lora32[64:128, :], func=AF.Copy), [lora32], [lora_bf])
        pend = []

        def pump(n):
            for _ in range(n):
                if pend:
                    pend.pop(0)()
        for th_ in rwkv_prep(0):
            th_()
        rwkv_blockdiag(0, bdv)
        for c in range(8):
            if c + 1 < 8:
                pend.extend(rwkv_prep(c + 1))
            rwkv_chunks(c, bdv, pump)
            pump(len(pend))
            if c + 1 < 8:
                rwkv_blockdiag(c + 1, bdv)
            rwkv_out(c)

    def rw_bufs(c):
        p_ = c % 2
        d = dict(r32=G[0], k32=G[1], v32=G[2], cump=G[6], Ginv=G[8], Gp=G[9], a32=G[10], kk32=G[11], kka=G[11],
                 rn=G[12], nkk=G[13], keff=G[15], t1=G[16])
        d.update(dict(gr32=(G[3], G[14])[p_], lw=(G[4], G[17])[p_], cum=(G[5], G[18])[p_], Gt=(G[7], G[19])[p_]))
        return d

    def rwkv_prep(c):
        B_ = rw_bufs(c)
        r32, k32, v32, gr32, lw, cum, cump, Gt, Ginv, Gp = [B_[k] for k in ('r32', 'k32', 'v32', 'gr32', 'lw', 'cum', 'cump', 'Gt', 'Ginv', 'Gp')]
        a32, kk32, kka, rn, nkk, keff, t1 = [B_[k] for k in ('a32', 'kk32', 'kka', 'rn', 'nkk', 'keff', 't1')]
        bon = cum
        T = []

        def projshift(dst, mch, nm, col):
            def f():
                ps = proj_x('w0in', mch)
                tshift(ps, dst, col, pcc('mu_' + nm, c), pdc('omu_' + nm, c))
            return f
        T.append(projshift(k32, 8 + c, 'k', 8 + c))

        def f_lw():
            ps = nextps()
            P.op('pe', lambda e: e.matmul(ps[:], lhsT=lup_bf[:, 0, c * 128:(c + 1) * 128], rhs=lora_bf[:], start=True, stop=True), [lup_bf, lora_bf], [ps])
            P.op('act', lambda e: e.activation(out=lw[:], in_=ps[:], func=AF.Sigmoid, bias=pcc('w0', c)), [ps, pc], [lw])
        T.append(f_lw)

        def f_a():
            ps = nextps()
            P.op('pe', lambda e: e.matmul(ps[:], lhsT=lup_bf[:, 1, c * 128:(c + 1) * 128], rhs=lora_bf[:], start=True, stop=True), [lup_bf, lora_bf], [ps])
            P.op('act', lambda e: e.activation(out=a32[:], in_=ps[:], func=AF.Sigmoid, bias=pcc('a0', c)), [ps, pc], [a32])
        T.append(f_a)
        T.append(lambda: P.op('dve', lambda e: e.tensor_tensor_scan(out=cum[:], data0=resetm[:], data1=lw[:], initial=0.0, op0=MUL, op1=ADD), [resetm, lw], [cum]))
        T.append(lambda: P.op('dve', lambda e: e.tensor_tensor(out=cump[:], in0=cum[:], in1=lw[:], op=SUB), [cum, lw], [cump]))
        T.append(lambda: P.op('act', lambda e: e.activation(out=Gt[:], in_=cum[:], func=AF.Exp, scale=-DECAY_SCALE), [cum], [Gt]))
        T.append(lambda: P.op('act', lambda e: e.activation(out=Ginv[:], in_=cum[:], func=AF.Exp, scale=DECAY_SCALE), [cum], [Ginv]))
        T.append(lambda: P.op('act', lambda e: e.activation(out=Gp[:], in_=cump[:], func=AF.Exp, scale=-DECAY_SCALE), [cump], [Gp]))
        T.append(lambda: P.op('dve', lambda e: e.tensor_scalar(out=kk32[:], in0=k32[:], scalar1=pcc('k_k', c), scalar2=None, op0=MUL), [k32, pc], [kk32]))
        sqk = bfv(t1)[:, 0:TT]
        T.append(lambda: P.op('act', lambda e: e.activation(out=sqk, in_=kk32[:], func=AF.Square), [kk32], [t1]))

        def f_rn():
            ps = nextps()
            P.op('pe', lambda e: e.matmul(ps[:], lhsT=bdones_bf[:], rhs=sqk, start=True, stop=True), [bdones_bf, t1], [ps])
            P.op('act', lambda e: e.activation(out=rn[:], in_=ps[:], func=AF.Ln, bias=1e-20), [ps], [rn])
            P.op('act', lambda e: e.activation(out=rn[:], in_=rn[:], func=AF.Exp, scale=-0.5), [rn], [rn])
        T.append(f_rn)
        T.append(lambda: P.op('dve', lambda e: e.scalar_tensor_tensor(out=nkk[:], in0=kk32[:], scalar=-1.0, in1=rn[:], op0=MUL, op1=MUL), [kk32, rn], [nkk]))
        T.append(lambda: P.op('dve', lambda e: e.scalar_tensor_tensor(out=kka[:], in0=nkk[:], scalar=-1.0, in1=a32[:], op0=MUL, op1=MUL), [nkk, a32], [kka]))
        T.append(lambda: P.op('pool', lambda e: e.tensor_scalar(out=t1[:], in0=a32[:], scalar1=-1.0, scalar2=pcc('k_a', c), op0=ADD, op1=MUL), [a32, pc], [t1]))
        T.append(lambda: P.op('dve', lambda e: e.scalar_tensor_tensor(out=keff[:], in0=t1[:], scalar=1.0, in1=k32[:], op0=ADD, op1=MUL), [t1, k32], [keff]))
        T.append(projshift(r32, c, 'r', c))
        rkr = bfv(t1)[:, 0:TT]
        T.append(lambda: P.op('dve', lambda e: e.scalar_tensor_tensor(out=rkr, in0=r32[:], scalar=pcc('r_k', c), in1=keff[:], op0=MUL, op1=MUL), [r32, pc, keff], [t1]))
        T.append(projshift(v32, 16 + c, 'v', 16 + c))

        def f_bon():
            ps = nextps()
            P.op('pe', lambda e: e.matmul(ps[:], lhsT=bdones_bf[:], rhs=rkr, start=True, stop=True), [bdones_bf, t1], [ps])
            P.op('dve', lambda e: e.tensor_tensor(out=bon[:], in0=ps[:], in1=v32[:], op=MUL), [ps, v32], [bon])
        T.append(f_bon)

        def f_gr():
            ps = proj_x('w0in', 33 + c)
            P.op('act', lambda e: e.activation(out=gr32[:], in_=ps[:], func=AF.Silu), [ps], [gr32])
        T.append(f_gr)
        return T

    def rwkv_blockdiag(c, bdv):
        B_ = rw_bufs(c)
        Rv, Av, Bv, Kv, Vv = bdv
        Rbd, Abd, Bbd, Kbd, Vbd = H[0:5]
        for hh in range(2):
            hs_ = slice(hh * 64, hh * 64 + 64)
            for i_, (dstv, dtb, a_, b_) in enumerate(((Rv, Rbd, B_['r32'], B_['Gt']), (Av, Abd, B_['nkk'], B_['Gp']), (Bv, Bbd, B_['kka'], B_['Ginv']), (Kv, Kbd, B_['keff'], B_['Ginv']))):
                P.op('dve' if (i_ + hh) % 2 == 0 else 'pool', lambda e, dstv=dstv, a_=a_, b_=b_, hs_=hs_: e.tensor_tensor(out=dstv[hs_, :, hs_], in0=r3(a_[hs_, :]), in1=r3(b_[hs_, :]), op=MUL),
                     [a_, b_], [dtb])
            P.op('act', lambda e, hs_=hs_: e.activation(out=Vv[hs_, :, hs_], in_=r3(B_['v32'][hs_, :]), func=AF.Copy), [B_['v32']], [Vbd])

    def rwkv_chunks(c, bdv, pump):
        B_ = rw_bufs(c)
        Gt, y32 = B_['Gt'], B_['lw']
        Rv, Av, Bv, Kv, Vv = bdv
        Rbd, Abd, Bbd, Kbd, Vbd = H[0:5]
        for hf in range(2):
            SCt = [H[5], H[6]]
            QTt = [H[7], H[8]]
            SCv = [RV(t).rearrange("p (j m k) -> p j m k", j=2, m=4) for t in SCt]
            QTv = [RV(t).rearrange("p (j m k) -> p j m k", j=2, m=4) for t in QTt]
            SCf = [t[:].rearrange("p (j m k) -> p j m k", j=2, m=4) for t in SCt]
            PQt = [H[9], H[10]]
            PQv = [[RV(PQt[b]).rearrange("p (q j m k) -> p q j m k", q=2, j=2, m=2)[:, par] for b in range(2)] for par in range(2)]
            Xt = H[11]
            Xv = [RV(Xt).rearrange("p (q j k) -> p q j k", q=2, j=4)[:, q] for q in range(2)]
            Xf = [Xt[:].rearrange("p (q j k) -> p q j k", q=2, j=4)[:, q] for q in range(2)]
            for j in range(4):
                n = hf * 4 + j
                sc = SCv[j // 2][:, j % 2]
                qt = QTv[j // 2][:, j % 2]
                A_, B2_ = (PS[2], PS[3]) if j % 2 == 0 else (PS[4], PS[5])
                for m, (l_, r_) in enumerate(((Bv, Av), (Kv, Av), (Bv, Rv), (Kv, Rv))):
                    P.op('pe', lambda e, m=m, l_=l_, r_=r_, n=n: e.matmul(A_[:, m * 128:(m + 1) * 128], lhsT=l_[:, n, :], rhs=r_[:, n, :], start=True, stop=True),
                         [Rbd, Abd, Bbd, Kbd], [A_])
                P.op('dve', lambda e, sc=sc: e.tensor_tensor(out=sc, in0=A_[:].rearrange("p (m k) -> p m k", m=4), in1=maskA[:], op=MUL),
                     [A_, maskA], [SCt[j // 2]])
                P.op('pe', lambda e, n=n: e.matmul(B2_[:, 0:128], lhsT=Av[:, n, :], rhs=Bv[:, n, :], start=True, stop=True), [Abd, Bbd], [B2_])
                for m, l_ in enumerate((Bv, Kv, Vv)):
                    P.op('pe', lambda e, m=m, l_=l_, n=n: e.matmul(B2_[:, (m + 1) * 128:(m + 2) * 128], lhsT=l_[:, n, :], rhs=identr[:], start=True, stop=True),
                         [Bbd, Kbd, Vbd, identr], [B2_])
                P.op('dve', lambda e, qt=qt: e.tensor_tensor(out=qt[:, 0, :], in0=B2_[:, 0:128], in1=maskB[:], op=MUL), [B2_, maskB], [QTt[j // 2]])
                P.op('act', lambda e, qt=qt: e.activation(out=qt[:, 1:4, :], in_=B2_[:, 128:512].rearrange("p (m k) -> p m k", m=3), func=AF.Copy),
                     [B2_], [QTt[j // 2]])
                pump(2)
            for j in range(4):
                P.op('dve', lambda e, j=j: e.tensor_tensor(out=Xv[1][:, j, :], in0=SCf[j // 2][:, j % 2, 0, :], in1=ident[:], op=ADD),
                     [SCt[j // 2], ident], [Xt])
            for lvl in range(1, 6):
                par = lvl % 2
                for b in range(2):
                    bank = PS[4 + b]
                    for jj in range(2):
                        j = 2 * b + jj
                        if lvl == 1:
                            Pp = SCv[j // 2][:, j % 2, 0, :]
                            Qp = QTv[j // 2][:, j % 2, 0, :]
                            rd = [SCt[j // 2], QTt[j // 2]]
                        else:
                            Pp = PQv[1 - par][b][:, jj, 0, :]
                            Qp = PQv[1 - par][b][:, jj, 1, :]
                            rd = [PQt[b]]
                        P.op('pe', lambda e, jj=jj, Pp=Pp, Qp=Qp, bank=bank: e.matmul(bank[:, (2 * jj) * 128:(2 * jj + 1) * 128], lhsT=Qp, rhs=Pp, start=True, stop=True), rd, [bank])
                        P.op('pe', lambda e, jj=jj, Pp=Pp, Qp=Qp, bank=bank: e.matmul(bank[:, (2 * jj + 1) * 128:(2 * jj + 2) * 128], lhsT=Pp, rhs=Qp, start=True, stop=True), rd, [bank])
                    if b == 0:
                        P.op('act', lambda e, b=b, bank=bank, par=par: e.activation(out=PQv[par][b], in_=bank[:].rearrange("p (j m k) -> p j m k", j=2, m=2), func=AF.Copy),
                             [bank], [PQt[b]])
                    else:
                        P.op('dve', lambda e, b=b, bank=bank, par=par: e.tensor_copy(out=PQv[par][b], in_=bank[:].rearrange("p (j m k) -> p j m k", j=2, m=2)),
                             [bank], [PQt[b]])
                XB = PS[6]
                for j in range(4):
                    Ql = PQv[par][j // 2][:, j % 2, 1, :]
                    P.op('pe', lambda e, j=j, Ql=Ql, par=par: e.matmul(XB[:, j * 128:(j + 1) * 128], lhsT=Ql, rhs=Xv[par][:, j, :], start=True, stop=True), [PQt[j // 2], Xt], [XB])
                P.op('dve', lambda e, par=par: e.tensor_tensor(out=Xv[1 - par], in0=XB[:].rearrange("p (j k) -> p j k", j=4), in1=Xf[par], op=ADD), [XB, Xt], [Xt])
                pump(1)
            for j in range(4):
                n = hf * 4 + j
                sc = SCv[j // 2][:, j % 2]
                qt = QTv[j // 2][:, j % 2]
                sct, qtt = SCt[j // 2], QTt[j // 2]
                S0 = Sst[c][spar[c]]
                S1 = Sst[c][1 - spar[c]]
                spar[c] = 1 - spar[c]
                P.op('pe', lambda e, n=n, S0=S0: e.matmul(psRHS[:], lhsT=Av[:, n, :], rhs=RV(S0), start=True, stop=False), [Abd, S0], [psRHS])
                P.op('pe', lambda e, sc=sc, qt=qt: e.matmul(psRHS[:], lhsT=sc[:, 1, :], rhs=qt[:, 3, :], start=False, stop=True), [sct, qtt], [psRHS])
                P.op('dve', lambda e: e.tensor_copy(out=rhs_sb[:], in_=psRHS[:]), [psRHS], [rhs_sb])
                P.op('pe', lambda e, j=j: e.matmul(psU[:], lhsT=Xv[0][:, j, :], rhs=rhs_sb[:], start=True, stop=True), [Xt, rhs_sb], [psU])
                P.op('dve', lambda e: e.tensor_copy(out=u_sb[:], in_=psU[:]), [psU], [u_sb])
                P.op('pe', lambda e, qt=qt: e.matmul(psSN[:], lhsT=qt[:, 1, :], rhs=u_sb[:], start=True, stop=False), [qtt, u_sb], [psSN])
                P.op('pe', lambda e, qt=qt: e.matmul(psSN[:], lhsT=qt[:, 2, :], rhs=qt[:, 3, :], start=False, stop=False), [qtt], [psSN])
                P.op('pe', lambda e, S0=S0: e.matmul(psSN[:], lhsT=identr[:], rhs=RV(S0), start=False, stop=True), [identr, S0], [psSN])
                P.op('dve', lambda e, n=n, S1=S1: e.tensor_scalar(out=RV(S1), in0=psSN[:], scalar1=Gt[:, n * 64 + 63:n * 64 + 64], scalar2=None, op0=MUL), [psSN, Gt], [S1])
                P.op('pe', lambda e, n=n, S0=S0: e.matmul(psYT[:], lhsT=RV(S0), rhs=Rv[:, n, :], start=True, stop=False), [S0, Rbd], [psYT])
                P.op('pe', lambda e, sc=sc: e.matmul(psYT[:], lhsT=u_sb[:], rhs=sc[:, 2, :], start=False, stop=False), [u_sb, sct], [psYT])
                P.op('pe', lambda e, sc=sc, qt=qt: e.matmul(psYT[:], lhsT=qt[:, 3, :], rhs=sc[:, 3, :], start=False, stop=True), [qtt, sct], [psYT])
                for hh in range(2):
                    hs_ = slice(hh * 64, hh * 64 + 64)
                    P.op('act', lambda e, hs_=hs_, n=n: e.activation(out=y32[hs_, n * 64:(n + 1) * 64], in_=psYT[hs_, hs_], func=AF.Copy), [psYT], [y32])
                pump(2)

    def rwkv_out(c):
        B_ = rw_bufs(c)
        y32, dd, bon, gr32 = B_['lw'], B_['Gt'], B_['cum'], B_['gr32']
        dump('y_rwkv%d' % c, y32, y32[:])
        yrv = RV(y32)
        P.op('act', lambda e: e.activation(out=yrv, in_=y32[:], func=AF.Copy), [y32], [y32])
        ps = nextps()
        P.op('pe', lambda e: e.matmul(ps[:], lhsT=bdones_r[:], rhs=yrv, start=True, stop=True), [bdones_r, y32], [ps])
        P.op('dve', lambda e: e.scalar_tensor_tensor(out=dd[:], in0=ps[:], scalar=-1.0 / 64, in1=y32[:], op0=MUL, op1=ADD), [ps, y32], [dd])
        sq, rs = H[5], H[6]
        sqv = RV(sq)[:, 0:TT]
        P.op('act', lambda e: e.activation(out=sqv, in_=dd[:], func=AF.Square), [dd], [sq])
        ps2 = nextps()
        P.op('pe', lambda e: e.matmul(ps2[:], lhsT=bdones_r[:], rhs=sqv, start=True, stop=True), [bdones_r, sq], [ps2])
        P.op('act', lambda e: e.activation(out=rs[:, 0:TT], in_=ps2[:], func=AF.Ln, scale=1.0 / 64, bias=64e-5), [ps2], [rs])
        P.op('act', lambda e: e.activation(out=rs[:, 0:TT], in_=rs[:, 0:TT], func=AF.Exp, scale=-0.5), [rs], [rs])
        P.op('dve', lambda e: e.tensor_tensor(out=dd[:], in0=dd[:], in1=rs[:, 0:TT], op=MUL), [dd, rs], [dd])
        P.op('act', lambda e: e.activation(out=dd[:], in_=dd[:], func=AF.Identity, scale=pcc('ln_w', c), bias=pcc('ln_b', c)), [dd, pc], [dd])
        P.op('pool', lambda e: e.tensor_tensor(out=dd[:], in0=dd[:], in1=bon[:], op=ADD), [dd, bon], [dd])
        P.op('dve', lambda e: e.tensor_tensor(out=merged[:, c, :], in0=dd[:], in1=gr32[:], op=MUL), [dd, gr32], [merged])

    if do_l0 and do_mlstm:
        causal = cload('causal')
        sel = P.sb("c_sel", [4, 4, 128])
        P.dma('sp', sel[:], din['sel'].ap(), writes=[sel])
        onesf = P.sb("onesf", [128, TT])
        P.op('pool', lambda e: e.memset(onesf[:], 1.0), [], [onesf])
        ones_r = P.sb("ones_r", [128, 128], SDT)
        P.op('dve', lambda e: e.tensor_copy(out=ones_r[:], in_=onesf[:, 0:128]), [onesf], [ones_r])
        m_tail = P.sb("m_tail", [128, 8, 3])
        P.op('pool', lambda e: e.memset(m_tail[:], 0.0), [], [m_tail])
        CT32 = P.sb("CT32", [128, 4, 2, 256])
        CTbf = P.sb("CTbf", [128, 4, 2, 256], BF16)
        nr32 = P.sb("nr32", [128, 4, 2, 128])
        nrbf = P.sb("nrbf", [128, 4, 2, 128], BF16)
        for t_ in (CT32, CTbf, nr32, nrbf):
            P.op('pool', lambda e, t_=t_: e.memset(t_[:], 0.0), [], [t_])
        mcar = P.sb("mcar", [4, 2])
        P.op('pool', lambda e: e.memset(mcar[:], 0.0), [], [mcar])
        MxE = P.sb("MxE", [4, 5])
        dec = P.sb("dec", [4, 4])
        smallb = P.sb("smallb", [128, 32])
        qe = P.sb("qe", [128, 2, 128], BF16)
        qw = P.sb("qw", [128, 2, 128], BF16)
        kg = P.sb("kg", [128, 2, 128], BF16)
        kgt = P.sb("kgt", [128, 256], BF16)
        st_bf = P.sb("st_bf", [128, 128], BF16)
        ddm = P.sb("ddm", [128, 128])
        recm = P.sb("recm", [128, 128])

    def mlstm(t0):
        qTt, kTt, vTt = [H[0], H[1]], [H[2], H[3]], [H[4], H[5]]
        ktt, vtt = [H[6], H[7]], [H[8], H[9]]
        xct = [H[10], H[11]]

        def fm(tl, c):
            return bfv(tl[c // 4]).rearrange("p (k t) -> p k t", k=4)[:, c % 4, :]

        def tokv(tl, tb):
            return bfv(tl[tb // 2]).rearrange("p (b d) -> p b d", b=2)[:, tb % 2, :]
        gi_ps, gf_ps = PS[2], PS[3]
        for c in range(8):
            acc, xmb = (G[9], G[10]) if c % 2 == 0 else (G[0], G[1])
            xmb_ap = bfv(xmb)[:, 0:TT]
            ext = exts[c % 2]
            ps = proj_x('w0in', 25 + c)
            conv4(ps, c, m_tail, 'mcw', 'mcb', acc, ext)
            P.op('act', lambda e, c=c: e.activation(out=fm(xct, c), in_=acc[:], func=AF.Silu), [acc], [xct[c // 4]])
            P.op('pool', lambda e: e.tensor_copy(out=xmb_ap, in_=ext[:, 3:TT + 3]), [ext], [xmb])
            for t_, (dst, src_ap, src_tb) in enumerate(((qTt, fm(xct, c), xct[c // 4]), (kTt, fm(xct, c), xct[c // 4]), (vTt, xmb_ap, xmb))):
                ps = nextps()
                P.op('pe', lambda e, ps=ps, t_=t_, src_ap=src_ap: e.matmul(ps[:], lhsT=wqkv_bf[:, t_, c, :], rhs=src_ap, start=True, stop=True),
                     [wqkv_bf, src_tb], [ps])
                P.op('act' if t_ != 1 else 'dve',
                     (lambda e, ps=ps, dst=dst: e.activation(out=fm(dst, c), in_=ps[:], func=AF.Copy)) if t_ != 1 else
                     (lambda e, ps=ps, dst=dst: e.tensor_copy(out=fm(dst, c), in_=ps[:])), [ps], [dst[c // 4]])
                j = t_ * 8 + c
                first, last = (c == 0 and t_ == 0), (c == 7 and t_ == 2)
                P.op('pe', lambda e, j=j, dst=dst, first=first, last=last: e.matmul(gi_ps[0:4, :], lhsT=wif_bf[:, 0, j, :], rhs=fm(dst, c), start=first, stop=last),
                     [wif_bf, dst[c // 4]], [gi_ps])
                P.op('pe', lambda e, j=j, dst=dst, first=first, last=last: e.matmul(gf_ps[0:4, :], lhsT=wif_bf[:, 1, j, :], rhs=fm(dst, c), start=first, stop=last),
                     [wif_bf, dst[c // 4]], [gf_ps])
            for t_, (dstl, src_ap, src_tb) in ((1, (ktt, fm(xct, c), xct[c // 4])), (2, (vtt, xmb_ap, xmb))):
                ps = nextps()
                for tb in range(NT):
                    P.op('pe', lambda e, ps=ps, tb=tb, t_=t_, src_ap=src_ap: e.matmul(ps[:, tb * 128:(tb + 1) * 128], lhsT=src_ap[:, tb * 128:(tb + 1) * 128], rhs=wqkv_bf[:, t_, c, :],
                                                                                      start=True, stop=True), [wqkv_bf, src_tb], [ps])
                for half in range(2):
                    dv = bfv(dstl[half]).rearrange("p (b d) -> p b d", b=2)[:, :, c * 128:(c + 1) * 128]
                    P.op('act' if half == 0 else 'dve',
                         (lambda e, ps=ps, dv=dv, half=half: e.activation(out=dv, in_=ps[:, half * 256:(half + 1) * 256].rearrange("p (b d) -> p b d", b=2), func=AF.Copy)) if half == 0 else
                         (lambda e, ps=ps, dv=dv, half=half: e.tensor_copy(out=dv, in_=ps[:, half * 256:(half + 1) * 256].rearrange("p (b d) -> p b d", b=2))),
                         [ps], [dstl[half]])
        rg_, re2, rwi, rem, rli, rFp, rag, rMx, rtmp = G[11], G[12], G[13], G[14], G[15], G[16], G[17], G[18], G[19]
        R4 = slice(0, 4)
        P.op('act', lambda e: e.activation(out=rli[R4, :], in_=gi_ps[R4, :], func=AF.Identity, bias=pc[R4, PC['bif_i']:PC['bif_i'] + 1]), [gi_ps, pc], [rli])
        P.op('act', lambda e: e.activation(out=rtmp[R4, :], in_=gf_ps[R4, :], func=AF.Exp, scale=-1.0, bias=pd[R4, PD['negbf']:PD['negbf'] + 1]), [gf_ps, pd], [rtmp])
        P.op('act', lambda e: e.activation(out=rtmp[R4, :], in_=rtmp[R4, :], func=AF.Ln, bias=1.0), [rtmp], [rtmp])
        P.op('dve', lambda e: e.tensor_tensor_scan(out=rFp[R4, :], data0=onesf[R4, :], data1=rtmp[R4, :], initial=mcar[:, 0:1], op0=MUL, op1=ADD),
             [onesf, rtmp, mcar], [rFp])
        P.op('pool', lambda e: e.tensor_tensor(out=rag[R4, :], in0=rli[R4, :], in1=rFp[R4, :], op=ADD), [rli, rFp], [rag])
        P.op('dve', lambda e: e.tensor_tensor_scan(out=rMx[R4, :], data0=rag[R4, :], data1=rag[R4, :], initial=mcar[:, 1:2], op0=MAX, op1=MAX),
             [rag, mcar], [rMx])
        P.op('pool', lambda e: e.tensor_copy(out=MxE[:, 0:1], in_=mcar[:, 1:2]), [mcar], [MxE])
        P.op('pool', lambda e: e.tensor_copy(out=MxE[:, 1:5], in_=rMx[R4, 127:TT:128]), [rMx], [MxE])
        P.op('pool', lambda e: e.tensor_copy(out=mcar[:, 0:1], in_=rFp[R4, TT - 1:TT]), [rFp], [mcar])
        P.op('pool', lambda e: e.tensor_copy(out=mcar[:, 1:2], in_=rMx[R4, TT - 1:TT]), [rMx], [mcar])
        v3 = lambda t_: t_[R4, :].rearrange("p (q t) -> p q t", q=NT)
        mend = MxE[:, 1:5].unsqueeze(2).broadcast_to([4, NT, 128])
        mprev = MxE[:, 0:4].unsqueeze(2).broadcast_to([4, NT, 128])
        P.op('dve', lambda e: e.tensor_tensor(out=v3(rg_), in0=v3(rag), in1=mend, op=SUB), [rag, MxE], [rg_])
        P.op('act', lambda e: e.activation(out=rg_[R4, :], in_=rg_[R4, :], func=AF.Exp), [rg_], [rg_])
        P.op('dve', lambda e: e.tensor_tensor(out=v3(re2), in0=mend, in1=v3(rMx), op=SUB), [rMx, MxE], [re2])
        P.op('act', lambda e: e.activation(out=re2[R4, :], in_=re2[R4, :], func=AF.Exp), [re2], [re2])
        P.op('dve', lambda e: e.tensor_tensor(out=v3(rwi), in0=mprev, in1=v3(rMx), op=SUB), [rMx, MxE], [rwi])
        P.op('act', lambda e: e.activation(out=rwi[R4, :], in_=rwi[R4, :], func=AF.Exp), [rwi], [rwi])
        P.op('dve', lambda e: e.tensor_tensor(out=rem[R4, :], in0=rFp[R4, :], in1=rMx[R4, :], op=SUB), [rFp, rMx], [rem])
        P.op('act', lambda e: e.activation(out=rem[R4, :], in_=rem[R4, :], func=AF.Exp), [rem], [rem])
        P.op('dve', lambda e: e.tensor_tensor(out=dec[:], in0=MxE[:, 0:4], in1=MxE[:, 1:5], op=SUB), [MxE], [dec])
        P.op('act', lambda e: e.activation(out=dec[:], in_=dec[:], func=AF.Exp), [dec], [dec])
        sp_ = PS[5]
        for q in range(NT):
            P.op('pe', lambda e, q=q: e.matmul(sp_[:, q * 4:(q + 1) * 4], lhsT=rg_[R4, q * 128:(q + 1) * 128], rhs=ident[0:4, 0:4], start=True, stop=True),
                 [rg_, ident], [sp_])
        for h in range(4):
            P.op('pe', lambda e, h=h: e.matmul(sp_[:, 16 + h * 4:16 + (h + 1) * 4], lhsT=sel[:, h, :], rhs=dec[:], start=True, stop=True), [sel, dec], [sp_])
        P.op('act', lambda e: e.activation(out=smallb[:], in_=sp_[:, 0:32], func=AF.Copy), [sp_], [smallb])
        for h in range(4):
            mlstm_head(h, qTt, kTt, ktt, vtt, xct, fm, tokv, (rg_, re2, rwi, rem))

    def mlstm_head(h, qTt, kTt, ktt, vtt, xct, fm, tokv, rows):
        bc = G[0:4]
        for i in range(4):
            ps = nextps()
            P.op('pe', lambda e, ps=ps, i=i: e.matmul(ps[:], lhsT=sel[:, h, :], rhs=rows[i][0:4, :], start=True, stop=True), [sel, rows[i]], [ps])
            P.op('act' if i % 2 == 0 else 'dve',
                 (lambda e, ps=ps, i=i: e.activation(out=bc[i][:], in_=ps[:], func=AF.Copy)) if i % 2 == 0 else
                 (lambda e, ps=ps, i=i: e.tensor_copy(out=bc[i][:], in_=ps[:])), [ps], [bc[i]])
        g_bc, e2_bc, wi_bc, em_bc = bc
        h32 = [G[4], G[5]]
        gms = [G[6], G[7]]
        for vc in range(2):
            ps = proj_x('w0in', 41 + 2 * h + vc)
            P.op('act', lambda e, ps=ps, vc=vc: e.activation(out=gms[vc][:], in_=ps[:], func=AF.Silu), [ps], [gms[vc]])
        qt_ = qTt[h // 2]
        kt_ = kTt[h // 2]
        qv = bfv(qt_).rearrange("p (k t) -> p k t", k=4)[:, 2 * (h % 2):2 * (h % 2) + 2, :]
        kv = bfv(kt_).rearrange("p (k t) -> p k t", k=4)[:, 2 * (h % 2):2 * (h % 2) + 2, :]
        CB, SB_, NB = PS[6], PS[7], PS[5]
        for q in range(NT):
            ts_ = slice(q * 128, (q + 1) * 128)
            bcast = lambda t_: t_[:, ts_].unsqueeze(1).broadcast_to([128, 2, 128])
            P.op('dve', lambda e: e.tensor_tensor(out=qe[:], in0=qv[:, :, ts_], in1=bcast(e2_bc), op=MUL), [qt_, e2_bc], [qe])
            P.op('pool', lambda e: e.tensor_tensor(out=qw[:], in0=qv[:, :, ts_], in1=bcast(wi_bc), op=MUL), [qt_, wi_bc], [qw])
            P.op('dve', lambda e: e.scalar_tensor_tensor(out=kg[:], in0=kv[:, :, ts_], scalar=0.0625, in1=bcast(g_bc), op0=MUL, op1=MUL), [kt_, g_bc], [kg])
            ktok = tokv(ktt, q)[:, h * 256:(h + 1) * 256]
            vtok = tokv(vtt, q)[:, h * 256:(h + 1) * 256]
            P.op('pool', lambda e: e.tensor_scalar(out=kgt[:], in0=ktok, scalar1=smallb[:, q * 4 + h:q * 4 + h + 1], scalar2=0.0625, op0=MUL, op1=MUL),
                 [ktt[q // 2], smallb], [kgt])
            for kc in range(2):
                P.op('pe', lambda e, kc=kc: e.matmul(SB_[:, kc * 256:(kc + 1) * 256], lhsT=kgt[:, kc * 128:(kc + 1) * 128], rhs=vtok, start=True, stop=True),
                     [kgt, vtt[q // 2]], [SB_])
                P.op('pe', lambda e, kc=kc: e.matmul(NB[:, 64 + kc * 128:64 + (kc + 1) * 128], lhsT=kgt[:, kc * 128:(kc + 1) * 128], rhs=ones_bf[:], start=True, stop=True),
                     [kgt, ones_bf], [NB])
            for kc in range(2):
                P.op('pe', lambda e, kc=kc: e.matmul(CB[:, 0:128], lhsT=kg[:, kc, :], rhs=qe[:, kc, :], start=(kc == 0), stop=(kc == 1)), [kg, qe], [CB])
            P.op('dve', lambda e: e.tensor_tensor(out=st_bf[:], in0=CB[:, 0:128], in1=causal[:], op=MUL), [CB, causal], [st_bf])
            for vc in range(2):
                o_ = CB[:, 128 + vc * 128:256 + vc * 128]
                P.op('pe', lambda e, o_=o_, vc=vc: e.matmul(o_, lhsT=vtok[:, vc * 128:(vc + 1) * 128], rhs=st_bf[:], start=True, stop=False), [vtt[q // 2], st_bf], [CB])
                for kc in range(2):
                    P.op('pe', lambda e, o_=o_, vc=vc, kc=kc: e.matmul(o_, lhsT=CTbf[:, h, kc, vc * 128:(vc + 1) * 128], rhs=qw[:, kc, :], start=False, stop=(kc == 1)),
                         [CTbf, qw], [CB])
            o_ = CB[:, 384:512]
            P.op('pe', lambda e, o_=o_: e.matmul(o_, lhsT=ones_bf[:], rhs=st_bf[:], start=True, stop=False), [ones_bf, st_bf], [CB])
            for kc in range(2):
                P.op('pe', lambda e, o_=o_, kc=kc: e.matmul(o_, lhsT=nrbf[:, h, kc, :], rhs=qw[:, kc, :], start=False, stop=(kc == 1)), [nrbf, qw], [CB])
            dcol = smallb[:, 16 + h * 4 + q:16 + h * 4 + q + 1]
            P.op('dve', lambda e: e.scalar_tensor_tensor(out=CT32[:, h], in0=CT32[:, h], scalar=dcol, in1=SB_[:].rearrange("p (k v) -> p k v", k=2), op0=MUL, op1=ADD),
                 [CT32, smallb, SB_], [CT32])
            P.op('act', lambda e: e.activation(out=CTbf[:, h], in_=CT32[:, h], func=AF.Copy), [CT32], [CTbf])
            P.op('dve', lambda e: e.scalar_tensor_tensor(out=nr32[:, h], in0=nr32[:, h], scalar=dcol, in1=NB[:, 64:320].rearrange("p (k v) -> p k v", k=2), op0=MUL, op1=ADD),
                 [nr32, smallb, NB], [nr32])
            P.op('pool', lambda e: e.tensor_copy(out=nrbf[:, h], in_=nr32[:, h]), [nr32], [nrbf])
            P.op('act', lambda e: e.activation(out=ddm[:], in_=CB[:, 384:512], func=AF.Abs), [CB], [ddm])
            P.op('dve', lambda e: e.tensor_tensor(out=ddm[:], in0=ddm[:], in1=em_bc[:, ts_], op=MAX), [ddm, em_bc], [ddm])
            P.op('dve', lambda e: e.tensor_scalar(out=ddm[:], in0=ddm[:], scalar1=1e-6, scalar2=None, op0=ADD), [ddm], [ddm])
            P.op('dve', lambda e: e.reciprocal(out=recm[:], in_=ddm[:]), [ddm], [recm])
            for vc in range(2):
                P.op('dve', lambda e, vc=vc: e.tensor_tensor(out=h32[vc][:, ts_], in0=CB[:, 128 + vc * 128:256 + vc * 128], in1=recm[:], op=MUL), [CB, recm], [h32[vc]])
        dump('h_m%d' % h, h32[0], h32[0][:])
        hr = [G[8], G[9]]
        ps = nextps()
        for vc in range(2):
            P.op('act', lambda e, vc=vc: e.activation(out=RV(hr[vc]), in_=h32[vc][:], func=AF.Copy), [h32[vc]], [hr[vc]])
            P.op('pe', lambda e, ps=ps, vc=vc: e.matmul(ps[:], lhsT=ones_r[:], rhs=RV(hr[vc]), start=(vc == 0), stop=(vc == 1)), [ones_r, hr[vc]], [ps])
        for vc in range(2):
            P.op('dve', lambda e, ps=ps, vc=vc: e.scalar_tensor_tensor(out=h32[vc][:], in0=ps[:], scalar=-1.0 / 256, in1=h32[vc][:], op0=MUL, op1=ADD), [ps, h32[vc]], [h32[vc]])
        ps2 = nextps()
        for vc in range(2):
            P.op('act', lambda e, vc=vc: e.activation(out=RV(hr[vc]), in_=h32[vc][:], func=AF.Square), [h32[vc]], [hr[vc]])
            P.op('pe', lambda e, ps2=ps2, vc=vc: e.matmul(ps2[:], lhsT=ones_r[:], rhs=RV(hr[vc]), start=(vc == 0), stop=(vc == 1)), [ones_r, hr[vc]], [ps2])
        rs = G[10]
        P.op('act', lambda e, ps2=ps2: e.activation(out=rs[:], in_=ps2[:], func=AF.Ln, scale=1.0 / 256, bias=1e-5), [ps2], [rs])
        P.op('act', lambda e: e.activation(out=rs[:], in_=rs[:], func=AF.Exp, scale=-0.5), [rs], [rs])
        for vc in range(2):
            ch = 2 * h + vc
            P.op('dve', lambda e, vc=vc: e.tensor_tensor(out=h32[vc][:], in0=h32[vc][:], in1=rs[:], op=MUL), [h32[vc], rs], [h32[vc]])
            P.op('act', lambda e, vc=vc, ch=ch: e.activation(out=h32[vc][:], in_=h32[vc][:], func=AF.Identity, scale=pcc('mnorm', ch)), [h32[vc], pc], [h32[vc]])
            P.op('dve', lambda e, vc=vc, ch=ch: e.scalar_tensor_tensor(out=h32[vc][:], in0=fm(xct, ch), scalar=pcc('mskip', ch), in1=h32[vc][:], op0=MUL, op1=ADD),
                 [xct[ch // 4], pc, h32[vc]], [h32[vc]])
            P.op('pool', lambda e, vc=vc, ch=ch: e.tensor_tensor(out=merged[:, 8 + ch, :], in0=h32[vc][:], in1=gms[vc][:], op=MUL), [h32[vc], gms[vc]], [merged])

    def layer0(t0):
        rmsnorm_x('mixn0')
        pstate['set'] = [0, 1]
        if do_rwkv:
            rwkv(t0)
        if do_mlstm:
            mlstm(t0)
        pstate['set'] = list(range(8))
        out_proj('w0out')

    if do_l0 and not (do_rwkv and do_mlstm):
        P.op('pool', lambda e: e.memset(merged[:], 0.0), [], [merged])

    for ti in range(ntiles):
        t0 = ti * TT
        for tb in range(NT):
            P.dma('sp', H[tb][:], x_d.ap()[t0 + tb * 128:t0 + (tb + 1) * 128, :], writes=[H[tb]])
        for kc in range(8):
            ps = nextps()
            for tb in range(NT):
                P.op('pe', lambda e, kc=kc, tb=tb, ps=ps: e.transpose(ps[:, tb * 128:(tb + 1) * 128], H[tb][:, kc * 128:(kc + 1) * 128], ident[:]),
                     [H[tb], ident], [ps])
            P.op('act' if kc % 2 == 0 else 'dve',
                 (lambda e, kc=kc, ps=ps: e.activation(out=hT[:, kc, :], in_=ps[:], func=AF.Copy)) if kc % 2 == 0 else
                 (lambda e, kc=kc, ps=ps: e.tensor_copy(out=hT[:, kc, :], in_=ps[:])), [ps], [hT])
        if do_l0:
            layer0(t0)
            ple(0, t0)
        if do_l1:
            layer1(t0)
            ple(1, t0)
        norm_stats()
        for k in range(8):
            of = G[k % 4]
            norm_apply(k, 'finn', of[:], of)
            ps = nextps()
            for tb in range(NT):
                P.op('pe', lambda e, tb=tb, ps=ps, of=of: e.transpose(ps[:, tb * 128:(tb + 1) * 128], of[:, tb * 128:(tb + 1) * 128], ident[:]),
                     [of, ident], [ps])
            for tb in range(NT):
                P.op('act' if k % 2 == 0 else 'dve',
                     (lambda e, k=k, ps=ps, tb=tb: e.activation(out=H[tb][:, k * 128:(k + 1) * 128], in_=ps[:, tb * 128:(tb + 1) * 128], func=AF.Copy)) if k % 2 == 0 else
                     (lambda e, k=k, ps=ps, tb=tb: e.tensor_copy(out=H[tb][:, k * 128:(k + 1) * 128], in_=ps[:, tb * 128:(tb + 1) * 128])),
                     [ps], [H[tb]])
        for tb in range(NT):
            P.dma('sp', o_d.ap()[t0 + tb * 128:t0 + (tb + 1) * 128, :], H[tb][:], reads=[H[tb]])
    P.finish()
    print("sbuf bytes/partition:", P.sbytes, "sems:", P.nsem, "instr:", {e: P.cnt[e] for e in ENGS})
    return nc


def kernel(**inputs):
    x = np.asarray(inputs['x'], np.float32)
    p = np.asarray(inputs['p'], np.float32)
    B, S, _ = x.shape
    shared = host_prep(inputs)
    nc = build_program(S)
    in_maps = []
    for b in range(B):
        m = dict(shared)
        m['x'] = np.ascontiguousarray(x[b])
        m['p'] = np.ascontiguousarray(p[:, b])
        in_maps.append(m)
    res = run_bass_kernel_spmd(nc, in_maps, core_ids=list(range(B)))
    return np.stack([np.asarray(r['out'], np.float32) for r in res.results], axis=0)
```

```python
import numpy as np
import concourse.bass as bass
import concourse.mybir as mybir
from concourse.bass_utils import run_bass_kernel_spmd
from contextlib import ExitStack

F32 = mybir.dt.float32
BF16 = mybir.dt.bfloat16
F32R = mybir.dt.float32r
SDT = F32R
AF = mybir.ActivationFunctionType
ALU = mybir.AluOpType
AX = mybir.AxisListType

ENGS = ('pe', 'act', 'dve', 'pool', 'sp')
EIDX = {e: i for i, e in enumerate(ENGS)}
EPOCH = 30000

D = 1024
TT = 512
NT = TT // 128
PLE = 256
DECAY_SCALE = 0.6065306597126334


class TB:
    __slots__ = ('name', 'h', 'lw', 'rd', 'dkey', 'root', 'psum')

    def __init__(self, name, h, root=None, psum=False):
        self.name = name
        self.h = h
        self.lw = None
        self.rd = {}
        self.dkey = None
        self.root = root if root is not None else self
        self.psum = psum or (root is not None and root.psum)

    def __getitem__(self, k):
        return self.h[k]


class _Rec:
    def __init__(self):
        self.call = None

    def __getattr__(self, name):
        def f(*a, **k):
            self.call = (name, a, k)
        return f


class Prog:
    def __init__(self, nc):
        self.nc = nc
        self.es = ExitStack()
        self.ops = {e: [] for e in ENGS}
        self.cnt = {e: 0 for e in ENGS}
        self.seen = {e: {} for e in ENGS}
        self.clk = {e: [] for e in ENGS}
        self.dcnt = {}
        self.dclk = {}
        self.sems = {}
        self.nsem = 0
        self.ndma = 0
        self.sbytes = 0

    def sb(self, name, shape, dt=F32):
        n = 1
        for s in shape[1:]:
            n *= s
        self.sbytes += n * (2 if dt == BF16 else 4)
        return TB(name, self.es.enter_context(self.nc.sbuf_tensor(name, list(shape), dt)))

    def ps(self, name, shape, dt=F32):
        return TB(name, self.es.enter_context(self.nc.psum_tensor(name, list(shape), dt)), psum=True)

    def dram(self, name, shape, dt, kind):
        return TB(name, self.nc.dram_tensor(name, list(shape), dt, kind=kind))

    def _sem(self, key):
        s = self.sems.get(key)
        if s is None:
            s = self.es.enter_context(self.nc.semaphore("s%d" % self.nsem))
            self.nsem += 1
            self.sems[key] = s
        return s

    def _semval(self, key, count):
        if key in EIDX:
            ep = (count - 1) // EPOCH
            return self._sem((key, ep)), count - ep * EPOCH
        return self._sem(key), 16 * count

    def _deps(self, e, reads, writes):
        seen = self.seen[e]
        need = {}

        def req(ev):
            if ev is None:
                return
            k, c = ev
            if k == 'pe' and e == 'pe':
                return
            if seen.get(k, 0) >= c:
                return
            if need.get(k, 0) < c:
                need[k] = c
        for t in reads:
            req(t.lw)
        for t in writes:
            req(t.lw)
            for k, c in t.rd.items():
                req((k, c))
        keys = list(need.keys())
        for k in keys:
            if k not in need:
                continue
            c = need[k]
            ck = self.clk[k][c - 1] if k in EIDX else self.dclk[(k, c)]
            for k2 in keys:
                if k2 != k and k2 in EIDX and k2 in need and ck[EIDX[k2]] >= need[k2]:
                    del need[k2]
        waits = []
        for k, c in need.items():
            waits.append(self._semval(k, c))
            ck = self.clk[k][c - 1] if k in EIDX else self.dclk[(k, c)]
            for e2, v in zip(ENGS, ck):
                if seen.get(e2, 0) < v:
                    seen[e2] = v
            if seen.get(k, 0) < c:
                seen[k] = c
        return waits

    def _snapshot(self, e):
        s = self.seen[e]
        return tuple(s.get(x, 0) for x in ENGS)

    def op(self, e, fn, reads=(), writes=()):
        writes = [t.root for t in writes] + [t.root for t in reads if t.psum]
        reads = [t.root for t in reads if not t.psum]
        rec = _Rec()
        fn(rec)
        fn = rec.call
        waits = self._deps(e, reads, writes)
        self.cnt[e] += 1
        c = self.cnt[e]
        snap = list(self._snapshot(e))
        snap[EIDX[e]] = c
        self.clk[e].append(tuple(snap))
        inc = self._semval(e, c)[0]
        self.ops[e].append((waits, fn, inc, 1))
        ev = (e, c)
        for t in reads:
            if t.rd.get(e, 0) < c:
                t.rd[e] = c
        for t in writes:
            t.lw = ev
            t.rd = {}
        return ev

    def dma(self, q, out, in_, reads=(), writes=(), key=None, **kw):
        if key is None:
            t0 = (list(writes) + list(reads))[0]
            if t0.dkey is None:
                t0.dkey = ('dma', self.ndma)
                self.ndma += 1
            key = t0.dkey
        reads = [t.root for t in reads]
        writes = [t.root for t in writes]
        waits = self._deps(q, reads, writes)
        self.dcnt[key] = self.dcnt.get(key, 0) + 1
        c = self.dcnt[key]
        self.dclk[(key, c)] = self._snapshot(q)
        sem = self._sem(key)
        self.ops[q].append((waits, ('dma_start', (), dict(out=out, in_=in_, **kw)), sem, 16))
        ev = (key, c)
        for t in reads:
            if t.rd.get(key, 0) < c:
                t.rd[key] = c
        for t in writes:
            t.lw = ev
            t.rd = {}
        return ev

    def finish(self):
        e = 'sp'
        tail = []
        for key, c in self.dcnt.items():
            tail.append(self._semval(key, c))
        for x in ENGS:
            if x != e and self.cnt[x] > 0:
                tail.append(self._semval(x, self.cnt[x]))
        blk = self.es.enter_context(self.nc.Block())
        engobj = {'pe': blk.tensor, 'act': blk.scalar, 'dve': blk.vector, 'pool': blk.gpsimd, 'sp': blk.sync}

        def make(ename):
            ops = self.ops[ename]

            def body(eng):
                for waits, fn, inc, n in ops:
                    for (s, v) in waits[1:]:
                        eng.wait_ge(s, v)
                    ins = getattr(eng, fn[0])(*fn[1], **fn[2])
                    if waits:
                        ins._wait_ge(waits[0][0], waits[0][1])
                    ins.then_inc(inc, n)
                if ename == e:
                    for (s, v) in tail:
                        eng.wait_ge(s, v)
            return body
        for ename in ENGS:
            if self.ops[ename] or ename == e:
                engobj[ename](make(ename))
        self.es.close()


PC_SPEC = [
    ('mixn0', 8), ('mixn1', 8), ('pen0', 8), ('pen1', 8), ('finn', 8),
    ('mu_r', 8), ('mu_k', 8), ('mu_v', 8), ('mu_l', 1),
    ('w0', 8), ('a0', 8), ('k_k', 8), ('k_a', 8), ('r_k', 8), ('ln_w', 8), ('ln_b', 8),
    ('mcw0', 8), ('mcw1', 8), ('mcw2', 8), ('mcw3', 8), ('mcb', 8), ('mnorm', 8), ('mskip', 8),
    ('bif_i', 1), ('bif_f', 1),
    ('ccw0', 16), ('ccw1', 16), ('ccw2', 16), ('ccw3', 16), ('ccb', 16), ('cbr', 16), ('cbi', 16), ('clam', 16),
]
PC = {}
_o = 0
for _n, _w in PC_SPEC:
    PC[_n] = _o
    _o += _w
NPC = _o
PD_SPEC = [('omu_r', 8), ('omu_k', 8), ('omu_v', 8), ('omu_l', 1), ('negbf', 1), ('csph', 16), ('cneg', 16), ('t0', 16), ('t1', 16)]
PD = {}
_o = 0
for _n, _w in PD_SPEC:
    PD[_n] = _o
    _o += _w
NPD = _o


def _cols(v):
    v = np.asarray(v, np.float32).reshape(-1)
    return np.ascontiguousarray(v.reshape(-1, 128).T)


def host_consts():
    c = {}
    c['ident'] = np.eye(128, dtype=np.float32)
    bd = np.zeros((128, 128), np.float32)
    bd[:64, :64] = 1
    bd[64:, 64:] = 1
    c['bdones'] = bd
    j = np.arange(128)[:, None]
    i = np.arange(128)[None, :]
    su = ((j < i) & ((j // 64) == (i // 64))).astype(np.float32)
    ui = ((j <= i) & ((j // 64) == (i // 64))).astype(np.float32)
    sl = ((j > i) & ((j // 64) == (i // 64))).astype(np.float32)
    on = np.ones((128, 128), np.float32)
    c['maskA'] = np.stack([su, su, ui, ui], axis=1)
    c['maskB'] = sl
    rm = np.ones((128, TT), np.float32)
    rm[:, ::64] = 0
    c['resetm'] = rm
    c['causal'] = (j <= i).astype(np.float32)
    sel = np.zeros((4, 4, 128), np.float32)
    for h in range(4):
        sel[h, h, :] = 1
    c['sel'] = sel
    return c


def host_prep(inp):
    g = {}
    f = lambda a: np.ascontiguousarray(np.asarray(a, np.float32))

    def chunked(w):
        K, M = w.shape
        return f(w.reshape(K // 128, 128, M // 128, 128).transpose(2, 1, 0, 3))
    g['w0in'] = chunked(f(inp['ab_w_in'])[0])
    g['w0out'] = chunked(f(inp['ab_w_out'])[0])
    g['w1in'] = chunked(f(inp['c_w_in'])[0])
    g['w1out'] = chunked(f(inp['c_w_out'])[0])
    g['pegate'] = np.stack([chunked(f(inp['pe_gate'])[i]) for i in range(2)])
    g['peup'] = np.stack([chunked(f(inp['pe_up'])[i]) for i in range(2)])
    wr = f(inp['c_wr'])[0]
    wi = f(inp['c_wi'])[0]
    g['wri'] = f(np.stack([wr, wi], axis=2))
    lup = np.zeros((128, 2, 1024), np.float32)
    lup[:64, 0, :] = f(inp['rwkv_w_up'])[0]
    lup[64:, 1, :] = f(inp['rwkv_a_up'])[0]
    g['lup'] = lup
    wbd = np.zeros((128, 3, 8, 128), np.float32)
    for t, nm in enumerate(['mlstm_wq', 'mlstm_wk', 'mlstm_wv']):
        w = f(inp[nm])[0]
        for c in range(8):
            for gg in range(32):
                wbd[4 * gg:4 * gg + 4, t, c, 4 * gg:4 * gg + 4] = w[c * 32 + gg]
    g['wqkv'] = wbd
    wif = f(inp['mlstm_w_if'])[0]
    g['wif'] = f(wif.reshape(24, 128, 2, 4).transpose(1, 2, 0, 3))
    pc = np.zeros((128, NPC), np.float32)

    def put(name, v):
        cc = _cols(v)
        pc[:, PC[name]:PC[name] + cc.shape[1]] = cc
    put('mixn0', f(inp['mix_norm'])[0]); put('mixn1', f(inp['mix_norm'])[1])
    put('pen0', f(inp['pe_norm'])[0]); put('pen1', f(inp['pe_norm'])[1])
    put('finn', f(inp['final_norm']))
    mu = f(inp['rwkv_mu'])[0]
    put('mu_r', mu[0]); put('mu_k', mu[1]); put('mu_v', mu[2])
    ml = f(inp['rwkv_mu_lora'])[0]
    put('mu_l', np.concatenate([ml[0], ml[1]]))
    put('w0', f(inp['rwkv_w0'])[0]); put('a0', f(inp['rwkv_a0'])[0])
    put('k_k', f(inp['rwkv_k_k'])[0]); put('k_a', f(inp['rwkv_k_a'])[0])
    put('r_k', f(inp['rwkv_r_k'])[0].reshape(-1))
    put('ln_w', f(inp['rwkv_ln_w'])[0]); put('ln_b', f(inp['rwkv_ln_b'])[0])
    mcw = f(inp['mlstm_conv_w'])[0]
    for j in range(4):
        put('mcw%d' % j, mcw[j])
    put('mcb', f(inp['mlstm_conv_b'])[0])
    put('mnorm', f(inp['mlstm_norm'])[0]); put('mskip', f(inp['mlstm_skip'])[0])
    bif = f(inp['mlstm_b_if'])[0]
    pc[0:4, PC['bif_i']] = bif[0:4]
    pc[0:4, PC['bif_f']] = bif[4:8]
    ccw = f(inp['c_conv_w'])[0]
    for j in range(4):
        put('ccw%d' % j, ccw[j])
    put('ccb', f(inp['c_conv_b'])[0]); put('cbr', f(inp['c_br'])[0]); put('cbi', f(inp['c_bi'])[0])
    put('clam', f(inp['c_lambda'])[0])
    g['pc'] = pc
    g.update(host_consts())
    return g


SHARED_SHAPES = {
    'w0in': [49, 128, 8, 128], 'w0out': [8, 128, 16, 128], 'w1in': [32, 128, 8, 128], 'w1out': [8, 128, 16, 128],
    'pegate': [2, 8, 128, 8, 128], 'peup': [2, 8, 128, 2, 128], 'wri': [16, 128, 2, 128], 'lup': [128, 2, 1024],
    'wqkv': [128, 3, 8, 128], 'wif': [128, 2, 24, 4], 'pc': [128, NPC],
    'ident': [128, 128], 'bdones': [128, 128], 'maskA': [128, 4, 128], 'maskB': [128, 128],
    'resetm': [128, TT], 'causal': [128, 128], 'sel': [4, 4, 128],
}


def build_program(S, do_l0=True, do_rwkv=True, do_mlstm=True, do_l1=True, dbg=None):
    nc = bass.Bass("TRN2", target_bir_lowering=False)
    P = Prog(nc)
    ntiles = S // TT
    din = {}
    for k, shp in SHARED_SHAPES.items():
        din[k] = nc.dram_tensor(k, shp, F32, kind="ExternalInput")
    x_d = nc.dram_tensor("x", [S, D], F32, kind="ExternalInput")
    p_d = nc.dram_tensor("p", [2, S, PLE], F32, kind="ExternalInput")
    o_d = nc.dram_tensor("out", [S, D], F32, kind="ExternalOutput")
    dbg_d = None
    dbg = dbg or []
    if dbg:
        dbg_d = nc.dram_tensor("dbg", [len(dbg), 128, TT], F32, kind="ExternalOutput")

    def dump(name, tb, ap):
        if name in dbg:
            P.dma('sp', dbg_d.ap()[dbg.index(name)], ap, reads=[tb], key=('dma', 'dbg'))

    MUL, ADD, SUB, MAX = ALU.mult, ALU.add, ALU.subtract, ALU.max

    def cload(name, dt=F32, rows=128):
        shp = SHARED_SHAPES[name]
        t = P.sb("c_" + name, shp, dt)
        P.dma('pool' if dt != F32 else 'sp', t[:], din[name].ap(), writes=[t])
        return t
    ident = cload('ident')
    pc = cload('pc')
    pd = P.sb("pd", [128, NPD])
    ones_bf = P.sb("ones_bf", [128, 128], BF16)
    P.op('pool', lambda e: e.memset(ones_bf[:], 1.0), [], [ones_bf])

    def pcc(name, c=0, n=1):
        return pc[:, PC[name] + c:PC[name] + c + n]

    def pdc(name, c=0, n=1):
        return pd[:, PD[name] + c:PD[name] + c + n]

    if do_l1:
        P.op('act', lambda e: e.activation(out=pdc('t0', 0, 16), in_=pcc('clam', 0, 16), func=AF.Exp, scale=-1.0), [pc], [pd])
        P.op('act', lambda e: e.activation(out=pdc('t1', 0, 16), in_=pdc('t0', 0, 16), func=AF.Ln, bias=1.0), [pd], [pd])
        P.op('dve', lambda e: e.tensor_scalar(out=pdc('csph', 0, 16), in0=pdc('t1', 0, 16), scalar1=4.0, scalar2=None, op0=MUL), [pd], [pd])
        P.op('dve', lambda e: e.tensor_scalar(out=pdc('cneg', 0, 16), in0=pdc('t1', 0, 16), scalar1=-8.0, scalar2=None, op0=MUL), [pd], [pd])
    if do_l0:
        for nm, w in (('r', 8), ('k', 8), ('v', 8), ('l', 1)):
            P.op('dve', lambda e, nm=nm, w=w: e.tensor_scalar(out=pdc('omu_' + nm, 0, w), in0=pcc('mu_' + nm, 0, w), scalar1=-1.0, scalar2=1.0, op0=MUL, op1=ADD),
                 [pc], [pd])
        P.op('dve', lambda e: e.tensor_scalar(out=pdc('negbf'), in0=pcc('bif_f'), scalar1=-1.0, scalar2=None, op0=MUL), [pc], [pd])

    scr = {}

    def mkscr(name, src_ap, shape, group=8):
        t = P.dram("scr_" + name, shape, BF16, "Internal")
        nm = shape[0]
        for j0 in range(0, nm, group):
            j1 = min(nm, j0 + group)
            P.dma('pool', t[j0:j1].rearrange("j p k m -> p j k m"), src_ap[j0:j1].rearrange("j p k m -> p j k m"),
                  writes=[t], key=('dma', 'scr_' + name))
        scr[name] = t
        return t
    if do_l0:
        lup_bf = cload('lup', BF16)
        wqkv_bf = cload('wqkv', BF16)
        wif_bf = cload('wif', BF16)
        mkscr('w0in', din['w0in'].ap(), [49, 128, 8, 128])
        mkscr('w0out', din['w0out'].ap(), [8, 128, 16, 128], group=4)
    for i in range(2):
        if (i == 0 and do_l0) or (i == 1 and do_l1):
            mkscr('pegate%d' % i, din['pegate'].ap()[i], [8, 128, 8, 128])
            mkscr('peup%d' % i, din['peup'].ap()[i], [8, 128, 2, 128])
    if do_l1:
        mkscr('w1in', din['w1in'].ap(), [32, 128, 8, 128])
        mkscr('wri', din['wri'].ap(), [16, 128, 2, 128], group=16)
        mkscr('w1out', din['w1out'].ap(), [8, 128, 16, 128], group=4)

    NWB = 4
    wring = [P.sb("wb%d" % i, [128, 16, 128], BF16) for i in range(NWB)]
    wstate = {'i': 0}

    def wld(name, j, nk):
        wb = wring[wstate['i'] % NWB]
        wstate['i'] += 1
        P.dma('sp', wb[:, 0:nk, :], scr[name][j], reads=[scr[name]], writes=[wb])
        return wb

    PS = [P.ps("ps%d" % i, [128, TT]) for i in range(8)]
    pstate = {'i': 0}

    pstate['set'] = list(range(8))

    def nextps():
        s_ = pstate['set']
        b = PS[s_[pstate['i'] % len(s_)]]
        pstate['i'] += 1
        return b

    hT = P.sb("hT", [128, 8, TT])
    xnT = P.sb("xnT", [128, 8, TT], BF16)
    merged = P.sb("merged", [128, 16, TT], BF16)
    ext = P.sb("ext", [128, TT + 3])
    exts = [ext, P.sb("ext2", [128, TT + 3])] if (do_l0 and do_mlstm) else [ext, ext]
    NG, NH = 20, 12
    G = [P.sb("g%d" % i, [128, TT]) for i in range(NG)]
    H = [P.sb("h%d" % i, [128, 2 * TT]) for i in range(NH)]
    lnv, rstd = G[19], G[18]
    _alias = {}

    def RV(tb):
        if SDT == F32:
            return tb[:]
        a = _alias.get(tb.name)
        if a is None:
            ml = nc.lookup_mloc(tb.h)
            a = nc.alloc_sbuf_tensor_at(tb.name + "_r", [128, int(ml.dims[1]) // 4], F32R, offset=int(ml.addr))
            _alias[tb.name] = a
        return a[:]
    ntmp = [G[16], G[17]]

    def bfv(tb, n=None):
        v = tb[:].bitcast(BF16)
        return v

    if do_l1:
        l1_tail = P.sb("l1_tail", [128, 16, 3])
        l1_h = P.sb("l1_h", [128, 16])
        P.op('pool', lambda e: e.memset(l1_tail[:], 0.0), [], [l1_tail])
        P.op('pool', lambda e: e.memset(l1_h[:], 0.0), [], [l1_h])

    def norm_apply(k, gname, out_ap, out_tb):
        if k % 2 == 0:
            P.op('dve', lambda e: e.scalar_tensor_tensor(out=out_ap, in0=hT[:, k, :], scalar=pcc(gname, k), in1=rstd[:],
                                                         op0=MUL, op1=MUL), [hT, pc, rstd], [out_tb])
        else:
            nt = ntmp[(k // 2) % 2]
            P.op('act', lambda e: e.activation(out=nt[:], in_=hT[:, k, :], func=AF.Identity, scale=pcc(gname, k)), [hT, pc], [nt])
            P.op('pool', lambda e: e.tensor_tensor(out=out_ap, in0=nt[:], in1=rstd[:], op=MUL), [nt, rstd], [out_tb])

    def norm_stats():
        ps = nextps()
        for half in range(2):
            sq = H[4 + half]
            sqv = bfv(sq).rearrange("p (k t) -> p k t", k=4)
            P.op('act', lambda e, half=half, sqv=sqv: e.activation(out=sqv, in_=hT[:, 4 * half:4 * half + 4, :], func=AF.Square), [hT], [sq])
            for k in range(4):
                P.op('pe', lambda e, k=k, half=half, sqv=sqv: e.matmul(ps[:], lhsT=ones_bf[:], rhs=sqv[:, k, :], start=(half == 0 and k == 0), stop=(half == 1 and k == 3)),
                     [ones_bf, sq], [ps])
        P.op('act', lambda e: e.activation(out=lnv[:], in_=ps[:], func=AF.Ln, scale=1.0 / D, bias=1e-6), [ps], [lnv])
        P.op('act', lambda e: e.activation(out=rstd[:], in_=lnv[:], func=AF.Exp, scale=-0.5), [lnv], [rstd])

    def rmsnorm_x(gname):
        norm_stats()
        for k in range(8):
            norm_apply(k, gname, xnT[:, k, :], xnT)

    def proj(ps, wb, nk, rhs_tb, rhs_fn):
        for k in range(nk):
            P.op('pe', lambda e, k=k: e.matmul(ps[:], lhsT=wb[:, k, :], rhs=rhs_fn(k), start=(k == 0), stop=(k == nk - 1)),
                 [wb, rhs_tb], [ps])

    def proj_x(name, j):
        w = wld(name, j, 8)
        ps = nextps()
        proj(ps, w, 8, xnT, lambda k: xnT[:, k, :])
        return ps

    def ple(i, t0):
        ptok = H[0]
        ptv = ptok[:].rearrange("p (tb d) -> p tb d", tb=NT)
        pTt = G[12]
        pT = bfv(pTt).rearrange("p (k t) -> p k t", k=2)
        sgs, tmps = [G[0], G[1]], [G[2], G[3]]
        P.dma('sp', ptv, p_d.ap()[i, t0:t0 + TT, :].rearrange("(tb p) d -> p tb d", p=128), writes=[ptok])
        for kc in range(2):
            ps = nextps()
            for tb in range(NT):
                P.op('pe', lambda e, kc=kc, tb=tb, ps=ps: e.transpose(ps[:, tb * 128:(tb + 1) * 128], ptv[:, tb, kc * 128:(kc + 1) * 128], ident[:]),
                     [ptok, ident], [ps])
            P.op('act', lambda e, kc=kc, ps=ps: e.activation(out=pT[:, kc, :], in_=ps[:], func=AF.Copy), [ps], [pTt])
        rmsnorm_x('pen%d' % i)
        for m in range(8):
            sg, tmpa = sgs[m % 2], tmps[m % 2]
            wg = wld('pegate%d' % i, m, 8)
            wu = wld('peup%d' % i, m, 2)
            ps = nextps()
            proj(ps, wg, 8, xnT, lambda k: xnT[:, k, :])
            P.op('act', lambda e, ps=ps: e.activation(out=sg[:], in_=ps[:], func=AF.Sigmoid), [ps], [sg])
            ps2 = nextps()
            proj(ps2, wu, 2, pTt, lambda k: pT[:, k, :])
            P.op('dve', lambda e, ps2=ps2: e.tensor_tensor(out=tmpa[:], in0=ps2[:], in1=sg[:], op=MUL), [ps2, sg], [tmpa])
            P.op('pool', lambda e, m=m: e.tensor_tensor(out=hT[:, m, :], in0=hT[:, m, :], in1=tmpa[:], op=ADD), [hT, tmpa], [hT])

    def out_proj(name):
        for m in range(8):
            wo = wld(name, m, 16)
            ps = nextps()
            proj(ps, wo, 16, merged, lambda k: merged[:, k, :])
            P.op('dve', lambda e, m=m, ps=ps: e.tensor_tensor(out=hT[:, m, :], in0=hT[:, m, :], in1=ps[:], op=ADD), [hT, ps], [hT])

    def conv4(ps, c, tail_tb, wname, bname, acc, ext=ext):
        P.op('pool', lambda e: e.tensor_copy(out=ext[:, 0:3], in_=tail_tb[:, c, :]), [tail_tb], [ext])
        P.op('act', lambda e: e.activation(out=ext[:, 3:TT + 3], in_=ps[:], func=AF.Copy), [ps], [ext])
        P.op('act', lambda e: e.activation(out=acc[:], in_=ps[:], func=AF.Identity, scale=pcc(wname + '3', c), bias=pcc(bname, c)), [ps, pc], [acc])
        P.op('pool', lambda e: e.tensor_copy(out=tail_tb[:, c, :], in_=ext[:, TT:TT + 3]), [ext], [tail_tb])
        for j in range(3):
            P.op('dve', lambda e, j=j: e.scalar_tensor_tensor(out=acc[:], in0=ext[:, j:j + TT], scalar=pcc(wname + str(j), c), in1=acc[:],
                                                              op0=MUL, op1=ADD), [ext, pc, acc], [acc])

    def layer1(t0):
        rmsnorm_x('mixn1')
        for g_ in range(4):
            bufs = []
            for k in range(4):
                s0, s1, s2 = H[3 * k], H[3 * k + 1], H[3 * k + 2]
                bufs.append(dict(xc32=(s0, s0[:, 0:TT]), xcb=(s0, s0[:, TT:2 * TT].bitcast(BF16)[:, 0:TT]),
                                 rg=(s1, s1[:, 0:TT]), ig=(s1, s1[:, TT:2 * TT]), a2=(s2, s2[:, 0:TT]), th=(s2, s2[:, TT:2 * TT])))
            cs = [4 * g_ + k for k in range(4)]
            for k, c in enumerate(cs):
                xt, xa = bufs[k]['xc32']
                bt, ba = bufs[k]['xcb']
                ext = exts[k % 2]
                ps = proj_x('w1in', c)
                P.op('pool', lambda e: e.tensor_copy(out=ext[:, 0:3], in_=l1_tail[:, c, :]), [l1_tail], [ext])
                P.op('act', lambda e: e.activation(out=ext[:, 3:TT + 3], in_=ps[:], func=AF.Copy), [ps], [ext])
                P.op('act', lambda e: e.activation(out=xa, in_=ps[:], func=AF.Identity, scale=pcc('ccw3', c), bias=pcc('ccb', c)), [ps, pc], [xt])
                P.op('pool', lambda e: e.tensor_copy(out=l1_tail[:, c, :], in_=ext[:, TT:TT + 3]), [ext], [l1_tail])
                for j in range(3):
                    P.op('dve', lambda e, j=j: e.scalar_tensor_tensor(out=xa, in0=ext[:, j:j + TT], scalar=pcc('ccw%d' % j, c), in1=xa, op0=MUL, op1=ADD), [ext, pc, xt], [xt])
                P.op('act', lambda e: e.activation(out=ba, in_=xa, func=AF.Copy), [xt], [bt])
            for k, c in enumerate(cs):
                bt, ba = bufs[k]['xcb']
                rt, ra = bufs[k]['rg']
                it, ia = bufs[k]['ig']
                wb = wld('wri', c, 2)
                ps_r = nextps()
                P.op('pe', lambda e: e.matmul(ps_r[:], lhsT=wb[:, 0, :], rhs=ba, start=True, stop=True), [wb, bt], [ps_r])
                ps_i = nextps()
                P.op('pe', lambda e: e.matmul(ps_i[:], lhsT=wb[:, 1, :], rhs=ba, start=True, stop=True), [wb, bt], [ps_i])
                P.op('act', lambda e: e.activation(out=ra, in_=ps_r[:], func=AF.Sigmoid, bias=pcc('cbr', c)), [ps_r, pc], [rt])
                P.op('act', lambda e: e.activation(out=ia, in_=ps_i[:], func=AF.Sigmoid, bias=pcc('cbi', c)), [ps_i, pc], [it])
            for k, c in enumerate(cs):
                rt, ra = bufs[k]['rg']
                at, aa_ = bufs[k]['a2']
                P.op('act', lambda e: e.activation(out=aa_, in_=ra, func=AF.Exp, scale=pdc('cneg', c)), [rt, pd], [at])
            for k, c in enumerate(cs):
                rt, ra = bufs[k]['rg']
                tt_, ta = bufs[k]['th']
                P.op('act', lambda e: e.activation(out=ta, in_=ra, func=AF.Tanh, scale=pdc('csph', c)), [rt, pd], [tt_])
            for k, c in enumerate(cs):
                at, aa_ = bufs[k]['a2']
                tt_, ta = bufs[k]['th']
                P.op('dve', lambda e: e.scalar_tensor_tensor(out=ta, in0=aa_, scalar=1.0, in1=ta, op0=ADD, op1=MUL), [at, tt_], [tt_])
                P.op('dve', lambda e: e.tensor_scalar(out=aa_, in0=ta, scalar1=-1.0, scalar2=1.0, op0=MUL, op1=ADD), [tt_], [at])
                P.op('dve', lambda e: e.scalar_tensor_tensor(out=ta, in0=aa_, scalar=1.0, in1=ta, op0=ADD, op1=MUL), [at, tt_], [tt_])
            for k, c in enumerate(cs):
                tt_, ta = bufs[k]['th']
                P.op('act', lambda e: e.activation(out=ta, in_=ta, func=AF.Sqrt), [tt_], [tt_])
            for k, c in enumerate(cs):
                xt, xa = bufs[k]['xc32']
                rt, ra = bufs[k]['rg']
                it, ia = bufs[k]['ig']
                at, aa_ = bufs[k]['a2']
                tt_, ta = bufs[k]['th']
                P.op('pool', lambda e: e.tensor_tensor(out=ia, in0=xa, in1=ia, op=MUL), [xt, it], [it])
                P.op('pool', lambda e: e.tensor_tensor(out=ia, in0=ia, in1=ta, op=MUL), [it, tt_], [it])
                P.op('dve', lambda e: e.tensor_tensor_scan(out=ra, data0=aa_, data1=ia, initial=l1_h[:, c:c + 1], op0=MUL, op1=ADD), [at, it, l1_h], [rt])
                P.op('pool', lambda e: e.tensor_copy(out=l1_h[:, c:c + 1], in_=ra[:, TT - 1:TT]), [rt], [l1_h])
            for k, c in enumerate(cs):
                xt, xa = bufs[k]['xc32']
                rt, ra = bufs[k]['rg']
                ps_g = proj_x('w1in', 16 + c)
                P.op('act', lambda e: e.activation(out=xa, in_=ps_g[:], func=AF.Silu), [ps_g], [xt])
                P.op('dve', lambda e: e.tensor_tensor(out=merged[:, c, :], in0=ra, in1=xa, op=MUL), [rt, xt], [merged])
        out_proj('w1out')

    if do_l0 and do_rwkv:
        identr = P.sb("identr", [128, 128], SDT)
        bdones = cload('bdones')
        bdones_bf = P.sb("bdones_bf", [128, 128], BF16)
        bdones_r = P.sb("bdones_r", [128, 128], SDT)
        maskA = cload('maskA')
        maskB = cload('maskB')
        resetm = cload('resetm')
        P.op('dve', lambda e: e.tensor_copy(out=identr[:], in_=ident[:]), [ident], [identr])
        P.op('dve', lambda e: e.tensor_copy(out=bdones_bf[:], in_=bdones[:]), [bdones], [bdones_bf])
        P.op('dve', lambda e: e.tensor_copy(out=bdones_r[:], in_=bdones[:]), [bdones], [bdones_r])
        rwc = P.sb("rwc", [128, 25])
        P.op('pool', lambda e: e.memset(rwc[:], 0.0), [], [rwc])
        lora_bf = P.sb("lora_bf", [128, TT], BF16)
        Sst = [[P.sb("S%d_%d" % (c, q), [128, 128], F32) for q in range(2)] for c in range(8)]
        for c in range(8):
            P.op('pool', lambda e, c=c: e.memset(Sst[c][0][:], 0.0), [], [Sst[c][0]])
        spar = [0] * 8
        rhs_sb = P.sb("rhs_sb", [128, 128], SDT)
        u_sb = P.sb("u_sb", [128, 128], SDT)
        s0g = P.sb("s0g", [128, 128])
        psRHS = TB("psRHS", PS[7].h[:, 0:128], root=PS[7])
        psU = TB("psU", PS[7].h[:, 128:256], root=PS[7])
        psYT = TB("psYT", PS[6].h[:, 0:128], root=PS[6])
        psSN = TB("psSN", PS[7].h[:, 384:512], root=PS[7])

    def tshift(ps, dst, col, mu_ap, omu_ap):
        P.op('act', lambda e: e.activation(out=dst[:], in_=ps[:], func=AF.Identity, scale=omu_ap), [ps, pd], [dst])
        P.op('dve', lambda e: e.scalar_tensor_tensor(out=dst[:, 1:TT], in0=ps[:, 0:TT - 1], scalar=mu_ap, in1=dst[:, 1:TT], op0=MUL, op1=ADD),
             [ps, pc, dst], [dst])
        P.op('dve', lambda e: e.scalar_tensor_tensor(out=dst[:, 0:1], in0=rwc[:, col:col + 1], scalar=mu_ap, in1=dst[:, 0:1], op0=MUL, op1=ADD),
             [rwc, pc, dst], [dst])
        P.op('act', lambda e: e.activation(out=rwc[:, col:col + 1], in_=ps[:, TT - 1:TT], func=AF.Copy), [ps], [rwc])

    def r3(ap, n=8):
        return ap.rearrange("p (n t) -> p n t", n=n)

    def rwkv(t0):
        Rbd, Abd, Bbd, Kbd, Vbd = H[0:5]
        for i in range(5):
            P.op('pool', lambda e, i=i: e.memset(H[i][:], 0.0), [], [H[i]])
        bdv = [r3(RV(H[i])) for i in range(5)]
        ps = proj_x('w0in', 24)
        lora32 = G[16]
        tshift(ps, lora32, 24, pcc('mu_l'), pdc('omu_l'))
        P.op('act', lambda e: e.activation(out=lora_bf[0:64, :], in_=lora32[0:64, :], func=AF.Tanh), [lora32], [lora_bf])
        P.op('act', lambda e: e.activation(out=lora_bf[64:128, :], in_=lora32[64:128, :], func=AF.Copy), [lora32], [lora_bf])
        pend = []

        def pump(n):
            for _ in range(n):
                if pend:
                    pend.pop(0)()
        for th_ in rwkv_prep(0):
            th_()
        rwkv_blockdiag(0, bdv)
        for c in range(8):
            if c + 1 < 8:
                pend.extend(rwkv_prep(c + 1))
            rwkv_chunks(c, bdv, pump)
            pump(len(pend))
            if c + 1 < 8:
                rwkv_blockdiag(c + 1, bdv)
            rwkv_out(c)

    def rw_bufs(c):
        p_ = c % 2
        d = dict(r32=G[0], k32=G[1], v32=G[2], cump=G[6], Ginv=G[8], Gp=G[9], a32=G[10], kk32=G[11], kka=G[11],
                 rn=G[12], nkk=G[13], keff=G[15], t1=G[16])
        d.update(dict(gr32=(G[3], G[14])[p_], lw=(G[4], G[17])[p_], cum=(G[5], G[18])[p_], Gt=(G[7], G[19])[p_]))
        return d

    def rwkv_prep(c):
        B_ = rw_bufs(c)
        r32, k32, v32, gr32, lw, cum, cump, Gt, Ginv, Gp = [B_[k] for k in ('r32', 'k32', 'v32', 'gr32', 'lw', 'cum', 'cump', 'Gt', 'Ginv', 'Gp')]
        a32, kk32, kka, rn, nkk, keff, t1 = [B_[k] for k in ('a32', 'kk32', 'kka', 'rn', 'nkk', 'keff', 't1')]
        bon = cum
        T = []

        def projshift(dst, mch, nm, col):
            def f():
                ps = proj_x('w0in', mch)
                tshift(ps, dst, col, pcc('mu_' + nm, c), pdc('omu_' + nm, c))
            return f
        T.append(projshift(k32, 8 + c, 'k', 8 + c))

        def f_lw():
            ps = nextps()
            P.op('pe', lambda e: e.matmul(ps[:], lhsT=lup_bf[:, 0, c * 128:(c + 1) * 128], rhs=lora_bf[:], start=True, stop=True), [lup_bf, lora_bf], [ps])
            P.op('act', lambda e: e.activation(out=lw[:], in_=ps[:], func=AF.Sigmoid, bias=pcc('w0', c)), [ps, pc], [lw])
        T.append(f_lw)

        def f_a():
            ps = nextps()
            P.op('pe', lambda e: e.matmul(ps[:], lhsT=lup_bf[:, 1, c * 128:(c + 1) * 128], rhs=lora_bf[:], start=True, stop=True), [lup_bf, lora_bf], [ps])
            P.op('act', lambda e: e.activation(out=a32[:], in_=ps[:], func=AF.Sigmoid, bias=pcc('a0', c)), [ps, pc], [a32])
        T.append(f_a)
        T.append(lambda: P.op('dve', lambda e: e.tensor_tensor_scan(out=cum[:], data0=resetm[:], data1=lw[:], initial=0.0, op0=MUL, op1=ADD), [resetm, lw], [cum]))
        T.append(lambda: P.op('dve', lambda e: e.tensor_tensor(out=cump[:], in0=cum[:], in1=lw[:], op=SUB), [cum, lw], [cump]))
        T.append(lambda: P.op('act', lambda e: e.activation(out=Gt[:], in_=cum[:], func=AF.Exp, scale=-DECAY_SCALE), [cum], [Gt]))
        T.append(lambda: P.op('act', lambda e: e.activation(out=Ginv[:], in_=cum[:], func=AF.Exp, scale=DECAY_SCALE), [cum], [Ginv]))
        T.append(lambda: P.op('act', lambda e: e.activation(out=Gp[:], in_=cump[:], func=AF.Exp, scale=-DECAY_SCALE), [cump], [Gp]))
        T.append(lambda: P.op('dve', lambda e: e.tensor_scalar(out=kk32[:], in0=k32[:], scalar1=pcc('k_k', c), scalar2=None, op0=MUL), [k32, pc], [kk32]))
        sqk = bfv(t1)[:, 0:TT]
        T.append(lambda: P.op('act', lambda e: e.activation(out=sqk, in_=kk32[:], func=AF.Square), [kk32], [t1]))

        def f_rn():
            ps = nextps()
            P.op('pe', lambda e: e.matmul(ps[:], lhsT=bdones_bf[:], rhs=sqk, start=True, stop=True), [bdones_bf, t1], [ps])
            P.op('act', lambda e: e.activation(out=rn[:], in_=ps[:], func=AF.Ln, bias=1e-20), [ps], [rn])
            P.op('act', lambda e: e.activation(out=rn[:], in_=rn[:], func=AF.Exp, scale=-0.5), [rn], [rn])
        T.append(f_rn)
        T.append(lambda: P.op('dve', lambda e: e.scalar_tensor_tensor(out=nkk[:], in0=kk32[:], scalar=-1.0, in1=rn[:], op0=MUL, op1=MUL), [kk32, rn], [nkk]))
        T.append(lambda: P.op('dve', lambda e: e.scalar_tensor_tensor(out=kka[:], in0=nkk[:], scalar=-1.0, in1=a32[:], op0=MUL, op1=MUL), [nkk, a32], [kka]))
        T.append(lambda: P.op('pool', lambda e: e.tensor_scalar(out=t1[:], in0=a32[:], scalar1=-1.0, scalar2=pcc('k_a', c), op0=ADD, op1=MUL), [a32, pc], [t1]))
        T.append(lambda: P.op('dve', lambda e: e.scalar_tensor_tensor(out=keff[:], in0=t1[:], scalar=1.0, in1=k32[:], op0=ADD, op1=MUL), [t1, k32], [keff]))
        T.append(projshift(r32, c, 'r', c))
        rkr = bfv(t1)[:, 0:TT]
        T.append(lambda: P.op('dve', lambda e: e.scalar_tensor_tensor(out=rkr, in0=r32[:], scalar=pcc('r_k', c), in1=keff[:], op0=MUL, op1=MUL), [r32, pc, keff], [t1]))
        T.append(projshift(v32, 16 + c, 'v', 16 + c))

        def f_bon():
            ps = nextps()
            P.op('pe', lambda e: e.matmul(ps[:], lhsT=bdones_bf[:], rhs=rkr, start=True, stop=True), [bdones_bf, t1], [ps])
            P.op('dve', lambda e: e.tensor_tensor(out=bon[:], in0=ps[:], in1=v32[:], op=MUL), [ps, v32], [bon])
        T.append(f_bon)

        def f_gr():
            ps = proj_x('w0in', 33 + c)
            P.op('act', lambda e: e.activation(out=gr32[:], in_=ps[:], func=AF.Silu), [ps], [gr32])
        T.append(f_gr)
        return T

    def rwkv_blockdiag(c, bdv):
        B_ = rw_bufs(c)
        Rv, Av, Bv, Kv, Vv = bdv
        Rbd, Abd, Bbd, Kbd, Vbd = H[0:5]
        for hh in range(2):
            hs_ = slice(hh * 64, hh * 64 + 64)
            for i_, (dstv, dtb, a_, b_) in enumerate(((Rv, Rbd, B_['r32'], B_['Gt']), (Av, Abd, B_['nkk'], B_['Gp']), (Bv, Bbd, B_['kka'], B_['Ginv']), (Kv, Kbd, B_['keff'], B_['Ginv']))):
                P.op('dve' if (i_ + hh) % 2 == 0 else 'pool', lambda e, dstv=dstv, a_=a_, b_=b_, hs_=hs_: e.tensor_tensor(out=dstv[hs_, :, hs_], in0=r3(a_[hs_, :]), in1=r3(b_[hs_, :]), op=MUL),
                     [a_, b_], [dtb])
            P.op('act', lambda e, hs_=hs_: e.activation(out=Vv[hs_, :, hs_], in_=r3(B_['v32'][hs_, :]), func=AF.Copy), [B_['v32']], [Vbd])

    def rwkv_chunks(c, bdv, pump):
        B_ = rw_bufs(c)
        Gt, y32 = B_['Gt'], B_['lw']
        Rv, Av, Bv, Kv, Vv = bdv
        Rbd, Abd, Bbd, Kbd, Vbd = H[0:5]
        for hf in range(2):
            SCt = [H[5], H[6]]
            QTt = [H[7], H[8]]
            SCv = [RV(t).rearrange("p (j m k) -> p j m k", j=2, m=4) for t in SCt]
            QTv = [RV(t).rearrange("p (j m k) -> p j m k", j=2, m=4) for t in QTt]
            SCf = [t[:].rearrange("p (j m k) -> p j m k", j=2, m=4) for t in SCt]
            PQt = [H[9], H[10]]
            PQv = [[RV(PQt[b]).rearrange("p (q j m k) -> p q j m k", q=2, j=2, m=2)[:, par] for b in range(2)] for par in range(2)]
            Xt = H[11]
            Xv = [RV(Xt).rearrange("p (q j k) -> p q j k", q=2, j=4)[:, q] for q in range(2)]
            Xf = [Xt[:].rearrange("p (q j k) -> p q j k", q=2, j=4)[:, q] for q in range(2)]
            for j in range(4):
                n = hf * 4 + j
                sc = SCv[j // 2][:, j % 2]
                qt = QTv[j // 2][:, j % 2]
                A_, B2_ = (PS[2], PS[3]) if j % 2 == 0 else (PS[4], PS[5])
                for m, (l_, r_) in enumerate(((Bv, Av), (Kv, Av), (Bv, Rv), (Kv, Rv))):
                    P.op('pe', lambda e, m=m, l_=l_, r_=r_, n=n: e.matmul(A_[:, m * 128:(m + 1) * 128], lhsT=l_[:, n, :], rhs=r_[:, n, :], start=True, stop=True),
                         [Rbd, Abd, Bbd, Kbd], [A_])
                P.op('dve', lambda e, sc=sc: e.tensor_tensor(out=sc, in0=A_[:].rearrange("p (m k) -> p m k", m=4), in1=maskA[:], op=MUL),
                     [A_, maskA], [SCt[j // 2]])
                P.op('pe', lambda e, n=n: e.matmul(B2_[:, 0:128], lhsT=Av[:, n, :], rhs=Bv[:, n, :], start=True, stop=True), [Abd, Bbd], [B2_])
                for m, l_ in enumerate((Bv, Kv, Vv)):
                    P.op('pe', lambda e, m=m, l_=l_, n=n: e.matmul(B2_[:, (m + 1) * 128:(m + 2) * 128], lhsT=l_[:, n, :], rhs=identr[:], start=True, stop=True),
                         [Bbd, Kbd, Vbd, identr], [B2_])
                P.op('dve', lambda e, qt=qt: e.tensor_tensor(out=qt[:, 0, :], in0=B2_[:, 0:128], in1=maskB[:], op=MUL), [B2_, maskB], [QTt[j // 2]])
                P.op('act', lambda e, qt=qt: e.activation(out=qt[:, 1:4, :], in_=B2_[:, 128:512].rearrange("p (m k) -> p m k", m=3), func=AF.Copy),
                     [B2_], [QTt[j // 2]])
                pump(2)
            for j in range(4):
                P.op('dve', lambda e, j=j: e.tensor_tensor(out=Xv[1][:, j, :], in0=SCf[j // 2][:, j % 2, 0, :], in1=ident[:], op=ADD),
                     [SCt[j // 2], ident], [Xt])
            for lvl in range(1, 6):
                par = lvl % 2
                for b in range(2):
                    bank = PS[4 + b]
                    for jj in range(2):
                        j = 2 * b + jj
                        if lvl == 1:
                            Pp = SCv[j // 2][:, j % 2, 0, :]
                            Qp = QTv[j // 2][:, j % 2, 0, :]
                            rd = [SCt[j // 2], QTt[j // 2]]
                        else:
                            Pp = PQv[1 - par][b][:, jj, 0, :]
                            Qp = PQv[1 - par][b][:, jj, 1, :]
                            rd = [PQt[b]]
                        P.op('pe', lambda e, jj=jj, Pp=Pp, Qp=Qp, bank=bank: e.matmul(bank[:, (2 * jj) * 128:(2 * jj + 1) * 128], lhsT=Qp, rhs=Pp, start=True, stop=True), rd, [bank])
                        P.op('pe', lambda e, jj=jj, Pp=Pp, Qp=Qp, bank=bank: e.matmul(bank[:, (2 * jj + 1) * 128:(2 * jj + 2) * 128], lhsT=Pp, rhs=Qp, start=True, stop=True), rd, [bank])
                    if b == 0:
                        P.op('act', lambda e, b=b, bank=bank, par=par: e.activation(out=PQv[par][b], in_=bank[:].rearrange("p (j m k) -> p j m k", j=2, m=2), func=AF.Copy),
                             [bank], [PQt[b]])
                    else:
                        P.op('dve', lambda e, b=b, bank=bank, par=par: e.tensor_copy(out=PQv[par][b], in_=bank[:].rearrange("p (j m k) -> p j m k", j=2, m=2)),
                             [bank], [PQt[b]])
                XB = PS[6]
                for j in range(4):
                    Ql = PQv[par][j // 2][:, j % 2, 1, :]
                    P.op('pe', lambda e, j=j, Ql=Ql, par=par: e.matmul(XB[:, j * 128:(j + 1) * 128], lhsT=Ql, rhs=Xv[par][:, j, :], start=True, stop=True), [PQt[j // 2], Xt], [XB])
                P.op('dve', lambda e, par=par: e.tensor_tensor(out=Xv[1 - par], in0=XB[:].rearrange("p (j k) -> p j k", j=4), in1=Xf[par], op=ADD), [XB, Xt], [Xt])
                pump(1)
            for j in range(4):
                n = hf * 4 + j
                sc = SCv[j // 2][:, j % 2]
                qt = QTv[j // 2][:, j % 2]
                sct, qtt = SCt[j // 2], QTt[j // 2]
                S0 = Sst[c][spar[c]]
                S1 = Sst[c][1 - spar[c]]
                spar[c] = 1 - spar[c]
                P.op('pool', lambda e, n=n, S0=S0: e.tensor_scalar(out=s0g[:], in0=S0[:], scalar1=Gt[:, n * 64 + 63:n * 64 + 64], scalar2=0.0, op0=MUL, op1=ADD), [S0, Gt], [s0g])
                P.op('pe', lambda e, n=n, S0=S0: e.matmul(psRHS[:], lhsT=Av[:, n, :], rhs=RV(S0), start=True, stop=False), [Abd, S0], [psRHS])
                P.op('pe', lambda e, sc=sc, qt=qt: e.matmul(psRHS[:], lhsT=sc[:, 1, :], rhs=qt[:, 3, :], start=False, stop=True), [sct, qtt], [psRHS])
                P.op('dve', lambda e: e.tensor_copy(out=rhs_sb[:], in_=psRHS[:]), [psRHS], [rhs_sb])
                P.op('pe', lambda e, j=j: e.matmul(psU[:], lhsT=Xv[0][:, j, :], rhs=rhs_sb[:], start=True, stop=True), [Xt, rhs_sb], [psU])
                P.op('dve', lambda e: e.tensor_copy(out=u_sb[:], in_=psU[:]), [psU], [u_sb])
                P.op('pe', lambda e, qt=qt: e.matmul(psSN[:], lhsT=qt[:, 1, :], rhs=u_sb[:], start=True, stop=False), [qtt, u_sb], [psSN])
                P.op('pe', lambda e, qt=qt: e.matmul(psSN[:], lhsT=qt[:, 2, :], rhs=qt[:, 3, :], start=False, stop=True), [qtt], [psSN])
                P.op('dve', lambda e, n=n, S1=S1: e.scalar_tensor_tensor(out=RV(S1), in0=psSN[:], scalar=Gt[:, n * 64 + 63:n * 64 + 64], in1=s0g[:], op0=MUL, op1=ADD), [psSN, Gt, s0g], [S1])
                P.op('pe', lambda e, n=n, S0=S0: e.matmul(psYT[:], lhsT=RV(S0), rhs=Rv[:, n, :], start=True, stop=False), [S0, Rbd], [psYT])
                P.op('pe', lambda e, sc=sc: e.matmul(psYT[:], lhsT=u_sb[:], rhs=sc[:, 2, :], start=False, stop=False), [u_sb, sct], [psYT])
                P.op('pe', lambda e, sc=sc, qt=qt: e.matmul(psYT[:], lhsT=qt[:, 3, :], rhs=sc[:, 3, :], start=False, stop=True), [qtt, sct], [psYT])
                for hh in range(2):
                    hs_ = slice(hh * 64, hh * 64 + 64)
                    P.op('act', lambda e, hs_=hs_, n=n: e.activation(out=y32[hs_, n * 64:(n + 1) * 64], in_=psYT[hs_, hs_], func=AF.Copy), [psYT], [y32])
                pump(2)

    def rwkv_out(c):
        B_ = rw_bufs(c)
        y32, dd, bon, gr32 = B_['lw'], B_['Gt'], B_['cum'], B_['gr32']
        dump('y_rwkv%d' % c, y32, y32[:])
        yrv = RV(y32)
        P.op('act', lambda e: e.activation(out=yrv, in_=y32[:], func=AF.Copy), [y32], [y32])
        ps = nextps()
        P.op('pe', lambda e: e.matmul(ps[:], lhsT=bdones_r[:], rhs=yrv, start=True, stop=True), [bdones_r, y32], [ps])
        P.op('dve', lambda e: e.scalar_tensor_tensor(out=dd[:], in0=ps[:], scalar=-1.0 / 64, in1=y32[:], op0=MUL, op1=ADD), [ps, y32], [dd])
        sq, rs = H[5], H[6]
        sqv = RV(sq)[:, 0:TT]
        P.op('act', lambda e: e.activation(out=sqv, in_=dd[:], func=AF.Square), [dd], [sq])
        ps2 = nextps()
        P.op('pe', lambda e: e.matmul(ps2[:], lhsT=bdones_r[:], rhs=sqv, start=True, stop=True), [bdones_r, sq], [ps2])
        P.op('act', lambda e: e.activation(out=rs[:, 0:TT], in_=ps2[:], func=AF.Ln, scale=1.0 / 64, bias=64e-5), [ps2], [rs])
        P.op('act', lambda e: e.activation(out=rs[:, 0:TT], in_=rs[:, 0:TT], func=AF.Exp, scale=-0.5), [rs], [rs])
        P.op('dve', lambda e: e.tensor_tensor(out=dd[:], in0=dd[:], in1=rs[:, 0:TT], op=MUL), [dd, rs], [dd])
        P.op('act', lambda e: e.activation(out=dd[:], in_=dd[:], func=AF.Identity, scale=pcc('ln_w', c), bias=pcc('ln_b', c)), [dd, pc], [dd])
        P.op('pool', lambda e: e.tensor_tensor(out=dd[:], in0=dd[:], in1=bon[:], op=ADD), [dd, bon], [dd])
        P.op('dve', lambda e: e.tensor_tensor(out=merged[:, c, :], in0=dd[:], in1=gr32[:], op=MUL), [dd, gr32], [merged])

    if do_l0 and do_mlstm:
        causal = cload('causal')
        sel = P.sb("c_sel", [4, 4, 128])
        P.dma('sp', sel[:], din['sel'].ap(), writes=[sel])
        onesf = P.sb("onesf", [128, TT])
        P.op('pool', lambda e: e.memset(onesf[:], 1.0), [], [onesf])
        ones_r = P.sb("ones_r", [128, 128], SDT)
        P.op('dve', lambda e: e.tensor_copy(out=ones_r[:], in_=onesf[:, 0:128]), [onesf], [ones_r])
        m_tail = P.sb("m_tail", [128, 8, 3])
        P.op('pool', lambda e: e.memset(m_tail[:], 0.0), [], [m_tail])
        CT32 = P.sb("CT32", [128, 4, 2, 256])
        CTbf = P.sb("CTbf", [128, 4, 2, 256], BF16)
        nr32 = P.sb("nr32", [128, 4, 2, 128])
        nrbf = P.sb("nrbf", [128, 4, 2, 128], BF16)
        for t_ in (CT32, CTbf, nr32, nrbf):
            P.op('pool', lambda e, t_=t_: e.memset(t_[:], 0.0), [], [t_])
        mcar = P.sb("mcar", [4, 2])
        P.op('pool', lambda e: e.memset(mcar[:], 0.0), [], [mcar])
        MxE = P.sb("MxE", [4, 5])
        dec = P.sb("dec", [4, 4])
        smallb = P.sb("smallb", [128, 32])
        qe = P.sb("qe", [128, 2, 128], BF16)
        qw = P.sb("qw", [128, 2, 128], BF16)
        kg = P.sb("kg", [128, 2, 128], BF16)
        kgt = P.sb("kgt", [128, 256], BF16)
        st_bf = P.sb("st_bf", [128, 128], BF16)
        ddm = P.sb("ddm", [128, 128])
        recm = P.sb("recm", [128, 128])

    def mlstm(t0):
        qTt, kTt, vTt = [H[0], H[1]], [H[2], H[3]], [H[4], H[5]]
        ktt, vtt = [H[6], H[7]], [H[8], H[9]]
        xct = [H[10], H[11]]

        def fm(tl, c):
            return bfv(tl[c // 4]).rearrange("p (k t) -> p k t", k=4)[:, c % 4, :]

        def tokv(tl, tb):
            return bfv(tl[tb // 2]).rearrange("p (b d) -> p b d", b=2)[:, tb % 2, :]
        gi_ps, gf_ps = PS[2], PS[3]
        pstate['set'] = [0, 1, 4, 5, 6, 7]
        for c in range(8):
            acc, xmb = (G[9], G[10]) if c % 2 == 0 else (G[0], G[1])
            xmb_ap = bfv(xmb)[:, 0:TT]
            ext = exts[c % 2]
            ps = proj_x('w0in', 25 + c)
            conv4(ps, c, m_tail, 'mcw', 'mcb', acc, ext)
            P.op('act', lambda e, c=c: e.activation(out=fm(xct, c), in_=acc[:], func=AF.Silu), [acc], [xct[c // 4]])
            P.op('pool', lambda e: e.tensor_copy(out=xmb_ap, in_=ext[:, 3:TT + 3]), [ext], [xmb])
            for t_, (dst, src_ap, src_tb) in enumerate(((qTt, fm(xct, c), xct[c // 4]), (kTt, fm(xct, c), xct[c // 4]), (vTt, xmb_ap, xmb))):
                ps = nextps()
                P.op('pe', lambda e, ps=ps, t_=t_, src_ap=src_ap: e.matmul(ps[:], lhsT=wqkv_bf[:, t_, c, :], rhs=src_ap, start=True, stop=True),
                     [wqkv_bf, src_tb], [ps])
                P.op('act' if t_ != 1 else 'dve',
                     (lambda e, ps=ps, dst=dst: e.activation(out=fm(dst, c), in_=ps[:], func=AF.Copy)) if t_ != 1 else
                     (lambda e, ps=ps, dst=dst: e.tensor_copy(out=fm(dst, c), in_=ps[:])), [ps], [dst[c // 4]])
                j = t_ * 8 + c
                first, last = (c == 0 and t_ == 0), (c == 7 and t_ == 2)
                P.op('pe', lambda e, j=j, dst=dst, first=first, last=last: e.matmul(gi_ps[0:4, :], lhsT=wif_bf[:, 0, j, :], rhs=fm(dst, c), start=first, stop=last),
                     [wif_bf, dst[c // 4]], [gi_ps])
                P.op('pe', lambda e, j=j, dst=dst, first=first, last=last: e.matmul(gf_ps[0:4, :], lhsT=wif_bf[:, 1, j, :], rhs=fm(dst, c), start=first, stop=last),
                     [wif_bf, dst[c // 4]], [gf_ps])
            for t_, (dstl, src_ap, src_tb) in ((1, (ktt, fm(xct, c), xct[c // 4])), (2, (vtt, xmb_ap, xmb))):
                ps = nextps()
                for tb in range(NT):
                    P.op('pe', lambda e, ps=ps, tb=tb, t_=t_, src_ap=src_ap: e.matmul(ps[:, tb * 128:(tb + 1) * 128], lhsT=src_ap[:, tb * 128:(tb + 1) * 128], rhs=wqkv_bf[:, t_, c, :],
                                                                                      start=True, stop=True), [wqkv_bf, src_tb], [ps])
                for half in range(2):
                    dv = bfv(dstl[half]).rearrange("p (b d) -> p b d", b=2)[:, :, c * 128:(c + 1) * 128]
                    P.op('act' if half == 0 else 'dve',
                         (lambda e, ps=ps, dv=dv, half=half: e.activation(out=dv, in_=ps[:, half * 256:(half + 1) * 256].rearrange("p (b d) -> p b d", b=2), func=AF.Copy)) if half == 0 else
                         (lambda e, ps=ps, dv=dv, half=half: e.tensor_copy(out=dv, in_=ps[:, half * 256:(half + 1) * 256].rearrange("p (b d) -> p b d", b=2))),
                         [ps], [dstl[half]])
        pstate['set'] = [0, 1]
        rg_, re2, rwi, rem, rli, rFp, rag, rMx, rtmp = G[11], G[12], G[13], G[14], G[15], G[16], G[17], G[18], G[19]
        R4 = slice(0, 4)
        P.op('act', lambda e: e.activation(out=rli[R4, :], in_=gi_ps[R4, :], func=AF.Identity, bias=pc[R4, PC['bif_i']:PC['bif_i'] + 1]), [gi_ps, pc], [rli])
        P.op('act', lambda e: e.activation(out=rtmp[R4, :], in_=gf_ps[R4, :], func=AF.Exp, scale=-1.0, bias=pd[R4, PD['negbf']:PD['negbf'] + 1]), [gf_ps, pd], [rtmp])
        P.op('act', lambda e: e.activation(out=rtmp[R4, :], in_=rtmp[R4, :], func=AF.Ln, bias=1.0), [rtmp], [rtmp])
        P.op('dve', lambda e: e.tensor_tensor_scan(out=rFp[R4, :], data0=onesf[R4, :], data1=rtmp[R4, :], initial=mcar[:, 0:1], op0=MUL, op1=ADD),
             [onesf, rtmp, mcar], [rFp])
        P.op('pool', lambda e: e.tensor_tensor(out=rag[R4, :], in0=rli[R4, :], in1=rFp[R4, :], op=ADD), [rli, rFp], [rag])
        P.op('dve', lambda e: e.tensor_tensor_scan(out=rMx[R4, :], data0=rag[R4, :], data1=rag[R4, :], initial=mcar[:, 1:2], op0=MAX, op1=MAX),
             [rag, mcar], [rMx])
        P.op('pool', lambda e: e.tensor_copy(out=MxE[:, 0:1], in_=mcar[:, 1:2]), [mcar], [MxE])
        P.op('pool', lambda e: e.tensor_copy(out=MxE[:, 1:5], in_=rMx[R4, 127:TT:128]), [rMx], [MxE])
        P.op('pool', lambda e: e.tensor_copy(out=mcar[:, 0:1], in_=rFp[R4, TT - 1:TT]), [rFp], [mcar])
        P.op('pool', lambda e: e.tensor_copy(out=mcar[:, 1:2], in_=rMx[R4, TT - 1:TT]), [rMx], [mcar])
        v3 = lambda t_: t_[R4, :].rearrange("p (q t) -> p q t", q=NT)
        mend = MxE[:, 1:5].unsqueeze(2).broadcast_to([4, NT, 128])
        mprev = MxE[:, 0:4].unsqueeze(2).broadcast_to([4, NT, 128])
        P.op('dve', lambda e: e.tensor_tensor(out=v3(rg_), in0=v3(rag), in1=mend, op=SUB), [rag, MxE], [rg_])
        P.op('act', lambda e: e.activation(out=rg_[R4, :], in_=rg_[R4, :], func=AF.Exp), [rg_], [rg_])
        P.op('dve', lambda e: e.tensor_tensor(out=v3(re2), in0=mend, in1=v3(rMx), op=SUB), [rMx, MxE], [re2])
        P.op('act', lambda e: e.activation(out=re2[R4, :], in_=re2[R4, :], func=AF.Exp), [re2], [re2])
        P.op('dve', lambda e: e.tensor_tensor(out=v3(rwi), in0=mprev, in1=v3(rMx), op=SUB), [rMx, MxE], [rwi])
        P.op('act', lambda e: e.activation(out=rwi[R4, :], in_=rwi[R4, :], func=AF.Exp), [rwi], [rwi])
        P.op('dve', lambda e: e.tensor_tensor(out=rem[R4, :], in0=rFp[R4, :], in1=rMx[R4, :], op=SUB), [rFp, rMx], [rem])
        P.op('act', lambda e: e.activation(out=rem[R4, :], in_=rem[R4, :], func=AF.Exp), [rem], [rem])
        P.op('dve', lambda e: e.tensor_tensor(out=dec[:], in0=MxE[:, 0:4], in1=MxE[:, 1:5], op=SUB), [MxE], [dec])
        P.op('act', lambda e: e.activation(out=dec[:], in_=dec[:], func=AF.Exp), [dec], [dec])
        sp_ = PS[5]
        for q in range(NT):
            P.op('pe', lambda e, q=q: e.matmul(sp_[:, q * 4:(q + 1) * 4], lhsT=rg_[R4, q * 128:(q + 1) * 128], rhs=ident[0:4, 0:4], start=True, stop=True),
                 [rg_, ident], [sp_])
        for h in range(4):
            P.op('pe', lambda e, h=h: e.matmul(sp_[:, 16 + h * 4:16 + (h + 1) * 4], lhsT=sel[:, h, :], rhs=dec[:], start=True, stop=True), [sel, dec], [sp_])
        P.op('act', lambda e: e.activation(out=smallb[:], in_=sp_[:, 0:32], func=AF.Copy), [sp_], [smallb])
        rows = (rg_, re2, rwi, rem)
        pstate['set'] = [0, 1, 2, 3, 4]
        bcsets = [G[0:4], G[15:19]]
        mlstm_bcast(0, bcsets[0], rows)
        for h in range(4):
            if h + 1 < 4:
                mlstm_bcast(h + 1, bcsets[(h + 1) % 2], rows)
            mlstm_head(h, qTt, kTt, ktt, vtt, xct, fm, tokv, bcsets[h % 2])

    def mlstm_bcast(h, bc, rows):
        for i in range(4):
            ps = nextps()
            P.op('pe', lambda e, ps=ps, i=i: e.matmul(ps[:], lhsT=sel[:, h, :], rhs=rows[i][0:4, :], start=True, stop=True), [sel, rows[i]], [ps])
            P.op('act' if i % 2 == 0 else 'dve',
                 (lambda e, ps=ps, i=i: e.activation(out=bc[i][:], in_=ps[:], func=AF.Copy)) if i % 2 == 0 else
                 (lambda e, ps=ps, i=i: e.tensor_copy(out=bc[i][:], in_=ps[:])), [ps], [bc[i]])

    def mlstm_head(h, qTt, kTt, ktt, vtt, xct, fm, tokv, bc):
        g_bc, e2_bc, wi_bc, em_bc = bc
        h32 = [G[4], G[5]]
        gms = [G[6], G[7]]
        for vc in range(2):
            ps = proj_x('w0in', 41 + 2 * h + vc)
            P.op('act', lambda e, ps=ps, vc=vc: e.activation(out=gms[vc][:], in_=ps[:], func=AF.Silu), [ps], [gms[vc]])
        qt_ = qTt[h // 2]
        kt_ = kTt[h // 2]
        qv = bfv(qt_).rearrange("p (k t) -> p k t", k=4)[:, 2 * (h % 2):2 * (h % 2) + 2, :]
        kv = bfv(kt_).rearrange("p (k t) -> p k t", k=4)[:, 2 * (h % 2):2 * (h % 2) + 2, :]
        CB, SB_, NB = PS[6], PS[7], PS[5]
        for q in range(NT):
            ts_ = slice(q * 128, (q + 1) * 128)
            bcast = lambda t_: t_[:, ts_].unsqueeze(1).broadcast_to([128, 2, 128])
            P.op('dve', lambda e: e.tensor_tensor(out=qe[:], in0=qv[:, :, ts_], in1=bcast(e2_bc), op=MUL), [qt_, e2_bc], [qe])
            P.op('pool', lambda e: e.tensor_tensor(out=qw[:], in0=qv[:, :, ts_], in1=bcast(wi_bc), op=MUL), [qt_, wi_bc], [qw])
            P.op('dve', lambda e: e.scalar_tensor_tensor(out=kg[:], in0=kv[:, :, ts_], scalar=0.0625, in1=bcast(g_bc), op0=MUL, op1=MUL), [kt_, g_bc], [kg])
            ktok = tokv(ktt, q)[:, h * 256:(h + 1) * 256]
            vtok = tokv(vtt, q)[:, h * 256:(h + 1) * 256]
            P.op('pool', lambda e: e.tensor_scalar(out=kgt[:], in0=ktok, scalar1=smallb[:, q * 4 + h:q * 4 + h + 1], scalar2=0.0625, op0=MUL, op1=MUL),
                 [ktt[q // 2], smallb], [kgt])
            for kc in range(2):
                P.op('pe', lambda e, kc=kc: e.matmul(SB_[:, kc * 256:(kc + 1) * 256], lhsT=kgt[:, kc * 128:(kc + 1) * 128], rhs=vtok, start=True, stop=True),
                     [kgt, vtt[q // 2]], [SB_])
                P.op('pe', lambda e, kc=kc: e.matmul(NB[:, 64 + kc * 128:64 + (kc + 1) * 128], lhsT=kgt[:, kc * 128:(kc + 1) * 128], rhs=ones_bf[:], start=True, stop=True),
                     [kgt, ones_bf], [NB])
            for kc in range(2):
                P.op('pe', lambda e, kc=kc: e.matmul(CB[:, 0:128], lhsT=kg[:, kc, :], rhs=qe[:, kc, :], start=(kc == 0), stop=(kc == 1)), [kg, qe], [CB])
            P.op('dve', lambda e: e.tensor_tensor(out=st_bf[:], in0=CB[:, 0:128], in1=causal[:], op=MUL), [CB, causal], [st_bf])
            for vc in range(2):
                o_ = CB[:, 128 + vc * 128:256 + vc * 128]
                P.op('pe', lambda e, o_=o_, vc=vc: e.matmul(o_, lhsT=vtok[:, vc * 128:(vc + 1) * 128], rhs=st_bf[:], start=True, stop=False), [vtt[q // 2], st_bf], [CB])
                for kc in range(2):
                    P.op('pe', lambda e, o_=o_, vc=vc, kc=kc: e.matmul(o_, lhsT=CTbf[:, h, kc, vc * 128:(vc + 1) * 128], rhs=qw[:, kc, :], start=False, stop=(kc == 1)),
                         [CTbf, qw], [CB])
            o_ = CB[:, 384:512]
            P.op('pe', lambda e, o_=o_: e.matmul(o_, lhsT=ones_bf[:], rhs=st_bf[:], start=True, stop=False), [ones_bf, st_bf], [CB])
            for kc in range(2):
                P.op('pe', lambda e, o_=o_, kc=kc: e.matmul(o_, lhsT=nrbf[:, h, kc, :], rhs=qw[:, kc, :], start=False, stop=(kc == 1)), [nrbf, qw], [CB])
            dcol = smallb[:, 16 + h * 4 + q:16 + h * 4 + q + 1]
            P.op('dve', lambda e: e.scalar_tensor_tensor(out=CT32[:, h], in0=CT32[:, h], scalar=dcol, in1=SB_[:].rearrange("p (k v) -> p k v", k=2), op0=MUL, op1=ADD),
                 [CT32, smallb, SB_], [CT32])
            P.op('act', lambda e: e.activation(out=CTbf[:, h], in_=CT32[:, h], func=AF.Copy), [CT32], [CTbf])
            P.op('dve', lambda e: e.scalar_tensor_tensor(out=nr32[:, h], in0=nr32[:, h], scalar=dcol, in1=NB[:, 64:320].rearrange("p (k v) -> p k v", k=2), op0=MUL, op1=ADD),
                 [nr32, smallb, NB], [nr32])
            P.op('pool', lambda e: e.tensor_copy(out=nrbf[:, h], in_=nr32[:, h]), [nr32], [nrbf])
            P.op('act', lambda e: e.activation(out=ddm[:], in_=CB[:, 384:512], func=AF.Abs), [CB], [ddm])
            P.op('dve', lambda e: e.tensor_tensor(out=ddm[:], in0=ddm[:], in1=em_bc[:, ts_], op=MAX), [ddm, em_bc], [ddm])
            P.op('dve', lambda e: e.tensor_scalar(out=ddm[:], in0=ddm[:], scalar1=1e-6, scalar2=None, op0=ADD), [ddm], [ddm])
            P.op('dve', lambda e: e.reciprocal(out=recm[:], in_=ddm[:]), [ddm], [recm])
            for vc in range(2):
                P.op('dve', lambda e, vc=vc: e.tensor_tensor(out=h32[vc][:, ts_], in0=CB[:, 128 + vc * 128:256 + vc * 128], in1=recm[:], op=MUL), [CB, recm], [h32[vc]])
        dump('h_m%d' % h, h32[0], h32[0][:])
        hr = [G[8], G[9]]
        ps = nextps()
        for vc in range(2):
            P.op('act', lambda e, vc=vc: e.activation(out=RV(hr[vc]), in_=h32[vc][:], func=AF.Copy), [h32[vc]], [hr[vc]])
            P.op('pe', lambda e, ps=ps, vc=vc: e.matmul(ps[:], lhsT=ones_r[:], rhs=RV(hr[vc]), start=(vc == 0), stop=(vc == 1)), [ones_r, hr[vc]], [ps])
        for vc in range(2):
            P.op('dve', lambda e, ps=ps, vc=vc: e.scalar_tensor_tensor(out=h32[vc][:], in0=ps[:], scalar=-1.0 / 256, in1=h32[vc][:], op0=MUL, op1=ADD), [ps, h32[vc]], [h32[vc]])
        ps2 = nextps()
        for vc in range(2):
            P.op('act', lambda e, vc=vc: e.activation(out=RV(hr[vc]), in_=h32[vc][:], func=AF.Square), [h32[vc]], [hr[vc]])
            P.op('pe', lambda e, ps2=ps2, vc=vc: e.matmul(ps2[:], lhsT=ones_r[:], rhs=RV(hr[vc]), start=(vc == 0), stop=(vc == 1)), [ones_r, hr[vc]], [ps2])
        rs = G[10]
        P.op('act', lambda e, ps2=ps2: e.activation(out=rs[:], in_=ps2[:], func=AF.Ln, scale=1.0 / 256, bias=1e-5), [ps2], [rs])
        P.op('act', lambda e: e.activation(out=rs[:], in_=rs[:], func=AF.Exp, scale=-0.5), [rs], [rs])
        for vc in range(2):
            ch = 2 * h + vc
            P.op('dve', lambda e, vc=vc: e.tensor_tensor(out=h32[vc][:], in0=h32[vc][:], in1=rs[:], op=MUL), [h32[vc], rs], [h32[vc]])
            P.op('act', lambda e, vc=vc, ch=ch: e.activation(out=h32[vc][:], in_=h32[vc][:], func=AF.Identity, scale=pcc('mnorm', ch)), [h32[vc], pc], [h32[vc]])
            P.op('dve', lambda e, vc=vc, ch=ch: e.scalar_tensor_tensor(out=h32[vc][:], in0=fm(xct, ch), scalar=pcc('mskip', ch), in1=h32[vc][:], op0=MUL, op1=ADD),
                 [xct[ch // 4], pc, h32[vc]], [h32[vc]])
            P.op('pool', lambda e, vc=vc, ch=ch: e.tensor_tensor(out=merged[:, 8 + ch, :], in0=h32[vc][:], in1=gms[vc][:], op=MUL), [h32[vc], gms[vc]], [merged])

    def layer0(t0):
        rmsnorm_x('mixn0')
        pstate['set'] = [0, 1]
        if do_rwkv:
            rwkv(t0)
        if do_mlstm:
            mlstm(t0)
        pstate['set'] = list(range(8))
        out_proj('w0out')

    if do_l0 and not (do_rwkv and do_mlstm):
        P.op('pool', lambda e: e.memset(merged[:], 0.0), [], [merged])

    for ti in range(ntiles):
        t0 = ti * TT
        for tb in range(NT):
            P.dma('sp', H[tb][:], x_d.ap()[t0 + tb * 128:t0 + (tb + 1) * 128, :], writes=[H[tb]])
        for kc in range(8):
            ps = nextps()
            for tb in range(NT):
                P.op('pe', lambda e, kc=kc, tb=tb, ps=ps: e.transpose(ps[:, tb * 128:(tb + 1) * 128], H[tb][:, kc * 128:(kc + 1) * 128], ident[:]),
                     [H[tb], ident], [ps])
            P.op('act' if kc % 2 == 0 else 'dve',
                 (lambda e, kc=kc, ps=ps: e.activation(out=hT[:, kc, :], in_=ps[:], func=AF.Copy)) if kc % 2 == 0 else
                 (lambda e, kc=kc, ps=ps: e.tensor_copy(out=hT[:, kc, :], in_=ps[:])), [ps], [hT])
        if do_l0:
            layer0(t0)
            ple(0, t0)
        if do_l1:
            layer1(t0)
            ple(1, t0)
        norm_stats()
        for k in range(8):
            of = G[k % 4]
            norm_apply(k, 'finn', of[:], of)
            ps = nextps()
            for tb in range(NT):
                P.op('pe', lambda e, tb=tb, ps=ps, of=of: e.transpose(ps[:, tb * 128:(tb + 1) * 128], of[:, tb * 128:(tb + 1) * 128], ident[:]),
                     [of, ident], [ps])
            for tb in range(NT):
                P.op('act' if k % 2 == 0 else 'dve',
                     (lambda e, k=k, ps=ps, tb=tb: e.activation(out=H[tb][:, k * 128:(k + 1) * 128], in_=ps[:, tb * 128:(tb + 1) * 128], func=AF.Copy)) if k % 2 == 0 else
                     (lambda e, k=k, ps=ps, tb=tb: e.tensor_copy(out=H[tb][:, k * 128:(k + 1) * 128], in_=ps[:, tb * 128:(tb + 1) * 128])),
                     [ps], [H[tb]])
        for tb in range(NT):
            P.dma('sp', o_d.ap()[t0 + tb * 128:t0 + (tb + 1) * 128, :], H[tb][:], reads=[H[tb]])
    P.finish()
    print("sbuf bytes/partition:", P.sbytes, "sems:", P.nsem, "instr:", {e: P.cnt[e] for e in ENGS})
    return nc


def kernel(**inputs):
    x = np.asarray(inputs['x'], np.float32)
    p = np.asarray(inputs['p'], np.float32)
    B, S, _ = x.shape
    shared = host_prep(inputs)
    nc = build_program(S)
    in_maps = []
    for b in range(B):
        m = dict(shared)
        m['x'] = np.ascontiguousarray(x[b])
        m['p'] = np.ascontiguousarray(p[:, b])
        in_maps.append(m)
    res = run_bass_kernel_spmd(nc, in_maps, core_ids=list(range(B)))
    return np.stack([np.asarray(r['out'], np.float32) for r in res.results], axis=0)
```

```python
import numpy as np
import concourse.bass as bass
import concourse.mybir as mybir
from concourse.bass_utils import run_bass_kernel_spmd
from contextlib import ExitStack

F32 = mybir.dt.float32
BF16 = mybir.dt.bfloat16
F32R = mybir.dt.float32r
SDT = F32R
AF = mybir.ActivationFunctionType
ALU = mybir.AluOpType
AX = mybir.AxisListType

ENGS = ('pe', 'act', 'dve', 'pool', 'sp')
EIDX = {e: i for i, e in enumerate(ENGS)}
EPOCH = 30000

D = 1024
TT = 512
NT = TT // 128
PLE = 256
DECAY_SCALE = 0.6065306597126334


class TB:
    __slots__ = ('name', 'h', 'lw', 'rd', 'dkey', 'root', 'psum')

    def __init__(self, name, h, root=None, psum=False):
        self.name = name
        self.h = h
        self.lw = None
        self.rd = {}
        self.dkey = None
        self.root = root if root is not None else self
        self.psum = psum or (root is not None and root.psum)

    def __getitem__(self, k):
        return self.h[k]


class _Rec:
    def __init__(self):
        self.call = None

    def __getattr__(self, name):
        def f(*a, **k):
            self.call = (name, a, k)
        return f


class Prog:
    def __init__(self, nc):
        self.nc = nc
        self.es = ExitStack()
        self.ops = {e: [] for e in ENGS}
        self.cnt = {e: 0 for e in ENGS}
        self.seen = {e: {} for e in ENGS}
        self.clk = {e: [] for e in ENGS}
        self.dcnt = {}
        self.dclk = {}
        self.sems = {}
        self.nsem = 0
        self.ndma = 0
        self.sbytes = 0

    def sb(self, name, shape, dt=F32):
        n = 1
        for s in shape[1:]:
            n *= s
        self.sbytes += n * (2 if dt == BF16 else 4)
        return TB(name, self.es.enter_context(self.nc.sbuf_tensor(name, list(shape), dt)))

    def ps(self, name, shape, dt=F32):
        return TB(name, self.es.enter_context(self.nc.psum_tensor(name, list(shape), dt)), psum=True)

    def dram(self, name, shape, dt, kind):
        return TB(name, self.nc.dram_tensor(name, list(shape), dt, kind=kind))

    def _sem(self, key):
        s = self.sems.get(key)
        if s is None:
            s = self.es.enter_context(self.nc.semaphore("s%d" % self.nsem))
            self.nsem += 1
            self.sems[key] = s
        return s

    def _semval(self, key, count):
        if key in EIDX:
            ep = (count - 1) // EPOCH
            return self._sem((key, ep)), count - ep * EPOCH
        return self._sem(key), 16 * count

    def _deps(self, e, reads, writes):
        seen = self.seen[e]
        need = {}

        def req(ev):
            if ev is None:
                return
            k, c = ev
            if k == 'pe' and e == 'pe':
                return
            if seen.get(k, 0) >= c:
                return
            if need.get(k, 0) < c:
                need[k] = c
        for t in reads:
            req(t.lw)
        for t in writes:
            req(t.lw)
            for k, c in t.rd.items():
                req((k, c))
        keys = list(need.keys())
        for k in keys:
            if k not in need:
                continue
            c = need[k]
            ck = self.clk[k][c - 1] if k in EIDX else self.dclk[(k, c)]
            for k2 in keys:
                if k2 != k and k2 in EIDX and k2 in need and ck[EIDX[k2]] >= need[k2]:
                    del need[k2]
        waits = []
        for k, c in need.items():
            waits.append(self._semval(k, c))
            ck = self.clk[k][c - 1] if k in EIDX else self.dclk[(k, c)]
            for e2, v in zip(ENGS, ck):
                if seen.get(e2, 0) < v:
                    seen[e2] = v
            if seen.get(k, 0) < c:
                seen[k] = c
        return waits

    def _snapshot(self, e):
        s = self.seen[e]
        return tuple(s.get(x, 0) for x in ENGS)

    def op(self, e, fn, reads=(), writes=()):
        writes = [t.root for t in writes] + [t.root for t in reads if t.psum]
        reads = [t.root for t in reads if not t.psum]
        rec = _Rec()
        fn(rec)
        fn = rec.call
        waits = self._deps(e, reads, writes)
        self.cnt[e] += 1
        c = self.cnt[e]
        snap = list(self._snapshot(e))
        snap[EIDX[e]] = c
        self.clk[e].append(tuple(snap))
        inc = self._semval(e, c)[0]
        self.ops[e].append((waits, fn, inc, 1))
        ev = (e, c)
        for t in reads:
            if t.rd.get(e, 0) < c:
                t.rd[e] = c
        for t in writes:
            t.lw = ev
            t.rd = {}
        return ev

    def dma(self, q, out, in_, reads=(), writes=(), key=None, **kw):
        if key is None:
            t0 = (list(writes) + list(reads))[0]
            if t0.dkey is None:
                t0.dkey = ('dma', self.ndma)
                self.ndma += 1
            key = t0.dkey
        reads = [t.root for t in reads]
        writes = [t.root for t in writes]
        waits = self._deps(q, reads, writes)
        self.dcnt[key] = self.dcnt.get(key, 0) + 1
        c = self.dcnt[key]
        self.dclk[(key, c)] = self._snapshot(q)
        sem = self._sem(key)
        self.ops[q].append((waits, ('dma_start', (), dict(out=out, in_=in_, **kw)), sem, 16))
        ev = (key, c)
        for t in reads:
            if t.rd.get(key, 0) < c:
                t.rd[key] = c
        for t in writes:
            t.lw = ev
            t.rd = {}
        return ev

    def finish(self):
        e = 'sp'
        tail = []
        for key, c in self.dcnt.items():
            tail.append(self._semval(key, c))
        for x in ENGS:
            if x != e and self.cnt[x] > 0:
                tail.append(self._semval(x, self.cnt[x]))
        blk = self.es.enter_context(self.nc.Block())
        engobj = {'pe': blk.tensor, 'act': blk.scalar, 'dve': blk.vector, 'pool': blk.gpsimd, 'sp': blk.sync}

        def make(ename):
            ops = self.ops[ename]

            def body(eng):
                for waits, fn, inc, n in ops:
                    for (s, v) in waits[1:]:
                        eng.wait_ge(s, v)
                    ins = getattr(eng, fn[0])(*fn[1], **fn[2])
                    if waits:
                        ins._wait_ge(waits[0][0], waits[0][1])
                    ins.then_inc(inc, n)
                if ename == e:
                    for (s, v) in tail:
                        eng.wait_ge(s, v)
            return body
        for ename in ENGS:
            if self.ops[ename] or ename == e:
                engobj[ename](make(ename))
        self.es.close()


PC_SPEC = [
    ('mixn0', 8), ('mixn1', 8), ('pen0', 8), ('pen1', 8), ('finn', 8),
    ('mu_r', 8), ('mu_k', 8), ('mu_v', 8), ('mu_l', 1),
    ('w0', 8), ('a0', 8), ('k_k', 8), ('k_a', 8), ('r_k', 8), ('ln_w', 8), ('ln_b', 8),
    ('mcw0', 8), ('mcw1', 8), ('mcw2', 8), ('mcw3', 8), ('mcb', 8), ('mnorm', 8), ('mskip', 8),
    ('bif_i', 1), ('bif_f', 1),
    ('ccw0', 16), ('ccw1', 16), ('ccw2', 16), ('ccw3', 16), ('ccb', 16), ('cbr', 16), ('cbi', 16), ('clam', 16),
]
PC = {}
_o = 0
for _n, _w in PC_SPEC:
    PC[_n] = _o
    _o += _w
NPC = _o
PD_SPEC = [('omu_r', 8), ('omu_k', 8), ('omu_v', 8), ('omu_l', 1), ('negbf', 1), ('csph', 16), ('cneg', 16), ('t0', 16), ('t1', 16)]
PD = {}
_o = 0
for _n, _w in PD_SPEC:
    PD[_n] = _o
    _o += _w
NPD = _o


def _cols(v):
    v = np.asarray(v, np.float32).reshape(-1)
    return np.ascontiguousarray(v.reshape(-1, 128).T)


def host_consts():
    c = {}
    c['ident'] = np.eye(128, dtype=np.float32)
    bd = np.zeros((128, 128), np.float32)
    bd[:64, :64] = 1
    bd[64:, 64:] = 1
    c['bdones'] = bd
    j = np.arange(128)[:, None]
    i = np.arange(128)[None, :]
    su = ((j < i) & ((j // 64) == (i // 64))).astype(np.float32)
    ui = ((j <= i) & ((j // 64) == (i // 64))).astype(np.float32)
    sl = ((j > i) & ((j // 64) == (i // 64))).astype(np.float32)
    on = np.ones((128, 128), np.float32)
    c['maskA'] = np.stack([su, su, ui, ui], axis=1)
    c['maskB'] = sl
    rm = np.ones((128, TT), np.float32)
    rm[:, ::64] = 0
    c['resetm'] = rm
    c['causal'] = (j <= i).astype(np.float32)
    sel = np.zeros((4, 4, 128), np.float32)
    for h in range(4):
        sel[h, h, :] = 1
    c['sel'] = sel
    return c


def host_prep(inp):
    g = {}
    f = lambda a: np.ascontiguousarray(np.asarray(a, np.float32))

    def chunked(w):
        K, M = w.shape
        return f(w.reshape(K // 128, 128, M // 128, 128).transpose(2, 1, 0, 3))
    g['w0in'] = chunked(f(inp['ab_w_in'])[0])
    g['w0out'] = chunked(f(inp['ab_w_out'])[0])
    g['w1in'] = chunked(f(inp['c_w_in'])[0])
    g['w1out'] = chunked(f(inp['c_w_out'])[0])
    g['pegate'] = np.stack([chunked(f(inp['pe_gate'])[i]) for i in range(2)])
    g['peup'] = np.stack([chunked(f(inp['pe_up'])[i]) for i in range(2)])
    wr = f(inp['c_wr'])[0]
    wi = f(inp['c_wi'])[0]
    g['wri'] = f(np.stack([wr, wi], axis=2))
    lup = np.zeros((128, 2, 1024), np.float32)
    lup[:64, 0, :] = f(inp['rwkv_w_up'])[0]
    lup[64:, 1, :] = f(inp['rwkv_a_up'])[0]
    g['lup'] = lup
    wbd = np.zeros((128, 3, 8, 128), np.float32)
    for t, nm in enumerate(['mlstm_wq', 'mlstm_wk', 'mlstm_wv']):
        w = f(inp[nm])[0]
        for c in range(8):
            for gg in range(32):
                wbd[4 * gg:4 * gg + 4, t, c, 4 * gg:4 * gg + 4] = w[c * 32 + gg]
    g['wqkv'] = wbd
    wif = f(inp['mlstm_w_if'])[0]
    g['wif'] = f(wif.reshape(24, 128, 2, 4).transpose(1, 2, 0, 3))
    pc = np.zeros((128, NPC), np.float32)

    def put(name, v):
        cc = _cols(v)
        pc[:, PC[name]:PC[name] + cc.shape[1]] = cc
    put('mixn0', f(inp['mix_norm'])[0]); put('mixn1', f(inp['mix_norm'])[1])
    put('pen0', f(inp['pe_norm'])[0]); put('pen1', f(inp['pe_norm'])[1])
    put('finn', f(inp['final_norm']))
    mu = f(inp['rwkv_mu'])[0]
    put('mu_r', mu[0]); put('mu_k', mu[1]); put('mu_v', mu[2])
    ml = f(inp['rwkv_mu_lora'])[0]
    put('mu_l', np.concatenate([ml[0], ml[1]]))
    put('w0', f(inp['rwkv_w0'])[0]); put('a0', f(inp['rwkv_a0'])[0])
    put('k_k', f(inp['rwkv_k_k'])[0]); put('k_a', f(inp['rwkv_k_a'])[0])
    put('r_k', f(inp['rwkv_r_k'])[0].reshape(-1))
    put('ln_w', f(inp['rwkv_ln_w'])[0]); put('ln_b', f(inp['rwkv_ln_b'])[0])
    mcw = f(inp['mlstm_conv_w'])[0]
    for j in range(4):
        put('mcw%d' % j, mcw[j])
    put('mcb', f(inp['mlstm_conv_b'])[0])
    put('mnorm', f(inp['mlstm_norm'])[0]); put('mskip', f(inp['mlstm_skip'])[0])
    bif = f(inp['mlstm_b_if'])[0]
    pc[0:4, PC['bif_i']] = bif[0:4]
    pc[0:4, PC['bif_f']] = bif[4:8]
    ccw = f(inp['c_conv_w'])[0]
    for j in range(4):
        put('ccw%d' % j, ccw[j])
    put('ccb', f(inp['c_conv_b'])[0]); put('cbr', f(inp['c_br'])[0]); put('cbi', f(inp['c_bi'])[0])
    put('clam', f(inp['c_lambda'])[0])
    g['pc'] = pc
    g.update(host_consts())
    return g


SHARED_SHAPES = {
    'w0in': [49, 128, 8, 128], 'w0out': [8, 128, 16, 128], 'w1in': [32, 128, 8, 128], 'w1out': [8, 128, 16, 128],
    'pegate': [2, 8, 128, 8, 128], 'peup': [2, 8, 128, 2, 128], 'wri': [16, 128, 2, 128], 'lup': [128, 2, 1024],
    'wqkv': [128, 3, 8, 128], 'wif': [128, 2, 24, 4], 'pc': [128, NPC],
    'ident': [128, 128], 'bdones': [128, 128], 'maskA': [128, 4, 128], 'maskB': [128, 128],
    'resetm': [128, TT], 'causal': [128, 128], 'sel': [4, 4, 128],
}


def build_program(S, do_l0=True, do_rwkv=True, do_mlstm=True, do_l1=True, dbg=None):
    nc = bass.Bass("TRN2", target_bir_lowering=False)
    P = Prog(nc)
    ntiles = S // TT
    din = {}
    for k, shp in SHARED_SHAPES.items():
        din[k] = nc.dram_tensor(k, shp, F32, kind="ExternalInput")
    x_d = nc.dram_tensor("x", [S, D], F32, kind="ExternalInput")
    p_d = nc.dram_tensor("p", [2, S, PLE], F32, kind="ExternalInput")
    o_d = nc.dram_tensor("out", [S, D], F32, kind="ExternalOutput")
    dbg_d = None
    dbg = dbg or []
    if dbg:
        dbg_d = nc.dram_tensor("dbg", [len(dbg), 128, TT], F32, kind="ExternalOutput")

    def dump(name, tb, ap):
        if name in dbg:
            P.dma('sp', dbg_d.ap()[dbg.index(name)], ap, reads=[tb], key=('dma', 'dbg'))

    MUL, ADD, SUB, MAX = ALU.mult, ALU.add, ALU.subtract, ALU.max

    def cload(name, dt=F32, rows=128):
        shp = SHARED_SHAPES[name]
        t = P.sb("c_" + name, shp, dt)
        P.dma('pool' if dt != F32 else 'sp', t[:], din[name].ap(), writes=[t])
        return t
    ident = cload('ident')
    pc = cload('pc')
    pd = P.sb("pd", [128, NPD])
    ones_bf = P.sb("ones_bf", [128, 128], BF16)
    P.op('pool', lambda e: e.memset(ones_bf[:], 1.0), [], [ones_bf])

    def pcc(name, c=0, n=1):
        return pc[:, PC[name] + c:PC[name] + c + n]

    def pdc(name, c=0, n=1):
        return pd[:, PD[name] + c:PD[name] + c + n]

    if do_l1:
        P.op('act', lambda e: e.activation(out=pdc('t0', 0, 16), in_=pcc('clam', 0, 16), func=AF.Exp, scale=-1.0), [pc], [pd])
        P.op('act', lambda e: e.activation(out=pdc('t1', 0, 16), in_=pdc('t0', 0, 16), func=AF.Ln, bias=1.0), [pd], [pd])
        P.op('dve', lambda e: e.tensor_scalar(out=pdc('csph', 0, 16), in0=pdc('t1', 0, 16), scalar1=4.0, scalar2=None, op0=MUL), [pd], [pd])
        P.op('dve', lambda e: e.tensor_scalar(out=pdc('cneg', 0, 16), in0=pdc('t1', 0, 16), scalar1=-8.0, scalar2=None, op0=MUL), [pd], [pd])
    if do_l0:
        for nm, w in (('r', 8), ('k', 8), ('v', 8), ('l', 1)):
            P.op('dve', lambda e, nm=nm, w=w: e.tensor_scalar(out=pdc('omu_' + nm, 0, w), in0=pcc('mu_' + nm, 0, w), scalar1=-1.0, scalar2=1.0, op0=MUL, op1=ADD),
                 [pc], [pd])
        P.op('dve', lambda e: e.tensor_scalar(out=pdc('negbf'), in0=pcc('bif_f'), scalar1=-1.0, scalar2=None, op0=MUL), [pc], [pd])

    scr = {}

    def mkscr(name, src_ap, shape, group=8):
        t = P.dram("scr_" + name, shape, BF16, "Internal")
        nm = shape[0]
        for j0 in range(0, nm, group):
            j1 = min(nm, j0 + group)
            P.dma('pool', t[j0:j1].rearrange("j p k m -> p j k m"), src_ap[j0:j1].rearrange("j p k m -> p j k m"),
                  writes=[t], key=('dma', 'scr_' + name))
        scr[name] = t
        return t
    if do_l0:
        lup_bf = cload('lup', BF16)
        wqkv_bf = cload('wqkv', BF16)
        wif_bf = cload('wif', BF16)
        mkscr('w0in', din['w0in'].ap(), [49, 128, 8, 128])
        mkscr('w0out', din['w0out'].ap(), [8, 128, 16, 128], group=4)
    for i in range(2):
        if (i == 0 and do_l0) or (i == 1 and do_l1):
            mkscr('pegate%d' % i, din['pegate'].ap()[i], [8, 128, 8, 128])
            mkscr('peup%d' % i, din['peup'].ap()[i], [8, 128, 2, 128])
    if do_l1:
        mkscr('w1in', din['w1in'].ap(), [32, 128, 8, 128])
        mkscr('wri', din['wri'].ap(), [16, 128, 2, 128], group=16)
        mkscr('w1out', din['w1out'].ap(), [8, 128, 16, 128], group=4)

    NWB = 4
    wring = [P.sb("wb%d" % i, [128, 16, 128], BF16) for i in range(NWB)]
    wstate = {'i': 0}

    def wld(name, j, nk):
        wb = wring[wstate['i'] % NWB]
        wstate['i'] += 1
        P.dma('sp', wb[:, 0:nk, :], scr[name][j], reads=[scr[name]], writes=[wb])
        return wb

    PS = [P.ps("ps%d" % i, [128, TT]) for i in range(8)]
    pstate = {'i': 0}

    pstate['set'] = list(range(8))

    def nextps():
        s_ = pstate['set']
        b = PS[s_[pstate['i'] % len(s_)]]
        pstate['i'] += 1
        return b

    hT = P.sb("hT", [128, 8, TT])
    xnT = P.sb("xnT", [128, 8, TT], BF16)
    merged = P.sb("merged", [128, 16, TT], BF16)
    ext = P.sb("ext", [128, TT + 3])
    exts = [ext, P.sb("ext2", [128, TT + 3])] if (do_l0 and do_mlstm) else [ext, ext]
    NG, NH = 20, 12
    G = [P.sb("g%d" % i, [128, TT]) for i in range(NG)]
    H = [P.sb("h%d" % i, [128, 2 * TT]) for i in range(NH)]
    lnv, rstd = G[19], G[18]
    _alias = {}

    def RV(tb):
        if SDT == F32:
            return tb[:]
        a = _alias.get(tb.name)
        if a is None:
            ml = nc.lookup_mloc(tb.h)
            a = nc.alloc_sbuf_tensor_at(tb.name + "_r", [128, int(ml.dims[1]) // 4], F32R, offset=int(ml.addr))
            _alias[tb.name] = a
        return a[:]
    ntmp = [G[16], G[17]]

    def bfv(tb, n=None):
        v = tb[:].bitcast(BF16)
        return v

    if do_l1:
        l1_tail = P.sb("l1_tail", [128, 16, 3])
        l1_h = P.sb("l1_h", [128, 16])
        P.op('pool', lambda e: e.memset(l1_tail[:], 0.0), [], [l1_tail])
        P.op('pool', lambda e: e.memset(l1_h[:], 0.0), [], [l1_h])

    def norm_apply(k, gname, out_ap, out_tb):
        if k % 2 == 0:
            P.op('dve', lambda e: e.scalar_tensor_tensor(out=out_ap, in0=hT[:, k, :], scalar=pcc(gname, k), in1=rstd[:],
                                                         op0=MUL, op1=MUL), [hT, pc, rstd], [out_tb])
        else:
            nt = ntmp[(k // 2) % 2]
            P.op('act', lambda e: e.activation(out=nt[:], in_=hT[:, k, :], func=AF.Identity, scale=pcc(gname, k)), [hT, pc], [nt])
            P.op('pool', lambda e: e.tensor_tensor(out=out_ap, in0=nt[:], in1=rstd[:], op=MUL), [nt, rstd], [out_tb])

    def norm_stats():
        ps = nextps()
        for half in range(2):
            sq = H[4 + half]
            sqv = bfv(sq).rearrange("p (k t) -> p k t", k=4)
            P.op('act', lambda e, half=half, sqv=sqv: e.activation(out=sqv, in_=hT[:, 4 * half:4 * half + 4, :], func=AF.Square), [hT], [sq])
            for k in range(4):
                P.op('pe', lambda e, k=k, half=half, sqv=sqv: e.matmul(ps[:], lhsT=ones_bf[:], rhs=sqv[:, k, :], start=(half == 0 and k == 0), stop=(half == 1 and k == 3)),
                     [ones_bf, sq], [ps])
        P.op('act', lambda e: e.activation(out=lnv[:], in_=ps[:], func=AF.Ln, scale=1.0 / D, bias=1e-6), [ps], [lnv])
        P.op('act', lambda e: e.activation(out=rstd[:], in_=lnv[:], func=AF.Exp, scale=-0.5), [lnv], [rstd])

    def rmsnorm_x(gname):
        norm_stats()
        for k in range(8):
            norm_apply(k, gname, xnT[:, k, :], xnT)

    def proj(ps, wb, nk, rhs_tb, rhs_fn):
        for k in range(nk):
            P.op('pe', lambda e, k=k: e.matmul(ps[:], lhsT=wb[:, k, :], rhs=rhs_fn(k), start=(k == 0), stop=(k == nk - 1)),
                 [wb, rhs_tb], [ps])

    def proj_x(name, j):
        w = wld(name, j, 8)
        ps = nextps()
        proj(ps, w, 8, xnT, lambda k: xnT[:, k, :])
        return ps

    def ple(i, t0):
        ptok = H[0]
        ptv = ptok[:].rearrange("p (tb d) -> p tb d", tb=NT)
        pTt = G[12]
        pT = bfv(pTt).rearrange("p (k t) -> p k t", k=2)
        sgs, tmps = [G[0], G[1]], [G[2], G[3]]
        P.dma('sp', ptv, p_d.ap()[i, t0:t0 + TT, :].rearrange("(tb p) d -> p tb d", p=128), writes=[ptok])
        for kc in range(2):
            ps = nextps()
            for tb in range(NT):
                P.op('pe', lambda e, kc=kc, tb=tb, ps=ps: e.transpose(ps[:, tb * 128:(tb + 1) * 128], ptv[:, tb, kc * 128:(kc + 1) * 128], ident[:]),
                     [ptok, ident], [ps])
            P.op('act', lambda e, kc=kc, ps=ps: e.activation(out=pT[:, kc, :], in_=ps[:], func=AF.Copy), [ps], [pTt])
        rmsnorm_x('pen%d' % i)
        for m in range(8):
            sg, tmpa = sgs[m % 2], tmps[m % 2]
            wg = wld('pegate%d' % i, m, 8)
            wu = wld('peup%d' % i, m, 2)
            ps = nextps()
            proj(ps, wg, 8, xnT, lambda k: xnT[:, k, :])
            P.op('act', lambda e, ps=ps: e.activation(out=sg[:], in_=ps[:], func=AF.Sigmoid), [ps], [sg])
            ps2 = nextps()
            proj(ps2, wu, 2, pTt, lambda k: pT[:, k, :])
            P.op('dve', lambda e, ps2=ps2: e.tensor_tensor(out=tmpa[:], in0=ps2[:], in1=sg[:], op=MUL), [ps2, sg], [tmpa])
            P.op('pool', lambda e, m=m: e.tensor_tensor(out=hT[:, m, :], in0=hT[:, m, :], in1=tmpa[:], op=ADD), [hT, tmpa], [hT])

    def out_proj(name):
        for m in range(8):
            wo = wld(name, m, 16)
            ps = nextps()
            proj(ps, wo, 16, merged, lambda k: merged[:, k, :])
            P.op('dve', lambda e, m=m, ps=ps: e.tensor_tensor(out=hT[:, m, :], in0=hT[:, m, :], in1=ps[:], op=ADD), [hT, ps], [hT])

    def conv4(ps, c, tail_tb, wname, bname, acc, ext=ext):
        P.op('pool', lambda e: e.tensor_copy(out=ext[:, 0:3], in_=tail_tb[:, c, :]), [tail_tb], [ext])
        P.op('act', lambda e: e.activation(out=ext[:, 3:TT + 3], in_=ps[:], func=AF.Copy), [ps], [ext])
        P.op('act', lambda e: e.activation(out=acc[:], in_=ps[:], func=AF.Identity, scale=pcc(wname + '3', c), bias=pcc(bname, c)), [ps, pc], [acc])
        P.op('pool', lambda e: e.tensor_copy(out=tail_tb[:, c, :], in_=ext[:, TT:TT + 3]), [ext], [tail_tb])
        for j in range(3):
            P.op('dve', lambda e, j=j: e.scalar_tensor_tensor(out=acc[:], in0=ext[:, j:j + TT], scalar=pcc(wname + str(j), c), in1=acc[:],
                                                              op0=MUL, op1=ADD), [ext, pc, acc], [acc])

    def layer1(t0):
        rmsnorm_x('mixn1')
        for g_ in range(4):
            bufs = []
            for k in range(4):
                s0, s1, s2 = H[3 * k], H[3 * k + 1], H[3 * k + 2]
                bufs.append(dict(xc32=(s0, s0[:, 0:TT]), xcb=(s0, s0[:, TT:2 * TT].bitcast(BF16)[:, 0:TT]),
                                 rg=(s1, s1[:, 0:TT]), ig=(s1, s1[:, TT:2 * TT]), a2=(s2, s2[:, 0:TT]), th=(s2, s2[:, TT:2 * TT])))
            cs = [4 * g_ + k for k in range(4)]
            for k, c in enumerate(cs):
                xt, xa = bufs[k]['xc32']
                bt, ba = bufs[k]['xcb']
                ext = exts[k % 2]
                ps = proj_x('w1in', c)
                P.op('pool', lambda e: e.tensor_copy(out=ext[:, 0:3], in_=l1_tail[:, c, :]), [l1_tail], [ext])
                P.op('act', lambda e: e.activation(out=ext[:, 3:TT + 3], in_=ps[:], func=AF.Copy), [ps], [ext])
                P.op('act', lambda e: e.activation(out=xa, in_=ps[:], func=AF.Identity, scale=pcc('ccw3', c), bias=pcc('ccb', c)), [ps, pc], [xt])
                P.op('pool', lambda e: e.tensor_copy(out=l1_tail[:, c, :], in_=ext[:, TT:TT + 3]), [ext], [l1_tail])
                for j in range(3):
                    P.op('dve', lambda e, j=j: e.scalar_tensor_tensor(out=xa, in0=ext[:, j:j + TT], scalar=pcc('ccw%d' % j, c), in1=xa, op0=MUL, op1=ADD), [ext, pc, xt], [xt])
                P.op('act', lambda e: e.activation(out=ba, in_=xa, func=AF.Copy), [xt], [bt])
            for k, c in enumerate(cs):
                bt, ba = bufs[k]['xcb']
                rt, ra = bufs[k]['rg']
                it, ia = bufs[k]['ig']
                wb = wld('wri', c, 2)
                ps_r = nextps()
                P.op('pe', lambda e: e.matmul(ps_r[:], lhsT=wb[:, 0, :], rhs=ba, start=True, stop=True), [wb, bt], [ps_r])
                ps_i = nextps()
                P.op('pe', lambda e: e.matmul(ps_i[:], lhsT=wb[:, 1, :], rhs=ba, start=True, stop=True), [wb, bt], [ps_i])
                P.op('act', lambda e: e.activation(out=ra, in_=ps_r[:], func=AF.Sigmoid, bias=pcc('cbr', c)), [ps_r, pc], [rt])
                P.op('act', lambda e: e.activation(out=ia, in_=ps_i[:], func=AF.Sigmoid, bias=pcc('cbi', c)), [ps_i, pc], [it])
            for k, c in enumerate(cs):
                rt, ra = bufs[k]['rg']
                at, aa_ = bufs[k]['a2']
                P.op('act', lambda e: e.activation(out=aa_, in_=ra, func=AF.Exp, scale=pdc('cneg', c)), [rt, pd], [at])
            for k, c in enumerate(cs):
                rt, ra = bufs[k]['rg']
                tt_, ta = bufs[k]['th']
                P.op('act', lambda e: e.activation(out=ta, in_=ra, func=AF.Tanh, scale=pdc('csph', c)), [rt, pd], [tt_])
            for k, c in enumerate(cs):
                at, aa_ = bufs[k]['a2']
                tt_, ta = bufs[k]['th']
                P.op('dve', lambda e: e.scalar_tensor_tensor(out=ta, in0=aa_, scalar=1.0, in1=ta, op0=ADD, op1=MUL), [at, tt_], [tt_])
                P.op('dve', lambda e: e.tensor_scalar(out=aa_, in0=ta, scalar1=-1.0, scalar2=1.0, op0=MUL, op1=ADD), [tt_], [at])
                P.op('dve', lambda e: e.scalar_tensor_tensor(out=ta, in0=aa_, scalar=1.0, in1=ta, op0=ADD, op1=MUL), [at, tt_], [tt_])
            for k, c in enumerate(cs):
                tt_, ta = bufs[k]['th']
                P.op('act', lambda e: e.activation(out=ta, in_=ta, func=AF.Sqrt), [tt_], [tt_])
            for k, c in enumerate(cs):
                xt, xa = bufs[k]['xc32']
                rt, ra = bufs[k]['rg']
                it, ia = bufs[k]['ig']
                at, aa_ = bufs[k]['a2']
                tt_, ta = bufs[k]['th']
                P.op('pool', lambda e: e.tensor_tensor(out=ia, in0=xa, in1=ia, op=MUL), [xt, it], [it])
                P.op('pool', lambda e: e.tensor_tensor(out=ia, in0=ia, in1=ta, op=MUL), [it, tt_], [it])
                P.op('dve', lambda e: e.tensor_tensor_scan(out=ra, data0=aa_, data1=ia, initial=l1_h[:, c:c + 1], op0=MUL, op1=ADD), [at, it, l1_h], [rt])
                P.op('pool', lambda e: e.tensor_copy(out=l1_h[:, c:c + 1], in_=ra[:, TT - 1:TT]), [rt], [l1_h])
            for k, c in enumerate(cs):
                xt, xa = bufs[k]['xc32']
                rt, ra = bufs[k]['rg']
                ps_g = proj_x('w1in', 16 + c)
                P.op('act', lambda e: e.activation(out=xa, in_=ps_g[:], func=AF.Silu), [ps_g], [xt])
                P.op('dve', lambda e: e.tensor_tensor(out=merged[:, c, :], in0=ra, in1=xa, op=MUL), [rt, xt], [merged])
        out_proj('w1out')

    if do_l0 and do_rwkv:
        identr = P.sb("identr", [128, 128], SDT)
        bdones = cload('bdones')
        bdones_bf = P.sb("bdones_bf", [128, 128], BF16)
        bdones_r = P.sb("bdones_r", [128, 128], SDT)
        maskA = cload('maskA')
        maskB = cload('maskB')
        resetm = cload('resetm')
        P.op('dve', lambda e: e.tensor_copy(out=identr[:], in_=ident[:]), [ident], [identr])
        P.op('dve', lambda e: e.tensor_copy(out=bdones_bf[:], in_=bdones[:]), [bdones], [bdones_bf])
        P.op('dve', lambda e: e.tensor_copy(out=bdones_r[:], in_=bdones[:]), [bdones], [bdones_r])
        rwc = P.sb("rwc", [128, 25])
        P.op('pool', lambda e: e.memset(rwc[:], 0.0), [], [rwc])
        lora_bf = P.sb("lora_bf", [128, TT], BF16)
        Sst = [[P.sb("S%d_%d" % (c, q), [128, 128], F32) for q in range(2)] for c in range(8)]
        for c in range(8):
            P.op('pool', lambda e, c=c: e.memset(Sst[c][0][:], 0.0), [], [Sst[c][0]])
        spar = [0] * 8
        rhs_sb = P.sb("rhs_sb", [128, 128], SDT)
        u_sb = P.sb("u_sb", [128, 128], SDT)
        s0g = P.sb("s0g", [128, 128])
        psRHS = TB("psRHS", PS[7].h[:, 0:128], root=PS[7])
        psU = TB("psU", PS[7].h[:, 128:256], root=PS[7])
        psYT = TB("psYT", PS[6].h[:, 0:128], root=PS[6])
        psSN = TB("psSN", PS[7].h[:, 384:512], root=PS[7])

    def tshift(ps, dst, col, mu_ap, omu_ap):
        P.op('act', lambda e: e.activation(out=dst[:], in_=ps[:], func=AF.Identity, scale=omu_ap), [ps, pd], [dst])
        P.op('dve', lambda e: e.scalar_tensor_tensor(out=dst[:, 1:TT], in0=ps[:, 0:TT - 1], scalar=mu_ap, in1=dst[:, 1:TT], op0=MUL, op1=ADD),
             [ps, pc, dst], [dst])
        P.op('dve', lambda e: e.scalar_tensor_tensor(out=dst[:, 0:1], in0=rwc[:, col:col + 1], scalar=mu_ap, in1=dst[:, 0:1], op0=MUL, op1=ADD),
             [rwc, pc, dst], [dst])
        P.op('act', lambda e: e.activation(out=rwc[:, col:col + 1], in_=ps[:, TT - 1:TT], func=AF.Copy), [ps], [rwc])

    def r3(ap, n=8):
        return ap.rearrange("p (n t) -> p n t", n=n)

    def rwkv(t0):
        Rbd, Abd, Bbd, Kbd, Vbd = H[0:5]
        for i in range(5):
            P.op('pool', lambda e, i=i: e.memset(H[i][:], 0.0), [], [H[i]])
        bdv = [r3(RV(H[i])) for i in range(5)]
        ps = proj_x('w0in', 24)
        lora32 = G[16]
        tshift(ps, lora32, 24, pcc('mu_l'), pdc('omu_l'))
        P.op('act', lambda e: e.activation(out=lora_bf[0:64, :], in_=lora32[0:64, :], func=AF.Tanh), [lora32], [lora_bf])
        P.op('act', lambda e: e.activation(out=lora_bf[64:128, :], in_=lora32[64:128, :], func=AF.Copy), [lora32], [lora_bf])
        pend = []

        def pump(n):
            for _ in range(n):
                if pend:
                    pend.pop(0)()
        for th_ in rwkv_prep(0):
            th_()
        rwkv_blockdiag(0, bdv)
        for c in range(8):
            if c + 1 < 8:
                pend.extend(rwkv_prep(c + 1))
            rwkv_chunks(c, bdv, pump)
            pump(len(pend))
            if c + 1 < 8:
                rwkv_blockdiag(c + 1, bdv)
            rwkv_out(c)

    def rw_bufs(c):
        p_ = c % 2
        d = dict(r32=G[0], k32=G[1], v32=G[2], cump=G[6], Ginv=G[8], Gp=G[9], a32=G[10], kk32=G[11], kka=G[11],
                 rn=G[12], nkk=G[13], keff=G[15], t1=G[16])
        d.update(dict(gr32=(G[3], G[14])[p_], lw=(G[4], G[17])[p_], cum=(G[5], G[18])[p_], Gt=(G[7], G[19])[p_]))
        return d

    def rwkv_prep(c):
        B_ = rw_bufs(c)
        r32, k32, v32, gr32, lw, cum, cump, Gt, Ginv, Gp = [B_[k] for k in ('r32', 'k32', 'v32', 'gr32', 'lw', 'cum', 'cump', 'Gt', 'Ginv', 'Gp')]
        a32, kk32, kka, rn, nkk, keff, t1 = [B_[k] for k in ('a32', 'kk32', 'kka', 'rn', 'nkk', 'keff', 't1')]
        bon = cum
        T = []

        def projshift(dst, mch, nm, col):
            def f():
                ps = proj_x('w0in', mch)
                tshift(ps, dst, col, pcc('mu_' + nm, c), pdc('omu_' + nm, c))
            return f
        kproj = projshift(k32, 8 + c, 'k', 8 + c)

        def f_lw():
            ps = nextps()
            P.op('pe', lambda e: e.matmul(ps[:], lhsT=lup_bf[:, 0, c * 128:(c + 1) * 128], rhs=lora_bf[:], start=True, stop=True), [lup_bf, lora_bf], [ps])
            P.op('act', lambda e: e.activation(out=lw[:], in_=ps[:], func=AF.Sigmoid, bias=pcc('w0', c)), [ps, pc], [lw])
        T.append(f_lw)

        def f_a():
            ps = nextps()
            P.op('pe', lambda e: e.matmul(ps[:], lhsT=lup_bf[:, 1, c * 128:(c + 1) * 128], rhs=lora_bf[:], start=True, stop=True), [lup_bf, lora_bf], [ps])
            P.op('act', lambda e: e.activation(out=a32[:], in_=ps[:], func=AF.Sigmoid, bias=pcc('a0', c)), [ps, pc], [a32])
        T.append(f_a)
        T.append(lambda: P.op('dve', lambda e: e.tensor_tensor_scan(out=cum[:], data0=resetm[:], data1=lw[:], initial=0.0, op0=MUL, op1=ADD), [resetm, lw], [cum]))
        T.append(lambda: P.op('dve', lambda e: e.tensor_tensor(out=cump[:], in0=cum[:], in1=lw[:], op=SUB), [cum, lw], [cump]))
        T.append(lambda: P.op('act', lambda e: e.activation(out=Gt[:], in_=cum[:], func=AF.Exp, scale=-DECAY_SCALE), [cum], [Gt]))
        T.append(lambda: P.op('act', lambda e: e.activation(out=Ginv[:], in_=cum[:], func=AF.Exp, scale=DECAY_SCALE), [cum], [Ginv]))
        T.append(lambda: P.op('act', lambda e: e.activation(out=Gp[:], in_=cump[:], func=AF.Exp, scale=-DECAY_SCALE), [cump], [Gp]))
        T.append(kproj)
        T.append(lambda: P.op('dve', lambda e: e.tensor_scalar(out=kk32[:], in0=k32[:], scalar1=pcc('k_k', c), scalar2=None, op0=MUL), [k32, pc], [kk32]))
        sqk = bfv(t1)[:, 0:TT]
        T.append(lambda: P.op('act', lambda e: e.activation(out=sqk, in_=kk32[:], func=AF.Square), [kk32], [t1]))

        def f_rn():
            ps = nextps()
            P.op('pe', lambda e: e.matmul(ps[:], lhsT=bdones_bf[:], rhs=sqk, start=True, stop=True), [bdones_bf, t1], [ps])
            P.op('act', lambda e: e.activation(out=rn[:], in_=ps[:], func=AF.Ln, bias=1e-20), [ps], [rn])
            P.op('act', lambda e: e.activation(out=rn[:], in_=rn[:], func=AF.Exp, scale=-0.5), [rn], [rn])
        T.append(f_rn)
        T.append(lambda: P.op('dve', lambda e: e.scalar_tensor_tensor(out=nkk[:], in0=kk32[:], scalar=-1.0, in1=rn[:], op0=MUL, op1=MUL), [kk32, rn], [nkk]))
        T.append(lambda: P.op('dve', lambda e: e.scalar_tensor_tensor(out=kka[:], in0=nkk[:], scalar=-1.0, in1=a32[:], op0=MUL, op1=MUL), [nkk, a32], [kka]))
        T.append(lambda: P.op('pool', lambda e: e.tensor_scalar(out=t1[:], in0=a32[:], scalar1=-1.0, scalar2=pcc('k_a', c), op0=ADD, op1=MUL), [a32, pc], [t1]))
        T.append(lambda: P.op('dve', lambda e: e.scalar_tensor_tensor(out=keff[:], in0=t1[:], scalar=1.0, in1=k32[:], op0=ADD, op1=MUL), [t1, k32], [keff]))
        T.append(projshift(r32, c, 'r', c))
        rkr = bfv(t1)[:, 0:TT]
        T.append(lambda: P.op('dve', lambda e: e.scalar_tensor_tensor(out=rkr, in0=r32[:], scalar=pcc('r_k', c), in1=keff[:], op0=MUL, op1=MUL), [r32, pc, keff], [t1]))
        T.append(projshift(v32, 16 + c, 'v', 16 + c))

        def f_bon():
            ps = nextps()
            P.op('pe', lambda e: e.matmul(ps[:], lhsT=bdones_bf[:], rhs=rkr, start=True, stop=True), [bdones_bf, t1], [ps])
            P.op('dve', lambda e: e.tensor_tensor(out=bon[:], in0=ps[:], in1=v32[:], op=MUL), [ps, v32], [bon])
        T.append(f_bon)

        def f_gr():
            ps = proj_x('w0in', 33 + c)
            P.op('act', lambda e: e.activation(out=gr32[:], in_=ps[:], func=AF.Silu), [ps], [gr32])
        T.append(f_gr)
        return T

    def rwkv_blockdiag(c, bdv):
        B_ = rw_bufs(c)
        Rv, Av, Bv, Kv, Vv = bdv
        Rbd, Abd, Bbd, Kbd, Vbd = H[0:5]
        for hh in range(2):
            hs_ = slice(hh * 64, hh * 64 + 64)
            for i_, (dstv, dtb, a_, b_) in enumerate(((Rv, Rbd, B_['r32'], B_['Gt']), (Av, Abd, B_['nkk'], B_['Gp']), (Bv, Bbd, B_['kka'], B_['Ginv']), (Kv, Kbd, B_['keff'], B_['Ginv']))):
                P.op('dve' if (i_ + hh) % 2 == 0 else 'pool', lambda e, dstv=dstv, a_=a_, b_=b_, hs_=hs_: e.tensor_tensor(out=dstv[hs_, :, hs_], in0=r3(a_[hs_, :]), in1=r3(b_[hs_, :]), op=MUL),
                     [a_, b_], [dtb])
            P.op('act', lambda e, hs_=hs_: e.activation(out=Vv[hs_, :, hs_], in_=r3(B_['v32'][hs_, :]), func=AF.Copy), [B_['v32']], [Vbd])

    def rwkv_chunks(c, bdv, pump):
        B_ = rw_bufs(c)
        Gt, y32 = B_['Gt'], B_['lw']
        Rv, Av, Bv, Kv, Vv = bdv
        Rbd, Abd, Bbd, Kbd, Vbd = H[0:5]
        for hf in range(2):
            SCt = [H[5], H[6]]
            QTt = [H[7], H[8]]
            SCv = [RV(t).rearrange("p (j m k) -> p j m k", j=2, m=4) for t in SCt]
            QTv = [RV(t).rearrange("p (j m k) -> p j m k", j=2, m=4) for t in QTt]
            SCf = [t[:].rearrange("p (j m k) -> p j m k", j=2, m=4) for t in SCt]
            PQt = [H[9], H[10]]
            PQv = [[RV(PQt[b]).rearrange("p (q j m k) -> p q j m k", q=2, j=2, m=2)[:, par] for b in range(2)] for par in range(2)]
            Xt = H[11]
            Xv = [RV(Xt).rearrange("p (q j k) -> p q j k", q=2, j=4)[:, q] for q in range(2)]
            Xf = [Xt[:].rearrange("p (q j k) -> p q j k", q=2, j=4)[:, q] for q in range(2)]
            for j in range(4):
                n = hf * 4 + j
                sc = SCv[j // 2][:, j % 2]
                qt = QTv[j // 2][:, j % 2]
                A_, B2_ = (PS[2], PS[3]) if j % 2 == 0 else (PS[4], PS[5])
                for m, (l_, r_) in enumerate(((Bv, Av), (Kv, Av), (Bv, Rv), (Kv, Rv))):
                    P.op('pe', lambda e, m=m, l_=l_, r_=r_, n=n: e.matmul(A_[:, m * 128:(m + 1) * 128], lhsT=l_[:, n, :], rhs=r_[:, n, :], start=True, stop=True),
                         [Rbd, Abd, Bbd, Kbd], [A_])
                P.op('dve', lambda e, sc=sc: e.tensor_tensor(out=sc, in0=A_[:].rearrange("p (m k) -> p m k", m=4), in1=maskA[:], op=MUL),
                     [A_, maskA], [SCt[j // 2]])
                P.op('pe', lambda e, n=n: e.matmul(B2_[:, 0:128], lhsT=Av[:, n, :], rhs=Bv[:, n, :], start=True, stop=True), [Abd, Bbd], [B2_])
                for m, l_ in enumerate((Bv, Kv, Vv)):
                    P.op('pe', lambda e, m=m, l_=l_, n=n: e.matmul(B2_[:, (m + 1) * 128:(m + 2) * 128], lhsT=l_[:, n, :], rhs=identr[:], start=True, stop=True),
                         [Bbd, Kbd, Vbd, identr], [B2_])
                P.op('dve', lambda e, qt=qt: e.tensor_tensor(out=qt[:, 0, :], in0=B2_[:, 0:128], in1=maskB[:], op=MUL), [B2_, maskB], [QTt[j // 2]])
                P.op('act', lambda e, qt=qt: e.activation(out=qt[:, 1:4, :], in_=B2_[:, 128:512].rearrange("p (m k) -> p m k", m=3), func=AF.Copy),
                     [B2_], [QTt[j // 2]])
                pump(1)
            for j in range(4):
                P.op('dve', lambda e, j=j: e.tensor_tensor(out=Xv[1][:, j, :], in0=SCf[j // 2][:, j % 2, 0, :], in1=ident[:], op=ADD),
                     [SCt[j // 2], ident], [Xt])
            for lvl in range(1, 6):
                par = lvl % 2
                for b in range(2):
                    bank = PS[4 + b]
                    for jj in range(2):
                        j = 2 * b + jj
                        if lvl == 1:
                            Pp = SCv[j // 2][:, j % 2, 0, :]
                            Qp = QTv[j // 2][:, j % 2, 0, :]
                            rd = [SCt[j // 2], QTt[j // 2]]
                        else:
                            Pp = PQv[1 - par][b][:, jj, 0, :]
                            Qp = PQv[1 - par][b][:, jj, 1, :]
                            rd = [PQt[b]]
                        P.op('pe', lambda e, jj=jj, Pp=Pp, Qp=Qp, bank=bank: e.matmul(bank[:, (2 * jj) * 128:(2 * jj + 1) * 128], lhsT=Qp, rhs=Pp, start=True, stop=True), rd, [bank])
                        P.op('pe', lambda e, jj=jj, Pp=Pp, Qp=Qp, bank=bank: e.matmul(bank[:, (2 * jj + 1) * 128:(2 * jj + 2) * 128], lhsT=Pp, rhs=Qp, start=True, stop=True), rd, [bank])
                    if b == 0:
                        P.op('act', lambda e, b=b, bank=bank, par=par: e.activation(out=PQv[par][b], in_=bank[:].rearrange("p (j m k) -> p j m k", j=2, m=2), func=AF.Copy),
                             [bank], [PQt[b]])
                    else:
                        P.op('dve', lambda e, b=b, bank=bank, par=par: e.tensor_copy(out=PQv[par][b], in_=bank[:].rearrange("p (j m k) -> p j m k", j=2, m=2)),
                             [bank], [PQt[b]])
                XB = PS[6]
                for j in range(4):
                    Ql = PQv[par][j // 2][:, j % 2, 1, :]
                    P.op('pe', lambda e, j=j, Ql=Ql, par=par: e.matmul(XB[:, j * 128:(j + 1) * 128], lhsT=Ql, rhs=Xv[par][:, j, :], start=True, stop=True), [PQt[j // 2], Xt], [XB])
                P.op('dve', lambda e, par=par: e.tensor_tensor(out=Xv[1 - par], in0=XB[:].rearrange("p (j k) -> p j k", j=4), in1=Xf[par], op=ADD), [XB, Xt], [Xt])
                pump(1)
            for j in range(4):
                n = hf * 4 + j
                sc = SCv[j // 2][:, j % 2]
                qt = QTv[j // 2][:, j % 2]
                sct, qtt = SCt[j // 2], QTt[j // 2]
                S0 = Sst[c][spar[c]]
                S1 = Sst[c][1 - spar[c]]
                spar[c] = 1 - spar[c]
                P.op('pool', lambda e, n=n, S0=S0: e.tensor_scalar(out=s0g[:], in0=S0[:], scalar1=Gt[:, n * 64 + 63:n * 64 + 64], scalar2=0.0, op0=MUL, op1=ADD), [S0, Gt], [s0g])
                P.op('pe', lambda e, n=n, S0=S0: e.matmul(psRHS[:], lhsT=Av[:, n, :], rhs=RV(S0), start=True, stop=False), [Abd, S0], [psRHS])
                P.op('pe', lambda e, sc=sc, qt=qt: e.matmul(psRHS[:], lhsT=sc[:, 1, :], rhs=qt[:, 3, :], start=False, stop=True), [sct, qtt], [psRHS])
                P.op('dve', lambda e: e.tensor_copy(out=rhs_sb[:], in_=psRHS[:]), [psRHS], [rhs_sb])
                P.op('pe', lambda e, j=j: e.matmul(psU[:], lhsT=Xv[0][:, j, :], rhs=rhs_sb[:], start=True, stop=True), [Xt, rhs_sb], [psU])
                P.op('dve', lambda e: e.tensor_copy(out=u_sb[:], in_=psU[:]), [psU], [u_sb])
                P.op('pe', lambda e, qt=qt: e.matmul(psSN[:], lhsT=qt[:, 1, :], rhs=u_sb[:], start=True, stop=False), [qtt, u_sb], [psSN])
                P.op('pe', lambda e, qt=qt: e.matmul(psSN[:], lhsT=qt[:, 2, :], rhs=qt[:, 3, :], start=False, stop=True), [qtt], [psSN])
                P.op('dve', lambda e, n=n, S1=S1: e.scalar_tensor_tensor(out=RV(S1), in0=psSN[:], scalar=Gt[:, n * 64 + 63:n * 64 + 64], in1=s0g[:], op0=MUL, op1=ADD), [psSN, Gt, s0g], [S1])
                P.op('pe', lambda e, n=n, S0=S0: e.matmul(psYT[:], lhsT=RV(S0), rhs=Rv[:, n, :], start=True, stop=False), [S0, Rbd], [psYT])
                P.op('pe', lambda e, sc=sc: e.matmul(psYT[:], lhsT=u_sb[:], rhs=sc[:, 2, :], start=False, stop=False), [u_sb, sct], [psYT])
                P.op('pe', lambda e, sc=sc, qt=qt: e.matmul(psYT[:], lhsT=qt[:, 3, :], rhs=sc[:, 3, :], start=False, stop=True), [qtt, sct], [psYT])
                for hh in range(2):
                    hs_ = slice(hh * 64, hh * 64 + 64)
                    P.op('act', lambda e, hs_=hs_, n=n: e.activation(out=y32[hs_, n * 64:(n + 1) * 64], in_=psYT[hs_, hs_], func=AF.Copy), [psYT], [y32])
                pump(3)

    def rwkv_out(c):
        B_ = rw_bufs(c)
        y32, dd, bon, gr32 = B_['lw'], B_['Gt'], B_['cum'], B_['gr32']
        dump('y_rwkv%d' % c, y32, y32[:])
        yrv = RV(y32)
        P.op('act', lambda e: e.activation(out=yrv, in_=y32[:], func=AF.Copy), [y32], [y32])
        ps = nextps()
        P.op('pe', lambda e: e.matmul(ps[:], lhsT=bdones_r[:], rhs=yrv, start=True, stop=True), [bdones_r, y32], [ps])
        P.op('dve', lambda e: e.scalar_tensor_tensor(out=dd[:], in0=ps[:], scalar=-1.0 / 64, in1=y32[:], op0=MUL, op1=ADD), [ps, y32], [dd])
        sq, rs = H[5], H[6]
        sqv = RV(sq)[:, 0:TT]
        P.op('act', lambda e: e.activation(out=sqv, in_=dd[:], func=AF.Square), [dd], [sq])
        ps2 = nextps()
        P.op('pe', lambda e: e.matmul(ps2[:], lhsT=bdones_r[:], rhs=sqv, start=True, stop=True), [bdones_r, sq], [ps2])
        P.op('act', lambda e: e.activation(out=rs[:, 0:TT], in_=ps2[:], func=AF.Ln, scale=1.0 / 64, bias=64e-5), [ps2], [rs])
        P.op('act', lambda e: e.activation(out=rs[:, 0:TT], in_=rs[:, 0:TT], func=AF.Exp, scale=-0.5), [rs], [rs])
        P.op('dve', lambda e: e.tensor_tensor(out=dd[:], in0=dd[:], in1=rs[:, 0:TT], op=MUL), [dd, rs], [dd])
        P.op('act', lambda e: e.activation(out=dd[:], in_=dd[:], func=AF.Identity, scale=pcc('ln_w', c), bias=pcc('ln_b', c)), [dd, pc], [dd])
        P.op('pool', lambda e: e.tensor_tensor(out=dd[:], in0=dd[:], in1=bon[:], op=ADD), [dd, bon], [dd])
        P.op('dve', lambda e: e.tensor_tensor(out=merged[:, c, :], in0=dd[:], in1=gr32[:], op=MUL), [dd, gr32], [merged])

    if do_l0 and do_mlstm:
        causal = cload('causal')
        sel = P.sb("c_sel", [4, 4, 128])
        P.dma('sp', sel[:], din['sel'].ap(), writes=[sel])
        onesf = P.sb("onesf", [128, TT])
        P.op('pool', lambda e: e.memset(onesf[:], 1.0), [], [onesf])
        ones_r = P.sb("ones_r", [128, 128], SDT)
        P.op('dve', lambda e: e.tensor_copy(out=ones_r[:], in_=onesf[:, 0:128]), [onesf], [ones_r])
        m_tail = P.sb("m_tail", [128, 8, 3])
        P.op('pool', lambda e: e.memset(m_tail[:], 0.0), [], [m_tail])
        CT32 = P.sb("CT32", [128, 4, 2, 256])
        CTbf = P.sb("CTbf", [128, 4, 2, 256], BF16)
        nr32 = P.sb("nr32", [128, 4, 2, 128])
        nrbf = P.sb("nrbf", [128, 4, 2, 128], BF16)
        for t_ in (CT32, CTbf, nr32, nrbf):
            P.op('pool', lambda e, t_=t_: e.memset(t_[:], 0.0), [], [t_])
        mcar = P.sb("mcar", [4, 2])
        P.op('pool', lambda e: e.memset(mcar[:], 0.0), [], [mcar])
        MxE = P.sb("MxE", [4, 5])
        dec = P.sb("dec", [4, 4])
        smallb = P.sb("smallb", [128, 32])
        qe = P.sb("qe", [128, 2, 128], BF16)
        qw = P.sb("qw", [128, 2, 128], BF16)
        kg = P.sb("kg", [128, 2, 128], BF16)
        kgt = P.sb("kgt", [128, 256], BF16)
        st_bf = P.sb("st_bf", [128, 128], BF16)
        ddm = P.sb("ddm", [128, 128])
        recm = P.sb("recm", [128, 128])

    def mlstm(t0):
        qTt, kTt, vTt = [H[0], H[1]], [H[2], H[3]], [H[4], H[5]]
        ktt, vtt = [H[6], H[7]], [H[8], H[9]]
        xct = [H[10], H[11]]

        def fm(tl, c):
            return bfv(tl[c // 4]).rearrange("p (k t) -> p k t", k=4)[:, c % 4, :]

        def tokv(tl, tb):
            return bfv(tl[tb // 2]).rearrange("p (b d) -> p b d", b=2)[:, tb % 2, :]
        gi_ps, gf_ps = PS[2], PS[3]
        pstate['set'] = [0, 1, 4, 5, 6, 7]
        for c in range(8):
            acc, xmb = (G[9], G[10]) if c % 2 == 0 else (G[0], G[1])
            xmb_ap = bfv(xmb)[:, 0:TT]
            ext = exts[c % 2]
            ps = proj_x('w0in', 25 + c)
            conv4(ps, c, m_tail, 'mcw', 'mcb', acc, ext)
            P.op('act', lambda e, c=c: e.activation(out=fm(xct, c), in_=acc[:], func=AF.Silu), [acc], [xct[c // 4]])
            P.op('pool', lambda e: e.tensor_copy(out=xmb_ap, in_=ext[:, 3:TT + 3]), [ext], [xmb])
            for t_, (dst, src_ap, src_tb) in enumerate(((qTt, fm(xct, c), xct[c // 4]), (kTt, fm(xct, c), xct[c // 4]), (vTt, xmb_ap, xmb))):
                ps = nextps()
                P.op('pe', lambda e, ps=ps, t_=t_, src_ap=src_ap: e.matmul(ps[:], lhsT=wqkv_bf[:, t_, c, :], rhs=src_ap, start=True, stop=True),
                     [wqkv_bf, src_tb], [ps])
                P.op('act' if t_ != 1 else 'dve',
                     (lambda e, ps=ps, dst=dst: e.activation(out=fm(dst, c), in_=ps[:], func=AF.Copy)) if t_ != 1 else
                     (lambda e, ps=ps, dst=dst: e.tensor_copy(out=fm(dst, c), in_=ps[:])), [ps], [dst[c // 4]])
                j = t_ * 8 + c
                first, last = (c == 0 and t_ == 0), (c == 7 and t_ == 2)
                P.op('pe', lambda e, j=j, dst=dst, first=first, last=last: e.matmul(gi_ps[0:4, :], lhsT=wif_bf[:, 0, j, :], rhs=fm(dst, c), start=first, stop=last),
                     [wif_bf, dst[c // 4]], [gi_ps])
                P.op('pe', lambda e, j=j, dst=dst, first=first, last=last: e.matmul(gf_ps[0:4, :], lhsT=wif_bf[:, 1, j, :], rhs=fm(dst, c), start=first, stop=last),
                     [wif_bf, dst[c // 4]], [gf_ps])
            for t_, (dstl, src_ap, src_tb) in ((1, (ktt, fm(xct, c), xct[c // 4])), (2, (vtt, xmb_ap, xmb))):
                ps = nextps()
                for tb in range(NT):
                    P.op('pe', lambda e, ps=ps, tb=tb, t_=t_, src_ap=src_ap: e.matmul(ps[:, tb * 128:(tb + 1) * 128], lhsT=src_ap[:, tb * 128:(tb + 1) * 128], rhs=wqkv_bf[:, t_, c, :],
                                                                                      start=True, stop=True), [wqkv_bf, src_tb], [ps])
                for half in range(2):
                    dv = bfv(dstl[half]).rearrange("p (b d) -> p b d", b=2)[:, :, c * 128:(c + 1) * 128]
                    P.op('act' if half == 0 else 'dve',
                         (lambda e, ps=ps, dv=dv, half=half: e.activation(out=dv, in_=ps[:, half * 256:(half + 1) * 256].rearrange("p (b d) -> p b d", b=2), func=AF.Copy)) if half == 0 else
                         (lambda e, ps=ps, dv=dv, half=half: e.tensor_copy(out=dv, in_=ps[:, half * 256:(half + 1) * 256].rearrange("p (b d) -> p b d", b=2))),
                         [ps], [dstl[half]])
        pstate['set'] = [0, 1]
        rg_, re2, rwi, rem, rli, rFp, rag, rMx, rtmp = G[11], G[12], G[13], G[14], G[15], G[16], G[17], G[18], G[19]
        R4 = slice(0, 4)
        P.op('act', lambda e: e.activation(out=rli[R4, :], in_=gi_ps[R4, :], func=AF.Identity, bias=pc[R4, PC['bif_i']:PC['bif_i'] + 1]), [gi_ps, pc], [rli])
        P.op('act', lambda e: e.activation(out=rtmp[R4, :], in_=gf_ps[R4, :], func=AF.Exp, scale=-1.0, bias=pd[R4, PD['negbf']:PD['negbf'] + 1]), [gf_ps, pd], [rtmp])
        P.op('act', lambda e: e.activation(out=rtmp[R4, :], in_=rtmp[R4, :], func=AF.Ln, bias=1.0), [rtmp], [rtmp])
        P.op('dve', lambda e: e.tensor_tensor_scan(out=rFp[R4, :], data0=onesf[R4, :], data1=rtmp[R4, :], initial=mcar[:, 0:1], op0=MUL, op1=ADD),
             [onesf, rtmp, mcar], [rFp])
        P.op('pool', lambda e: e.tensor_tensor(out=rag[R4, :], in0=rli[R4, :], in1=rFp[R4, :], op=ADD), [rli, rFp], [rag])
        P.op('dve', lambda e: e.tensor_tensor_scan(out=rMx[R4, :], data0=rag[R4, :], data1=rag[R4, :], initial=mcar[:, 1:2], op0=MAX, op1=MAX),
             [rag, mcar], [rMx])
        P.op('pool', lambda e: e.tensor_copy(out=MxE[:, 0:1], in_=mcar[:, 1:2]), [mcar], [MxE])
        P.op('pool', lambda e: e.tensor_copy(out=MxE[:, 1:5], in_=rMx[R4, 127:TT:128]), [rMx], [MxE])
        P.op('pool', lambda e: e.tensor_copy(out=mcar[:, 0:1], in_=rFp[R4, TT - 1:TT]), [rFp], [mcar])
        P.op('pool', lambda e: e.tensor_copy(out=mcar[:, 1:2], in_=rMx[R4, TT - 1:TT]), [rMx], [mcar])
        v3 = lambda t_: t_[R4, :].rearrange("p (q t) -> p q t", q=NT)
        mend = MxE[:, 1:5].unsqueeze(2).broadcast_to([4, NT, 128])
        mprev = MxE[:, 0:4].unsqueeze(2).broadcast_to([4, NT, 128])
        P.op('dve', lambda e: e.tensor_tensor(out=v3(rg_), in0=v3(rag), in1=mend, op=SUB), [rag, MxE], [rg_])
        P.op('act', lambda e: e.activation(out=rg_[R4, :], in_=rg_[R4, :], func=AF.Exp), [rg_], [rg_])
        P.op('dve', lambda e: e.tensor_tensor(out=v3(re2), in0=mend, in1=v3(rMx), op=SUB), [rMx, MxE], [re2])
        P.op('act', lambda e: e.activation(out=re2[R4, :], in_=re2[R4, :], func=AF.Exp), [re2], [re2])
        P.op('dve', lambda e: e.tensor_tensor(out=v3(rwi), in0=mprev, in1=v3(rMx), op=SUB), [rMx, MxE], [rwi])
        P.op('act', lambda e: e.activation(out=rwi[R4, :], in_=rwi[R4, :], func=AF.Exp), [rwi], [rwi])
        P.op('dve', lambda e: e.tensor_tensor(out=rem[R4, :], in0=rFp[R4, :], in1=rMx[R4, :], op=SUB), [rFp, rMx], [rem])
        P.op('act', lambda e: e.activation(out=rem[R4, :], in_=rem[R4, :], func=AF.Exp), [rem], [rem])
        P.op('dve', lambda e: e.tensor_tensor(out=dec[:], in0=MxE[:, 0:4], in1=MxE[:, 1:5], op=SUB), [MxE], [dec])
        P.op('act', lambda e: e.activation(out=dec[:], in_=dec[:], func=AF.Exp), [dec], [dec])
        sp_ = PS[5]
        for q in range(NT):
            P.op('pe', lambda e, q=q: e.matmul(sp_[:, q * 4:(q + 1) * 4], lhsT=rg_[R4, q * 128:(q + 1) * 128], rhs=ident[0:4, 0:4], start=True, stop=True),
                 [rg_, ident], [sp_])
        for h in range(4):
            P.op('pe', lambda e, h=h: e.matmul(sp_[:, 16 + h * 4:16 + (h + 1) * 4], lhsT=sel[:, h, :], rhs=dec[:], start=True, stop=True), [sel, dec], [sp_])
        P.op('act', lambda e: e.activation(out=smallb[:], in_=sp_[:, 0:32], func=AF.Copy), [sp_], [smallb])
        rows = (rg_, re2, rwi, rem)
        pstate['set'] = [0, 1, 2, 3, 4]
        bcsets = [G[0:4], G[15:19]]
        mlstm_bcast(0, bcsets[0], rows)
        for h in range(4):
            if h + 1 < 4:
                mlstm_bcast(h + 1, bcsets[(h + 1) % 2], rows)
            mlstm_head(h, qTt, kTt, ktt, vtt, xct, fm, tokv, bcsets[h % 2])

    def mlstm_bcast(h, bc, rows):
        for i in range(4):
            ps = nextps()
            P.op('pe', lambda e, ps=ps, i=i: e.matmul(ps[:], lhsT=sel[:, h, :], rhs=rows[i][0:4, :], start=True, stop=True), [sel, rows[i]], [ps])
            P.op('act' if i % 2 == 0 else 'dve',
                 (lambda e, ps=ps, i=i: e.activation(out=bc[i][:], in_=ps[:], func=AF.Copy)) if i % 2 == 0 else
                 (lambda e, ps=ps, i=i: e.tensor_copy(out=bc[i][:], in_=ps[:])), [ps], [bc[i]])

    def mlstm_head(h, qTt, kTt, ktt, vtt, xct, fm, tokv, bc):
        g_bc, e2_bc, wi_bc, em_bc = bc
        h32 = [G[4], G[5]]
        gms = [G[6], G[7]]
        for vc in range(2):
            ps = proj_x('w0in', 41 + 2 * h + vc)
            P.op('act', lambda e, ps=ps, vc=vc: e.activation(out=gms[vc][:], in_=ps[:], func=AF.Silu), [ps], [gms[vc]])
        qt_ = qTt[h // 2]
        kt_ = kTt[h // 2]
        qv = bfv(qt_).rearrange("p (k t) -> p k t", k=4)[:, 2 * (h % 2):2 * (h % 2) + 2, :]
        kv = bfv(kt_).rearrange("p (k t) -> p k t", k=4)[:, 2 * (h % 2):2 * (h % 2) + 2, :]
        CB, SB_, NB = PS[6], PS[7], PS[5]
        for q in range(NT):
            ts_ = slice(q * 128, (q + 1) * 128)
            bcast = lambda t_: t_[:, ts_].unsqueeze(1).broadcast_to([128, 2, 128])
            P.op('dve', lambda e: e.tensor_tensor(out=qe[:], in0=qv[:, :, ts_], in1=bcast(e2_bc), op=MUL), [qt_, e2_bc], [qe])
            P.op('pool', lambda e: e.tensor_tensor(out=qw[:], in0=qv[:, :, ts_], in1=bcast(wi_bc), op=MUL), [qt_, wi_bc], [qw])
            P.op('dve', lambda e: e.scalar_tensor_tensor(out=kg[:], in0=kv[:, :, ts_], scalar=0.0625, in1=bcast(g_bc), op0=MUL, op1=MUL), [kt_, g_bc], [kg])
            ktok = tokv(ktt, q)[:, h * 256:(h + 1) * 256]
            vtok = tokv(vtt, q)[:, h * 256:(h + 1) * 256]
            P.op('pool', lambda e: e.tensor_scalar(out=kgt[:], in0=ktok, scalar1=smallb[:, q * 4 + h:q * 4 + h + 1], scalar2=0.0625, op0=MUL, op1=MUL),
                 [ktt[q // 2], smallb], [kgt])
            for kc in range(2):
                P.op('pe', lambda e, kc=kc: e.matmul(SB_[:, kc * 256:(kc + 1) * 256], lhsT=kgt[:, kc * 128:(kc + 1) * 128], rhs=vtok, start=True, stop=True),
                     [kgt, vtt[q // 2]], [SB_])
                P.op('pe', lambda e, kc=kc: e.matmul(NB[:, 64 + kc * 128:64 + (kc + 1) * 128], lhsT=kgt[:, kc * 128:(kc + 1) * 128], rhs=ones_bf[:], start=True, stop=True),
                     [kgt, ones_bf], [NB])
            for kc in range(2):
                P.op('pe', lambda e, kc=kc: e.matmul(CB[:, 0:128], lhsT=kg[:, kc, :], rhs=qe[:, kc, :], start=(kc == 0), stop=(kc == 1)), [kg, qe], [CB])
            P.op('dve', lambda e: e.tensor_tensor(out=st_bf[:], in0=CB[:, 0:128], in1=causal[:], op=MUL), [CB, causal], [st_bf])
            for vc in range(2):
                o_ = CB[:, 128 + vc * 128:256 + vc * 128]
                P.op('pe', lambda e, o_=o_, vc=vc: e.matmul(o_, lhsT=vtok[:, vc * 128:(vc + 1) * 128], rhs=st_bf[:], start=True, stop=False), [vtt[q // 2], st_bf], [CB])
                for kc in range(2):
                    P.op('pe', lambda e, o_=o_, vc=vc, kc=kc: e.matmul(o_, lhsT=CTbf[:, h, kc, vc * 128:(vc + 1) * 128], rhs=qw[:, kc, :], start=False, stop=(kc == 1)),
                         [CTbf, qw], [CB])
            o_ = CB[:, 384:512]
            P.op('pe', lambda e, o_=o_: e.matmul(o_, lhsT=ones_bf[:], rhs=st_bf[:], start=True, stop=False), [ones_bf, st_bf], [CB])
            for kc in range(2):
                P.op('pe', lambda e, o_=o_, kc=kc: e.matmul(o_, lhsT=nrbf[:, h, kc, :], rhs=qw[:, kc, :], start=False, stop=(kc == 1)), [nrbf, qw], [CB])
            dcol = smallb[:, 16 + h * 4 + q:16 + h * 4 + q + 1]
            P.op('dve', lambda e: e.scalar_tensor_tensor(out=CT32[:, h], in0=CT32[:, h], scalar=dcol, in1=SB_[:].rearrange("p (k v) -> p k v", k=2), op0=MUL, op1=ADD),
                 [CT32, smallb, SB_], [CT32])
            P.op('act', lambda e: e.activation(out=CTbf[:, h], in_=CT32[:, h], func=AF.Copy), [CT32], [CTbf])
            P.op('dve', lambda e: e.scalar_tensor_tensor(out=nr32[:, h], in0=nr32[:, h], scalar=dcol, in1=NB[:, 64:320].rearrange("p (k v) -> p k v", k=2), op0=MUL, op1=ADD),
                 [nr32, smallb, NB], [nr32])
            P.op('pool', lambda e: e.tensor_copy(out=nrbf[:, h], in_=nr32[:, h]), [nr32], [nrbf])
            P.op('act', lambda e: e.activation(out=ddm[:], in_=CB[:, 384:512], func=AF.Abs), [CB], [ddm])
            P.op('dve', lambda e: e.tensor_tensor(out=ddm[:], in0=ddm[:], in1=em_bc[:, ts_], op=MAX), [ddm, em_bc], [ddm])
            P.op('dve', lambda e: e.tensor_scalar(out=ddm[:], in0=ddm[:], scalar1=1e-6, scalar2=None, op0=ADD), [ddm], [ddm])
            P.op('dve', lambda e: e.reciprocal(out=recm[:], in_=ddm[:]), [ddm], [recm])
            for vc in range(2):
                P.op('dve', lambda e, vc=vc: e.tensor_tensor(out=h32[vc][:, ts_], in0=CB[:, 128 + vc * 128:256 + vc * 128], in1=recm[:], op=MUL), [CB, recm], [h32[vc]])
        dump('h_m%d' % h, h32[0], h32[0][:])
        hr = [G[8], G[9]]
        ps = nextps()
        for vc in range(2):
            P.op('act', lambda e, vc=vc: e.activation(out=RV(hr[vc]), in_=h32[vc][:], func=AF.Copy), [h32[vc]], [hr[vc]])
            P.op('pe', lambda e, ps=ps, vc=vc: e.matmul(ps[:], lhsT=ones_r[:], rhs=RV(hr[vc]), start=(vc == 0), stop=(vc == 1)), [ones_r, hr[vc]], [ps])
        for vc in range(2):
            P.op('dve', lambda e, ps=ps, vc=vc: e.scalar_tensor_tensor(out=h32[vc][:], in0=ps[:], scalar=-1.0 / 256, in1=h32[vc][:], op0=MUL, op1=ADD), [ps, h32[vc]], [h32[vc]])
        ps2 = nextps()
        for vc in range(2):
            P.op('act', lambda e, vc=vc: e.activation(out=RV(hr[vc]), in_=h32[vc][:], func=AF.Square), [h32[vc]], [hr[vc]])
            P.op('pe', lambda e, ps2=ps2, vc=vc: e.matmul(ps2[:], lhsT=ones_r[:], rhs=RV(hr[vc]), start=(vc == 0), stop=(vc == 1)), [ones_r, hr[vc]], [ps2])
        rs = G[10]
        P.op('act', lambda e, ps2=ps2: e.activation(out=rs[:], in_=ps2[:], func=AF.Ln, scale=1.0 / 256, bias=1e-5), [ps2], [rs])
        P.op('act', lambda e: e.activation(out=rs[:], in_=rs[:], func=AF.Exp, scale=-0.5), [rs], [rs])
        for vc in range(2):
            ch = 2 * h + vc
            P.op('dve', lambda e, vc=vc: e.tensor_tensor(out=h32[vc][:], in0=h32[vc][:], in1=rs[:], op=MUL), [h32[vc], rs], [h32[vc]])
            P.op('act', lambda e, vc=vc, ch=ch: e.activation(out=h32[vc][:], in_=h32[vc][:], func=AF.Identity, scale=pcc('mnorm', ch)), [h32[vc], pc], [h32[vc]])
            P.op('dve', lambda e, vc=vc, ch=ch: e.scalar_tensor_tensor(out=h32[vc][:], in0=fm(xct, ch), scalar=pcc('mskip', ch), in1=h32[vc][:], op0=MUL, op1=ADD),
                 [xct[ch // 4], pc, h32[vc]], [h32[vc]])
            P.op('pool', lambda e, vc=vc, ch=ch: e.tensor_tensor(out=merged[:, 8 + ch, :], in0=h32[vc][:], in1=gms[vc][:], op=MUL), [h32[vc], gms[vc]], [merged])

    def layer0(t0):
        rmsnorm_x('mixn0')
        pstate['set'] = [0, 1]
        if do_rwkv:
            rwkv(t0)
        if do_mlstm:
            mlstm(t0)
        pstate['set'] = list(range(8))
        out_proj('w0out')

    if do_l0 and not (do_rwkv and do_mlstm):
        P.op('pool', lambda e: e.memset(merged[:], 0.0), [], [merged])

    for ti in range(ntiles):
        t0 = ti * TT
        for tb in range(NT):
            P.dma('sp', H[tb][:], x_d.ap()[t0 + tb * 128:t0 + (tb + 1) * 128, :], writes=[H[tb]])
        for kc in range(8):
            ps = nextps()
            for tb in range(NT):
                P.op('pe', lambda e, kc=kc, tb=tb, ps=ps: e.transpose(ps[:, tb * 128:(tb + 1) * 128], H[tb][:, kc * 128:(kc + 1) * 128], ident[:]),
                     [H[tb], ident], [ps])
            P.op('act' if kc % 2 == 0 else 'dve',
                 (lambda e, kc=kc, ps=ps: e.activation(out=hT[:, kc, :], in_=ps[:], func=AF.Copy)) if kc % 2 == 0 else
                 (lambda e, kc=kc, ps=ps: e.tensor_copy(out=hT[:, kc, :], in_=ps[:])), [ps], [hT])
        if do_l0:
            layer0(t0)
            ple(0, t0)
        if do_l1:
            layer1(t0)
            ple(1, t0)
        norm_stats()
        for k in range(8):
            of = G[k % 4]
            norm_apply(k, 'finn', of[:], of)
            ps = nextps()
            for tb in range(NT):
                P.op('pe', lambda e, tb=tb, ps=ps, of=of: e.transpose(ps[:, tb * 128:(tb + 1) * 128], of[:, tb * 128:(tb + 1) * 128], ident[:]),
                     [of, ident], [ps])
            for tb in range(NT):
                P.op('act' if k % 2 == 0 else 'dve',
                     (lambda e, k=k, ps=ps, tb=tb: e.activation(out=H[tb][:, k * 128:(k + 1) * 128], in_=ps[:, tb * 128:(tb + 1) * 128], func=AF.Copy)) if k % 2 == 0 else
                     (lambda e, k=k, ps=ps, tb=tb: e.tensor_copy(out=H[tb][:, k * 128:(k + 1) * 128], in_=ps[:, tb * 128:(tb + 1) * 128])),
                     [ps], [H[tb]])
        for tb in range(NT):
            P.dma('sp', o_d.ap()[t0 + tb * 128:t0 + (tb + 1) * 128, :], H[tb][:], reads=[H[tb]])
    P.finish()
    print("sbuf bytes/partition:", P.sbytes, "sems:", P.nsem, "instr:", {e: P.cnt[e] for e in ENGS})
    return nc


def kernel(**inputs):
    x = np.asarray(inputs['x'], np.float32)
    p = np.asarray(inputs['p'], np.float32)
    B, S, _ = x.shape
    shared = host_prep(inputs)
    nc = build_program(S)
    in_maps = []
    for b in range(B):
        m = dict(shared)
        m['x'] = np.ascontiguousarray(x[b])
        m['p'] = np.ascontiguousarray(p[:, b])
        in_maps.append(m)
    res = run_bass_kernel_spmd(nc, in_maps, core_ids=list(range(B)))
    return np.stack([np.asarray(r['out'], np.float32) for r in res.results], axis=0)
```
